# Optimizing a Trainium2 kernel written in Bass

```python
import jax
import jax.numpy as jnp
from jax import lax
import numpy as np

D_MODEL = 1024
BATCH = 8
SEQ = 2048
DEPTH = 4

GRID_W = 64
CTX_LEN = 256
N_MIXERS = 4
N_LAYERS_NA = (DEPTH + N_MIXERS - 1) // N_MIXERS
N_LAYERS_CONV = (DEPTH + N_MIXERS - 2) // N_MIXERS
N_LAYERS_GLA = (DEPTH + N_MIXERS - 3) // N_MIXERS
N_LAYERS_RWKV = (DEPTH + N_MIXERS - 4) // N_MIXERS
N_SUB = 3
ALPHA = (2.0 * DEPTH) ** 0.25
BETA = (8.0 * DEPTH) ** -0.25
LN_EPS = 1e-5
D_FF = 2816
NA_HEADS = 16
NA_HEAD_DIM = D_MODEL // NA_HEADS
NA_KH = 8
NA_KW = 16
NA_QCB = NA_KW
NA_KCB = 2 * NA_KW
NA_NCB = GRID_W // NA_QCB
CONV_WIDTH = 31
GLA_HEADS = 4
GLA_DK = D_MODEL // 2 // GLA_HEADS
GLA_DV = D_MODEL // GLA_HEADS
GLA_GATE_RANK = 16
GLA_NORMALIZER = 16.0
GLA_CHUNK = 64
ROPE_BASE = 10000.0
RW_HEAD = 64
RW_HEADS = D_MODEL // RW_HEAD
RW_DECAY_RANK = 64
RW_A_RANK = 64
RW_GATE_RANK = 128
RW_GN_EPS = 64e-5

kernel_name = 'hybrid_na_conv_gla_rwkv7_dit_block'


def _layer_norm(x, g, b, eps=LN_EPS):
    xf = x.astype(jnp.float32)
    mu = jnp.mean(xf, axis=-1, keepdims=True)
    var = jnp.mean(jnp.square(xf - mu), axis=-1, keepdims=True)
    return ((xf - mu) * lax.rsqrt(var + eps) * g + b).astype(x.dtype)


def _modulate(h, m, j):
    return h * (1 + m[:, :, j, 1]) + m[:, :, j, 0]


def _post_norm(h, y, m, j, g, b):
    return _layer_norm(ALPHA * h + m[:, :, j, 2] * y, g, b)


def _swiglu(h, w13, w2):
    a, u = jnp.split(h @ w13, 2, axis=-1)
    return (jax.nn.silu(a) * u) @ w2


def _rope_1d(x, pos):
    half = x.shape[-1] // 2
    freqs = ROPE_BASE ** (-jnp.arange(half, dtype=jnp.float32) / half)
    ang = pos.astype(jnp.float32)[:, None] * freqs
    cos, sin = jnp.cos(ang)[:, None, :], jnp.sin(ang)[:, None, :]
    x1, x2 = x[..., :half], x[..., half:]
    return jnp.concatenate([x1 * cos - x2 * sin, x1 * sin + x2 * cos], axis=-1).astype(x.dtype)


def _axial_rope(x):
    t = jnp.arange(x.shape[1])
    half = x.shape[-1] // 2
    return jnp.concatenate([_rope_1d(x[..., :half], t // GRID_W), _rope_1d(x[..., half:], t % GRID_W)], axis=-1)


def _centred_shift(h):
    prev = jnp.pad(h, ((0, 0), (1, 0), (0, 0)))[:, :-1]
    nxt = jnp.pad(h, ((0, 0), (0, 1), (0, 0)))[:, 1:]
    return 0.5 * (prev + nxt)


def _neighbourhood_attention(hx, hc, w_qkv, w_o, rpb, ctx_out):
    B, N, D = hx.shape
    L = hc.shape[1]
    rows = N // GRID_W
    kh = min(NA_KH, rows)
    scale = NA_HEAD_DIM ** -0.5
    qkv = (hx @ w_qkv).reshape(B, N, 3, NA_HEADS, NA_HEAD_DIM)
    q, k, v = qkv[:, :, 0] * scale, qkv[:, :, 1], qkv[:, :, 2]
    qkv_c = (hc @ w_qkv).reshape(B, L, 3, NA_HEADS, NA_HEAD_DIM)
    qc, kc, vc = qkv_c[:, :, 0] * scale, qkv_c[:, :, 1], qkv_c[:, :, 2]

    qg = q.reshape(B, rows, NA_NCB, NA_QCB, NA_HEADS, NA_HEAD_DIM)
    kg = k.reshape(B, rows, GRID_W, NA_HEADS, NA_HEAD_DIM)
    vg = v.reshape(B, rows, GRID_W, NA_HEADS, NA_HEAD_DIM)
    qcol = np.arange(GRID_W).reshape(NA_NCB, NA_QCB)
    cstart = np.clip(qcol - NA_KW // 2, 0, GRID_W - NA_KW)
    band0 = np.clip(np.arange(NA_NCB) * NA_QCB - NA_KW // 2, 0, GRID_W - NA_KCB)
    band = band0[:, None] + np.arange(NA_KCB)
    kcol = band[:, None, :]
    col_ok = (kcol >= cstart[..., None]) & (kcol < cstart[..., None] + NA_KW)
    dc_idx = np.clip(kcol - qcol[..., None] + NA_KW - 1, 0, 2 * NA_KW - 2)
    rpb_c = rpb[:, :, dc_idx]

    def row_block(r):
        r0 = jnp.clip(r - kh // 2, 0, rows - kh)
        k_rows = lax.dynamic_slice_in_dim(kg, r0, kh, axis=1)[:, :, band]
        v_rows = lax.dynamic_slice_in_dim(vg, r0, kh, axis=1)[:, :, band]
        q_r = lax.dynamic_index_in_dim(qg, r, axis=1, keepdims=False)
        bias = lax.dynamic_slice_in_dim(rpb_c, r0 - r + NA_KH - 1, kh, axis=1)
        s_loc = jnp.einsum('bjqhd,bajkhd->bhjqak', q_r, k_rows).astype(jnp.float32)
        s_loc = s_loc + bias.transpose(0, 2, 3, 1, 4)[None].astype(jnp.float32)
        s_loc = jnp.where(col_ok[:, :, None, :], s_loc, -jnp.inf)
        s_ctx = jnp.einsum('bjqhd,blhd->bhjql', q_r, kc).astype(jnp.float32)
        s = jnp.concatenate([s_loc.reshape(B, NA_HEADS, NA_NCB, NA_QCB, kh * NA_KCB), s_ctx], axis=-1)
        p = jax.nn.softmax(s, axis=-1).astype(v.dtype)
        p_loc = p[..., :kh * NA_KCB].reshape(B, NA_HEADS, NA_NCB, NA_QCB, kh, NA_KCB)
        p_ctx = p[..., kh * NA_KCB:]
        return (jnp.einsum('bhjqak,bajkhd->bjqhd', p_loc, v_rows)
                + jnp.einsum('bhjql,blhd->bjqhd', p_ctx, vc))

    o = lax.map(row_block, jnp.arange(rows))
    yx = jnp.moveaxis(o, 0, 1).reshape(B, N, D) @ w_o
    yc = None
    if ctx_out:
        sc = jnp.einsum('blhd,bmhd->bhlm', qc, kc).astype(jnp.float32)
        pc = jax.nn.softmax(sc, axis=-1).astype(vc.dtype)
        yc = jnp.einsum('bhlm,bmhd->blhd', pc, vc).reshape(B, L, D) @ w_o
    return yx, yc


def _conv_module(hx, hc, w1, b1, w_dw, b_dw, ln_g, ln_b, w2, b2, ctx_out):
    def branch(h):
        a, gate = jnp.split(h @ w1 + b1, 2, axis=-1)
        u = a * jax.nn.sigmoid(gate)
        u = lax.conv_general_dilated(u, w_dw[:, None, :], (1,), [(CONV_WIDTH // 2, CONV_WIDTH // 2)],
                                     dimension_numbers=('NWC', 'WIO', 'NWC'),
                                     feature_group_count=u.shape[-1]) + b_dw
        u = jax.nn.silu(_layer_norm(u, ln_g, ln_b))
        return u @ w2 + b2
    return branch(hx), (branch(hc) if ctx_out else None)


def _gla_chunked(q, k, v, log_a, s0):
    B, T, H, _ = q.shape
    dv = v.shape[-1]
    n = T // GLA_CHUNK
    chunks = lambda t: t.astype(jnp.float32).reshape(B, n, GLA_CHUNK, H, t.shape[-1])
    q, k, v, log_a = chunks(q), chunks(k), chunks(v), chunks(log_a)
    b = jnp.cumsum(log_a, axis=2)
    b_last = b[:, :, -1:]
    qd = q * jnp.exp(b)
    kd = k * jnp.exp(-b)
    kt = k * jnp.exp(b_last - b)
    mask = jnp.tril(jnp.ones((GLA_CHUNK, GLA_CHUNK), bool))
    att = jnp.where(mask, jnp.einsum('bnchd,bnshd->bnhcs', qd, kd), 0.0)
    o = jnp.einsum('bnhcs,bnshe->bnche', att, v)
    chunk_state = jnp.einsum('bnchd,bnche->nbhde', kt, v)
    chunk_decay = jnp.moveaxis(jnp.exp(b_last[:, :, 0]), 1, 0)

    def step(s, inp):
        dec, st = inp
        return dec[..., None] * s + st, s

    s_fin, s_in = lax.scan(step, s0, (chunk_decay, chunk_state))
    o = o + jnp.einsum('bnchd,nbhde->bnche', qd, s_in)
    return o.reshape(B, T, H, dv), s_fin


def _gla_bidir(q, k, v, log_a, s_f, s_b):
    fl = lambda t: jnp.flip(t, axis=1)
    o_f, s_f = _gla_chunked(q, k, v, log_a[0], s_f)
    o_b, s_b = _gla_chunked(fl(q), fl(k), fl(v), fl(log_a[1]), s_b)
    return o_f + fl(o_b), s_f, s_b


def _gla(hx, hc, w_in, w_a1, w_a2, b_a, norm_g, w_o, ctx_out):
    nqk = GLA_HEADS * GLA_DK

    def project(h, rotary):
        B, T, _ = h.shape
        q, k, v, g = jnp.split(h @ w_in, [nqk, 2 * nqk, 2 * nqk + D_MODEL], axis=-1)
        q = q.reshape(B, T, GLA_HEADS, GLA_DK) * GLA_DK ** -0.5
        k = k.reshape(B, T, GLA_HEADS, GLA_DK)
        if rotary:
            q, k = _axial_rope(q), _axial_rope(k)
        v = v.reshape(B, T, GLA_HEADS, GLA_DV)
        log_a = [(jax.nn.log_sigmoid(((h @ w_a1[d]) @ w_a2[d] + b_a[d]).astype(jnp.float32))
                  / GLA_NORMALIZER).reshape(B, T, GLA_HEADS, GLA_DK) for d in range(2)]
        return q, k, v, g, log_a

    def finish(o, g, dtype):
        B, T, H, dv = o.shape
        o = o * lax.rsqrt(jnp.mean(o * o, axis=-1, keepdims=True) + LN_EPS) * norm_g
        o = o * jax.nn.silu(g.astype(jnp.float32)).reshape(B, T, H, dv)
        return o.reshape(B, T, H * dv).astype(dtype) @ w_o

    qc, kc, vc, gc, lac = project(hc, False)
    qx, kx, vx, gx, lax_ = project(hx, True)
    s0 = jnp.zeros((hx.shape[0], GLA_HEADS, GLA_DK, GLA_DV), jnp.float32)
    oc, sc_f, sc_b = _gla_bidir(qc, kc, vc, lac, s0, s0)
    ox, _, _ = _gla_bidir(qx, kx, vx, lax_, sc_f, sc_b)
    return finish(ox, gx, hx.dtype), (finish(oc, gc, hc.dtype) if ctx_out else None)


def _rwkv7_inputs(h, mu, w_rkv, w0, w1, w2, a0, a1, a2, g1, g2, k_k, k_a):
    B, T, _ = h.shape
    heads = lambda t: t.astype(jnp.float32).reshape(B, T, RW_HEADS, RW_HEAD)
    xx = _centred_shift(h) - h
    xr, xw, xk, xv, xa, xg = (h + xx * mu[j] for j in range(6))
    r = heads(xr @ w_rkv[0])
    k = xk @ w_rkv[1]
    v = heads(xv @ w_rkv[2])
    g = jax.nn.sigmoid(xg @ g1) @ g2
    kk = heads(k * k_k)
    kk = kk / jnp.maximum(jnp.sqrt(jnp.sum(kk * kk, axis=-1, keepdims=True)), 1e-12)
    dirs = []
    for d in range(2):
        w_log = -jax.nn.softplus(-(w0[d] + jnp.tanh(xw @ w1[d]) @ w2[d])) - 0.5
        decay = jnp.exp(-jnp.exp(heads(w_log)))
        a = jax.nn.sigmoid(a0[d] + (xa @ a1[d]) @ a2[d])
        k_d = heads(k * (1 + (a - 1) * k_a))
        dirs.append((decay, k_d, kk * heads(a)))
    return r, v, kk, g, dirs


def _rwkv7_scan(r, decay, k, v, kk, b, s0):
    def step(s, inp):
        r_t, w_t, k_t, v_t, kk_t, b_t = inp
        sa = jnp.einsum('bhvk,bhk->bhv', s, -kk_t)
        s = s * w_t[:, :, None, :] + sa[..., None] * b_t[:, :, None, :] + v_t[..., None] * k_t[:, :, None, :]
        return s, jnp.einsum('bhvk,bhk->bhv', s, r_t)
    xs = tuple(jnp.moveaxis(t, 1, 0) for t in (r, decay, k, v, kk, b))
    s_fin, y = lax.scan(step, s0, xs)
    return jnp.moveaxis(y, 0, 1), s_fin


def _rwkv7_bidir(r, v, kk, dirs, s_f, s_b):
    (w_f, k_f, b_f), (w_b, k_b, b_b) = dirs
    fl = lambda t: jnp.flip(t, axis=1)
    y_f, s_f = _rwkv7_scan(r, w_f, k_f, v, kk, b_f, s_f)
    y_b, s_b = _rwkv7_scan(fl(r), fl(w_b), fl(k_b), fl(v), fl(kk), fl(b_b), s_b)
    return y_f + fl(y_b), s_f, s_b


def _rwkv7(hx, hc, mu, w_rkv, w0, w1, w2, a0, a1, a2, g1, g2, k_k, k_a, r_k, gn_g, gn_b, w_o, ctx_out):
    p = (mu, w_rkv, w0, w1, w2, a0, a1, a2, g1, g2, k_k, k_a)

    def finish(y, r, v, dirs, g, dtype):
        B, T, H, N = y.shape
        m = jnp.mean(y, axis=-1, keepdims=True)
        var = jnp.mean(jnp.square(y - m), axis=-1, keepdims=True)
        yn = ((y - m) * lax.rsqrt(var + RW_GN_EPS)).reshape(B, T, H * N) * gn_g + gn_b
        bonus = (jnp.sum(r * dirs[0][1] * r_k, axis=-1, keepdims=True)
                 + jnp.sum(r * dirs[1][1] * r_k, axis=-1, keepdims=True)) * v
        return ((yn + bonus.reshape(B, T, H * N)) * g).astype(dtype) @ w_o

    rc, vc, kkc, gc, dc = _rwkv7_inputs(hc, *p)
    rx, vx, kkx, gx, dx = _rwkv7_inputs(hx, *p)
    s0 = jnp.zeros((hx.shape[0], RW_HEADS, RW_HEAD, RW_HEAD), jnp.float32)
    yc, sc_f, sc_b = _rwkv7_bidir(rc, vc, kkc, dc, s0, s0)
    yx, _, _ = _rwkv7_bidir(rx, vx, kkx, dx, sc_f, sc_b)
    return finish(yx, rx, vx, dx, gx, hx.dtype), (finish(yc, rc, vc, dc, gc, hc.dtype) if ctx_out else None)


def setup_inputs(seed: int = 0) -> dict:
    key = jax.random.key(seed)
    ks = iter(jax.random.split(key, 64))
    nrm = lambda shape, s: s * jax.random.normal(next(ks), shape, jnp.float32)
    uni = lambda shape, lo, hi: jax.random.uniform(next(ks), shape, jnp.float32, lo, hi)
    D = D_MODEL
    fan = D ** -0.5
    na, nb, nc, nd = N_LAYERS_NA, N_LAYERS_CONV, N_LAYERS_GLA, N_LAYERS_RWKV
    return {
        'x': nrm((BATCH, SEQ, D), 1.0),
        'c': nrm((BATCH, D), 1.0),
        'ctx': nrm((BATCH, CTX_LEN, D), 1.0),
        'c_ctx': nrm((D,), 1.0),
        'ada_w': nrm((DEPTH, D, N_SUB * 3 * D), fan),
        'ada_b': nrm((DEPTH, N_SUB * 3 * D), 0.02),
        'ln_g': 1.0 + nrm((DEPTH, N_SUB, D), 0.02),
        'ln_b': nrm((DEPTH, N_SUB, D), 0.02),
        'ffn_w13': nrm((DEPTH, 2, D, 2 * D_FF), fan),
        'ffn_w2': nrm((DEPTH, 2, D_FF, D), D_FF ** -0.5 * BETA),
        'na_wqkv': nrm((na, D, 3 * D), fan),
        'na_wo': nrm((na, D, D), fan * BETA),
        'na_rpb': nrm((na, NA_HEADS, 2 * NA_KH - 1, 2 * NA_KW - 1), 0.1),
        'cv_w1': nrm((nb, D, 2 * D), fan),
        'cv_b1': nrm((nb, 2 * D), 0.02),
        'cv_wdw': nrm((nb, CONV_WIDTH, D), CONV_WIDTH ** -0.5),
        'cv_bdw': nrm((nb, D), 0.02),
        'cv_ln_g': 1.0 + nrm((nb, D), 0.02),
        'cv_ln_b': nrm((nb, D), 0.02),
        'cv_w2': nrm((nb, D, D), fan * BETA),
        'cv_b2': nrm((nb, D), 0.02),
        'gla_win': nrm((nc, D, 2 * GLA_HEADS * GLA_DK + 2 * D), fan),
        'gla_wa1': nrm((nc, 2, D, GLA_GATE_RANK), fan),
        'gla_wa2': nrm((nc, 2, GLA_GATE_RANK, GLA_HEADS * GLA_DK), GLA_GATE_RANK ** -0.5),
        'gla_ba': nrm((nc, 2, GLA_HEADS * GLA_DK), 0.02),
        'gla_norm_g': 1.0 + nrm((nc, GLA_DV), 0.02),
        'gla_wo': nrm((nc, D, D), fan * BETA),
        'rw_mu': uni((nd, 6, D), 0.0, 1.0),
        'rw_wrkv': nrm((nd, 3, D, D), fan),
        'rw_w0': uni((nd, 2, D), -6.0, -1.0),
        'rw_w1': nrm((nd, 2, D, RW_DECAY_RANK), fan),
        'rw_w2': nrm((nd, 2, RW_DECAY_RANK, D), 0.1 * RW_DECAY_RANK ** -0.5),
        'rw_a0': nrm((nd, 2, D), 0.1),
        'rw_a1': nrm((nd, 2, D, RW_A_RANK), fan),
        'rw_a2': nrm((nd, 2, RW_A_RANK, D), 0.1 * RW_A_RANK ** -0.5),
        'rw_g1': nrm((nd, D, RW_GATE_RANK), fan),
        'rw_g2': nrm((nd, RW_GATE_RANK, D), RW_GATE_RANK ** -0.5),
        'rw_kk': 0.85 + nrm((nd, D), 0.02),
        'rw_ka': 1.0 + nrm((nd, D), 0.02),
        'rw_rk': nrm((nd, RW_HEADS, RW_HEAD), 0.05),
        'rw_gn_g': 1.0 + nrm((nd, D), 0.02),
        'rw_gn_b': nrm((nd, D), 0.02),
        'rw_wo': nrm((nd, D, D), fan * BETA),
    }


def reference(x, c, ctx, c_ctx, ada_w, ada_b, ln_g, ln_b, ffn_w13, ffn_w2,
              na_wqkv, na_wo, na_rpb,
              cv_w1, cv_b1, cv_wdw, cv_bdw, cv_ln_g, cv_ln_b, cv_w2, cv_b2,
              gla_win, gla_wa1, gla_wa2, gla_ba, gla_norm_g, gla_wo,
              rw_mu, rw_wrkv, rw_w0, rw_w1, rw_w2, rw_a0, rw_a1, rw_a2, rw_g1, rw_g2,
              rw_kk, rw_ka, rw_rk, rw_gn_g, rw_gn_b, rw_wo):
    B = x.shape[0]
    h_x, h_c = x, ctx
    cond = jax.nn.silu(jnp.concatenate([c, c_ctx[None, :]], axis=0))
    for i in range(DEPTH):
        last = i == DEPTH - 1
        mod = (cond @ ada_w[i] + ada_b[i]).reshape(B + 1, 1, N_SUB, 3, D_MODEL)
        mx, mc = mod[:B], mod[B:]
        h_x = _post_norm(h_x, 0.5 * _swiglu(_modulate(h_x, mx, 0), ffn_w13[i, 0], ffn_w2[i, 0]), mx, 0, ln_g[i, 0], ln_b[i, 0])
        h_c = _post_norm(h_c, 0.5 * _swiglu(_modulate(h_c, mc, 0), ffn_w13[i, 0], ffn_w2[i, 0]), mc, 0, ln_g[i, 0], ln_b[i, 0])
        ux, uc = _modulate(h_x, mx, 1), _modulate(h_c, mc, 1)
        kind, j = i % N_MIXERS, i // N_MIXERS
        if kind == 0:
            yx, yc = _neighbourhood_attention(ux, uc, na_wqkv[j], na_wo[j], na_rpb[j], not last)
        elif kind == 1:
            yx, yc = _conv_module(ux, uc, cv_w1[j], cv_b1[j], cv_wdw[j], cv_bdw[j], cv_ln_g[j], cv_ln_b[j],
                                  cv_w2[j], cv_b2[j], not last)
        elif kind == 2:
            yx, yc = _gla(ux, uc, gla_win[j], gla_wa1[j], gla_wa2[j], gla_ba[j], gla_norm_g[j], gla_wo[j], not last)
        else:
            yx, yc = _rwkv7(ux, uc, rw_mu[j], rw_wrkv[j], rw_w0[j], rw_w1[j], rw_w2[j], rw_a0[j], rw_a1[j], rw_a2[j],
                            rw_g1[j], rw_g2[j], rw_kk[j], rw_ka[j], rw_rk[j], rw_gn_g[j], rw_gn_b[j], rw_wo[j], not last)
        h_x = _post_norm(h_x, yx, mx, 1, ln_g[i, 1], ln_b[i, 1])
        h_x = _post_norm(h_x, 0.5 * _swiglu(_modulate(h_x, mx, 2), ffn_w13[i, 1], ffn_w2[i, 1]), mx, 2, ln_g[i, 2], ln_b[i, 2])
        if not last:
            h_c = _post_norm(h_c, yc, mc, 1, ln_g[i, 1], ln_b[i, 1])
            h_c = _post_norm(h_c, 0.5 * _swiglu(_modulate(h_c, mc, 2), ffn_w13[i, 1], ffn_w2[i, 1]), mc, 2, ln_g[i, 2], ln_b[i, 2])
    return h_x
```

```python
import numpy as np
from contextlib import ExitStack
import concourse.bass as bass
import concourse.mybir as mybir
from concourse.bass_utils import run_bass_kernel_spmd

F32 = mybir.dt.float32
BF16 = mybir.dt.bfloat16
AF = mybir.ActivationFunctionType
ALU = mybir.AluOpType

D = 1024
NC_ = 8
SEQ = 2048
CTX = 256
T = SEQ + CTX
DEPTH = 4
DFF = 2816
NF = DFF // 128
ALPHA = (2.0 * DEPTH) ** 0.25
LN_EPS = 1e-5
EPS_P = LN_EPS / (ALPHA * ALPHA)

BLOCKS = [(0, 512, 0), (512, 512, 0), (1024, 512, 0), (1536, 512, 0), (2048, 256, 1)]
HALVES = [[0, 1], [2, 3, 4]]


class DebugStop(Exception):
    pass


class Reg:
    __slots__ = ("w", "r", "name")

    def __init__(self, name=""):
        self.w = None
        self.r = {}
        self.name = name


class DSem:
    def __init__(self, sem):
        self.sem = sem
        self.cnt = 0


class KB:
    def __init__(self):
        self.nc = bass.Bass("TRN2", target_bir_lowering=False)
        nc = self.nc
        self.es = ExitStack()
        self.eng = {"pe": nc.tensor, "dve": nc.vector, "act": nc.scalar, "pool": nc.gpsimd, "sp": nc.sync}
        self.sem = {e: self.es.enter_context(nc.semaphore("prog_" + e)) for e in self.eng}
        self.cnt = {e: 0 for e in self.eng}
        self.seen = {e: {} for e in self.eng}
        self.dsems = []
        self.n_ins = 0

    def dsem(self, name):
        self.n_ds = getattr(self, "n_ds", 0) + 1
        d = DSem(self.es.enter_context(self.nc.semaphore(f"d_{name}_{self.n_ds}")))
        self.dsems.append(d)
        return d

    def sb(self, es, name, shape, dt):
        self.n_sb = getattr(self, "n_sb", 0) + 1
        return es.enter_context(self.nc.sbuf_tensor(f"{name}_s{self.n_sb}", shape, dt))

    def _wait(self, e, ev):
        if ev is None:
            return
        sem, val = ev[0], ev[1]
        if len(ev) > 2:
            val = max(val, ev[2].cnt)
        if e == "pe" and sem is self.sem["pe"]:
            return
        k = id(sem)
        if self.seen[e].get(k, 0) >= val:
            return
        self.eng[e].wait_ge(sem, val)
        self.seen[e][k] = val

    def _deps(self, e, reads, writes):
        for r in reads:
            self._wait(e, r.w)
        for r in writes:
            self._wait(e, r.w)
            for ev in r.r.values():
                self._wait(e, ev)

    def op(self, e, fn, reads=(), writes=()):
        self._deps(e, reads, writes)
        ins = fn(self.eng[e])
        self.cnt[e] += 1
        self.n_ins += 1
        ins.then_inc(self.sem[e], 1)
        ev = (self.sem[e], self.cnt[e])
        for r in reads:
            r.r[id(ev[0])] = ev
        for r in writes:
            r.w = ev
            r.r = {}
        return ev

    def dma(self, q, ds, out, in_, reads=(), writes=()):
        self._deps(q, reads, writes)
        if ds.cnt > 0:
            self._wait(q, (ds.sem, ds.cnt, ds))
        ins = self.eng[q].dma_start(out=out, in_=in_)
        ds.cnt += 16
        self.n_ins += 1
        ins.then_inc(ds.sem, 16)
        ev = (ds.sem, ds.cnt, ds)
        for r in reads:
            r.r[id(ev[0])] = ev
        for r in writes:
            r.w = ev
            r.r = {}
        return ev

    def barrier(self):
        for e in self.eng:
            for e2 in self.eng:
                if e2 != e and self.cnt[e2] > 0:
                    self._wait(e, (self.sem[e2], self.cnt[e2]))
            for d in self.dsems:
                if d.cnt > 0:
                    self._wait(e, (d.sem, d.cnt))


def midx(j, k, c):
    return j * 24 + k * 8 + c


class Prog:
    def __init__(self, layers, final_layer_is_last=True, load_h=True):
        self.kb = KB()
        kb = self.kb
        nc = kb.nc
        self.nc = nc
        es = kb.es
        self.layers = layers
        dr = lambda name, shape, dt=F32, kind="ExternalInput": nc.dram_tensor(name, shape, dt, kind=kind).ap()
        self.d_hin = dr("hin", [NC_, 128, T])
        self.d_hout = dr("hout", [NC_, 128, T], kind="ExternalOutput")
        self.d_cond = dr("cond", [128, NC_, 2])
        self.d_adaw = dr("adaw", [DEPTH, 72, 128, NC_ * 128])
        self.d_adab = dr("adab", [DEPTH, 128, 72])
        self.d_lng = dr("lng", [128, DEPTH * 3 * NC_])
        self.d_lnb = dr("lnb", [128, DEPTH * 3 * NC_])
        self.d_w13 = dr("w13", [DEPTH, 2, NF, 128, 2 * NC_ * 128])
        self.d_w2 = dr("w2", [DEPTH, 2, NC_, 128, NF * 128])
        self.d_ident = dr("ident", [128, 128])
        self.d_rwprm = dr("rwprm", [128, 120])
        self.d_rwcstf = dr("rwcstf", [128, 256])
        self.d_rwgmask = dr("rwgmask", [64, 2 * 5 * 2 * 64])
        self.d_rwlr = dr("rwlr", [3, 128, NC_ * 128])
        self.d_rwlrw = dr("rwlrw", [NC_, 3, 128, 128])
        self.d_rwwrkv = dr("rwwrkv", [3, NC_, 128, NC_ * 128])
        self.d_rwwo = dr("rwwo", [NC_, 128, NC_ * 128])
        self.d_glawqk = dr("glawqk", [8, 128, NC_ * 128])
        self.d_glawqkp = dr("glawqkp", [8, 128, NC_ * 128])
        self.d_glawv = dr("glawv", [4, 128, NC_ * 256])
        self.d_glawg = dr("glawg", [8, 128, NC_ * 128])
        self.d_glawo = dr("glawo", [4, NC_, 128, 2 * 128])
        self.d_glawa1 = dr("glawa1", [128, NC_ * 32])
        self.d_glawa2 = dr("glawa2", [32, 2 * 512])
        self.d_glarope = dr("glarope", [2, 128, SEQ])
        self.d_glacst = dr("glacst", [128, 4 * 64 + 128 + 2 + 8])
        self.d_nawqkv = dr("nawqkv", [3 * NC_, 128, NC_ * 128])
        self.d_nawo = dr("nawo", [NC_, 128, NC_ * 128])
        self.d_nastrip = dr("nastrip", [16, 128, 37 * 64])
        self.d_cvw1 = dr("cvw1", [NC_, 128, 2 * NC_ * 128])
        self.d_cvw2 = dr("cvw2", [NC_, 128, NC_ * 128])
        self.d_cvprm = dr("cvprm", [128, 6 * NC_ + 31 * NC_])
        self.U = kb.sb(es, "U", [128, NC_, T], BF16)
        self.Hr = [[Reg(f"H{c}_{b}") for b in range(5)] for c in range(NC_)]
        self.Ur = [[Reg(f"U{c}_{b}") for b in range(5)] for c in range(NC_)]
        self.P = kb.sb(es, "P", [128, 72, 2], F32)
        self.Pr = Reg("P")
        self.condT = kb.sb(es, "condT", [128, NC_, 2], F32)
        self.condr = Reg("cond")
        self.lng = kb.sb(es, "lng", [128, DEPTH * 3 * NC_], F32)
        self.lnb = kb.sb(es, "lnb", [128, DEPTH * 3 * NC_], F32)
        self.lnr = Reg("ln")
        self.ones_bf = kb.sb(es, "ones_bf", [128, 128], BF16)
        self.onesr = Reg("ones")
        self.epsP = kb.sb(es, "epsP", [128, 1], F32)
        self.onesf = kb.sb(es, "onesf", [128, 64], F32)
        self.ps = [es.enter_context(nc.psum_tensor(f"ps{i}", [128, 512], F32)) for i in range(8)]
        self.psr = [Reg(f"ps{i}") for i in range(8)]
        self.ds_misc = kb.dsem("misc")
        self.ds_out = kb.dsem("out")
        self.ds_h = [kb.dsem(f"h{i}") for i in range(4)]
        self.psrot = 0
        self.es_H = ExitStack()
        self.H = kb.sb(self.es_H, "H", [128, NC_, T], F32)

    def hregs(self, c, t0, n):
        return [self.Hr[c][b] for b, (bt, bn, _) in enumerate(BLOCKS) if bt < t0 + n and t0 < bt + bn]

    def uregs(self, c, t0, n):
        return [self.Ur[c][b] for b, (bt, bn, _) in enumerate(BLOCKS) if bt < t0 + n and t0 < bt + bn]

    def setup(self):
        kb = self.kb
        kb.dma("sp", self.ds_misc, self.condT[:], self.d_cond, writes=[self.condr])
        kb.dma("sp", self.ds_misc, self.lng[:], self.d_lng, writes=[self.lnr])
        kb.dma("sp", self.ds_misc, self.lnb[:], self.d_lnb, writes=[self.lnr])
        for c in range(NC_):
            kb.dma("sp", self.ds_h[c % 4], self.H[:, c, :], self.d_hin[c], writes=self.Hr[c])
        kb.op("dve", lambda e: e.memset(self.ones_bf[:], 1.0 / 1024.0), writes=[self.onesr])
        kb.op("dve", lambda e: e.memset(self.epsP[:], EPS_P), writes=[self.onesr])
        kb.op("dve", lambda e: e.memset(self.onesf[:], 1.0), writes=[self.onesr])
        kb.op("act", lambda e: e.activation(out=self.condT[:], in_=self.condT[:], func=AF.Silu),
              reads=[self.condr], writes=[self.condr])

    def store(self):
        kb = self.kb
        for c in range(NC_):
            kb.dma("sp", self.ds_h[c % 4], self.d_hout[c], self.H[:, c, :], reads=self.Hr[c])
        for d_ in self.ds_h:
            kb.eng["sp"].wait_ge(d_.sem, d_.cnt)

    def ada(self, L):
        kb = self.kb
        kb.barrier()
        with ExitStack() as es:
            NG = 4
            wst = [kb.sb(es, f"adaw{s}", [128, NG, NC_ * 128], BF16) for s in range(2)]
            condb = kb.sb(es, "condb", [128, NC_, 2], BF16)
            kb.op("dve", lambda e: e.tensor_copy(out=condb[:], in_=self.condT[:]), reads=[self.condr], writes=[self.condr])
            wr = [Reg(), Reg()]
            wds = [kb.dsem(f"adaw{L}_{s}") for s in range(2)]
            bia = kb.sb(es, "adab", [128, 72], F32)
            br = Reg()
            kb.dma("sp", self.ds_misc, bia[:], self.d_adab[L], writes=[br])
            pst = self.ps[0]
            psreg = self.psr[0]
            for g in range(72 // NG):
                s = g % 2
                kb.dma("pool", wds[s], wst[s][:], self.d_adaw[L, g * NG:(g + 1) * NG].rearrange("o p f -> p o f"),
                       writes=[wr[s]])
                for o in range(NG):
                    oc = g * NG + o
                    for kc in range(NC_):
                        kb.op("pe", lambda e: e.matmul(pst[:, oc * 2:oc * 2 + 2], lhsT=wst[s][:, o, kc * 128:(kc + 1) * 128],
                                                        rhs=condb[:, kc, :], start=(kc == 0), stop=(kc == NC_ - 1)),
                              reads=[wr[s], self.condr], writes=[psreg])
            for kind in range(2):
                kb.op("dve", lambda e: e.tensor_tensor(out=self.P[:, :, kind], in0=pst[:, 0:144].rearrange("p (o k) -> p o k", k=2)[:, :, kind],
                                                       in1=bia[:], op=ALU.add),
                      reads=[psreg, br], writes=[self.Pr])
            for j in range(3):
                wj = (1.0 if j == 1 else 0.5) / ALPHA
                kb.op("dve", lambda e: e.tensor_scalar(out=self.P[:, midx(j, 1, 0):midx(j, 1, 0) + 8, :], in0=self.P[:, midx(j, 1, 0):midx(j, 1, 0) + 8, :],
                                                       scalar1=1.0, scalar2=None, op0=ALU.add),
                      reads=[self.Pr], writes=[self.Pr])
                kb.op("dve", lambda e: e.tensor_scalar(out=self.P[:, midx(j, 2, 0):midx(j, 2, 0) + 8, :], in0=self.P[:, midx(j, 2, 0):midx(j, 2, 0) + 8, :],
                                                       scalar1=wj, scalar2=None, op0=ALU.mult),
                      reads=[self.Pr], writes=[self.Pr])
            kb.barrier()

    def modulate(self, j, blocks=range(5)):
        kb = self.kb
        for c in range(NC_):
            for b in blocks:
                t0, n, kind = BLOCKS[b]
                kb.op("act", lambda e: e.activation(out=self.U[:, c, t0:t0 + n], in_=self.H[:, c, t0:t0 + n], func=AF.Identity,
                                                    scale=self.P[:, midx(j, 1, c), kind:kind + 1],
                                                    bias=self.P[:, midx(j, 0, c), kind:kind + 1]),
                      reads=[self.Hr[c][b], self.Pr], writes=[self.Ur[c][b]])

    def ffn(self, L, j, kidx, blocks=range(5)):
        kb = self.kb
        blocks = list(blocks)
        self.modulate(j, blocks)
        with ExitStack() as es:
            G = kb.sb(es, "G", [128, NF, 1280], BF16)
            wA = [kb.sb(es, f"wA{s}", [128, 2, NC_, 128], BF16) for s in range(2)]
            wAr = [Reg(), Reg()]
            wAd = [kb.dsem(f"wA{L}{j}{s}") for s in range(2)]
            wB = [kb.sb(es, f"wB{s}", [128, NF, 128], BF16) for s in range(2)]
            wBr = [Reg(), Reg()]
            wBd = [kb.dsem(f"wB{L}{j}{s}") for s in range(2)]
            sa = [kb.sb(es, f"sa{s}", [128, 512], BF16) for s in range(2)]
            sar = [Reg(), Reg()]
            for half in HALVES:
                blks = [b for b in half if b in blocks]
                if not blks:
                    continue
                base = BLOCKS[blks[0]][0]
                Gr = [[Reg() for _ in blks] for _ in range(NF)]
                cntA = 0
                for jf in range(NF):
                    s = jf % 2
                    kb.dma("pool", wAd[s], wA[s][:].rearrange("p s k m -> p (s k m)"), self.d_w13[L, kidx, jf], writes=[wAr[s]])
                    for bi, b in enumerate(blks):
                        t0, n, kind = BLOCKS[b]
                        pa = (cntA % 4) * 2
                        pu = pa + 1
                        ss = cntA % 2
                        cntA += 1
                        for (pb, si) in ((pa, 0), (pu, 1)):
                            for kc in range(NC_):
                                kb.op("pe", lambda e: e.matmul(self.ps[pb][:, :n], lhsT=wA[s][:, si, kc, :], rhs=self.U[:, kc, t0:t0 + n],
                                                                start=(kc == 0), stop=(kc == NC_ - 1)),
                                      reads=[wAr[s], self.Ur[kc][b]], writes=[self.psr[pb]])
                        kb.op("act", lambda e: e.activation(out=sa[ss][:, :n], in_=self.ps[pa][:, :n], func=AF.Silu),
                              reads=[self.psr[pa]], writes=[sar[ss]])
                        kb.op("dve", lambda e: e.tensor_tensor(out=G[:, jf, t0 - base:t0 - base + n], in0=sa[ss][:, :n], in1=self.ps[pu][:, :n], op=ALU.mult),
                              reads=[sar[ss], self.psr[pu]], writes=[Gr[jf][bi]])
                cntB = 0
                for dc in range(NC_):
                    s = dc % 2
                    kb.dma("pool", wBd[s], wB[s][:].rearrange("p f m -> p (f m)"), self.d_w2[L, kidx, dc], writes=[wBr[s]])
                    for bi, b in enumerate(blks):
                        t0, n, kind = BLOCKS[b]
                        pb = cntB % 8
                        cntB += 1
                        for fc in range(NF):
                            kb.op("pe", lambda e: e.matmul(self.ps[pb][:, :n], lhsT=wB[s][:, fc, :], rhs=G[:, fc, t0 - base:t0 - base + n],
                                                            start=(fc == 0), stop=(fc == NF - 1)),
                                  reads=[wBr[s], Gr[fc][bi]], writes=[self.psr[pb]])
                        kb.op("dve", lambda e: e.scalar_tensor_tensor(out=self.H[:, dc, t0:t0 + n], in0=self.ps[pb][:, :n],
                                                                      scalar=self.P[:, midx(j, 2, dc), kind:kind + 1],
                                                                      in1=self.H[:, dc, t0:t0 + n], op0=ALU.mult, op1=ALU.add),
                              reads=[self.psr[pb], self.Pr, self.Hr[dc][b]], writes=[self.Hr[dc][b]])
                kb.barrier()
        self.layernorm(L, j, blocks)

    def layernorm(self, L, j, blocks=range(5)):
        kb = self.kb
        goff = (L * 3 + j) * NC_
        with ExitStack() as es:
            zb = [kb.sb(es, f"zb{s}", [128, NC_, 512], BF16) for s in range(2)]
            z2 = [kb.sb(es, f"z2{s}", [128, NC_, 512], BF16) for s in range(2)]
            zr = [[Reg() for _ in range(NC_)] for _ in range(2)]
            z2r = [[Reg() for _ in range(NC_)] for _ in range(2)]
            msq = [kb.sb(es, f"msq{s}", [128, 512], F32) for s in range(2)]
            rstd = [kb.sb(es, f"rstd{s}", [128, 512], F32) for s in range(2)]
            tmp = [kb.sb(es, f"lntmp{s}", [128, 512], F32) for s in range(4)]
            msqr = [Reg(), Reg()]
            rstdr = [Reg(), Reg()]
            tmpr = [Reg() for _ in range(4)]
            tcnt = 0
            for bi, b in enumerate(blocks):
                t0, n, kind = BLOCKS[b]
                s = bi % 2
                pm = (bi % 4) * 2
                pq = pm + 1
                for c in range(NC_):
                    kb.op("pool", lambda e: e.tensor_copy(out=zb[s][:, c, :n], in_=self.H[:, c, t0:t0 + n]),
                          reads=[self.Hr[c][b]], writes=[zr[s][c]])
                    kb.op("act", lambda e: e.activation(out=z2[s][:, c, :n], in_=self.H[:, c, t0:t0 + n], func=AF.Square),
                          reads=[self.Hr[c][b]], writes=[z2r[s][c]])
                for c in range(NC_):
                    kb.op("pe", lambda e: e.matmul(self.ps[pm][:, :n], lhsT=self.ones_bf[:], rhs=zb[s][:, c, :n],
                                                    start=(c == 0), stop=(c == NC_ - 1)),
                          reads=[self.onesr, zr[s][c]], writes=[self.psr[pm]])
                for c in range(NC_):
                    kb.op("pe", lambda e: e.matmul(self.ps[pq][:, :n], lhsT=self.ones_bf[:], rhs=z2[s][:, c, :n],
                                                    start=(c == 0), stop=(c == NC_ - 1)),
                          reads=[self.onesr, z2r[s][c]], writes=[self.psr[pq]])
                kb.op("act", lambda e: e.activation(out=msq[s][:, :n], in_=self.ps[pm][:, :n], func=AF.Square),
                      reads=[self.psr[pm]], writes=[msqr[s]])
                kb.op("dve", lambda e: e.tensor_tensor(out=msq[s][:, :n], in0=self.ps[pq][:, :n], in1=msq[s][:, :n], op=ALU.subtract),
                      reads=[self.psr[pq], msqr[s]], writes=[msqr[s]])
                kb.op("act", lambda e: e.activation(out=msq[s][:, :n], in_=msq[s][:, :n], func=AF.Ln, bias=self.epsP[:, 0:1]),
                      reads=[msqr[s], self.onesr], writes=[msqr[s]])
                kb.op("act", lambda e: e.activation(out=rstd[s][:, :n], in_=msq[s][:, :n], func=AF.Exp, scale=-0.5),
                      reads=[msqr[s]], writes=[rstdr[s]])
                for c in range(NC_):
                    ts = tcnt % 4
                    tcnt += 1
                    kb.op("dve", lambda e: e.tensor_tensor(out=tmp[ts][:, :n], in0=self.H[:, c, t0:t0 + n], in1=self.ps[pm][:, :n], op=ALU.subtract),
                          reads=[self.Hr[c][b], self.psr[pm]], writes=[tmpr[ts]])
                    kb.op("pool", lambda e: e.tensor_tensor(out=tmp[ts][:, :n], in0=tmp[ts][:, :n], in1=rstd[s][:, :n], op=ALU.mult),
                          reads=[tmpr[ts], rstdr[s]], writes=[tmpr[ts]])
                    kb.op("act", lambda e: e.activation(out=self.H[:, c, t0:t0 + n], in_=tmp[ts][:, :n], func=AF.Identity,
                                                        scale=self.lng[:, goff + c:goff + c + 1], bias=self.lnb[:, goff + c:goff + c + 1]),
                          reads=[tmpr[ts], self.lnr], writes=[self.Hr[c][b]])
            kb.barrier()


    def proj(self, es, Wd, ocs, src, src_regs, evac, blocks, nk=NC_, tag="pj", src_off=0):
        kb = self.kb
        w = [kb.sb(es, f"{tag}w{s}", [128, nk, 128], BF16) for s in range(2)]
        wr = [Reg(), Reg()]
        wd = [kb.dsem(f"{tag}{s}") for s in range(2)]
        cnt = 0
        for i, oc in enumerate(ocs):
            s = i % 2
            kb.dma("pool", wd[s], w[s][:].rearrange("p k m -> p (k m)"), Wd[oc], writes=[wr[s]])
            for b in blocks:
                t0, n, kind = BLOCKS[b]
                pb = self.psrot % 8
                self.psrot += 1
                for kc in range(nk):
                    kb.op("pe", lambda e: e.matmul(self.ps[pb][:, :n], lhsT=w[s][:, kc, :], rhs=src[:, kc, src_off + t0:src_off + t0 + n],
                                                    start=(kc == 0), stop=(kc == nk - 1)),
                          reads=[wr[s], src_regs[kc][b]], writes=[self.psr[pb]])
                evac(oc, b, pb, t0, n, kind)

    def resid_evac(self, j, bias=None, bias_reg=None, es=None):
        kb = self.kb
        if bias is not None:
            tmpy = [kb.sb(es, f"tmpy{s}", [128, 512], F32) for s in range(2)]
            tmpr = [Reg(), Reg()]
        state = {"c": 0}

        def evac(dc, b, pb, t0, n, kind):
            if bias is not None:
                s = state["c"] % 2
                state["c"] += 1
                kb.op("act", lambda e: e.activation(out=tmpy[s][:, :n], in_=self.ps[pb][:, :n], func=AF.Identity, bias=bias[:, dc:dc + 1]),
                      reads=[self.psr[pb], bias_reg], writes=[tmpr[s]])
                src, sreg = tmpy[s], tmpr[s]
            else:
                src, sreg = self.ps[pb], self.psr[pb]
            kb.op("dve", lambda e: e.scalar_tensor_tensor(out=self.H[:, dc, t0:t0 + n], in0=src[:, :n],
                                                          scalar=self.P[:, midx(j, 2, dc), kind:kind + 1],
                                                          in1=self.H[:, dc, t0:t0 + n], op0=ALU.mult, op1=ALU.add),
                  reads=[sreg, self.Pr, self.Hr[dc][b]], writes=[self.Hr[dc][b]])
        return evac

    def feat_stats(self, es_unused, src_fn, b, n, s, zsq, zsqr, msq, msqr, rstd, rstdr, eps_ap, nchunks=NC_, ones=None):
        kb = self.kb
        ones = self.ones_bf if ones is None else ones
        pm = self.psrot % 8
        pq = (self.psrot + 1) % 8
        self.psrot += 2
        for c in range(nchunks):
            ap, rg = src_fn(c)
            kb.op("act", lambda e: e.activation(out=zsq[s][:, c, :n], in_=ap, func=AF.Square),
                  reads=[rg], writes=[zsqr[s][c]])
        for c in range(nchunks):
            ap, rg = src_fn(c)
            kb.op("pe", lambda e: e.matmul(self.ps[pm][:, :n], lhsT=ones[:], rhs=ap, start=(c == 0), stop=(c == nchunks - 1)),
                  reads=[self.onesr, rg], writes=[self.psr[pm]])
        for c in range(nchunks):
            kb.op("pe", lambda e: e.matmul(self.ps[pq][:, :n], lhsT=ones[:], rhs=zsq[s][:, c, :n], start=(c == 0), stop=(c == nchunks - 1)),
                  reads=[self.onesr, zsqr[s][c]], writes=[self.psr[pq]])
        kb.op("act", lambda e: e.activation(out=msq[s][:, :n], in_=self.ps[pm][:, :n], func=AF.Square),
              reads=[self.psr[pm]], writes=[msqr[s]])
        kb.op("dve", lambda e: e.tensor_tensor(out=msq[s][:, :n], in0=self.ps[pq][:, :n], in1=msq[s][:, :n], op=ALU.subtract),
              reads=[self.psr[pq], msqr[s]], writes=[msqr[s]])
        kb.op("act", lambda e: e.activation(out=msq[s][:, :n], in_=msq[s][:, :n], func=AF.Ln, bias=eps_ap),
              reads=[msqr[s], self.onesr], writes=[msqr[s]])
        kb.op("act", lambda e: e.activation(out=rstd[s][:, :n], in_=msq[s][:, :n], func=AF.Exp, scale=-0.5),
              reads=[msqr[s]], writes=[rstdr[s]])
        return pm

    def conv_mixer(self, L, last):
        kb = self.kb
        j = 1
        blocks = [0, 1, 2, 3] if last else [0, 1, 2, 3, 4]
        self.modulate(j, blocks)
        PADL = 15
        OFFX = PADL
        OFFC = PADL + SEQ + 2 * PADL
        VW = OFFC + CTX + PADL
        voff = lambda b: (OFFX if BLOCKS[b][2] == 0 else OFFC - SEQ)
        with ExitStack() as es:
            V = kb.sb(es, "cvV", [128, NC_, VW], BF16)
            Vr = [[Reg() for _ in range(5)] for _ in range(NC_)]
            prm = kb.sb(es, "cvprm", [128, 6 * NC_ + 31 * NC_], F32)
            prmr = Reg()
            ident = kb.sb(es, "ident", [128, 128], F32)
            idr = Reg()
            kb.dma("sp", self.ds_misc, prm[:], self.d_cvprm, writes=[prmr])
            kb.dma("sp", self.ds_misc, ident[:], self.d_ident, writes=[idr])
            for c in range(NC_):
                kb.op("pool", lambda e: e.memset(V[:, c, :], 0.0), writes=Vr[c])
            B1A, B1G, BDW, LNG, LNB, B2, WDW = 0, 8, 16, 24, 32, 40, 48
            sg = [kb.sb(es, f"cvsg{s}", [128, 512], F32) for s in range(2)]
            sgr = [Reg(), Reg()]
            with ExitStack() as es1:
                wA = [kb.sb(es1, f"cvw1{s}", [128, 2, NC_, 128], BF16) for s in range(2)]
                wAr = [Reg(), Reg()]
                wAd = [kb.dsem(f"cvw1{s}") for s in range(2)]
                cnt = 0
                for c in range(NC_):
                    s = c % 2
                    kb.dma("pool", wAd[s], wA[s][:].rearrange("p s k m -> p (s k m)"), self.d_cvw1[c], writes=[wAr[s]])
                    for b in blocks:
                        t0, n, kind = BLOCKS[b]
                        pa = (cnt % 4) * 2
                        pg = pa + 1
                        ss = cnt % 2
                        cnt += 1
                        for (pb, si) in ((pa, 0), (pg, 1)):
                            for kc in range(NC_):
                                kb.op("pe", lambda e: e.matmul(self.ps[pb][:, :n], lhsT=wA[s][:, si, kc, :], rhs=self.U[:, kc, t0:t0 + n],
                                                                start=(kc == 0), stop=(kc == NC_ - 1)),
                                      reads=[wAr[s], self.Ur[kc][b]], writes=[self.psr[pb]])
                        kb.op("act", lambda e: e.activation(out=sg[ss][:, :n], in_=self.ps[pg][:, :n], func=AF.Sigmoid, bias=prm[:, B1G + c:B1G + c + 1]),
                              reads=[self.psr[pg], prmr], writes=[sgr[ss]])
                        kb.op("dve", lambda e: e.scalar_tensor_tensor(out=V[:, c, voff(b) + t0:voff(b) + t0 + n], in0=self.ps[pa][:, :n],
                                                                      scalar=prm[:, B1A + c:B1A + c + 1], in1=sg[ss][:, :n],
                                                                      op0=ALU.add, op1=ALU.mult),
                              reads=[self.psr[pa], prmr, sgr[ss]], writes=[Vr[c][b]])
                kb.barrier()
            with ExitStack() as es2:
                Dg = [kb.sb(es2, f"cvDg{s}", [128, 31, 128], BF16) for s in range(2)]
                Dgr = [Reg(), Reg()]
                for c in range(NC_):
                    s = c % 2
                    for k in range(31):
                        kb.op("dve" if k % 2 == 0 else "pool",
                              lambda e: e.tensor_scalar(out=Dg[s][:, k, :], in0=ident[:], scalar1=prm[:, WDW + k * NC_ + c:WDW + k * NC_ + c + 1],
                                                        scalar2=None, op0=ALU.mult),
                              reads=[idr, prmr], writes=[Dgr[s]])
                    for b in blocks:
                        t0, n, kind = BLOCKS[b]
                        pb = self.psrot % 8
                        self.psrot += 1
                        vregs = [Vr[c][bb] for bb in blocks if BLOCKS[bb][2] == kind]
                        for k in range(31):
                            col = voff(b) + t0 + k - 15
                            kb.op("pe", lambda e: e.matmul(self.ps[pb][:, :n], lhsT=Dg[s][:, k, :], rhs=V[:, c, col:col + n],
                                                            start=(k == 0), stop=(k == 30)),
                                  reads=[Dgr[s]] + vregs, writes=[self.psr[pb]])
                        kb.op("act", lambda e: e.activation(out=self.U[:, c, t0:t0 + n], in_=self.ps[pb][:, :n], func=AF.Identity,
                                                            bias=prm[:, BDW + c:BDW + c + 1]),
                              reads=[self.psr[pb], prmr], writes=[self.Ur[c][b]])
                kb.barrier()
            zsq = [kb.sb(es, f"cvz2{s}", [128, NC_, 512], BF16) for s in range(2)]
            zsqr = [[Reg() for _ in range(NC_)] for _ in range(2)]
            msq = [kb.sb(es, f"cvmsq{s}", [128, 512], F32) for s in range(2)]
            rstd = [kb.sb(es, f"cvrstd{s}", [128, 512], F32) for s in range(2)]
            tmp = [kb.sb(es, f"cvtmp{s}", [128, 512], F32) for s in range(4)]
            msqr = [Reg(), Reg()]
            rstdr = [Reg(), Reg()]
            tmpr = [Reg() for _ in range(4)]
            epsl = kb.sb(es, "cveps", [128, 1], F32)
            kb.op("dve", lambda e: e.memset(epsl[:], LN_EPS), writes=[self.onesr])
            tc = 0
            for bi, b in enumerate(blocks):
                t0, n, kind = BLOCKS[b]
                s = bi % 2
                pm = self.feat_stats(None, lambda c: (self.U[:, c, t0:t0 + n], self.Ur[c][b]), b, n, s, zsq, zsqr, msq, msqr, rstd, rstdr, epsl[:, 0:1])
                for c in range(NC_):
                    ts = tc % 4
                    tc += 1
                    kb.op("dve", lambda e: e.tensor_tensor(out=tmp[ts][:, :n], in0=self.U[:, c, t0:t0 + n], in1=self.ps[pm][:, :n], op=ALU.subtract),
                          reads=[self.Ur[c][b], self.psr[pm]], writes=[tmpr[ts]])
                    kb.op("pool", lambda e: e.tensor_tensor(out=tmp[ts][:, :n], in0=tmp[ts][:, :n], in1=rstd[s][:, :n], op=ALU.mult),
                          reads=[tmpr[ts], rstdr[s]], writes=[tmpr[ts]])
                    kb.op("act", lambda e: e.activation(out=self.U[:, c, t0:t0 + n], in_=tmp[ts][:, :n], func=AF.Silu,
                                                        scale=prm[:, LNG + c:LNG + c + 1], bias=prm[:, LNB + c:LNB + c + 1]),
                          reads=[tmpr[ts], prmr], writes=[self.Ur[c][b]])
            b2t = prm[:, B2:B2 + NC_]
            self.proj(es, self.d_cvw2, range(NC_), self.U, self.Ur, self.resid_evac(j, bias=b2t, bias_reg=prmr, es=es), blocks, tag="cvw2")
            kb.barrier()
        self.layernorm(L, j, blocks)


    def na_mixer(self, L, last):
        kb = self.kb
        j = 1
        blocks = [0, 1, 2, 3, 4]
        self.modulate(j, blocks)
        NB_I, NB_F = 23, 14
        FB = NB_I
        SW = (NB_I + NB_F) * 64
        qtiles = []
        qtiles.append((0, 256, [(128 * i, (FB + 6 - 2 * i) * 64) for i in range(4)]))
        for q0r in (4, 12, 20):
            qtiles.append((64 * q0r, 512, [(64 * (q0r - 4) + 128 * i, (15 - 2 * i) * 64) for i in range(8)]))
        qtiles.append((1792, 256, [(1536 + 128 * i, (FB + 10 - 2 * i) * 64) for i in range(4)]))
        qtiles.append((2048, 256, []))
        ctx_keys = [2048, 2176]
        bregs = lambda regs, t0, n: [regs[b] for b, (bt, bn, _) in enumerate(BLOCKS) if bt < t0 + n and t0 < bt + bn]
        with ExitStack() as es:
            ZT = kb.sb(es, "naZT", [128, NC_, T], BF16)
            ZTr = [[Reg() for _ in range(5)] for _ in range(NC_)]
            ones1 = kb.sb(es, "naones", [128, 128], BF16)
            o1r = Reg()
            kb.op("dve", lambda e: e.memset(ones1[:], 1.0), writes=[o1r])
            with ExitStack() as es1:
                QT = [kb.sb(es1, f"naQ{s}", [128, T], BF16) for s in range(2)]
                KT = [kb.sb(es1, f"naK{s}", [128, T], BF16) for s in range(2)]
                VT = [kb.sb(es1, f"naV{s}", [128, 18, 128], BF16) for s in range(2)]
                QTr = [[Reg() for _ in range(5)] for _ in range(2)]
                KTr = [[Reg() for _ in range(5)] for _ in range(2)]
                VTr = [Reg(), Reg()]
                strip = [kb.sb(es1, f"nastrip{s}", [128, SW], BF16) for s in range(2)]
                stripr = [Reg(), Reg()]
                stripd = [kb.dsem(f"nastrip{s}") for s in range(2)]
                tmp = [kb.sb(es1, f"natmp{s}", [128, 512], F32) for s in range(3)]
                tmpr = [Reg() for _ in range(3)]
                PT = [kb.sb(es1, f"naPT{s}", [128, 512], BF16) for s in range(3)]
                PTr = [Reg() for _ in range(3)]
                rc = [kb.sb(es1, f"narc{s}", [128, 512], F32) for s in range(2)]
                rcr = [Reg(), Reg()]
                wv = [kb.sb(es1, f"nawv{s}", [128, NC_, 128], BF16) for s in range(2)]
                wvr = [Reg(), Reg()]
                wvd = [kb.dsem(f"nawv{s}") for s in range(2)]
                pw = [kb.sb(es1, f"napw{s}", [128, NC_, 128], BF16) for s in range(2)]
                pwr = [Reg(), Reg()]
                pwd = [kb.dsem(f"napw{s}") for s in range(2)]
                cnt = {"s": 0, "t": 0, "p": 0, "o": 0, "w": 0}
                for ch in range(NC_):
                    sl = ch % 2
                    for (dst, dstr, oc) in ((QT[sl], QTr[sl], ch), (KT[sl], KTr[sl], NC_ + ch)):
                        ws = cnt["w"] % 2
                        cnt["w"] += 1
                        kb.dma("pool", pwd[ws], pw[ws][:].rearrange("p k m -> p (k m)"), self.d_nawqkv[oc], writes=[pwr[ws]])
                        for b in blocks:
                            t0, n, kind = BLOCKS[b]
                            pb = 4 + (self.psrot % 4)
                            self.psrot += 1
                            for kc in range(NC_):
                                kb.op("pe", lambda e: e.matmul(self.ps[pb][:, :n], lhsT=pw[ws][:, kc, :], rhs=self.U[:, kc, t0:t0 + n],
                                                                start=(kc == 0), stop=(kc == NC_ - 1)),
                                      reads=[pwr[ws], self.Ur[kc][b]], writes=[self.psr[pb]])
                            kb.op("act", lambda e: e.activation(out=dst[:, t0:t0 + n], in_=self.ps[pb][:, :n], func=AF.Copy),
                                  reads=[self.psr[pb]], writes=[dstr[b]])
                    kb.dma("pool", wvd[sl], wv[sl][:].rearrange("p k m -> p (k m)"), self.d_nawqkv[2 * NC_ + ch], writes=[wvr[sl]])
                    for g in range(5):
                        tiles = list(range(4 * g, min(18, 4 * g + 4)))
                        pb = 4 + (self.psrot % 4)
                        self.psrot += 1
                        for ti, tt in enumerate(tiles):
                            b = min(tt // 4, 4)
                            for kc in range(NC_):
                                kb.op("pe", lambda e: e.matmul(self.ps[pb][:, ti * 128:(ti + 1) * 128], lhsT=self.U[:, kc, tt * 128:(tt + 1) * 128],
                                                                rhs=wv[sl][:, kc, :], start=(kc == 0), stop=(kc == NC_ - 1)),
                                      reads=[wvr[sl], self.Ur[kc][b]], writes=[self.psr[pb]])
                        nt = len(tiles)
                        kb.op("dve", lambda e: e.tensor_copy(out=VT[sl][:, tiles[0]:tiles[0] + nt, :].rearrange("p t m -> p (t m)"),
                                                             in_=self.ps[pb][:, :nt * 128]),
                              reads=[self.psr[pb]], writes=[VTr[sl]])
                    items = []
                    for hh in range(2):
                        for qi, (q0, nq, loc) in enumerate(qtiles):
                            keys = [(k0, c0) for (k0, c0) in loc] + [(k0, None) for k0 in ctx_keys]
                            for ki, (k0, c0) in enumerate(keys):
                                items.append((hh, qi, ki, len(keys), k0, c0))
                    obase = cnt["o"]
                    cnt["o"] += 2 * len(qtiles)
                    ibase = cnt["s"]
                    cnt["s"] += len(items)

                    def stage_a(it, idx):
                        hh, qi, ki, nk, k0, c0 = it
                        q0, nq, _ = qtiles[qi]
                        h = 2 * ch + hh
                        r0 = hh * 64
                        ss = h % 2
                        if qi == 0 and ki == 0:
                            kb.dma("pool", stripd[ss], strip[ss][:], self.d_nastrip[h], writes=[stripr[ss]])
                        g = ibase + idx
                        pss = g % 4
                        pt = g % 3
                        kb.op("pe", lambda e: e.matmul(self.ps[pss][:, :nq], lhsT=KT[sl][r0:r0 + 64, k0:k0 + 128],
                                                        rhs=QT[sl][r0:r0 + 64, q0:q0 + nq], start=True, stop=True),
                              reads=bregs(KTr[sl], k0, 128) + bregs(QTr[sl], q0, nq), writes=[self.psr[pss]])
                        if c0 is not None:
                            tm = g % 3
                            kb.op("dve", lambda e: e.scalar_tensor_tensor(out=tmp[tm][:, :nq], in0=self.ps[pss][:, :nq], scalar=0.125,
                                                                          in1=strip[ss][:, c0:c0 + nq], op0=ALU.mult, op1=ALU.add),
                                  reads=[self.psr[pss], stripr[ss]], writes=[tmpr[tm]])
                            kb.op("act", lambda e: e.activation(out=PT[pt][:, :nq], in_=tmp[tm][:, :nq], func=AF.Exp),
                                  reads=[tmpr[tm]], writes=[PTr[pt]])
                        else:
                            kb.op("act", lambda e: e.activation(out=PT[pt][:, :nq], in_=self.ps[pss][:, :nq], func=AF.Exp, scale=0.125),
                                  reads=[self.psr[pss]], writes=[PTr[pt]])

                    def stage_b(it, idx):
                        hh, qi, ki, nk, k0, c0 = it
                        q0, nq, _ = qtiles[qi]
                        r0 = hh * 64
                        o_ = obase + hh * len(qtiles) + qi
                        ppv = 4 + (o_ % 2)
                        psm = 6 + (o_ % 2)
                        pt = (ibase + idx) % 3
                        kt = k0 // 128
                        kb.op("pe", lambda e: e.matmul(self.ps[ppv][:, :nq], lhsT=VT[sl][:, kt, :], rhs=PT[pt][:, :nq],
                                                        start=(ki == 0), stop=(ki == nk - 1)),
                              reads=[VTr[sl], PTr[pt]], writes=[self.psr[ppv]])
                        kb.op("pe", lambda e: e.matmul(self.ps[psm][:, :nq], lhsT=ones1[:], rhs=PT[pt][:, :nq],
                                                        start=(ki == 0), stop=(ki == nk - 1)),
                              reads=[o1r, PTr[pt]], writes=[self.psr[psm]])
                        if ki == nk - 1:
                            rs = o_ % 2
                            kb.op("act", lambda e: e.activation(out=rc[rs][r0:r0 + 64, :nq], in_=self.ps[psm][r0:r0 + 64, :nq], func=AF.Ln),
                                  reads=[self.psr[psm]], writes=[rcr[rs]])
                            kb.op("act", lambda e: e.activation(out=rc[rs][r0:r0 + 64, :nq], in_=rc[rs][r0:r0 + 64, :nq], func=AF.Exp, scale=-1.0),
                                  reads=[rcr[rs]], writes=[rcr[rs]])
                            kb.op("dve", lambda e: e.tensor_tensor(out=ZT[r0:r0 + 64, ch, q0:q0 + nq], in0=self.ps[ppv][r0:r0 + 64, :nq],
                                                                   in1=rc[rs][r0:r0 + 64, :nq], op=ALU.mult),
                                  reads=[self.psr[ppv], rcr[rs]], writes=bregs(ZTr[ch], q0, nq))

                    LA = 2
                    for idx in range(len(items) + LA):
                        if idx < len(items):
                            stage_a(items[idx], idx)
                        if idx >= LA:
                            stage_b(items[idx - LA], idx - LA)
                kb.barrier()
            self.proj(es, self.d_nawo, range(NC_), ZT, ZTr, self.resid_evac(j), blocks, tag="nawo")
            kb.barrier()
        self.layernorm(L, j, blocks)


    def gla_mixer(self, L, last):
        kb = self.kb
        j = 1
        blocks = [0, 1, 2, 3, 4]
        oblocks = [0, 1, 2, 3] if last else blocks
        self.modulate(j, blocks)
        SC = 128.0 ** -0.5
        NCH = T // 64
        with ExitStack() as es:
            cst = kb.sb(es, "glcst", [128, 4 * 64 + 128 + 2 + 8 + 8], F32)
            cstr = Reg()
            kb.dma("sp", self.ds_misc, cst[:, 0:4 * 64 + 128 + 2 + 8], self.d_glacst, writes=[cstr])
            MK, ID, NG, BA, NBA = 0, 256, 384, 386, 394
            kb.op("dve", lambda e: e.tensor_scalar(out=cst[:, NBA:NBA + 8], in0=cst[:, BA:BA + 8], scalar1=-1.0, scalar2=None, op0=ALU.mult),
                  reads=[cstr], writes=[cstr])
            maskb = kb.sb(es, "glmask", [128, 4, 64], BF16)
            identb = kb.sb(es, "glidb", [128, 128], BF16)
            ones256 = kb.sb(es, "glones", [128, 128], BF16)
            onecol = kb.sb(es, "glone", [128, 1], F32)
            epsc = kb.sb(es, "gleps", [128, 1], F32)
            kb.op("dve", lambda e: e.tensor_copy(out=maskb[:].rearrange("p a b -> p (a b)"), in_=cst[:, MK:MK + 256]), reads=[cstr], writes=[cstr])
            kb.op("dve", lambda e: e.tensor_copy(out=identb[:], in_=cst[:, ID:ID + 128]), reads=[cstr], writes=[cstr])
            kb.op("dve", lambda e: e.memset(ones256[:], 1.0 / 256.0), writes=[cstr])
            kb.op("dve", lambda e: e.memset(onecol[:], 1.0), writes=[cstr])
            kb.op("dve", lambda e: e.memset(epsc[:], LN_EPS), writes=[cstr])
            wa1 = kb.sb(es, "glwa1", [128, NC_, 32], BF16)
            wa1r = Reg()
            kb.dma("pool", self.ds_misc, wa1[:].rearrange("p k m -> p (k m)"), self.d_glawa1, writes=[wa1r])
            wa2 = kb.sb(es, "glwa2", [32, 2, 512], F32)
            wa2r = Reg()
            kb.dma("sp", self.ds_misc, wa2[:].rearrange("p d m -> p (d m)"), self.d_glawa2, writes=[wa2r])
            rT = kb.sb(es, "glrT", [32, T], F32)
            rTr = [Reg() for _ in range(5)]
            for b in blocks:
                t0, n, kind = BLOCKS[b]
                pb = self.psrot % 8
                self.psrot += 1
                for kc in range(NC_):
                    kb.op("pe", lambda e: e.matmul(self.ps[pb][0:32, :n], lhsT=wa1[:, kc, :], rhs=self.U[:, kc, t0:t0 + n],
                                                    start=(kc == 0), stop=(kc == NC_ - 1)),
                          reads=[wa1r, self.Ur[kc][b]], writes=[self.psr[pb]])
                kb.op("act", lambda e: e.activation(out=rT[:, t0:t0 + n], in_=self.ps[pb][0:32, :n], func=AF.Copy),
                      reads=[self.psr[pb]], writes=[rTr[b]])
            QK = kb.sb(es, "glQK", [128, 2, T], BF16)
            QKr = [[Reg() for _ in range(5)] for _ in range(2)]
            VT = kb.sb(es, "glVT", [128, 18, 256], BF16)
            VTr = Reg()
            O = kb.sb(es, "glO", [128, 2, T], F32)
            Or = [Reg() for _ in range(5)]
            Z, Zr = QK, QKr
            SA = kb.sb(es, "glSA", [128, 2, T], F32)
            SAr = Reg()
            sad = kb.dsem("glrope")
            pw = [kb.sb(es, f"glpw{s}", [128, NC_, 128], BF16) for s in range(2)]
            pwr = [Reg(), Reg()]
            pwd = [kb.dsem(f"glpw{s}") for s in range(2)]
            wv = kb.sb(es, "glwv", [128, NC_, 256], BF16)
            wvr = Reg()
            wvd = kb.dsem("glwv")
            t1 = [kb.sb(es, f"glt1{s}", [128, 512], F32) for s in range(2)]
            t1r = [Reg(), Reg()]
            t2 = [kb.sb(es, f"glt2{s}", [128, 512], F32) for s in range(2)]
            t2r = [Reg(), Reg()]
            S = kb.sb(es, "glS", [128, 256], F32)
            Sb = kb.sb(es, "glSb", [128, 256], BF16)
            Sr, Sbr = Reg(), Reg()
            kd = [kb.sb(es, f"glkd{s}", [128, 128], BF16) for s in range(2)]
            kt = [kb.sb(es, f"glkt{s}", [128, 128], BF16) for s in range(2)]
            kdr = [Reg(), Reg()]
            ktr = [Reg(), Reg()]
            for s_ in range(2):
                kb.op("dve", lambda e: e.memset(kd[s_][:], 0.0), writes=[kdr[s_]])
                kb.op("dve", lambda e: e.memset(kt[s_][:], 0.0), writes=[ktr[s_]])
            ktT = [kb.sb(es, f"glktT{s}", [128, 128], BF16) for s in range(3)]
            ktTr = [Reg() for _ in range(3)]
            qd = [kb.sb(es, f"glqd{s}", [128, 64], BF16) for s in range(3)]
            qdr = [Reg() for _ in range(3)]
            ex = [kb.sb(es, f"glex{s}", [128, 3, 64], F32) for s in range(3)]
            exr = [Reg() for _ in range(3)]
            attm = [kb.sb(es, f"glatt{s}", [128, 64], BF16) for s in range(3)]
            attr = [Reg() for _ in range(3)]
            nbp = [kb.sb(es, f"glnb{s}", [128, 2], F32) for s in range(3)]
            nbr = [Reg() for _ in range(3)]
            psb = {2: self.ps[2].bitcast(BF16), 3: self.ps[3].bitcast(BF16)}
            wcnt = 0
            for hd in range(4):
                kb.dma("sp", sad, SA[:, :, 0:SEQ], self.d_glarope.rearrange("a p t -> p a t"), writes=[SAr])
                for qk in range(2):
                    oc = qk * 4 + hd
                    ws0 = wcnt % 2
                    ws1 = (wcnt + 1) % 2
                    wcnt += 2
                    kb.dma("pool", pwd[ws0], pw[ws0][:].rearrange("p k m -> p (k m)"), self.d_glawqk[oc], writes=[pwr[ws0]])
                    kb.dma("pool", pwd[ws1], pw[ws1][:].rearrange("p k m -> p (k m)"), self.d_glawqkp[oc], writes=[pwr[ws1]])
                    for b in blocks:
                        t0, n, kind = BLOCKS[b]
                        pa = (self.psrot % 4) * 2
                        pp = pa + 1
                        ts = self.psrot % 2
                        self.psrot += 1
                        for kc in range(NC_):
                            kb.op("pe", lambda e: e.matmul(self.ps[pa][:, :n], lhsT=pw[ws0][:, kc, :], rhs=self.U[:, kc, t0:t0 + n],
                                                            start=(kc == 0), stop=(kc == NC_ - 1)),
                                  reads=[pwr[ws0], self.Ur[kc][b]], writes=[self.psr[pa]])
                        if kind == 1:
                            kb.op("act", lambda e: e.activation(out=QK[:, qk, t0:t0 + n], in_=self.ps[pa][:, :n], func=AF.Copy),
                                  reads=[self.psr[pa]], writes=[QKr[qk][b]])
                            continue
                        for kc in range(NC_):
                            kb.op("pe", lambda e: e.matmul(self.ps[pp][:, :n], lhsT=pw[ws1][:, kc, :], rhs=self.U[:, kc, t0:t0 + n],
                                                            start=(kc == 0), stop=(kc == NC_ - 1)),
                                  reads=[pwr[ws1], self.Ur[kc][b]], writes=[self.psr[pp]])
                        kb.op("dve", lambda e: e.tensor_tensor(out=t1[ts][:, :n], in0=self.ps[pa][:, :n], in1=SA[:, 0, t0:t0 + n], op=ALU.mult),
                              reads=[self.psr[pa], SAr], writes=[t1r[ts]])
                        kb.op("dve", lambda e: e.tensor_tensor(out=t2[ts][:, :n], in0=self.ps[pp][:, :n], in1=SA[:, 1, t0:t0 + n], op=ALU.mult),
                              reads=[self.psr[pp], SAr], writes=[t2r[ts]])
                        kb.op("pool", lambda e: e.tensor_tensor(out=QK[:, qk, t0:t0 + n], in0=t1[ts][:, :n], in1=t2[ts][:, :n], op=ALU.add),
                              reads=[t1r[ts], t2r[ts]], writes=[QKr[qk][b]])
                kb.dma("pool", wvd, wv[:].rearrange("p k m -> p (k m)"), self.d_glawv[hd], writes=[wvr])
                for g in range(9):
                    pb = self.psrot % 8
                    self.psrot += 1
                    for ti in range(2):
                        tt = 2 * g + ti
                        b = min(tt // 4, 4)
                        for kc in range(NC_):
                            kb.op("pe", lambda e: e.matmul(self.ps[pb][:, ti * 256:(ti + 1) * 256], lhsT=self.U[:, kc, tt * 128:(tt + 1) * 128],
                                                            rhs=wv[:, kc, :], start=(kc == 0), stop=(kc == NC_ - 1)),
                                  reads=[wvr, self.Ur[kc][b]], writes=[self.psr[pb]])
                    kb.op("act", lambda e: e.activation(out=VT[:, 2 * g:2 * g + 2, :].rearrange("p t m -> p (t m)"), in_=self.ps[pb][:, :512], func=AF.Copy),
                          reads=[self.psr[pb]], writes=[VTr])
                for d in range(2):
                    for b in blocks:
                        t0, n, kind = BLOCKS[b]
                        pb = self.psrot % 8
                        ts = self.psrot % 2
                        self.psrot += 1
                        kb.op("pe", lambda e: e.matmul(self.ps[pb][:, :n], lhsT=wa2[:, d, hd * 128:(hd + 1) * 128], rhs=rT[:, t0:t0 + n],
                                                        start=True, stop=True),
                              reads=[wa2r, rTr[b]], writes=[self.psr[pb]])
                        kb.op("act", lambda e: e.activation(out=t1[ts][:, :n], in_=self.ps[pb][:, :n], func=AF.Exp, scale=-1.0,
                                                            bias=cst[:, NBA + d * 4 + hd:NBA + d * 4 + hd + 1]),
                              reads=[self.psr[pb], cstr], writes=[t1r[ts]])
                        kb.op("act", lambda e: e.activation(out=SA[:, 0, t0:t0 + n], in_=t1[ts][:, :n], func=AF.Ln, bias=onecol[:, 0:1]),
                              reads=[t1r[ts], cstr], writes=[SAr])
                    for n_ in range(NCH):
                        c0 = 64 * n_
                        kb.op("dve", lambda e: e.tensor_tensor_scan(out=SA[:, 1, c0:c0 + 64], data0=self.onesf[:, 0:64], data1=SA[:, 0, c0:c0 + 64],
                                                                    initial=0.0, op0=ALU.mult, op1=ALU.add),
                              reads=[SAr, self.onesr], writes=[SAr])
                    if d == 1:
                        kb.op("pool", lambda e: e.tensor_tensor(out=SA[:, 0, :], in0=SA[:, 0, :], in1=SA[:, 1, :], op=ALU.subtract),
                              reads=[SAr], writes=[SAr])
                    kb.op("dve", lambda e: e.memset(S[:], 0.0), writes=[Sr])
                    kb.op("dve", lambda e: e.memset(Sb[:], 0.0), writes=[Sbr])
                    order = ([32, 33, 34, 35] + list(range(32))) if d == 0 else ([35, 34, 33, 32] + list(range(31, -1, -1)))
                    def chunk_gen(ci, n_):
                        c0 = 64 * n_
                        tt, par = n_ // 2, n_ % 2
                        tb = tt % 2
                        b = min(c0 // 512, 4)
                        xs = ci % 3
                        last_col = c0 + 63
                        kb.op("dve", lambda e: e.tensor_scalar(out=nbp[xs][:, 0:1], in0=SA[:, 1, last_col:last_col + 1], scalar1=-1.0 / 16.0, scalar2=None, op0=ALU.mult),
                              reads=[SAr], writes=[nbr[xs]])
                        kb.op("dve", lambda e: e.tensor_scalar(out=nbp[xs][:, 1:2], in0=SA[:, 1, last_col:last_col + 1], scalar1=1.0 / 16.0, scalar2=None, op0=ALU.mult),
                              reads=[SAr], writes=[nbr[xs]])
                        if d == 0:
                            src = SA[:, 1, c0:c0 + 64]
                            kb.op("act", lambda e: e.activation(out=ex[xs][:, 0, :], in_=src, func=AF.Exp, scale=-1.0 / 16.0), reads=[SAr], writes=[exr[xs]])
                            kb.op("act", lambda e: e.activation(out=ex[xs][:, 1, :], in_=src, func=AF.Exp, scale=1.0 / 16.0), reads=[SAr], writes=[exr[xs]])
                            kb.op("act", lambda e: e.activation(out=ex[xs][:, 2, :], in_=src, func=AF.Exp, scale=1.0 / 16.0, bias=nbp[xs][:, 0:1]),
                                  reads=[SAr, nbr[xs]], writes=[exr[xs]])
                            dec = ex[xs][:, 0, 63:64]
                        else:
                            src = SA[:, 0, c0:c0 + 64]
                            kb.op("act", lambda e: e.activation(out=ex[xs][:, 0, :], in_=src, func=AF.Exp, scale=-1.0 / 16.0, bias=nbp[xs][:, 0:1]),
                                  reads=[SAr, nbr[xs]], writes=[exr[xs]])
                            kb.op("act", lambda e: e.activation(out=ex[xs][:, 1, :], in_=src, func=AF.Exp, scale=1.0 / 16.0, bias=nbp[xs][:, 1:2]),
                                  reads=[SAr, nbr[xs]], writes=[exr[xs]])
                            kb.op("act", lambda e: e.activation(out=ex[xs][:, 2, :], in_=src, func=AF.Exp, scale=1.0 / 16.0), reads=[SAr], writes=[exr[xs]])
                            dec = ex[xs][:, 0, 0:1]
                        yield
                        kb.op("dve", lambda e: e.tensor_tensor(out=qd[xs][:], in0=QK[:, 0, c0:c0 + 64], in1=ex[xs][:, 0, :], op=ALU.mult),
                              reads=[QKr[0][b], exr[xs]], writes=[qdr[xs]])
                        kb.op("dve", lambda e: e.tensor_tensor(out=kd[tb][:, par * 64:par * 64 + 64], in0=QK[:, 1, c0:c0 + 64], in1=ex[xs][:, 1, :], op=ALU.mult),
                              reads=[QKr[1][b], exr[xs]], writes=[kdr[tb]])
                        kb.op("pool", lambda e: e.tensor_tensor(out=kt[tb][:, par * 64:par * 64 + 64], in0=QK[:, 1, c0:c0 + 64], in1=ex[xs][:, 2, :], op=ALU.mult),
                              reads=[QKr[1][b], exr[xs]], writes=[ktr[tb]])
                        yield
                        pa = ci % 2
                        ptb = 2 + ci % 2
                        po = 4 + ci % 2
                        pS = 6 + ci % 2
                        kb.op("pe", lambda e: e.matmul(self.ps[pa][:, 0:64], lhsT=kd[tb][:], rhs=qd[xs][:], start=True, stop=True),
                              reads=[kdr[tb], qdr[xs]], writes=[self.psr[pa]])
                        kb.op("dve", lambda e: e.tensor_tensor(out=attm[xs][:], in0=self.ps[pa][:, 0:64], in1=maskb[:, par * 2 + d, :], op=ALU.mult),
                              reads=[self.psr[pa], cstr], writes=[attr[xs]])
                        yield
                        kb.op("pe", lambda e: e.transpose(psb[ptb][:, 0:128], kt[tb][:], identb[:]),
                              reads=[ktr[tb], cstr], writes=[self.psr[ptb]])
                        kb.op("act", lambda e: e.activation(out=ktT[xs][par * 64:par * 64 + 64, :], in_=psb[ptb][par * 64:par * 64 + 64, 0:128], func=AF.Copy),
                              reads=[self.psr[ptb]], writes=[ktTr[xs]])
                        yield
                        for ec in range(2):
                            kb.op("pe", lambda e: e.matmul(self.ps[po][:, ec * 64:ec * 64 + 64], lhsT=VT[:, tt, ec * 128:(ec + 1) * 128], rhs=attm[xs][:],
                                                            start=True, stop=False),
                                  reads=[VTr, attr[xs]], writes=[self.psr[po]])
                            kb.op("pe", lambda e: e.matmul(self.ps[po][:, ec * 64:ec * 64 + 64], lhsT=Sb[:, ec * 128:(ec + 1) * 128], rhs=qd[xs][:],
                                                            start=False, stop=True),
                                  reads=[Sbr, qdr[xs]], writes=[self.psr[po]])
                        yield
                        kb.op("pe", lambda e: e.matmul(self.ps[pS][:, 0:256], lhsT=ktT[xs][par * 64:par * 64 + 64, :], rhs=VT[par * 64:par * 64 + 64, tt, :],
                                                        start=True, stop=True),
                              reads=[ktTr[xs], VTr], writes=[self.psr[pS]])
                        kb.op("dve", lambda e: e.scalar_tensor_tensor(out=S[:], in0=S[:], scalar=dec, in1=self.ps[pS][:, 0:256], op0=ALU.mult, op1=ALU.add),
                              reads=[Sr, exr[xs], self.psr[pS]], writes=[Sr])
                        kb.op("act", lambda e: e.activation(out=Sb[:], in_=S[:], func=AF.Copy), reads=[Sr], writes=[Sbr])
                        yield
                        pov = self.ps[po][:, 0:128].rearrange("p (e c) -> p e c", e=2)
                        if d == 0:
                            kb.op("act", lambda e: e.activation(out=O[:, :, c0:c0 + 64], in_=pov, func=AF.Identity, scale=SC),
                                  reads=[self.psr[po]], writes=[Or[b]])
                        else:
                            kb.op("dve", lambda e: e.scalar_tensor_tensor(out=O[:, :, c0:c0 + 64], in0=pov, scalar=SC, in1=O[:, :, c0:c0 + 64],
                                                                          op0=ALU.mult, op1=ALU.add),
                                  reads=[self.psr[po], Or[b]], writes=[Or[b]])
                    active = []
                    pending = list(enumerate(order))
                    while pending or active:
                        if pending and len(active) < 3:
                            active.append(chunk_gen(*pending.pop(0)))
                        for g_ in list(active):
                            try:
                                next(g_)
                            except StopIteration:
                                active.remove(g_)
                if getattr(self, "debug", None) == f"gla_scan{hd}":
                    dd = kb.dsem("dbg")
                    for e2 in range(2):
                        kb.dma("sp", dd, self.d_hout[e2], O[:, e2, :], reads=Or)
                        kb.dma("pool", dd, self.d_hout[2 + e2], QK[:, e2, :], reads=QKr[0] + QKr[1])
                        kb.dma("sp", dd, self.d_hout[4 + e2], SA[:, e2, :], reads=[SAr])
                    kb.dma("sp", dd, self.d_hout[6][:, 0:256], S[:], reads=[Sr])
                    kb.dma("sp", dd, self.d_hout[7][0:32, :], rT[:], reads=rTr)
                    kb.eng["sp"].wait_ge(dd.sem, dd.cnt)
                    kb.eng["pool"].wait_ge(dd.sem, dd.cnt)
                    raise DebugStop()
                for ec in range(2):
                    ws = wcnt % 2
                    wcnt += 1
                    kb.dma("pool", pwd[ws], pw[ws][:].rearrange("p k m -> p (k m)"), self.d_glawg[hd * 2 + ec], writes=[pwr[ws]])
                    for b in oblocks:
                        t0, n, kind = BLOCKS[b]
                        pg = self.psrot % 8
                        pq = (self.psrot + 1) % 8
                        ts = (self.psrot // 2) % 2
                        self.psrot += 2
                        for kc in range(NC_):
                            kb.op("pe", lambda e: e.matmul(self.ps[pg][:, :n], lhsT=pw[ws][:, kc, :], rhs=self.U[:, kc, t0:t0 + n],
                                                            start=(kc == 0), stop=(kc == NC_ - 1)),
                                  reads=[pwr[ws], self.Ur[kc][b]], writes=[self.psr[pg]])
                        for e2 in range(2):
                            kb.op("act", lambda e: e.activation(out=t1[ts][:, :n], in_=O[:, e2, t0:t0 + n], func=AF.Square), reads=[Or[b]], writes=[t1r[ts]])
                            kb.op("act", lambda e: e.activation(out=Z[:, ec, t0:t0 + n], in_=t1[ts][:, :n], func=AF.Copy), reads=[t1r[ts]], writes=[Zr[ec][b]])
                            kb.op("pe", lambda e: e.matmul(self.ps[pq][:, :n], lhsT=ones256[:], rhs=Z[:, ec, t0:t0 + n], start=(e2 == 0), stop=(e2 == 1)),
                                  reads=[cstr, Zr[ec][b]], writes=[self.psr[pq]])
                        kb.op("act", lambda e: e.activation(out=t1[ts][:, :n], in_=self.ps[pq][:, :n], func=AF.Ln, bias=epsc[:, 0:1]),
                              reads=[self.psr[pq], cstr], writes=[t1r[ts]])
                        kb.op("act", lambda e: e.activation(out=t1[ts][:, :n], in_=t1[ts][:, :n], func=AF.Exp, scale=-0.5), reads=[t1r[ts]], writes=[t1r[ts]])
                        kb.op("dve", lambda e: e.tensor_tensor(out=t1[ts][:, :n], in0=O[:, ec, t0:t0 + n], in1=t1[ts][:, :n], op=ALU.mult),
                              reads=[Or[b], t1r[ts]], writes=[t1r[ts]])
                        kb.op("act", lambda e: e.activation(out=t2[ts][:, :n], in_=self.ps[pg][:, :n], func=AF.Silu), reads=[self.psr[pg]], writes=[t2r[ts]])
                        kb.op("dve", lambda e: e.scalar_tensor_tensor(out=Z[:, ec, t0:t0 + n], in0=t1[ts][:, :n], scalar=cst[:, NG + ec:NG + ec + 1],
                                                                      in1=t2[ts][:, :n], op0=ALU.mult, op1=ALU.mult),
                              reads=[t1r[ts], t2r[ts], cstr], writes=[Zr[ec][b]])
                if getattr(self, "debug", None) == f"gla_fin{hd}":
                    dd = kb.dsem("dbg")
                    for e2 in range(2):
                        kb.dma("pool", dd, self.d_hout[e2], Z[:, e2, :], reads=Zr[0] + Zr[1])
                        kb.dma("sp", dd, self.d_hout[2 + e2], O[:, e2, :], reads=Or)
                    kb.eng["sp"].wait_ge(dd.sem, dd.cnt)
                    kb.eng["pool"].wait_ge(dd.sem, dd.cnt)
                    raise DebugStop()
                with ExitStack() as es2:
                    self.proj(es2, self.d_glawo[hd], range(NC_), Z, Zr, self.resid_evac(j), oblocks, nk=2, tag=f"glwo{hd}")
                    kb.barrier()
                    if getattr(self, "debug", None) == f"gla_wo{hd}":
                        self.store()
                        raise DebugStop()
            kb.barrier()
        self.layernorm(L, j, oblocks)


    def park_h(self):
        kb = self.kb
        if not hasattr(self, "d_hpark"):
            self.d_hpark = self.nc.dram_tensor("hpark", [NC_, 128, T], F32, kind="Internal").ap()
            self.ds_park = kb.dsem("park")
        for c in range(NC_):
            kb.dma("sp", self.ds_h[c % 4], self.d_hpark[c], self.H[:, c, :], reads=self.Hr[c])

    def unpark_h(self):
        kb = self.kb
        self.es_H = ExitStack()
        self.H = kb.sb(self.es_H, "H", [128, NC_, T], F32)
        for c in range(NC_):
            kb.dma("sp", self.ds_h[c % 4], self.H[:, c, :], self.d_hpark[c], writes=self.Hr[c])

    def rwkv_mixer(self, L, last):
        kb = self.kb
        assert last, "rwkv mixer implemented for the final layer (context output unused)"
        j = 1
        blocks = [0, 1, 2, 3, 4]
        lblocks = [0, 1, 2, 3]
        self.modulate(j, blocks)
        NCH = T // 64
        G = 4
        EM05 = float(np.exp(-0.5))
        U = self.U
        nbank = lambda: self._nb()
        W0, A0, KK_, KA, RK, GNG, GNB, MU, OMKA, OMU, HMU = 0, 16, 32, 40, 48, 56, 64, 72, 120, 128, 176
        self.park_h()
        kb.barrier()
        self.es_H.close()
        d_zpark = self.nc.dram_tensor("zpark", [NC_, 128, SEQ], BF16, kind="Internal").ap()
        ds_z = kb.dsem("zpark")
        dbg = getattr(self, "debug", None)
        if dbg == "rw0":
            return 'stop'
        with ExitStack() as es:
            prm = kb.sb(es, "rwprm", [128, 224], F32)
            prmr = Reg()
            kb.dma("sp", self.ds_misc, prm[:, 0:120], self.d_rwprm, writes=[prmr])
            kb.op("dve", lambda e: e.tensor_scalar(out=prm[:, OMKA:OMKA + 8], in0=prm[:, KA:KA + 8], scalar1=-1.0, scalar2=1.0, op0=ALU.mult, op1=ALU.add),
                  reads=[prmr], writes=[prmr])
            kb.op("dve", lambda e: e.tensor_scalar(out=prm[:, OMU:OMU + 48], in0=prm[:, MU:MU + 48], scalar1=-1.0, scalar2=1.0, op0=ALU.mult, op1=ALU.add),
                  reads=[prmr], writes=[prmr])
            kb.op("dve", lambda e: e.tensor_scalar(out=prm[:, HMU:HMU + 48], in0=prm[:, MU:MU + 48], scalar1=0.5, scalar2=None, op0=ALU.mult),
                  reads=[prmr], writes=[prmr])
            cstf = kb.sb(es, "rwcstf", [128, 256], F32)
            cstr = Reg()
            kb.dma("sp", self.ds_misc, cstf[:], self.d_rwcstf, writes=[cstr])
            ident = cstf[:, 0:128]
            bdmask = cstf[:, 128:256]
            gmask = kb.sb(es, "rwgmask", [64, 2, 5, 2, 64], BF16)
            kb.dma("pool", self.ds_misc, gmask[:].rearrange("p a b c d -> p (a b c d)"), self.d_rwgmask, writes=[cstr])
            identb = kb.sb(es, "rwidb", [128, 128], BF16)
            id64 = kb.sb(es, "rwid64", [64, 2, 64], BF16)
            onesbd = kb.sb(es, "rwonesbd", [128, 128], BF16)
            ones64 = kb.sb(es, "rwones64", [128, 128], BF16)
            kb.op("dve", lambda e: e.tensor_copy(out=identb[:], in_=ident), reads=[cstr], writes=[cstr])
            for h in range(2):
                kb.op("dve", lambda e: e.tensor_copy(out=id64[:, h, :], in_=cstf[0:64, 0:64]), reads=[cstr], writes=[cstr])
            kb.op("dve", lambda e: e.tensor_copy(out=onesbd[:], in_=bdmask), reads=[cstr], writes=[cstr])
            kb.op("dve", lambda e: e.tensor_scalar(out=ones64[:], in0=bdmask, scalar1=1.0 / 64.0, scalar2=None, op0=ALU.mult), reads=[cstr], writes=[cstr])
            epsg = kb.sb(es, "rwepsg", [128, 2], F32)
            kb.op("dve", lambda e: e.memset(epsg[:, 0:1], 64e-5), writes=[cstr])
            kb.op("dve", lambda e: e.memset(epsg[:, 1:2], 1e-24), writes=[cstr])
            XX = kb.sb(es, "rwXX", [128, NC_, T], BF16)
            XXr = [[Reg() for _ in range(5)] for _ in range(NC_)]
            with ExitStack() as es0:
                tf = [kb.sb(es0, f"rwtf{s_}", [128, T], F32) for s_ in range(2)]
                tfr = [Reg(), Reg()]
                for c in range(NC_):
                    s_ = c % 2
                    for (s0, s1) in ((0, SEQ), (SEQ, T)):
                        kb.op("pool", lambda e: e.tensor_tensor(out=tf[s_][:, s0 + 1:s1 - 1], in0=U[:, c, s0:s1 - 2], in1=U[:, c, s0 + 2:s1], op=ALU.add),
                              reads=self.Ur[c], writes=[tfr[s_]])
                        kb.op("pool", lambda e: e.tensor_copy(out=tf[s_][:, s0:s0 + 1], in_=U[:, c, s0 + 1:s0 + 2]), reads=self.Ur[c], writes=[tfr[s_]])
                        kb.op("pool", lambda e: e.tensor_copy(out=tf[s_][:, s1 - 1:s1], in_=U[:, c, s1 - 2:s1 - 1]), reads=self.Ur[c], writes=[tfr[s_]])
                    kb.op("dve", lambda e: e.scalar_tensor_tensor(out=XX[:, c, :], in0=tf[s_][:], scalar=0.5, in1=U[:, c, :], op0=ALU.mult, op1=ALU.subtract),
                          reads=[tfr[s_]] + self.Ur[c], writes=XXr[c])
                kb.barrier()
            pw = [kb.sb(es, f"rwpw{s_}", [128, NC_, 128], BF16) for s_ in range(2)]
            pws = [kb.sb(es, f"rwpws{s_}", [128, NC_, 128], BF16) for s_ in range(2)]
            pwr = [Reg(), Reg()]
            pwsr = [Reg(), Reg()]
            pwd = [kb.dsem(f"rwpw{s_}") for s_ in range(2)]
            wc = {"n": 0}

            def xproj(Wd_oc, jkind, blks, evac):
                s_ = wc["n"] % 2
                wc["n"] += 1
                kb.dma("pool", pwd[s_], pw[s_][:].rearrange("p k m -> p (k m)"), Wd_oc, writes=[pwr[s_]])
                for kc in range(NC_):
                    kb.op("dve" if kc % 2 == 0 else "pool",
                          lambda e: e.tensor_scalar(out=pws[s_][:, kc, :], in0=pw[s_][:, kc, :], scalar1=prm[:, MU + jkind * 8 + kc:MU + jkind * 8 + kc + 1],
                                                    scalar2=None, op0=ALU.mult),
                          reads=[pwr[s_], prmr], writes=[pwsr[s_]])
                for b in blks:
                    t0, n, kind = BLOCKS[b]
                    pb = nbank()
                    for kc in range(NC_):
                        kb.op("pe", lambda e: e.matmul(self.ps[pb][:, :n], lhsT=pw[s_][:, kc, :], rhs=U[:, kc, t0:t0 + n], start=(kc == 0), stop=False),
                              reads=[pwr[s_], self.Ur[kc][b]], writes=[self.psr[pb]])
                    for kc in range(NC_):
                        kb.op("pe", lambda e: e.matmul(self.ps[pb][:, :n], lhsT=pws[s_][:, kc, :], rhs=XX[:, kc, t0:t0 + n], start=False, stop=(kc == NC_ - 1)),
                              reads=[pwsr[s_], XXr[kc][b]], writes=[self.psr[pb]])
                    evac(b, pb, t0, n)

            LR = kb.sb(es, "rwLR", [128, 3, T], BF16)
            LRr = [[Reg() for _ in range(5)] for _ in range(3)]
            for (li, jk, fn) in ((0, 5, AF.Sigmoid), (1, 1, AF.Tanh), (2, 4, AF.Copy)):
                def ev(b, pb, t0, n, li=li, fn=fn):
                    kb.op("act", lambda e: e.activation(out=LR[:, li, t0:t0 + n], in_=self.ps[pb][:, :n], func=fn),
                          reads=[self.psr[pb]], writes=[LRr[li][b]])
                xproj(self.d_rwlr[li], jk, lblocks if li == 0 else blocks, ev)
            if dbg == "rw1":
                kb.barrier()
                return 'stop'
            lrw = kb.sb(es, "rwlrw", [128, 3, 2, 128], BF16)
            lrwr = [Reg(), Reg()]
            lrwd = [kb.dsem(f"rwlrw{s_}") for s_ in range(2)]
            ZT1 = kb.sb(es, "rwZT1", [128, SEQ], BF16)
            ZT1r = [Reg() for _ in range(5)]
            Rr_ = kb.sb(es, "rwR", [128, SEQ], BF16); Rr = [Reg() for _ in range(5)]
            Kk = kb.sb(es, "rwK", [128, T], BF16); Kr = [Reg() for _ in range(5)]
            KKn = kb.sb(es, "rwKK", [128, T], BF16); KKr = [Reg() for _ in range(5)]
            VTf = kb.sb(es, "rwVT", [128, T], BF16); VTr = [Reg() for _ in range(5)]
            Gg = kb.sb(es, "rwG", [128, SEQ], BF16); Ggr = [Reg() for _ in range(5)]
            BON = kb.sb(es, "rwBON", [128, SEQ], BF16); BONr = [Reg() for _ in range(5)]
            YA = kb.sb(es, "rwYA", [128, SEQ], F32); YAr = [Reg() for _ in range(5)]
            LW = kb.sb(es, "rwLW", [128, T], F32); LWr = [Reg() for _ in range(5)]
            KD = kb.sb(es, "rwKD", [128, T], BF16); KDr = [Reg() for _ in range(5)]
            BD = kb.sb(es, "rwBD", [128, T], BF16); BDr = [Reg() for _ in range(5)]
            tA = [kb.sb(es, f"rwtA{s_}", [128, 512], F32) for s_ in range(2)]
            tAr = [Reg(), Reg()]
            tB = [kb.sb(es, f"rwtB{s_}", [128, 512], BF16) for s_ in range(2)]
            tBr = [Reg(), Reg()]
            SCs = [kb.sb(es, f"rwSC{s_}", [128, 2, 64], F32) for s_ in range(G)]; SCr = [Reg() for _ in range(G)]
            TOT = [kb.sb(es, f"rwTOT{s_}", [128, 2], F32) for s_ in range(G)]; TOTr = [Reg() for _ in range(G)]
            EE = [kb.sb(es, f"rwEE{s_}", [128, 4, 64], F32) for s_ in range(G)]; EEr = [Reg() for _ in range(G)]
            OPS = [kb.sb(es, f"rwOPS{s_}", [128, 6, 64], BF16) for s_ in range(G)]; OPSr = [Reg() for _ in range(G)]
            RT32 = [kb.sb(es, f"rwRT{s_}", [128, 64], F32) for s_ in range(G)]; RT32r = [Reg() for _ in range(G)]
            TOK = [kb.sb(es, f"rwTOK{s_}", [64, 4, 128], BF16) for s_ in range(G)]; TOKr = [Reg() for _ in range(G)]
            GM = [kb.sb(es, f"rwGM{s_}", [64, 5, 2, 64], BF16) for s_ in range(G)]; GMr = [Reg() for _ in range(G)]
            NF = [kb.sb(es, f"rwNF{s_}", [64, 2, 2, 64], F32) for s_ in range(G)]; NFr = [Reg() for _ in range(G)]
            PP = [kb.sb(es, f"rwPP{s_}", [64, 2, 2, 2, 64], F32) for s_ in range(G)]; PPr = [[Reg() for _ in range(2)] for _ in range(G)]
            TT32 = [kb.sb(es, f"rwTT{s_}", [64, 2, 2, 64], F32) for s_ in range(G)]; TTr = [[Reg() for _ in range(2)] for _ in range(G)]
            TTb = [kb.sb(es, f"rwTTb{s_}", [64, 2, 64], BF16) for s_ in range(G)]; TTbr = [Reg() for _ in range(G)]
            id64f = kb.sb(es, "rwid64f", [64, 2, 64], F32)
            for h in range(2):
                kb.op("dve", lambda e: e.tensor_copy(out=id64f[:, h, :], in_=cstf[0:64, 0:64]), reads=[cstr], writes=[cstr])
            GS = [kb.sb(es, f"rwGS{s_}", [64, 3, 2, 64], BF16) for s_ in range(G)]; GSr = [[Reg() for _ in range(3)] for _ in range(G)]
            QH = [kb.sb(es, f"rwQH{s_}", [128, 64], BF16) for s_ in range(G)]; QHr = [Reg() for _ in range(G)]
            PH = [kb.sb(es, f"rwPH{s_}", [128, 128], F32) for s_ in range(G)]; PHr = [Reg() for _ in range(G)]
            Abd = kb.sb(es, "rwA", [128, 128], F32); Ar = Reg()
            Abf = kb.sb(es, "rwAbf", [128, 128], BF16); Abfr = Reg()
            psbf = [self.ps[i].bitcast(BF16) for i in range(8)]

            for pr in range(NC_):
                sl = pr % 2
                kb.dma("pool", lrwd[sl], lrw[:, :, sl, :], self.d_rwlrw[pr].rearrange("a p m -> p a m"), writes=[lrwr[sl]])
                def ev_r(b, pb, t0, n):
                    kb.op("act", lambda e: e.activation(out=Rr_[:, t0:t0 + n], in_=self.ps[pb][:, :n], func=AF.Copy), reads=[self.psr[pb]], writes=[Rr[b]])
                xproj(self.d_rwwrkv[0, pr], 0, lblocks, ev_r)

                def ev_k(b, pb, t0, n):
                    s_ = b % 2
                    kb.op("act", lambda e: e.activation(out=Kk[:, t0:t0 + n], in_=self.ps[pb][:, :n], func=AF.Copy), reads=[self.psr[pb]], writes=[Kr[b]])
                    kb.op("act", lambda e: e.activation(out=tA[s_][:, :n], in_=self.ps[pb][:, :n], func=AF.Copy, scale=prm[:, KK_ + pr:KK_ + pr + 1]),
                          reads=[self.psr[pb], prmr], writes=[tAr[s_]])
                    kb.op("act", lambda e: e.activation(out=tB[s_][:, :n], in_=tA[s_][:, :n], func=AF.Square), reads=[tAr[s_]], writes=[tBr[s_]])
                    p2 = nbank()
                    kb.op("pe", lambda e: e.matmul(self.ps[p2][:, :n], lhsT=onesbd[:], rhs=tB[s_][:, :n], start=True, stop=True),
                          reads=[cstr, tBr[s_]], writes=[self.psr[p2]])
                    kb.op("act", lambda e: e.activation(out=tB[s_][:, :n], in_=self.ps[p2][:, :n], func=AF.Ln, bias=epsg[:, 1:2]), reads=[self.psr[p2], cstr], writes=[tBr[s_]])
                    kb.op("act", lambda e: e.activation(out=tB[s_][:, :n], in_=tB[s_][:, :n], func=AF.Exp, scale=-0.5), reads=[tBr[s_]], writes=[tBr[s_]])
                    kb.op("dve", lambda e: e.tensor_tensor(out=KKn[:, t0:t0 + n], in0=tA[s_][:, :n], in1=tB[s_][:, :n], op=ALU.mult),
                          reads=[tAr[s_], tBr[s_]], writes=[KKr[b]])
                xproj(self.d_rwwrkv[1, pr], 2, blocks, ev_k)

                def ev_v(b, pb, t0, n):
                    kb.op("act", lambda e: e.activation(out=VTf[:, t0:t0 + n], in_=self.ps[pb][:, :n], func=AF.Copy), reads=[self.psr[pb]], writes=[VTr[b]])
                xproj(self.d_rwwrkv[2, pr], 3, blocks, ev_v)
                for b in lblocks:
                    t0, n, kind = BLOCKS[b]
                    pb = nbank()
                    kb.op("pe", lambda e: e.matmul(self.ps[pb][:, :n], lhsT=lrw[:, 0, sl, :], rhs=LR[:, 0, t0:t0 + n], start=True, stop=True),
                          reads=[lrwr[sl], LRr[0][b]], writes=[self.psr[pb]])
                    kb.op("act", lambda e: e.activation(out=Gg[:, t0:t0 + n], in_=self.ps[pb][:, :n], func=AF.Copy), reads=[self.psr[pb]], writes=[Ggr[b]])
                for d in range(2):
                    for b in blocks:
                        t0, n, kind = BLOCKS[b]
                        s_ = b % 2
                        pb = nbank()
                        kb.op("pe", lambda e: e.matmul(self.ps[pb][:, :n], lhsT=lrw[d * 64:(d + 1) * 64, 1, sl, :], rhs=LR[d * 64:(d + 1) * 64, 1, t0:t0 + n], start=True, stop=True),
                              reads=[lrwr[sl], LRr[1][b]], writes=[self.psr[pb]])
                        kb.op("act", lambda e: e.activation(out=tA[s_][:, :n], in_=self.ps[pb][:, :n], func=AF.Sigmoid, bias=prm[:, W0 + d * 8 + pr:W0 + d * 8 + pr + 1]),
                              reads=[self.psr[pb], prmr], writes=[tAr[s_]])
                        kb.op("pool", lambda e: e.tensor_scalar(out=LW[:, t0:t0 + n], in0=tA[s_][:, :n], scalar1=-EM05, scalar2=None, op0=ALU.mult),
                              reads=[tAr[s_]], writes=[LWr[b]])
                        pb = nbank()
                        kb.op("pe", lambda e: e.matmul(self.ps[pb][:, :n], lhsT=lrw[d * 64:(d + 1) * 64, 2, sl, :], rhs=LR[d * 64:(d + 1) * 64, 2, t0:t0 + n], start=True, stop=True),
                              reads=[lrwr[sl], LRr[2][b]], writes=[self.psr[pb]])
                        kb.op("act", lambda e: e.activation(out=tA[s_][:, :n], in_=self.ps[pb][:, :n], func=AF.Sigmoid, bias=prm[:, A0 + d * 8 + pr:A0 + d * 8 + pr + 1]),
                              reads=[self.psr[pb], prmr], writes=[tAr[s_]])
                        kb.op("pool", lambda e: e.tensor_tensor(out=BD[:, t0:t0 + n], in0=KKn[:, t0:t0 + n], in1=tA[s_][:, :n], op=ALU.mult),
                              reads=[KKr[b], tAr[s_]], writes=[BDr[b]])
                        kb.op("act", lambda e: e.activation(out=tA[s_][:, :n], in_=tA[s_][:, :n], func=AF.Identity, scale=prm[:, KA + pr:KA + pr + 1], bias=prm[:, OMKA + pr:OMKA + pr + 1]),
                              reads=[tAr[s_], prmr], writes=[tAr[s_]])
                        kb.op("dve", lambda e: e.tensor_tensor(out=KD[:, t0:t0 + n], in0=Kk[:, t0:t0 + n], in1=tA[s_][:, :n], op=ALU.mult),
                              reads=[Kr[b], tAr[s_]], writes=[KDr[b]])
                        if kind == 0:
                            kb.op("dve", lambda e: e.scalar_tensor_tensor(out=tB[s_][:, :n], in0=KD[:, t0:t0 + n], scalar=prm[:, RK + pr:RK + pr + 1], in1=Rr_[:, t0:t0 + n], op0=ALU.mult, op1=ALU.mult),
                                  reads=[KDr[b], Rr[b], prmr], writes=[tBr[s_]])
                            pb = nbank()
                            kb.op("pe", lambda e: e.matmul(self.ps[pb][:, :n], lhsT=onesbd[:], rhs=tB[s_][:, :n], start=True, stop=True),
                                  reads=[cstr, tBr[s_]], writes=[self.psr[pb]])
                            if d == 0:
                                kb.op("act", lambda e: e.activation(out=BON[:, t0:t0 + n], in_=self.ps[pb][:, :n], func=AF.Copy), reads=[self.psr[pb]], writes=[BONr[b]])
                            else:
                                kb.op("dve", lambda e: e.tensor_tensor(out=BON[:, t0:t0 + n], in0=self.ps[pb][:, :n], in1=BON[:, t0:t0 + n], op=ALU.add),
                                      reads=[self.psr[pb], BONr[b]], writes=[BONr[b]])
                    if dbg == "rw2":
                        kb.barrier()
                        return 'stop'
                    kb.op("dve", lambda e: e.memset(Abd[:], 0.0), writes=[Ar])
                    kb.op("dve", lambda e: e.memset(Abf[:], 0.0), writes=[Abfr])
                    order = ([32, 33, 34, 35] + list(range(32))) if d == 0 else ([35, 34, 33, 32] + list(range(31, -1, -1)))
                    def chunk_gen(ci, n_):
                        c0 = 64 * n_
                        b = min(c0 // 512, 4)
                        lat = n_ < 32
                        q = ci % G
                        cbs = {'i': 0}

                        def nbank():
                            cbs['i'] += 1
                            return 2 * q + (cbs['i'] % 2)
                        ng = 5 if lat else 3
                        kb.op("dve", lambda e: e.tensor_tensor_scan(out=SCs[q][:, 0, :], data0=self.onesf[:, 0:64], data1=LW[:, c0:c0 + 64], initial=0.0, op0=ALU.mult, op1=ALU.add),
                              reads=[LWr[b], self.onesr], writes=[SCr[q]])
                        kb.op("dve", lambda e: e.tensor_tensor(out=SCs[q][:, 1, :], in0=SCs[q][:, 0, :], in1=LW[:, c0:c0 + 64], op=ALU.subtract),
                              reads=[SCr[q], LWr[b]], writes=[SCr[q]])
                        kb.op("dve", lambda e: e.tensor_copy(out=TOT[q][:, 0:1], in_=SCs[q][:, 0, 63:64]), reads=[SCr[q]], writes=[TOTr[q]])
                        kb.op("dve", lambda e: e.tensor_scalar(out=TOT[q][:, 1:2], in0=SCs[q][:, 0, 63:64], scalar1=-1.0, scalar2=None, op0=ALU.mult), reads=[SCr[q]], writes=[TOTr[q]])
                        cs, cxf = SCs[q][:, 0, :], SCs[q][:, 1, :]
                        tot, ntot = TOT[q][:, 0:1], TOT[q][:, 1:2]
                        if d == 0:
                            exs = [(cxf, 1.0, None), (cs, -1.0, None), (cs, 1.0, None), (cs, -1.0, tot)]
                        else:
                            exs = [(cs, -1.0, tot), (cxf, 1.0, ntot), (cxf, -1.0, tot), (cxf, 1.0, None)]
                        for i, (src, sc_, bi) in enumerate(exs):
                            if i == 2 and not lat:
                                continue
                            if bi is None:
                                kb.op("act", lambda e: e.activation(out=EE[q][:, i, :], in_=src, func=AF.Exp, scale=sc_), reads=[SCr[q]], writes=[EEr[q]])
                            else:
                                kb.op("act", lambda e: e.activation(out=EE[q][:, i, :], in_=src, func=AF.Exp, scale=sc_, bias=bi), reads=[SCr[q], TOTr[q]], writes=[EEr[q]])
                        yield
                        opl = [(0, KKn, KKr, 0), (1, BD, BDr, 1), (2, KD, KDr, 1), (4, BD, BDr, 3), (5, KD, KDr, 3)]
                        for oi, (o_, srcb, srcr, ei) in enumerate(opl):
                            kb.op("dve" if oi % 2 == 0 else "pool",
                                  lambda e: e.tensor_tensor(out=OPS[q][:, o_, :], in0=srcb[:, c0:c0 + 64], in1=EE[q][:, ei, :], op=ALU.mult),
                                  reads=[srcr[b], EEr[q]], writes=[OPSr[q]])
                        if lat:
                            kb.op("dve", lambda e: e.tensor_tensor(out=RT32[q][:], in0=Rr_[:, c0:c0 + 64], in1=EE[q][:, 2, :], op=ALU.mult),
                                  reads=[Rr[b], EEr[q]], writes=[RT32r[q]])
                            kb.op("pool", lambda e: e.tensor_copy(out=OPS[q][:, 3, :], in_=RT32[q][:]), reads=[RT32r[q]], writes=[OPSr[q]])
                        yield
                        pb = nbank()
                        for i, o_ in enumerate((0, 4, 5)):
                            kb.op("pe", lambda e: e.transpose(psbf[pb][0:64, i * 128:(i + 1) * 128], OPS[q][:, o_, :], identb[:]),
                                  reads=[OPSr[q], cstr], writes=[self.psr[pb]])
                        kb.op("pe", lambda e: e.transpose(psbf[pb][0:64, 384:512], VTf[:, c0:c0 + 64], identb[:]),
                              reads=[VTr[b], cstr], writes=[self.psr[pb]])
                        kb.op("act", lambda e: e.activation(out=TOK[q][:].rearrange("p a b -> p (a b)"), in_=psbf[pb][0:64, 0:512], func=AF.Copy),
                              reads=[self.psr[pb]], writes=[TOKr[q]])
                        yield
                        gpairs = [(0, 1), (1, 0), (2, 0), (1, 3), (2, 3)]
                        pbh = [nbank(), nbank()]
                        for wi in range(ng):
                            li_, ri_ = gpairs[wi]
                            for h in range(2):
                                kb.op("pe", lambda e: e.matmul(self.ps[pbh[h]][0:64, wi * 64:(wi + 1) * 64], lhsT=OPS[q][h * 64:(h + 1) * 64, li_, :],
                                                                rhs=OPS[q][h * 64:(h + 1) * 64, ri_, :], start=True, stop=True),
                                      reads=[OPSr[q]], writes=[self.psr[pbh[h]]])
                        for h in range(2):
                            kb.op("dve", lambda e: e.tensor_tensor(out=GM[q][:, 0:ng, h, :], in0=self.ps[pbh[h]][0:64, 0:ng * 64].rearrange("p (a c) -> p a c", a=ng),
                                                                   in1=gmask[:, d, 0:ng, h, :], op=ALU.mult),
                                  reads=[self.psr[pbh[h]], cstr], writes=[GMr[q]])
                            kb.op("dve", lambda e: e.tensor_tensor(out=NF[q][:, :, h, :], in0=self.ps[pbh[h]][0:64, 0:128].rearrange("p (a c) -> p a c", a=2),
                                                                   in1=gmask[:, d, 0:2, h, :], op=ALU.mult),
                                  reads=[self.psr[pbh[h]], cstr], writes=[NFr[q]])
                        yield
                        kb.op("pool", lambda e: e.tensor_tensor(out=TT32[q][:, 0, :, :], in0=NF[q][:, 1, :, :], in1=id64f[:], op=ALU.add),
                              reads=[NFr[q], cstr], writes=[TTr[q][0]])
                        Pm = lambda lv, tr: (NF[q][:, tr, :, :] if lv == 0 else PP[q][:, lv % 2, tr, :, :])
                        Pmr = lambda lv: (NFr[q] if lv == 0 else PPr[q][lv % 2])
                        for lv in range(1, 6):
                            pb = nbank()
                            ntr = 2 if lv < 5 else 1
                            for tr in range(ntr):
                                for h in range(2):
                                    lt, rt = (1, 0) if tr == 0 else (0, 1)
                                    kb.op("pe", lambda e: e.matmul(self.ps[pb][0:64, (tr * 2 + h) * 64:(tr * 2 + h + 1) * 64], lhsT=Pm(lv - 1, lt)[:, h, :], rhs=Pm(lv - 1, rt)[:, h, :],
                                                                    start=True, stop=True),
                                          reads=[Pmr(lv - 1)], writes=[self.psr[pb]])
                            kb.op("act", lambda e: e.activation(out=PP[q][:, lv % 2, 0:ntr, :, :].rearrange("p a b c -> p (a b c)"), in_=self.ps[pb][0:64, 0:ntr * 128], func=AF.Copy),
                                  reads=[self.psr[pb]], writes=[PPr[q][lv % 2]])
                            yield
                            pb = nbank()
                            for h in range(2):
                                kb.op("pe", lambda e: e.matmul(self.ps[pb][0:64, h * 64:(h + 1) * 64], lhsT=PP[q][:, lv % 2, 0, h, :], rhs=TT32[q][:, (lv - 1) % 2, h, :], start=True, stop=True),
                                      reads=[PPr[q][lv % 2], TTr[q][(lv - 1) % 2]], writes=[self.psr[pb]])
                            kb.op("dve", lambda e: e.tensor_tensor(out=TT32[q][:, lv % 2, :, :].rearrange("p b c -> p (b c)"), in0=self.ps[pb][0:64, 0:128],
                                                                   in1=TT32[q][:, (lv - 1) % 2, :, :].rearrange("p b c -> p (b c)"), op=ALU.add),
                                  reads=[self.psr[pb], TTr[q][(lv - 1) % 2]], writes=[TTr[q][lv % 2]])
                            yield
                        kb.op("act", lambda e: e.activation(out=TTb[q][:].rearrange("p b c -> p (b c)"), in_=TT32[q][:, 1, :, :].rearrange("p b c -> p (b c)"), func=AF.Copy),
                              reads=[TTr[q][1]], writes=[TTbr[q]])
                        TTf = TTb[q]
                        TTfr = TTbr[q]
                        yield
                        pb = nbank()
                        for h in range(2):
                            kb.op("pe", lambda e: e.matmul(self.ps[pb][0:64, h * 64:(h + 1) * 64], lhsT=TTf[:, h, :], rhs=TOK[q][:, 0, h * 64:(h + 1) * 64], start=True, stop=True),
                                  reads=[TTfr, TOKr[q]], writes=[self.psr[pb]])
                            kb.op("pe", lambda e: e.matmul(self.ps[pb][0:64, 128 + h * 64:128 + (h + 1) * 64], lhsT=GM[q][:, 2, h, :], rhs=TOK[q][:, 3, h * 64:(h + 1) * 64], start=True, stop=True),
                                  reads=[GMr[q], TOKr[q]], writes=[self.psr[pb]])
                        kb.op("act", lambda e: e.activation(out=GS[q][:, 0:2, :, :].rearrange("p a b c -> p (a b c)"), in_=self.ps[pb][0:64, 0:256], func=AF.Copy),
                              reads=[self.psr[pb]], writes=[GSr[q][0], GSr[q][1]])
                        yield
                        pb = nbank()
                        for h in range(2):
                            kb.op("pe", lambda e: e.matmul(self.ps[pb][0:64, h * 64:(h + 1) * 64], lhsT=TTf[:, h, :], rhs=GS[q][:, 1, h, :], start=True, stop=True),
                                  reads=[TTfr, GSr[q][1]], writes=[self.psr[pb]])
                        kb.op("act", lambda e: e.activation(out=GS[q][:, 2, :, :].rearrange("p b c -> p (b c)"), in_=self.ps[pb][0:64, 0:128], func=AF.Copy),
                              reads=[self.psr[pb]], writes=[GSr[q][2]])
                        Gfl = GS[q][:, 0, :, :].rearrange("p b c -> p (b c)")
                        U0fl = GS[q][:, 2, :, :].rearrange("p b c -> p (b c)")
                        yield
                        gcol = EE[q][:, 2, 63:64] if d == 0 else EE[q][:, 2, 0:1]
                        if not lat:
                            kb.op("act", lambda e: e.activation(out=EE[q][:, 2, 0:1], in_=TOT[q][:, 0:1], func=AF.Exp), reads=[TOTr[q], EEr[q]], writes=[EEr[q]])
                            gcol = EE[q][:, 2, 0:1]
                        pb = nbank()
                        kb.op("pe", lambda e: e.matmul(self.ps[pb][:, 0:128], lhsT=Gfl, rhs=TOK[q][:, 1, :], start=True, stop=True),
                              reads=[GSr[q][0], TOKr[q]], writes=[self.psr[pb]])
                        kb.op("dve", lambda e: e.scalar_tensor_tensor(out=PH[q][:], in0=ident, scalar=gcol, in1=self.ps[pb][:, 0:128], op0=ALU.mult, op1=ALU.subtract),
                              reads=[cstr, EEr[q], self.psr[pb]], writes=[PHr[q]])
                        kb.op("pool", lambda e: e.tensor_tensor(out=PH[q][:], in0=PH[q][:], in1=bdmask, op=ALU.mult), reads=[PHr[q], cstr], writes=[PHr[q]])
                        yield
                        if lat:
                            pb = nbank()
                            for h in range(2):
                                kb.op("pe", lambda e: e.matmul(self.ps[pb][:, h * 64:(h + 1) * 64], lhsT=Gfl, rhs=GM[q][:, 3, h, :], start=True, stop=True),
                                      reads=[GSr[q][0], GMr[q]], writes=[self.psr[pb]])
                            for h in range(2):
                                kb.op("dve", lambda e: e.tensor_tensor(out=QH[q][h * 64:(h + 1) * 64, :], in0=RT32[q][h * 64:(h + 1) * 64, :],
                                                                       in1=self.ps[pb][h * 64:(h + 1) * 64, h * 64:(h + 1) * 64], op=ALU.subtract),
                                      reads=[RT32r[q], self.psr[pb]], writes=[QHr[q]])
                            yield
                            pb = nbank()
                            for h in range(2):
                                kb.op("pe", lambda e: e.matmul(self.ps[pb][:, h * 64:(h + 1) * 64], lhsT=U0fl, rhs=GM[q][:, 3, h, :], start=True, stop=False),
                                      reads=[GSr[q][2], GMr[q]], writes=[self.psr[pb]])
                                kb.op("pe", lambda e: e.matmul(self.ps[pb][:, h * 64:(h + 1) * 64], lhsT=TOK[q][:, 3, :], rhs=GM[q][:, 4, h, :], start=False, stop=False),
                                      reads=[TOKr[q], GMr[q]], writes=[self.psr[pb]])
                                kb.op("pe", lambda e: e.matmul(self.ps[pb][:, h * 64:(h + 1) * 64], lhsT=Abf[:], rhs=QH[q][:], start=False, stop=True),
                                      reads=[Abfr, QHr[q]], writes=[self.psr[pb]])
                            for h in range(2):
                                if d == 0:
                                    kb.op("act", lambda e: e.activation(out=YA[h * 64:(h + 1) * 64, c0:c0 + 64], in_=self.ps[pb][h * 64:(h + 1) * 64, h * 64:(h + 1) * 64], func=AF.Copy),
                                          reads=[self.psr[pb]], writes=[YAr[b]])
                                else:
                                    kb.op("dve", lambda e: e.tensor_tensor(out=YA[h * 64:(h + 1) * 64, c0:c0 + 64], in0=self.ps[pb][h * 64:(h + 1) * 64, h * 64:(h + 1) * 64],
                                                                           in1=YA[h * 64:(h + 1) * 64, c0:c0 + 64], op=ALU.add),
                                          reads=[self.psr[pb], YAr[b]], writes=[YAr[b]])
                        yield
                        pb = nbank()
                        kb.op("pe", lambda e: e.matmul(self.ps[pb][:, 0:128], lhsT=TOK[q][:, 1, :], rhs=U0fl, start=True, stop=False),
                              reads=[TOKr[q], GSr[q][2]], writes=[self.psr[pb]])
                        kb.op("pe", lambda e: e.matmul(self.ps[pb][:, 0:128], lhsT=TOK[q][:, 2, :], rhs=TOK[q][:, 3, :], start=False, stop=False),
                              reads=[TOKr[q]], writes=[self.psr[pb]])
                        kb.op("pe", lambda e: e.matmul(self.ps[pb][:, 0:128], lhsT=PH[q][:], rhs=Abd[:], start=False, stop=True),
                              reads=[PHr[q], Ar], writes=[self.psr[pb]])
                        kb.op("dve", lambda e: e.tensor_tensor(out=Abd[:], in0=self.ps[pb][:, 0:128], in1=bdmask, op=ALU.mult),
                              reads=[self.psr[pb], cstr], writes=[Ar])
                        kb.op("pool", lambda e: e.tensor_copy(out=Abf[:], in_=Abd[:]), reads=[Ar], writes=[Abfr])
                    active = []
                    pending = list(enumerate(order))
                    while pending or active:
                        if pending and len(active) < G:
                            active.append(chunk_gen(*pending.pop(0)))
                        for g_ in list(active):
                            try:
                                next(g_)
                            except StopIteration:
                                active.remove(g_)
                    if dbg == f"rwdump{pr}_{d}":
                        dd = kb.dsem("dbg")
                        self.d_dbg = self.nc.dram_tensor("dbgout", [8, 128, T], F32, kind="ExternalOutput").ap()
                        kb.dma("sp", dd, self.d_dbg[0][:, 0:SEQ], YA[:], reads=YAr)
                        kb.dma("sp", dd, self.d_dbg[1], LW[:], reads=LWr)
                        kb.dma("pool", dd, self.d_dbg[2], KKn[:], reads=KKr)
                        kb.dma("pool", dd, self.d_dbg[3], KD[:], reads=KDr)
                        kb.dma("pool", dd, self.d_dbg[4], BD[:], reads=BDr)
                        kb.dma("pool", dd, self.d_dbg[5][:, 0:SEQ], Rr_[:], reads=Rr)
                        kb.dma("pool", dd, self.d_dbg[6], VTf[:], reads=VTr)
                        kb.dma("sp", dd, self.d_dbg[7][:, 0:128], Abd[:], reads=[Ar])
                        kb.eng["sp"].wait_ge(dd.sem, dd.cnt)
                        kb.eng["pool"].wait_ge(dd.sem, dd.cnt)
                        kb.barrier()
                        return 'stop'
                for b in lblocks:
                    t0, n, kind = BLOCKS[b]
                    s_ = b % 2
                    kb.op("act", lambda e: e.activation(out=tB[s_][:, :n], in_=YA[:, t0:t0 + n], func=AF.Copy), reads=[YAr[b]], writes=[tBr[s_]])
                    pm = nbank()
                    kb.op("pe", lambda e: e.matmul(self.ps[pm][:, :n], lhsT=ones64[:], rhs=tB[s_][:, :n], start=True, stop=True), reads=[cstr, tBr[s_]], writes=[self.psr[pm]])
                    kb.op("act", lambda e: e.activation(out=tB[s_][:, :n], in_=YA[:, t0:t0 + n], func=AF.Square), reads=[YAr[b]], writes=[tBr[s_]])
                    pq = nbank()
                    kb.op("pe", lambda e: e.matmul(self.ps[pq][:, :n], lhsT=ones64[:], rhs=tB[s_][:, :n], start=True, stop=True), reads=[cstr, tBr[s_]], writes=[self.psr[pq]])
                    kb.op("act", lambda e: e.activation(out=tA[s_][:, :n], in_=self.ps[pm][:, :n], func=AF.Square), reads=[self.psr[pm]], writes=[tAr[s_]])
                    kb.op("dve", lambda e: e.tensor_tensor(out=tA[s_][:, :n], in0=self.ps[pq][:, :n], in1=tA[s_][:, :n], op=ALU.subtract), reads=[self.psr[pq], tAr[s_]], writes=[tAr[s_]])
                    kb.op("act", lambda e: e.activation(out=tA[s_][:, :n], in_=tA[s_][:, :n], func=AF.Ln, bias=epsg[:, 0:1]), reads=[tAr[s_], cstr], writes=[tAr[s_]])
                    kb.op("act", lambda e: e.activation(out=tA[s_][:, :n], in_=tA[s_][:, :n], func=AF.Exp, scale=-0.5), reads=[tAr[s_]], writes=[tAr[s_]])
                    kb.op("dve", lambda e: e.tensor_tensor(out=YA[:, t0:t0 + n], in0=YA[:, t0:t0 + n], in1=self.ps[pm][:, :n], op=ALU.subtract), reads=[YAr[b], self.psr[pm]], writes=[YAr[b]])
                    kb.op("pool", lambda e: e.tensor_tensor(out=YA[:, t0:t0 + n], in0=YA[:, t0:t0 + n], in1=tA[s_][:, :n], op=ALU.mult), reads=[YAr[b], tAr[s_]], writes=[YAr[b]])
                    kb.op("act", lambda e: e.activation(out=YA[:, t0:t0 + n], in_=YA[:, t0:t0 + n], func=AF.Identity, scale=prm[:, GNG + pr:GNG + pr + 1], bias=prm[:, GNB + pr:GNB + pr + 1]),
                          reads=[YAr[b], prmr], writes=[YAr[b]])
                    kb.op("dve", lambda e: e.tensor_tensor(out=BON[:, t0:t0 + n], in0=BON[:, t0:t0 + n], in1=VTf[:, t0:t0 + n], op=ALU.mult), reads=[BONr[b], VTr[b]], writes=[BONr[b]])
                    kb.op("pool", lambda e: e.tensor_tensor(out=YA[:, t0:t0 + n], in0=YA[:, t0:t0 + n], in1=BON[:, t0:t0 + n], op=ALU.add), reads=[YAr[b], BONr[b]], writes=[YAr[b]])
                    kb.op("dve", lambda e: e.tensor_tensor(out=ZT1[:, t0:t0 + n], in0=YA[:, t0:t0 + n], in1=Gg[:, t0:t0 + n], op=ALU.mult), reads=[YAr[b], Ggr[b]], writes=[ZT1r[b]])
                kb.dma("sp", ds_z, d_zpark[pr], ZT1[:], reads=ZT1r)
                kb.barrier()
        self.unpark_h()
        with ExitStack() as es:
            ZT = kb.sb(es, "rwZT", [128, NC_, SEQ], BF16)
            ZTr = [[Reg() for _ in range(5)] for _ in range(NC_)]
            for c in range(NC_):
                kb.dma("sp", ds_z, ZT[:, c, :], d_zpark[c], writes=ZTr[c])
            self.proj(es, self.d_rwwo, range(NC_), ZT, ZTr, self.resid_evac(j), lblocks, tag="rwwo")
            kb.barrier()
        self.layernorm(L, j, lblocks)

    def _nb(self):
        self.psrot += 1
        return self.psrot % 8

def _prep_common(inp):
    f = np.float32
    out = {}
    aw = np.asarray(inp["ada_w"], f).reshape(DEPTH, NC_, 128, 72, 128)
    out["adaw"] = np.ascontiguousarray(aw.transpose(0, 3, 2, 1, 4)).reshape(DEPTH, 72, 128, NC_ * 128)
    out["adab"] = np.ascontiguousarray(np.asarray(inp["ada_b"], f).reshape(DEPTH, 72, 128).transpose(0, 2, 1))
    out["lng"] = np.ascontiguousarray(np.asarray(inp["ln_g"], f).reshape(DEPTH * 3 * NC_, 128).T)
    out["lnb"] = np.ascontiguousarray(np.asarray(inp["ln_b"], f).reshape(DEPTH * 3 * NC_, 128).T)
    w13 = np.asarray(inp["ffn_w13"], f).reshape(DEPTH, 2, NC_, 128, 2, NF, 128)
    out["w13"] = np.ascontiguousarray(w13.transpose(0, 1, 5, 3, 4, 2, 6)).reshape(DEPTH, 2, NF, 128, 2 * NC_ * 128)
    w2 = np.asarray(inp["ffn_w2"], f).reshape(DEPTH, 2, NF, 128, NC_, 128)
    out["w2"] = np.ascontiguousarray(w2.transpose(0, 1, 4, 3, 2, 5)).reshape(DEPTH, 2, NC_, 128, NF * 128)
    out["ident"] = np.eye(128, dtype=f)
    fm = lambda v: np.asarray(v, f).reshape(-1, 128).T
    wl = lambda W: np.ascontiguousarray(np.asarray(W, f).reshape(W.shape[0] // 128, 128, W.shape[1] // 128, 128)
                                        .transpose(2, 1, 0, 3)).reshape(W.shape[1] // 128, 128, W.shape[0])
    w1 = np.asarray(inp["cv_w1"][0], f).reshape(NC_, 128, 2, NC_, 128)
    out["cvw1"] = np.ascontiguousarray(w1.transpose(3, 1, 2, 0, 4)).reshape(NC_, 128, 2 * NC_ * 128)
    out["cvw2"] = wl(inp["cv_w2"][0])
    wdw = np.asarray(inp["cv_wdw"][0], f).reshape(31, NC_, 128).transpose(2, 0, 1).reshape(128, 31 * NC_)
    out["cvprm"] = np.ascontiguousarray(np.concatenate(
        [fm(inp["cv_b1"][0]), fm(inp["cv_bdw"][0]), fm(inp["cv_ln_g"][0]), fm(inp["cv_ln_b"][0]), fm(inp["cv_b2"][0]), wdw], axis=1))
    out["nawqkv"] = wl(inp["na_wqkv"][0])
    out["nawo"] = wl(inp["na_wo"][0])
    out["nastrip"] = _na_strips(np.asarray(inp["na_rpb"][0], f))
    win = np.asarray(inp["gla_win"][0], f)
    wlq = wl(win)
    out["glawqk"] = np.ascontiguousarray(wlq[0:8])
    m = np.arange(128)
    partner = (m // 64) * 64 + ((m % 64) + 32) % 64
    qk = win[:, 0:1024].reshape(D, 8, 128)[:, :, partner].reshape(D, 1024)
    out["glawqkp"] = wl(qk)
    wv = win[:, 1024:2048].reshape(NC_, 128, 4, 256)
    out["glawv"] = np.ascontiguousarray(wv.transpose(2, 1, 0, 3)).reshape(4, 128, NC_ * 256)
    out["glawg"] = np.ascontiguousarray(wlq[16:24])
    wo = np.asarray(inp["gla_wo"][0], f).reshape(4, 2, 128, NC_, 128)
    out["glawo"] = np.ascontiguousarray(wo.transpose(0, 3, 2, 1, 4)).reshape(4, NC_, 128, 256)
    wa1 = np.asarray(inp["gla_wa1"][0], f)
    wa1c = np.concatenate([wa1[0], wa1[1]], axis=1).reshape(NC_, 128, 32)
    out["glawa1"] = np.ascontiguousarray(wa1c.transpose(1, 0, 2)).reshape(128, NC_ * 32)
    wa2 = np.asarray(inp["gla_wa2"][0], f)
    wa2p = np.zeros((32, 2, 512), f)
    wa2p[0:16, 0] = wa2[0]
    wa2p[16:32, 1] = wa2[1]
    out["glawa2"] = wa2p.reshape(32, 1024)
    i = (m % 64) % 32
    freq = (10000.0 ** (-(i.astype(np.float64)) / 32.0))[:, None]
    t = np.arange(SEQ)[None, :]
    pos = np.where((m < 64)[:, None], t // 64, t % 64)
    ang = (pos.astype(np.float32) * freq.astype(np.float32)).astype(np.float32)
    sgn = np.where((m % 64) < 32, -1.0, 1.0)[:, None]
    out["glarope"] = np.stack([np.cos(ang), sgn * np.sin(ang)]).astype(f)
    sl = np.arange(64)[:, None]
    cc = np.arange(64)[None, :]
    masks = np.zeros((128, 4, 64), f)
    for par in range(2):
        masks[par * 64:(par + 1) * 64, par * 2 + 0] = (cc >= sl)
        masks[par * 64:(par + 1) * 64, par * 2 + 1] = (cc <= sl)
    ba = np.asarray(inp["gla_ba"][0], f).reshape(2, 4, 128).transpose(2, 0, 1).reshape(128, 8)
    out["glacst"] = np.ascontiguousarray(np.concatenate(
        [masks.reshape(128, 256), np.eye(128, dtype=f), fm(inp["gla_norm_g"][0]), ba], axis=1))
    out["rwwrkv"] = np.stack([wl(inp["rw_wrkv"][0][i]) for i in range(3)])
    out["rwwo"] = wl(inp["rw_wo"][0])
    cat2 = lambda a: np.concatenate([np.asarray(a[0], f), np.asarray(a[1], f)], axis=1)
    out["rwlr"] = np.stack([wl(np.asarray(inp["rw_g1"][0], f))[0], wl(cat2(inp["rw_w1"][0]))[0], wl(cat2(inp["rw_a1"][0]))[0]])
    g2 = np.asarray(inp["rw_g2"][0], f).reshape(128, NC_, 128)
    w2 = np.concatenate([np.asarray(inp["rw_w2"][0][0], f), np.asarray(inp["rw_w2"][0][1], f)], axis=0).reshape(128, NC_, 128)
    a2 = np.concatenate([np.asarray(inp["rw_a2"][0][0], f), np.asarray(inp["rw_a2"][0][1], f)], axis=0).reshape(128, NC_, 128)
    out["rwlrw"] = np.ascontiguousarray(np.stack([g2, w2, a2]).transpose(2, 0, 1, 3))
    out["rwprm"] = np.ascontiguousarray(np.concatenate(
        [fm(inp["rw_w0"][0]), fm(inp["rw_a0"][0]), fm(inp["rw_kk"][0]), fm(inp["rw_ka"][0]), fm(inp["rw_rk"][0]),
         fm(inp["rw_gn_g"][0]), fm(inp["rw_gn_b"][0]), fm(inp["rw_mu"][0])], axis=1))
    bd = np.zeros((128, 128), f)
    bd[:64, :64] = 1.0
    bd[64:, 64:] = 1.0
    out["rwcstf"] = np.ascontiguousarray(np.concatenate([np.eye(128, dtype=f), bd], axis=1))
    pi = np.arange(64)[:, None]
    fi = np.arange(64)[None, :]
    gm = np.zeros((64, 2, 5, 2, 64), f)
    for d_ in range(2):
        lt = (fi < pi) if d_ == 0 else (fi > pi)
        st = (pi < fi) if d_ == 0 else (pi > fi)
        se = (pi <= fi) if d_ == 0 else (pi >= fi)
        for h_ in range(2):
            gm[:, d_, 0, h_] = -1.0 * lt
            gm[:, d_, 1, h_] = -1.0 * st
            gm[:, d_, 2, h_] = -1.0 * st
            gm[:, d_, 3, h_] = 1.0 * se
            gm[:, d_, 4, h_] = 1.0 * se
    out["rwgmask"] = np.ascontiguousarray(gm.reshape(64, -1))
    return out


def _na_strips(rpb):
    MASK = np.float32(-30000.0)
    kc = np.arange(64)[:, None]
    qc = np.arange(64)[None, :]
    cstart = np.clip(qc - 8, 0, 48)
    col_ok = (kc >= cstart) & (kc < cstart + 16)
    dc_idx = np.clip(kc - qc + 15, 0, 30)
    R = np.where(col_ok[None, None], rpb[:, :, dc_idx], MASK)
    maskblk = np.full((16, 64, 64), MASK, np.float32)
    strip = np.empty((16, 2, 64, 37, 64), np.float32)
    for half in range(2):
        for jj in range(23):
            d = 11 - jj + half
            strip[:, half, :, jj, :] = R[:, d + 7] if -4 <= d <= 3 else maskblk
        for jj in range(14):
            d = 6 - jj + half
            strip[:, half, :, 23 + jj, :] = R[:, d + 7] if -7 <= d <= 7 else maskblk
    return np.ascontiguousarray(strip.reshape(16, 128, 37 * 64))


def _prep_core(inp, b):
    f = np.float32
    xb = np.concatenate([np.asarray(inp["x"][b], f), np.asarray(inp["ctx"][b], f)], axis=0)
    hin = np.ascontiguousarray(xb.T).reshape(NC_, 128, T)
    cond = np.stack([np.asarray(inp["c"][b], f), np.asarray(inp["c_ctx"], f)], axis=-1)
    cond = np.ascontiguousarray(cond.reshape(NC_, 128, 2).transpose(1, 0, 2))
    return {"hin": hin, "cond": cond}


def build_full():
    p = Prog(list(range(DEPTH)))
    p.setup()
    mixers = [p.na_mixer, p.conv_mixer, p.gla_mixer, p.rwkv_mixer]
    for L in range(DEPTH):
        last = L == DEPTH - 1
        p.ada(L)
        p.ffn(L, 0, 0)
        mixers[L % 4](L, last)
        p.ffn(L, 2, 1, blocks=([0, 1, 2, 3] if last else range(5)))
    p.store()
    p.es_H.close()
    p.kb.es.close()
    return p


def kernel(**inputs):
    common = _prep_common(inputs)
    p = build_full()
    in_maps = []
    for b in range(8):
        m = dict(common)
        m.update(_prep_core(inputs, b))
        in_maps.append(m)
    res = run_bass_kernel_spmd(p.nc, in_maps, core_ids=list(range(8)))
    out = np.stack([np.asarray(r["hout"]).reshape(D, T)[:, :SEQ].T for r in res.results], axis=0)
    return np.ascontiguousarray(out.astype(np.float32))
```

```python
import numpy as np
from contextlib import ExitStack
import concourse.bass as bass
import concourse.mybir as mybir
from concourse.bass_utils import run_bass_kernel_spmd

F32 = mybir.dt.float32
BF16 = mybir.dt.bfloat16
AF = mybir.ActivationFunctionType
ALU = mybir.AluOpType

D = 1024
NC_ = 8
SEQ = 2048
CTX = 256
T = SEQ + CTX
DEPTH = 4
DFF = 2816
NF = DFF // 128
ALPHA = (2.0 * DEPTH) ** 0.25
LN_EPS = 1e-5
EPS_P = LN_EPS / (ALPHA * ALPHA)

BLOCKS = [(0, 512, 0), (512, 512, 0), (1024, 512, 0), (1536, 512, 0), (2048, 256, 1)]
HALVES = [[0, 1], [2, 3, 4]]


class DebugStop(Exception):
    pass


class Reg:
    __slots__ = ("w", "r", "name")

    def __init__(self, name=""):
        self.w = None
        self.r = {}
        self.name = name


class DSem:
    def __init__(self, sem):
        self.sem = sem
        self.cnt = 0


class KB:
    def __init__(self):
        self.nc = bass.Bass("TRN2", target_bir_lowering=False)
        nc = self.nc
        self.es = ExitStack()
        self.eng = {"pe": nc.tensor, "dve": nc.vector, "act": nc.scalar, "pool": nc.gpsimd, "sp": nc.sync}
        self.sem = {e: self.es.enter_context(nc.semaphore("prog_" + e)) for e in self.eng}
        self.cnt = {e: 0 for e in self.eng}
        self.seen = {e: {} for e in self.eng}
        self.dsems = []
        self.n_ins = 0

    def dsem(self, name):
        self.n_ds = getattr(self, "n_ds", 0) + 1
        d = DSem(self.es.enter_context(self.nc.semaphore(f"d_{name}_{self.n_ds}")))
        self.dsems.append(d)
        return d

    def sb(self, es, name, shape, dt):
        self.n_sb = getattr(self, "n_sb", 0) + 1
        return es.enter_context(self.nc.sbuf_tensor(f"{name}_s{self.n_sb}", shape, dt))

    def _wait(self, e, ev):
        if ev is None:
            return
        sem, val = ev[0], ev[1]
        if len(ev) > 2:
            val = max(val, ev[2].cnt)
        if e == "pe" and sem is self.sem["pe"]:
            return
        k = id(sem)
        if self.seen[e].get(k, 0) >= val:
            return
        self.eng[e].wait_ge(sem, val)
        self.seen[e][k] = val

    def _deps(self, e, reads, writes):
        for r in reads:
            self._wait(e, r.w)
        for r in writes:
            self._wait(e, r.w)
            for ev in r.r.values():
                self._wait(e, ev)

    def op(self, e, fn, reads=(), writes=()):
        self._deps(e, reads, writes)
        ins = fn(self.eng[e])
        self.cnt[e] += 1
        self.n_ins += 1
        ins.then_inc(self.sem[e], 1)
        ev = (self.sem[e], self.cnt[e])
        for r in reads:
            r.r[id(ev[0])] = ev
        for r in writes:
            r.w = ev
            r.r = {}
        return ev

    def dma(self, q, ds, out, in_, reads=(), writes=()):
        self._deps(q, reads, writes)
        if ds.cnt > 0:
            self._wait(q, (ds.sem, ds.cnt, ds))
        ins = self.eng[q].dma_start(out=out, in_=in_)
        ds.cnt += 16
        self.n_ins += 1
        ins.then_inc(ds.sem, 16)
        ev = (ds.sem, ds.cnt, ds)
        for r in reads:
            r.r[id(ev[0])] = ev
        for r in writes:
            r.w = ev
            r.r = {}
        return ev

    def barrier(self):
        for e in self.eng:
            for e2 in self.eng:
                if e2 != e and self.cnt[e2] > 0:
                    self._wait(e, (self.sem[e2], self.cnt[e2]))
            for d in self.dsems:
                if d.cnt > 0:
                    self._wait(e, (d.sem, d.cnt))


def midx(j, k, c):
    return j * 24 + k * 8 + c


class Prog:
    def __init__(self, layers, final_layer_is_last=True, load_h=True):
        self.kb = KB()
        kb = self.kb
        nc = kb.nc
        self.nc = nc
        es = kb.es
        self.layers = layers
        dr = lambda name, shape, dt=F32, kind="ExternalInput": nc.dram_tensor(name, shape, dt, kind=kind).ap()
        self.d_hin = dr("hin", [NC_, 128, T])
        self.d_hout = dr("hout", [NC_, 128, T], kind="ExternalOutput")
        self.d_cond = dr("cond", [128, NC_, 2])
        self.d_adaw = dr("adaw", [DEPTH, 72, 128, NC_ * 128])
        self.d_adab = dr("adab", [DEPTH, 128, 72])
        self.d_lng = dr("lng", [128, DEPTH * 3 * NC_])
        self.d_lnb = dr("lnb", [128, DEPTH * 3 * NC_])
        self.d_w13 = dr("w13", [DEPTH, 2, NF, 128, 2 * NC_ * 128])
        self.d_w2 = dr("w2", [DEPTH, 2, NC_, 128, NF * 128])
        self.d_ident = dr("ident", [128, 128])
        self.d_rwprm = dr("rwprm", [128, 120])
        self.d_rwcstf = dr("rwcstf", [128, 256])
        self.d_rwgmask = dr("rwgmask", [64, 2 * 5 * 2 * 64])
        self.d_rwlr = dr("rwlr", [3, 128, NC_ * 128])
        self.d_rwlrw = dr("rwlrw", [NC_, 3, 128, 128])
        self.d_rwwrkv = dr("rwwrkv", [3, NC_, 128, NC_ * 128])
        self.d_rwwo = dr("rwwo", [NC_, 128, NC_ * 128])
        self.d_glawqk = dr("glawqk", [8, 128, NC_ * 128])
        self.d_glawqkp = dr("glawqkp", [8, 128, NC_ * 128])
        self.d_glawv = dr("glawv", [4, 128, NC_ * 256])
        self.d_glawg = dr("glawg", [8, 128, NC_ * 128])
        self.d_glawo = dr("glawo", [4, NC_, 128, 2 * 128])
        self.d_glawa1 = dr("glawa1", [128, NC_ * 32])
        self.d_glawa2 = dr("glawa2", [32, 2 * 512])
        self.d_glarope = dr("glarope", [2, 128, SEQ])
        self.d_glacst = dr("glacst", [128, 4 * 64 + 128 + 2 + 8])
        self.d_nawqkv = dr("nawqkv", [3 * NC_, 128, NC_ * 128])
        self.d_nawo = dr("nawo", [NC_, 128, NC_ * 128])
        self.d_nastrip = dr("nastrip", [16, 128, 37 * 64])
        self.d_cvw1 = dr("cvw1", [NC_, 128, 2 * NC_ * 128])
        self.d_cvw2 = dr("cvw2", [NC_, 128, NC_ * 128])
        self.d_cvprm = dr("cvprm", [128, 6 * NC_ + 31 * NC_])
        self.U = kb.sb(es, "U", [128, NC_, T], BF16)
        self.Hr = [[Reg(f"H{c}_{b}") for b in range(5)] for c in range(NC_)]
        self.Ur = [[Reg(f"U{c}_{b}") for b in range(5)] for c in range(NC_)]
        self.P = kb.sb(es, "P", [128, 72, 2], F32)
        self.Pr = Reg("P")
        self.condT = kb.sb(es, "condT", [128, NC_, 2], F32)
        self.condr = Reg("cond")
        self.lng = kb.sb(es, "lng", [128, DEPTH * 3 * NC_], F32)
        self.lnb = kb.sb(es, "lnb", [128, DEPTH * 3 * NC_], F32)
        self.lnr = Reg("ln")
        self.ones_bf = kb.sb(es, "ones_bf", [128, 128], BF16)
        self.onesr = Reg("ones")
        self.epsP = kb.sb(es, "epsP", [128, 1], F32)
        self.onesf = kb.sb(es, "onesf", [128, 64], F32)
        self.ps = [es.enter_context(nc.psum_tensor(f"ps{i}", [128, 512], F32)) for i in range(8)]
        self.psr = [Reg(f"ps{i}") for i in range(8)]
        self.ds_misc = kb.dsem("misc")
        self.ds_out = kb.dsem("out")
        self.ds_h = [kb.dsem(f"h{i}") for i in range(4)]
        self.psrot = 0
        self.es_H = ExitStack()
        self.H = kb.sb(self.es_H, "H", [128, NC_, T], F32)

    def hregs(self, c, t0, n):
        return [self.Hr[c][b] for b, (bt, bn, _) in enumerate(BLOCKS) if bt < t0 + n and t0 < bt + bn]

    def uregs(self, c, t0, n):
        return [self.Ur[c][b] for b, (bt, bn, _) in enumerate(BLOCKS) if bt < t0 + n and t0 < bt + bn]

    def setup(self):
        kb = self.kb
        kb.dma("sp", self.ds_misc, self.condT[:], self.d_cond, writes=[self.condr])
        kb.dma("sp", self.ds_misc, self.lng[:], self.d_lng, writes=[self.lnr])
        kb.dma("sp", self.ds_misc, self.lnb[:], self.d_lnb, writes=[self.lnr])
        for c in range(NC_):
            kb.dma("sp", self.ds_h[c % 4], self.H[:, c, :], self.d_hin[c], writes=self.Hr[c])
        kb.op("dve", lambda e: e.memset(self.ones_bf[:], 1.0 / 1024.0), writes=[self.onesr])
        kb.op("dve", lambda e: e.memset(self.epsP[:], EPS_P), writes=[self.onesr])
        kb.op("dve", lambda e: e.memset(self.onesf[:], 1.0), writes=[self.onesr])
        kb.op("act", lambda e: e.activation(out=self.condT[:], in_=self.condT[:], func=AF.Silu),
              reads=[self.condr], writes=[self.condr])

    def store(self):
        kb = self.kb
        for c in range(NC_):
            kb.dma("sp", self.ds_h[c % 4], self.d_hout[c], self.H[:, c, :], reads=self.Hr[c])
        for d_ in self.ds_h:
            kb.eng["sp"].wait_ge(d_.sem, d_.cnt)

    def ada(self, L):
        kb = self.kb
        kb.barrier()
        with ExitStack() as es:
            NG = 4
            wst = [kb.sb(es, f"adaw{s}", [128, NG, NC_ * 128], BF16) for s in range(2)]
            condb = kb.sb(es, "condb", [128, NC_, 2], BF16)
            kb.op("dve", lambda e: e.tensor_copy(out=condb[:], in_=self.condT[:]), reads=[self.condr], writes=[self.condr])
            wr = [Reg(), Reg()]
            wds = [kb.dsem(f"adaw{L}_{s}") for s in range(2)]
            bia = kb.sb(es, "adab", [128, 72], F32)
            br = Reg()
            kb.dma("sp", self.ds_misc, bia[:], self.d_adab[L], writes=[br])
            pst = self.ps[0]
            psreg = self.psr[0]
            for g in range(72 // NG):
                s = g % 2
                kb.dma("pool", wds[s], wst[s][:], self.d_adaw[L, g * NG:(g + 1) * NG].rearrange("o p f -> p o f"),
                       writes=[wr[s]])
                for o in range(NG):
                    oc = g * NG + o
                    for kc in range(NC_):
                        kb.op("pe", lambda e: e.matmul(pst[:, oc * 2:oc * 2 + 2], lhsT=wst[s][:, o, kc * 128:(kc + 1) * 128],
                                                        rhs=condb[:, kc, :], start=(kc == 0), stop=(kc == NC_ - 1)),
                              reads=[wr[s], self.condr], writes=[psreg])
            for kind in range(2):
                kb.op("dve", lambda e: e.tensor_tensor(out=self.P[:, :, kind], in0=pst[:, 0:144].rearrange("p (o k) -> p o k", k=2)[:, :, kind],
                                                       in1=bia[:], op=ALU.add),
                      reads=[psreg, br], writes=[self.Pr])
            for j in range(3):
                wj = (1.0 if j == 1 else 0.5) / ALPHA
                kb.op("dve", lambda e: e.tensor_scalar(out=self.P[:, midx(j, 1, 0):midx(j, 1, 0) + 8, :], in0=self.P[:, midx(j, 1, 0):midx(j, 1, 0) + 8, :],
                                                       scalar1=1.0, scalar2=None, op0=ALU.add),
                      reads=[self.Pr], writes=[self.Pr])
                kb.op("dve", lambda e: e.tensor_scalar(out=self.P[:, midx(j, 2, 0):midx(j, 2, 0) + 8, :], in0=self.P[:, midx(j, 2, 0):midx(j, 2, 0) + 8, :],
                                                       scalar1=wj, scalar2=None, op0=ALU.mult),
                      reads=[self.Pr], writes=[self.Pr])
            kb.barrier()

    def modulate(self, j, blocks=range(5)):
        kb = self.kb
        for c in range(NC_):
            for b in blocks:
                t0, n, kind = BLOCKS[b]
                kb.op("act", lambda e: e.activation(out=self.U[:, c, t0:t0 + n], in_=self.H[:, c, t0:t0 + n], func=AF.Identity,
                                                    scale=self.P[:, midx(j, 1, c), kind:kind + 1],
                                                    bias=self.P[:, midx(j, 0, c), kind:kind + 1]),
                      reads=[self.Hr[c][b], self.Pr], writes=[self.Ur[c][b]])

    def ffn(self, L, j, kidx, blocks=range(5)):
        kb = self.kb
        blocks = list(blocks)
        self.modulate(j, blocks)
        with ExitStack() as es:
            G = kb.sb(es, "G", [128, NF, 1280], BF16)
            wA = [kb.sb(es, f"wA{s}", [128, 2, NC_, 128], BF16) for s in range(2)]
            wAr = [Reg(), Reg()]
            wAd = [kb.dsem(f"wA{L}{j}{s}") for s in range(2)]
            wB = [kb.sb(es, f"wB{s}", [128, NF, 128], BF16) for s in range(2)]
            wBr = [Reg(), Reg()]
            wBd = [kb.dsem(f"wB{L}{j}{s}") for s in range(2)]
            sa = [kb.sb(es, f"sa{s}", [128, 512], BF16) for s in range(2)]
            sar = [Reg(), Reg()]
            for half in HALVES:
                blks = [b for b in half if b in blocks]
                if not blks:
                    continue
                base = BLOCKS[blks[0]][0]
                Gr = [[Reg() for _ in blks] for _ in range(NF)]
                cntA = 0
                for jf in range(NF):
                    s = jf % 2
                    kb.dma("pool", wAd[s], wA[s][:].rearrange("p s k m -> p (s k m)"), self.d_w13[L, kidx, jf], writes=[wAr[s]])
                    for bi, b in enumerate(blks):
                        t0, n, kind = BLOCKS[b]
                        pa = (cntA % 4) * 2
                        pu = pa + 1
                        ss = cntA % 2
                        cntA += 1
                        for (pb, si) in ((pa, 0), (pu, 1)):
                            for kc in range(NC_):
                                kb.op("pe", lambda e: e.matmul(self.ps[pb][:, :n], lhsT=wA[s][:, si, kc, :], rhs=self.U[:, kc, t0:t0 + n],
                                                                start=(kc == 0), stop=(kc == NC_ - 1)),
                                      reads=[wAr[s], self.Ur[kc][b]], writes=[self.psr[pb]])
                        kb.op("act", lambda e: e.activation(out=sa[ss][:, :n], in_=self.ps[pa][:, :n], func=AF.Silu),
                              reads=[self.psr[pa]], writes=[sar[ss]])
                        kb.op("dve", lambda e: e.tensor_tensor(out=G[:, jf, t0 - base:t0 - base + n], in0=sa[ss][:, :n], in1=self.ps[pu][:, :n], op=ALU.mult),
                              reads=[sar[ss], self.psr[pu]], writes=[Gr[jf][bi]])
                cntB = 0
                for dc in range(NC_):
                    s = dc % 2
                    kb.dma("pool", wBd[s], wB[s][:].rearrange("p f m -> p (f m)"), self.d_w2[L, kidx, dc], writes=[wBr[s]])
                    for bi, b in enumerate(blks):
                        t0, n, kind = BLOCKS[b]
                        pb = cntB % 8
                        cntB += 1
                        for fc in range(NF):
                            kb.op("pe", lambda e: e.matmul(self.ps[pb][:, :n], lhsT=wB[s][:, fc, :], rhs=G[:, fc, t0 - base:t0 - base + n],
                                                            start=(fc == 0), stop=(fc == NF - 1)),
                                  reads=[wBr[s], Gr[fc][bi]], writes=[self.psr[pb]])
                        kb.op("dve", lambda e: e.scalar_tensor_tensor(out=self.H[:, dc, t0:t0 + n], in0=self.ps[pb][:, :n],
                                                                      scalar=self.P[:, midx(j, 2, dc), kind:kind + 1],
                                                                      in1=self.H[:, dc, t0:t0 + n], op0=ALU.mult, op1=ALU.add),
                              reads=[self.psr[pb], self.Pr, self.Hr[dc][b]], writes=[self.Hr[dc][b]])
                kb.barrier()
        self.layernorm(L, j, blocks)

    def layernorm(self, L, j, blocks=range(5)):
        kb = self.kb
        goff = (L * 3 + j) * NC_
        with ExitStack() as es:
            zb = [kb.sb(es, f"zb{s}", [128, NC_, 512], BF16) for s in range(2)]
            z2 = [kb.sb(es, f"z2{s}", [128, NC_, 512], BF16) for s in range(2)]
            zr = [[Reg() for _ in range(NC_)] for _ in range(2)]
            z2r = [[Reg() for _ in range(NC_)] for _ in range(2)]
            msq = [kb.sb(es, f"msq{s}", [128, 512], F32) for s in range(2)]
            rstd = [kb.sb(es, f"rstd{s}", [128, 512], F32) for s in range(2)]
            tmp = [kb.sb(es, f"lntmp{s}", [128, 512], F32) for s in range(4)]
            msqr = [Reg(), Reg()]
            rstdr = [Reg(), Reg()]
            tmpr = [Reg() for _ in range(4)]
            tcnt = 0
            for bi, b in enumerate(blocks):
                t0, n, kind = BLOCKS[b]
                s = bi % 2
                pm = (bi % 4) * 2
                pq = pm + 1
                for c in range(NC_):
                    kb.op("act", lambda e: e.activation(out=zb[s][:, c, :n], in_=self.H[:, c, t0:t0 + n], func=AF.Copy),
                          reads=[self.Hr[c][b]], writes=[zr[s][c]])
                    kb.op("act", lambda e: e.activation(out=z2[s][:, c, :n], in_=self.H[:, c, t0:t0 + n], func=AF.Square),
                          reads=[self.Hr[c][b]], writes=[z2r[s][c]])
                for c in range(NC_):
                    kb.op("pe", lambda e: e.matmul(self.ps[pm][:, :n], lhsT=self.ones_bf[:], rhs=zb[s][:, c, :n],
                                                    start=(c == 0), stop=(c == NC_ - 1)),
                          reads=[self.onesr, zr[s][c]], writes=[self.psr[pm]])
                for c in range(NC_):
                    kb.op("pe", lambda e: e.matmul(self.ps[pq][:, :n], lhsT=self.ones_bf[:], rhs=z2[s][:, c, :n],
                                                    start=(c == 0), stop=(c == NC_ - 1)),
                          reads=[self.onesr, z2r[s][c]], writes=[self.psr[pq]])
                kb.op("act", lambda e: e.activation(out=msq[s][:, :n], in_=self.ps[pm][:, :n], func=AF.Square),
                      reads=[self.psr[pm]], writes=[msqr[s]])
                kb.op("dve", lambda e: e.tensor_tensor(out=msq[s][:, :n], in0=self.ps[pq][:, :n], in1=msq[s][:, :n], op=ALU.subtract),
                      reads=[self.psr[pq], msqr[s]], writes=[msqr[s]])
                kb.op("act", lambda e: e.activation(out=msq[s][:, :n], in_=msq[s][:, :n], func=AF.Ln, bias=self.epsP[:, 0:1]),
                      reads=[msqr[s], self.onesr], writes=[msqr[s]])
                kb.op("act", lambda e: e.activation(out=rstd[s][:, :n], in_=msq[s][:, :n], func=AF.Exp, scale=-0.5),
                      reads=[msqr[s]], writes=[rstdr[s]])
                for c in range(NC_):
                    ts = tcnt % 4
                    tcnt += 1
                    kb.op("dve", lambda e: e.tensor_tensor(out=tmp[ts][:, :n], in0=self.H[:, c, t0:t0 + n], in1=self.ps[pm][:, :n], op=ALU.subtract),
                          reads=[self.Hr[c][b], self.psr[pm]], writes=[tmpr[ts]])
                    kb.op("pool", lambda e: e.tensor_tensor(out=tmp[ts][:, :n], in0=tmp[ts][:, :n], in1=rstd[s][:, :n], op=ALU.mult),
                          reads=[tmpr[ts], rstdr[s]], writes=[tmpr[ts]])
                    kb.op("act", lambda e: e.activation(out=self.H[:, c, t0:t0 + n], in_=tmp[ts][:, :n], func=AF.Identity,
                                                        scale=self.lng[:, goff + c:goff + c + 1], bias=self.lnb[:, goff + c:goff + c + 1]),
                          reads=[tmpr[ts], self.lnr], writes=[self.Hr[c][b]])
            kb.barrier()


    def proj(self, es, Wd, ocs, src, src_regs, evac, blocks, nk=NC_, tag="pj", src_off=0):
        kb = self.kb
        w = [kb.sb(es, f"{tag}w{s}", [128, nk, 128], BF16) for s in range(2)]
        wr = [Reg(), Reg()]
        wd = [kb.dsem(f"{tag}{s}") for s in range(2)]
        cnt = 0
        for i, oc in enumerate(ocs):
            s = i % 2
            kb.dma("pool", wd[s], w[s][:].rearrange("p k m -> p (k m)"), Wd[oc], writes=[wr[s]])
            for b in blocks:
                t0, n, kind = BLOCKS[b]
                pb = self.psrot % 8
                self.psrot += 1
                for kc in range(nk):
                    kb.op("pe", lambda e: e.matmul(self.ps[pb][:, :n], lhsT=w[s][:, kc, :], rhs=src[:, kc, src_off + t0:src_off + t0 + n],
                                                    start=(kc == 0), stop=(kc == nk - 1)),
                          reads=[wr[s], src_regs[kc][b]], writes=[self.psr[pb]])
                evac(oc, b, pb, t0, n, kind)

    def resid_evac(self, j, bias=None, bias_reg=None, es=None):
        kb = self.kb
        if bias is not None:
            tmpy = [kb.sb(es, f"tmpy{s}", [128, 512], F32) for s in range(2)]
            tmpr = [Reg(), Reg()]
        state = {"c": 0}

        def evac(dc, b, pb, t0, n, kind):
            if bias is not None:
                s = state["c"] % 2
                state["c"] += 1
                kb.op("act", lambda e: e.activation(out=tmpy[s][:, :n], in_=self.ps[pb][:, :n], func=AF.Identity, bias=bias[:, dc:dc + 1]),
                      reads=[self.psr[pb], bias_reg], writes=[tmpr[s]])
                src, sreg = tmpy[s], tmpr[s]
            else:
                src, sreg = self.ps[pb], self.psr[pb]
            kb.op("dve", lambda e: e.scalar_tensor_tensor(out=self.H[:, dc, t0:t0 + n], in0=src[:, :n],
                                                          scalar=self.P[:, midx(j, 2, dc), kind:kind + 1],
                                                          in1=self.H[:, dc, t0:t0 + n], op0=ALU.mult, op1=ALU.add),
                  reads=[sreg, self.Pr, self.Hr[dc][b]], writes=[self.Hr[dc][b]])
        return evac

    def feat_stats(self, es_unused, src_fn, b, n, s, zsq, zsqr, msq, msqr, rstd, rstdr, eps_ap, nchunks=NC_, ones=None):
        kb = self.kb
        ones = self.ones_bf if ones is None else ones
        pm = self.psrot % 8
        pq = (self.psrot + 1) % 8
        self.psrot += 2
        for c in range(nchunks):
            ap, rg = src_fn(c)
            kb.op("act", lambda e: e.activation(out=zsq[s][:, c, :n], in_=ap, func=AF.Square),
                  reads=[rg], writes=[zsqr[s][c]])
        for c in range(nchunks):
            ap, rg = src_fn(c)
            kb.op("pe", lambda e: e.matmul(self.ps[pm][:, :n], lhsT=ones[:], rhs=ap, start=(c == 0), stop=(c == nchunks - 1)),
                  reads=[self.onesr, rg], writes=[self.psr[pm]])
        for c in range(nchunks):
            kb.op("pe", lambda e: e.matmul(self.ps[pq][:, :n], lhsT=ones[:], rhs=zsq[s][:, c, :n], start=(c == 0), stop=(c == nchunks - 1)),
                  reads=[self.onesr, zsqr[s][c]], writes=[self.psr[pq]])
        kb.op("act", lambda e: e.activation(out=msq[s][:, :n], in_=self.ps[pm][:, :n], func=AF.Square),
              reads=[self.psr[pm]], writes=[msqr[s]])
        kb.op("dve", lambda e: e.tensor_tensor(out=msq[s][:, :n], in0=self.ps[pq][:, :n], in1=msq[s][:, :n], op=ALU.subtract),
              reads=[self.psr[pq], msqr[s]], writes=[msqr[s]])
        kb.op("act", lambda e: e.activation(out=msq[s][:, :n], in_=msq[s][:, :n], func=AF.Ln, bias=eps_ap),
              reads=[msqr[s], self.onesr], writes=[msqr[s]])
        kb.op("act", lambda e: e.activation(out=rstd[s][:, :n], in_=msq[s][:, :n], func=AF.Exp, scale=-0.5),
              reads=[msqr[s]], writes=[rstdr[s]])
        return pm

    def conv_mixer(self, L, last):
        kb = self.kb
        j = 1
        blocks = [0, 1, 2, 3] if last else [0, 1, 2, 3, 4]
        self.modulate(j, blocks)
        PADL = 15
        OFFX = PADL
        OFFC = PADL + SEQ + 2 * PADL
        VW = OFFC + CTX + PADL
        voff = lambda b: (OFFX if BLOCKS[b][2] == 0 else OFFC - SEQ)
        with ExitStack() as es:
            V = kb.sb(es, "cvV", [128, NC_, VW], BF16)
            Vr = [[Reg() for _ in range(5)] for _ in range(NC_)]
            prm = kb.sb(es, "cvprm", [128, 6 * NC_ + 31 * NC_], F32)
            prmr = Reg()
            ident = kb.sb(es, "ident", [128, 128], F32)
            idr = Reg()
            kb.dma("sp", self.ds_misc, prm[:], self.d_cvprm, writes=[prmr])
            kb.dma("sp", self.ds_misc, ident[:], self.d_ident, writes=[idr])
            for c in range(NC_):
                kb.op("pool", lambda e: e.memset(V[:, c, :], 0.0), writes=Vr[c])
            B1A, B1G, BDW, LNG, LNB, B2, WDW = 0, 8, 16, 24, 32, 40, 48
            sg = [kb.sb(es, f"cvsg{s}", [128, 512], F32) for s in range(2)]
            sgr = [Reg(), Reg()]
            with ExitStack() as es1:
                wA = [kb.sb(es1, f"cvw1{s}", [128, 2, NC_, 128], BF16) for s in range(2)]
                wAr = [Reg(), Reg()]
                wAd = [kb.dsem(f"cvw1{s}") for s in range(2)]
                cnt = 0
                for c in range(NC_):
                    s = c % 2
                    kb.dma("pool", wAd[s], wA[s][:].rearrange("p s k m -> p (s k m)"), self.d_cvw1[c], writes=[wAr[s]])
                    for b in blocks:
                        t0, n, kind = BLOCKS[b]
                        pa = (cnt % 4) * 2
                        pg = pa + 1
                        ss = cnt % 2
                        cnt += 1
                        for (pb, si) in ((pa, 0), (pg, 1)):
                            for kc in range(NC_):
                                kb.op("pe", lambda e: e.matmul(self.ps[pb][:, :n], lhsT=wA[s][:, si, kc, :], rhs=self.U[:, kc, t0:t0 + n],
                                                                start=(kc == 0), stop=(kc == NC_ - 1)),
                                      reads=[wAr[s], self.Ur[kc][b]], writes=[self.psr[pb]])
                        kb.op("act", lambda e: e.activation(out=sg[ss][:, :n], in_=self.ps[pg][:, :n], func=AF.Sigmoid, bias=prm[:, B1G + c:B1G + c + 1]),
                              reads=[self.psr[pg], prmr], writes=[sgr[ss]])
                        kb.op("dve", lambda e: e.scalar_tensor_tensor(out=V[:, c, voff(b) + t0:voff(b) + t0 + n], in0=self.ps[pa][:, :n],
                                                                      scalar=prm[:, B1A + c:B1A + c + 1], in1=sg[ss][:, :n],
                                                                      op0=ALU.add, op1=ALU.mult),
                              reads=[self.psr[pa], prmr, sgr[ss]], writes=[Vr[c][b]])
                kb.barrier()
            with ExitStack() as es2:
                Dg = [kb.sb(es2, f"cvDg{s}", [128, 31, 128], BF16) for s in range(2)]
                Dgr = [Reg(), Reg()]
                for c in range(NC_):
                    s = c % 2
                    for k in range(31):
                        kb.op("dve" if k % 2 == 0 else "pool",
                              lambda e: e.tensor_scalar(out=Dg[s][:, k, :], in0=ident[:], scalar1=prm[:, WDW + k * NC_ + c:WDW + k * NC_ + c + 1],
                                                        scalar2=None, op0=ALU.mult),
                              reads=[idr, prmr], writes=[Dgr[s]])
                    for b in blocks:
                        t0, n, kind = BLOCKS[b]
                        pb = self.psrot % 8
                        self.psrot += 1
                        vregs = [Vr[c][bb] for bb in blocks if BLOCKS[bb][2] == kind]
                        for k in range(31):
                            col = voff(b) + t0 + k - 15
                            kb.op("pe", lambda e: e.matmul(self.ps[pb][:, :n], lhsT=Dg[s][:, k, :], rhs=V[:, c, col:col + n],
                                                            start=(k == 0), stop=(k == 30)),
                                  reads=[Dgr[s]] + vregs, writes=[self.psr[pb]])
                        kb.op("act", lambda e: e.activation(out=self.U[:, c, t0:t0 + n], in_=self.ps[pb][:, :n], func=AF.Identity,
                                                            bias=prm[:, BDW + c:BDW + c + 1]),
                              reads=[self.psr[pb], prmr], writes=[self.Ur[c][b]])
                kb.barrier()
            zsq = [kb.sb(es, f"cvz2{s}", [128, NC_, 512], BF16) for s in range(2)]
            zsqr = [[Reg() for _ in range(NC_)] for _ in range(2)]
            msq = [kb.sb(es, f"cvmsq{s}", [128, 512], F32) for s in range(2)]
            rstd = [kb.sb(es, f"cvrstd{s}", [128, 512], F32) for s in range(2)]
            tmp = [kb.sb(es, f"cvtmp{s}", [128, 512], F32) for s in range(4)]
            msqr = [Reg(), Reg()]
            rstdr = [Reg(), Reg()]
            tmpr = [Reg() for _ in range(4)]
            epsl = kb.sb(es, "cveps", [128, 1], F32)
            kb.op("dve", lambda e: e.memset(epsl[:], LN_EPS), writes=[self.onesr])
            tc = 0
            for bi, b in enumerate(blocks):
                t0, n, kind = BLOCKS[b]
                s = bi % 2
                pm = self.feat_stats(None, lambda c: (self.U[:, c, t0:t0 + n], self.Ur[c][b]), b, n, s, zsq, zsqr, msq, msqr, rstd, rstdr, epsl[:, 0:1])
                for c in range(NC_):
                    ts = tc % 4
                    tc += 1
                    kb.op("dve", lambda e: e.tensor_tensor(out=tmp[ts][:, :n], in0=self.U[:, c, t0:t0 + n], in1=self.ps[pm][:, :n], op=ALU.subtract),
                          reads=[self.Ur[c][b], self.psr[pm]], writes=[tmpr[ts]])
                    kb.op("pool", lambda e: e.tensor_tensor(out=tmp[ts][:, :n], in0=tmp[ts][:, :n], in1=rstd[s][:, :n], op=ALU.mult),
                          reads=[tmpr[ts], rstdr[s]], writes=[tmpr[ts]])
                    kb.op("act", lambda e: e.activation(out=self.U[:, c, t0:t0 + n], in_=tmp[ts][:, :n], func=AF.Silu,
                                                        scale=prm[:, LNG + c:LNG + c + 1], bias=prm[:, LNB + c:LNB + c + 1]),
                          reads=[tmpr[ts], prmr], writes=[self.Ur[c][b]])
            b2t = prm[:, B2:B2 + NC_]
            self.proj(es, self.d_cvw2, range(NC_), self.U, self.Ur, self.resid_evac(j, bias=b2t, bias_reg=prmr, es=es), blocks, tag="cvw2")
            kb.barrier()
        self.layernorm(L, j, blocks)


    def na_mixer(self, L, last):
        kb = self.kb
        j = 1
        blocks = [0, 1, 2, 3, 4]
        self.modulate(j, blocks)
        NB_I, NB_F = 23, 14
        FB = NB_I
        SW = (NB_I + NB_F) * 64
        qtiles = []
        qtiles.append((0, 256, [(128 * i, (FB + 6 - 2 * i) * 64) for i in range(4)]))
        for q0r in (4, 12, 20):
            qtiles.append((64 * q0r, 512, [(64 * (q0r - 4) + 128 * i, (15 - 2 * i) * 64) for i in range(8)]))
        qtiles.append((1792, 256, [(1536 + 128 * i, (FB + 10 - 2 * i) * 64) for i in range(4)]))
        qtiles.append((2048, 256, []))
        ctx_keys = [2048, 2176]
        bregs = lambda regs, t0, n: [regs[b] for b, (bt, bn, _) in enumerate(BLOCKS) if bt < t0 + n and t0 < bt + bn]
        with ExitStack() as es:
            ZT = kb.sb(es, "naZT", [128, NC_, T], BF16)
            ZTr = [[Reg() for _ in range(5)] for _ in range(NC_)]
            ones1 = kb.sb(es, "naones", [128, 128], BF16)
            o1r = Reg()
            kb.op("dve", lambda e: e.memset(ones1[:], 1.0), writes=[o1r])
            with ExitStack() as es1:
                QT = [kb.sb(es1, f"naQ{s}", [128, T], BF16) for s in range(2)]
                KT = [kb.sb(es1, f"naK{s}", [128, T], BF16) for s in range(2)]
                VT = [kb.sb(es1, f"naV{s}", [128, 18, 128], BF16) for s in range(2)]
                QTr = [[Reg() for _ in range(5)] for _ in range(2)]
                KTr = [[Reg() for _ in range(5)] for _ in range(2)]
                VTr = [Reg(), Reg()]
                strip = [kb.sb(es1, f"nastrip{s}", [128, SW], BF16) for s in range(2)]
                stripr = [Reg(), Reg()]
                stripd = [kb.dsem(f"nastrip{s}") for s in range(2)]
                tmp = [kb.sb(es1, f"natmp{s}", [128, 512], F32) for s in range(3)]
                tmpr = [Reg() for _ in range(3)]
                PT = [kb.sb(es1, f"naPT{s}", [128, 512], BF16) for s in range(3)]
                PTr = [Reg() for _ in range(3)]
                rc = [kb.sb(es1, f"narc{s}", [128, 512], F32) for s in range(2)]
                rcr = [Reg(), Reg()]
                wv = [kb.sb(es1, f"nawv{s}", [128, NC_, 128], BF16) for s in range(2)]
                wvr = [Reg(), Reg()]
                wvd = [kb.dsem(f"nawv{s}") for s in range(2)]
                pw = [kb.sb(es1, f"napw{s}", [128, NC_, 128], BF16) for s in range(2)]
                pwr = [Reg(), Reg()]
                pwd = [kb.dsem(f"napw{s}") for s in range(2)]
                cnt = {"s": 0, "t": 0, "p": 0, "o": 0, "w": 0}
                for ch in range(NC_):
                    sl = ch % 2
                    for (dst, dstr, oc) in ((QT[sl], QTr[sl], ch), (KT[sl], KTr[sl], NC_ + ch)):
                        ws = cnt["w"] % 2
                        cnt["w"] += 1
                        kb.dma("pool", pwd[ws], pw[ws][:].rearrange("p k m -> p (k m)"), self.d_nawqkv[oc], writes=[pwr[ws]])
                        for b in blocks:
                            t0, n, kind = BLOCKS[b]
                            pb = 4 + (self.psrot % 4)
                            self.psrot += 1
                            for kc in range(NC_):
                                kb.op("pe", lambda e: e.matmul(self.ps[pb][:, :n], lhsT=pw[ws][:, kc, :], rhs=self.U[:, kc, t0:t0 + n],
                                                                start=(kc == 0), stop=(kc == NC_ - 1)),
                                      reads=[pwr[ws], self.Ur[kc][b]], writes=[self.psr[pb]])
                            kb.op("act", lambda e: e.activation(out=dst[:, t0:t0 + n], in_=self.ps[pb][:, :n], func=AF.Copy),
                                  reads=[self.psr[pb]], writes=[dstr[b]])
                    kb.dma("pool", wvd[sl], wv[sl][:].rearrange("p k m -> p (k m)"), self.d_nawqkv[2 * NC_ + ch], writes=[wvr[sl]])
                    for g in range(5):
                        tiles = list(range(4 * g, min(18, 4 * g + 4)))
                        pb = 4 + (self.psrot % 4)
                        self.psrot += 1
                        for ti, tt in enumerate(tiles):
                            b = min(tt // 4, 4)
                            for kc in range(NC_):
                                kb.op("pe", lambda e: e.matmul(self.ps[pb][:, ti * 128:(ti + 1) * 128], lhsT=self.U[:, kc, tt * 128:(tt + 1) * 128],
                                                                rhs=wv[sl][:, kc, :], start=(kc == 0), stop=(kc == NC_ - 1)),
                                      reads=[wvr[sl], self.Ur[kc][b]], writes=[self.psr[pb]])
                        nt = len(tiles)
                        kb.op("dve", lambda e: e.tensor_copy(out=VT[sl][:, tiles[0]:tiles[0] + nt, :].rearrange("p t m -> p (t m)"),
                                                             in_=self.ps[pb][:, :nt * 128]),
                              reads=[self.psr[pb]], writes=[VTr[sl]])
                    items = []
                    for hh in range(2):
                        for qi, (q0, nq, loc) in enumerate(qtiles):
                            keys = [(k0, c0) for (k0, c0) in loc] + [(k0, None) for k0 in ctx_keys]
                            for ki, (k0, c0) in enumerate(keys):
                                items.append((hh, qi, ki, len(keys), k0, c0))
                    obase = cnt["o"]
                    cnt["o"] += 2 * len(qtiles)
                    ibase = cnt["s"]
                    cnt["s"] += len(items)

                    def stage_a(it, idx):
                        hh, qi, ki, nk, k0, c0 = it
                        q0, nq, _ = qtiles[qi]
                        h = 2 * ch + hh
                        r0 = hh * 64
                        ss = h % 2
                        if qi == 0 and ki == 0:
                            kb.dma("pool", stripd[ss], strip[ss][:], self.d_nastrip[h], writes=[stripr[ss]])
                        g = ibase + idx
                        pss = g % 4
                        pt = g % 3
                        kb.op("pe", lambda e: e.matmul(self.ps[pss][:, :nq], lhsT=KT[sl][r0:r0 + 64, k0:k0 + 128],
                                                        rhs=QT[sl][r0:r0 + 64, q0:q0 + nq], start=True, stop=True),
                              reads=bregs(KTr[sl], k0, 128) + bregs(QTr[sl], q0, nq), writes=[self.psr[pss]])
                        if c0 is not None:
                            tm = g % 3
                            kb.op("dve", lambda e: e.scalar_tensor_tensor(out=tmp[tm][:, :nq], in0=self.ps[pss][:, :nq], scalar=0.125,
                                                                          in1=strip[ss][:, c0:c0 + nq], op0=ALU.mult, op1=ALU.add),
                                  reads=[self.psr[pss], stripr[ss]], writes=[tmpr[tm]])
                            kb.op("act", lambda e: e.activation(out=PT[pt][:, :nq], in_=tmp[tm][:, :nq], func=AF.Exp),
                                  reads=[tmpr[tm]], writes=[PTr[pt]])
                        else:
                            kb.op("act", lambda e: e.activation(out=PT[pt][:, :nq], in_=self.ps[pss][:, :nq], func=AF.Exp, scale=0.125),
                                  reads=[self.psr[pss]], writes=[PTr[pt]])

                    def stage_b(it, idx):
                        hh, qi, ki, nk, k0, c0 = it
                        q0, nq, _ = qtiles[qi]
                        r0 = hh * 64
                        o_ = obase + hh * len(qtiles) + qi
                        ppv = 4 + (o_ % 2)
                        psm = 6 + (o_ % 2)
                        pt = (ibase + idx) % 3
                        kt = k0 // 128
                        kb.op("pe", lambda e: e.matmul(self.ps[ppv][:, :nq], lhsT=VT[sl][:, kt, :], rhs=PT[pt][:, :nq],
                                                        start=(ki == 0), stop=(ki == nk - 1)),
                              reads=[VTr[sl], PTr[pt]], writes=[self.psr[ppv]])
                        kb.op("pe", lambda e: e.matmul(self.ps[psm][:, :nq], lhsT=ones1[:], rhs=PT[pt][:, :nq],
                                                        start=(ki == 0), stop=(ki == nk - 1)),
                              reads=[o1r, PTr[pt]], writes=[self.psr[psm]])
                        if ki == nk - 1:
                            rs = o_ % 2
                            kb.op("act", lambda e: e.activation(out=rc[rs][r0:r0 + 64, :nq], in_=self.ps[psm][r0:r0 + 64, :nq], func=AF.Ln),
                                  reads=[self.psr[psm]], writes=[rcr[rs]])
                            kb.op("act", lambda e: e.activation(out=rc[rs][r0:r0 + 64, :nq], in_=rc[rs][r0:r0 + 64, :nq], func=AF.Exp, scale=-1.0),
                                  reads=[rcr[rs]], writes=[rcr[rs]])
                            kb.op("dve", lambda e: e.tensor_tensor(out=ZT[r0:r0 + 64, ch, q0:q0 + nq], in0=self.ps[ppv][r0:r0 + 64, :nq],
                                                                   in1=rc[rs][r0:r0 + 64, :nq], op=ALU.mult),
                                  reads=[self.psr[ppv], rcr[rs]], writes=bregs(ZTr[ch], q0, nq))

                    LA = 2
                    for idx in range(len(items) + LA):
                        if idx < len(items):
                            stage_a(items[idx], idx)
                        if idx >= LA:
                            stage_b(items[idx - LA], idx - LA)
                kb.barrier()
            self.proj(es, self.d_nawo, range(NC_), ZT, ZTr, self.resid_evac(j), blocks, tag="nawo")
            kb.barrier()
        self.layernorm(L, j, blocks)


    def gla_mixer(self, L, last):
        kb = self.kb
        j = 1
        blocks = [0, 1, 2, 3, 4]
        oblocks = [0, 1, 2, 3] if last else blocks
        self.modulate(j, blocks)
        SC = 128.0 ** -0.5
        NCH = T // 64
        with ExitStack() as es:
            cst = kb.sb(es, "glcst", [128, 4 * 64 + 128 + 2 + 8 + 8], F32)
            cstr = Reg()
            kb.dma("sp", self.ds_misc, cst[:, 0:4 * 64 + 128 + 2 + 8], self.d_glacst, writes=[cstr])
            MK, ID, NG, BA, NBA = 0, 256, 384, 386, 394
            kb.op("dve", lambda e: e.tensor_scalar(out=cst[:, NBA:NBA + 8], in0=cst[:, BA:BA + 8], scalar1=-1.0, scalar2=None, op0=ALU.mult),
                  reads=[cstr], writes=[cstr])
            maskb = kb.sb(es, "glmask", [128, 4, 64], BF16)
            identb = kb.sb(es, "glidb", [128, 128], BF16)
            ones256 = kb.sb(es, "glones", [128, 128], BF16)
            onecol = kb.sb(es, "glone", [128, 1], F32)
            epsc = kb.sb(es, "gleps", [128, 1], F32)
            kb.op("dve", lambda e: e.tensor_copy(out=maskb[:].rearrange("p a b -> p (a b)"), in_=cst[:, MK:MK + 256]), reads=[cstr], writes=[cstr])
            kb.op("dve", lambda e: e.tensor_copy(out=identb[:], in_=cst[:, ID:ID + 128]), reads=[cstr], writes=[cstr])
            kb.op("dve", lambda e: e.memset(ones256[:], 1.0 / 256.0), writes=[cstr])
            kb.op("dve", lambda e: e.memset(onecol[:], 1.0), writes=[cstr])
            kb.op("dve", lambda e: e.memset(epsc[:], LN_EPS), writes=[cstr])
            wa1 = kb.sb(es, "glwa1", [128, NC_, 32], BF16)
            wa1r = Reg()
            kb.dma("pool", self.ds_misc, wa1[:].rearrange("p k m -> p (k m)"), self.d_glawa1, writes=[wa1r])
            wa2 = kb.sb(es, "glwa2", [32, 2, 512], F32)
            wa2r = Reg()
            kb.dma("sp", self.ds_misc, wa2[:].rearrange("p d m -> p (d m)"), self.d_glawa2, writes=[wa2r])
            rT = kb.sb(es, "glrT", [32, T], F32)
            rTr = [Reg() for _ in range(5)]
            for b in blocks:
                t0, n, kind = BLOCKS[b]
                pb = self.psrot % 8
                self.psrot += 1
                for kc in range(NC_):
                    kb.op("pe", lambda e: e.matmul(self.ps[pb][0:32, :n], lhsT=wa1[:, kc, :], rhs=self.U[:, kc, t0:t0 + n],
                                                    start=(kc == 0), stop=(kc == NC_ - 1)),
                          reads=[wa1r, self.Ur[kc][b]], writes=[self.psr[pb]])
                kb.op("act", lambda e: e.activation(out=rT[:, t0:t0 + n], in_=self.ps[pb][0:32, :n], func=AF.Copy),
                      reads=[self.psr[pb]], writes=[rTr[b]])
            QK = kb.sb(es, "glQK", [128, 2, T], BF16)
            QKr = [[Reg() for _ in range(5)] for _ in range(2)]
            VT = kb.sb(es, "glVT", [128, 18, 256], BF16)
            VTr = Reg()
            O = kb.sb(es, "glO", [128, 2, T], F32)
            Or = [Reg() for _ in range(5)]
            Z, Zr = QK, QKr
            SA = kb.sb(es, "glSA", [128, 2, T], F32)
            SAr = Reg()
            sad = kb.dsem("glrope")
            pw = [kb.sb(es, f"glpw{s}", [128, NC_, 128], BF16) for s in range(2)]
            pwr = [Reg(), Reg()]
            pwd = [kb.dsem(f"glpw{s}") for s in range(2)]
            wv = kb.sb(es, "glwv", [128, NC_, 256], BF16)
            wvr = Reg()
            wvd = kb.dsem("glwv")
            t1 = [kb.sb(es, f"glt1{s}", [128, 512], F32) for s in range(2)]
            t1r = [Reg(), Reg()]
            t2 = [kb.sb(es, f"glt2{s}", [128, 512], F32) for s in range(2)]
            t2r = [Reg(), Reg()]
            S = kb.sb(es, "glS", [128, 256], F32)
            Sb = kb.sb(es, "glSb", [128, 256], BF16)
            Sr, Sbr = Reg(), Reg()
            kd = [kb.sb(es, f"glkd{s}", [128, 128], BF16) for s in range(2)]
            kt = [kb.sb(es, f"glkt{s}", [128, 128], BF16) for s in range(2)]
            kdr = [Reg(), Reg()]
            ktr = [Reg(), Reg()]
            for s_ in range(2):
                kb.op("dve", lambda e: e.memset(kd[s_][:], 0.0), writes=[kdr[s_]])
                kb.op("dve", lambda e: e.memset(kt[s_][:], 0.0), writes=[ktr[s_]])
            ktT = [kb.sb(es, f"glktT{s}", [128, 128], BF16) for s in range(3)]
            ktTr = [Reg() for _ in range(3)]
            qd = [kb.sb(es, f"glqd{s}", [128, 64], BF16) for s in range(3)]
            qdr = [Reg() for _ in range(3)]
            ex = [kb.sb(es, f"glex{s}", [128, 3, 64], F32) for s in range(3)]
            exr = [Reg() for _ in range(3)]
            attm = [kb.sb(es, f"glatt{s}", [128, 64], BF16) for s in range(3)]
            attr = [Reg() for _ in range(3)]
            nbp = [kb.sb(es, f"glnb{s}", [128, 2], F32) for s in range(3)]
            nbr = [Reg() for _ in range(3)]
            psb = {2: self.ps[2].bitcast(BF16), 3: self.ps[3].bitcast(BF16)}
            wcnt = 0
            for hd in range(4):
                kb.dma("sp", sad, SA[:, :, 0:SEQ], self.d_glarope.rearrange("a p t -> p a t"), writes=[SAr])
                for qk in range(2):
                    oc = qk * 4 + hd
                    ws0 = wcnt % 2
                    ws1 = (wcnt + 1) % 2
                    wcnt += 2
                    kb.dma("pool", pwd[ws0], pw[ws0][:].rearrange("p k m -> p (k m)"), self.d_glawqk[oc], writes=[pwr[ws0]])
                    kb.dma("pool", pwd[ws1], pw[ws1][:].rearrange("p k m -> p (k m)"), self.d_glawqkp[oc], writes=[pwr[ws1]])
                    for b in blocks:
                        t0, n, kind = BLOCKS[b]
                        pa = (self.psrot % 4) * 2
                        pp = pa + 1
                        ts = self.psrot % 2
                        self.psrot += 1
                        for kc in range(NC_):
                            kb.op("pe", lambda e: e.matmul(self.ps[pa][:, :n], lhsT=pw[ws0][:, kc, :], rhs=self.U[:, kc, t0:t0 + n],
                                                            start=(kc == 0), stop=(kc == NC_ - 1)),
                                  reads=[pwr[ws0], self.Ur[kc][b]], writes=[self.psr[pa]])
                        if kind == 1:
                            kb.op("act", lambda e: e.activation(out=QK[:, qk, t0:t0 + n], in_=self.ps[pa][:, :n], func=AF.Copy),
                                  reads=[self.psr[pa]], writes=[QKr[qk][b]])
                            continue
                        for kc in range(NC_):
                            kb.op("pe", lambda e: e.matmul(self.ps[pp][:, :n], lhsT=pw[ws1][:, kc, :], rhs=self.U[:, kc, t0:t0 + n],
                                                            start=(kc == 0), stop=(kc == NC_ - 1)),
                                  reads=[pwr[ws1], self.Ur[kc][b]], writes=[self.psr[pp]])
                        kb.op("dve", lambda e: e.tensor_tensor(out=t1[ts][:, :n], in0=self.ps[pa][:, :n], in1=SA[:, 0, t0:t0 + n], op=ALU.mult),
                              reads=[self.psr[pa], SAr], writes=[t1r[ts]])
                        kb.op("dve", lambda e: e.tensor_tensor(out=t2[ts][:, :n], in0=self.ps[pp][:, :n], in1=SA[:, 1, t0:t0 + n], op=ALU.mult),
                              reads=[self.psr[pp], SAr], writes=[t2r[ts]])
                        kb.op("pool", lambda e: e.tensor_tensor(out=QK[:, qk, t0:t0 + n], in0=t1[ts][:, :n], in1=t2[ts][:, :n], op=ALU.add),
                              reads=[t1r[ts], t2r[ts]], writes=[QKr[qk][b]])
                kb.dma("pool", wvd, wv[:].rearrange("p k m -> p (k m)"), self.d_glawv[hd], writes=[wvr])
                for g in range(9):
                    pb = self.psrot % 8
                    self.psrot += 1
                    for ti in range(2):
                        tt = 2 * g + ti
                        b = min(tt // 4, 4)
                        for kc in range(NC_):
                            kb.op("pe", lambda e: e.matmul(self.ps[pb][:, ti * 256:(ti + 1) * 256], lhsT=self.U[:, kc, tt * 128:(tt + 1) * 128],
                                                            rhs=wv[:, kc, :], start=(kc == 0), stop=(kc == NC_ - 1)),
                                  reads=[wvr, self.Ur[kc][b]], writes=[self.psr[pb]])
                    kb.op("act", lambda e: e.activation(out=VT[:, 2 * g:2 * g + 2, :].rearrange("p t m -> p (t m)"), in_=self.ps[pb][:, :512], func=AF.Copy),
                          reads=[self.psr[pb]], writes=[VTr])
                for d in range(2):
                    for b in blocks:
                        t0, n, kind = BLOCKS[b]
                        pb = self.psrot % 8
                        ts = self.psrot % 2
                        self.psrot += 1
                        kb.op("pe", lambda e: e.matmul(self.ps[pb][:, :n], lhsT=wa2[:, d, hd * 128:(hd + 1) * 128], rhs=rT[:, t0:t0 + n],
                                                        start=True, stop=True),
                              reads=[wa2r, rTr[b]], writes=[self.psr[pb]])
                        kb.op("act", lambda e: e.activation(out=t1[ts][:, :n], in_=self.ps[pb][:, :n], func=AF.Exp, scale=-1.0,
                                                            bias=cst[:, NBA + d * 4 + hd:NBA + d * 4 + hd + 1]),
                              reads=[self.psr[pb], cstr], writes=[t1r[ts]])
                        kb.op("act", lambda e: e.activation(out=SA[:, 0, t0:t0 + n], in_=t1[ts][:, :n], func=AF.Ln, bias=onecol[:, 0:1]),
                              reads=[t1r[ts], cstr], writes=[SAr])
                    for n_ in range(NCH):
                        c0 = 64 * n_
                        kb.op("dve", lambda e: e.tensor_tensor_scan(out=SA[:, 1, c0:c0 + 64], data0=self.onesf[:, 0:64], data1=SA[:, 0, c0:c0 + 64],
                                                                    initial=0.0, op0=ALU.mult, op1=ALU.add),
                              reads=[SAr, self.onesr], writes=[SAr])
                    if d == 1:
                        kb.op("pool", lambda e: e.tensor_tensor(out=SA[:, 0, :], in0=SA[:, 0, :], in1=SA[:, 1, :], op=ALU.subtract),
                              reads=[SAr], writes=[SAr])
                    kb.op("dve", lambda e: e.memset(S[:], 0.0), writes=[Sr])
                    kb.op("dve", lambda e: e.memset(Sb[:], 0.0), writes=[Sbr])
                    order = ([32, 33, 34, 35] + list(range(32))) if d == 0 else ([35, 34, 33, 32] + list(range(31, -1, -1)))
                    def chunk_gen(ci, n_):
                        c0 = 64 * n_
                        tt, par = n_ // 2, n_ % 2
                        tb = tt % 2
                        b = min(c0 // 512, 4)
                        xs = ci % 3
                        last_col = c0 + 63
                        kb.op("dve", lambda e: e.tensor_scalar(out=nbp[xs][:, 0:1], in0=SA[:, 1, last_col:last_col + 1], scalar1=-1.0 / 16.0, scalar2=None, op0=ALU.mult),
                              reads=[SAr], writes=[nbr[xs]])
                        kb.op("dve", lambda e: e.tensor_scalar(out=nbp[xs][:, 1:2], in0=SA[:, 1, last_col:last_col + 1], scalar1=1.0 / 16.0, scalar2=None, op0=ALU.mult),
                              reads=[SAr], writes=[nbr[xs]])
                        if d == 0:
                            src = SA[:, 1, c0:c0 + 64]
                            kb.op("act", lambda e: e.activation(out=ex[xs][:, 0, :], in_=src, func=AF.Exp, scale=-1.0 / 16.0), reads=[SAr], writes=[exr[xs]])
                            kb.op("act", lambda e: e.activation(out=ex[xs][:, 1, :], in_=src, func=AF.Exp, scale=1.0 / 16.0), reads=[SAr], writes=[exr[xs]])
                            kb.op("act", lambda e: e.activation(out=ex[xs][:, 2, :], in_=src, func=AF.Exp, scale=1.0 / 16.0, bias=nbp[xs][:, 0:1]),
                                  reads=[SAr, nbr[xs]], writes=[exr[xs]])
                            dec = ex[xs][:, 0, 63:64]
                        else:
                            src = SA[:, 0, c0:c0 + 64]
                            kb.op("act", lambda e: e.activation(out=ex[xs][:, 0, :], in_=src, func=AF.Exp, scale=-1.0 / 16.0, bias=nbp[xs][:, 0:1]),
                                  reads=[SAr, nbr[xs]], writes=[exr[xs]])
                            kb.op("act", lambda e: e.activation(out=ex[xs][:, 1, :], in_=src, func=AF.Exp, scale=1.0 / 16.0, bias=nbp[xs][:, 1:2]),
                                  reads=[SAr, nbr[xs]], writes=[exr[xs]])
                            kb.op("act", lambda e: e.activation(out=ex[xs][:, 2, :], in_=src, func=AF.Exp, scale=1.0 / 16.0), reads=[SAr], writes=[exr[xs]])
                            dec = ex[xs][:, 0, 0:1]
                        yield
                        kb.op("dve", lambda e: e.tensor_tensor(out=qd[xs][:], in0=QK[:, 0, c0:c0 + 64], in1=ex[xs][:, 0, :], op=ALU.mult),
                              reads=[QKr[0][b], exr[xs]], writes=[qdr[xs]])
                        kb.op("dve", lambda e: e.tensor_tensor(out=kd[tb][:, par * 64:par * 64 + 64], in0=QK[:, 1, c0:c0 + 64], in1=ex[xs][:, 1, :], op=ALU.mult),
                              reads=[QKr[1][b], exr[xs]], writes=[kdr[tb]])
                        kb.op("pool", lambda e: e.tensor_tensor(out=kt[tb][:, par * 64:par * 64 + 64], in0=QK[:, 1, c0:c0 + 64], in1=ex[xs][:, 2, :], op=ALU.mult),
                              reads=[QKr[1][b], exr[xs]], writes=[ktr[tb]])
                        yield
                        pa = ci % 2
                        ptb = 2 + ci % 2
                        po = 4 + ci % 2
                        pS = 6 + ci % 2
                        kb.op("pe", lambda e: e.matmul(self.ps[pa][:, 0:64], lhsT=kd[tb][:], rhs=qd[xs][:], start=True, stop=True),
                              reads=[kdr[tb], qdr[xs]], writes=[self.psr[pa]])
                        kb.op("dve", lambda e: e.tensor_tensor(out=attm[xs][:], in0=self.ps[pa][:, 0:64], in1=maskb[:, par * 2 + d, :], op=ALU.mult),
                              reads=[self.psr[pa], cstr], writes=[attr[xs]])
                        yield
                        kb.op("pe", lambda e: e.transpose(psb[ptb][:, 0:128], kt[tb][:], identb[:]),
                              reads=[ktr[tb], cstr], writes=[self.psr[ptb]])
                        kb.op("act", lambda e: e.activation(out=ktT[xs][par * 64:par * 64 + 64, :], in_=psb[ptb][par * 64:par * 64 + 64, 0:128], func=AF.Copy),
                              reads=[self.psr[ptb]], writes=[ktTr[xs]])
                        yield
                        for ec in range(2):
                            kb.op("pe", lambda e: e.matmul(self.ps[po][:, ec * 64:ec * 64 + 64], lhsT=VT[:, tt, ec * 128:(ec + 1) * 128], rhs=attm[xs][:],
                                                            start=True, stop=False),
                                  reads=[VTr, attr[xs]], writes=[self.psr[po]])
                            kb.op("pe", lambda e: e.matmul(self.ps[po][:, ec * 64:ec * 64 + 64], lhsT=Sb[:, ec * 128:(ec + 1) * 128], rhs=qd[xs][:],
                                                            start=False, stop=True),
                                  reads=[Sbr, qdr[xs]], writes=[self.psr[po]])
                        yield
                        kb.op("pe", lambda e: e.matmul(self.ps[pS][:, 0:256], lhsT=ktT[xs][par * 64:par * 64 + 64, :], rhs=VT[par * 64:par * 64 + 64, tt, :],
                                                        start=True, stop=True),
                              reads=[ktTr[xs], VTr], writes=[self.psr[pS]])
                        kb.op("dve", lambda e: e.scalar_tensor_tensor(out=S[:], in0=S[:], scalar=dec, in1=self.ps[pS][:, 0:256], op0=ALU.mult, op1=ALU.add),
                              reads=[Sr, exr[xs], self.psr[pS]], writes=[Sr])
                        kb.op("act", lambda e: e.activation(out=Sb[:], in_=S[:], func=AF.Copy), reads=[Sr], writes=[Sbr])
                        yield
                        pov = self.ps[po][:, 0:128].rearrange("p (e c) -> p e c", e=2)
                        if d == 0:
                            kb.op("act", lambda e: e.activation(out=O[:, :, c0:c0 + 64], in_=pov, func=AF.Identity, scale=SC),
                                  reads=[self.psr[po]], writes=[Or[b]])
                        else:
                            kb.op("dve", lambda e: e.scalar_tensor_tensor(out=O[:, :, c0:c0 + 64], in0=pov, scalar=SC, in1=O[:, :, c0:c0 + 64],
                                                                          op0=ALU.mult, op1=ALU.add),
                                  reads=[self.psr[po], Or[b]], writes=[Or[b]])
                    active = []
                    pending = list(enumerate(order))
                    while pending or active:
                        if pending and len(active) < 3:
                            active.append(chunk_gen(*pending.pop(0)))
                        for g_ in list(active):
                            try:
                                next(g_)
                            except StopIteration:
                                active.remove(g_)
                if getattr(self, "debug", None) == f"gla_scan{hd}":
                    dd = kb.dsem("dbg")
                    for e2 in range(2):
                        kb.dma("sp", dd, self.d_hout[e2], O[:, e2, :], reads=Or)
                        kb.dma("pool", dd, self.d_hout[2 + e2], QK[:, e2, :], reads=QKr[0] + QKr[1])
                        kb.dma("sp", dd, self.d_hout[4 + e2], SA[:, e2, :], reads=[SAr])
                    kb.dma("sp", dd, self.d_hout[6][:, 0:256], S[:], reads=[Sr])
                    kb.dma("sp", dd, self.d_hout[7][0:32, :], rT[:], reads=rTr)
                    kb.eng["sp"].wait_ge(dd.sem, dd.cnt)
                    kb.eng["pool"].wait_ge(dd.sem, dd.cnt)
                    raise DebugStop()
                for ec in range(2):
                    ws = wcnt % 2
                    wcnt += 1
                    kb.dma("pool", pwd[ws], pw[ws][:].rearrange("p k m -> p (k m)"), self.d_glawg[hd * 2 + ec], writes=[pwr[ws]])
                    for b in oblocks:
                        t0, n, kind = BLOCKS[b]
                        pg = self.psrot % 8
                        pq = (self.psrot + 1) % 8
                        ts = (self.psrot // 2) % 2
                        self.psrot += 2
                        for kc in range(NC_):
                            kb.op("pe", lambda e: e.matmul(self.ps[pg][:, :n], lhsT=pw[ws][:, kc, :], rhs=self.U[:, kc, t0:t0 + n],
                                                            start=(kc == 0), stop=(kc == NC_ - 1)),
                                  reads=[pwr[ws], self.Ur[kc][b]], writes=[self.psr[pg]])
                        for e2 in range(2):
                            kb.op("act", lambda e: e.activation(out=t1[ts][:, :n], in_=O[:, e2, t0:t0 + n], func=AF.Square), reads=[Or[b]], writes=[t1r[ts]])
                            kb.op("act", lambda e: e.activation(out=Z[:, ec, t0:t0 + n], in_=t1[ts][:, :n], func=AF.Copy), reads=[t1r[ts]], writes=[Zr[ec][b]])
                            kb.op("pe", lambda e: e.matmul(self.ps[pq][:, :n], lhsT=ones256[:], rhs=Z[:, ec, t0:t0 + n], start=(e2 == 0), stop=(e2 == 1)),
                                  reads=[cstr, Zr[ec][b]], writes=[self.psr[pq]])
                        kb.op("act", lambda e: e.activation(out=t1[ts][:, :n], in_=self.ps[pq][:, :n], func=AF.Ln, bias=epsc[:, 0:1]),
                              reads=[self.psr[pq], cstr], writes=[t1r[ts]])
                        kb.op("act", lambda e: e.activation(out=t1[ts][:, :n], in_=t1[ts][:, :n], func=AF.Exp, scale=-0.5), reads=[t1r[ts]], writes=[t1r[ts]])
                        kb.op("dve", lambda e: e.tensor_tensor(out=t1[ts][:, :n], in0=O[:, ec, t0:t0 + n], in1=t1[ts][:, :n], op=ALU.mult),
                              reads=[Or[b], t1r[ts]], writes=[t1r[ts]])
                        kb.op("act", lambda e: e.activation(out=t2[ts][:, :n], in_=self.ps[pg][:, :n], func=AF.Silu), reads=[self.psr[pg]], writes=[t2r[ts]])
                        kb.op("dve", lambda e: e.scalar_tensor_tensor(out=Z[:, ec, t0:t0 + n], in0=t1[ts][:, :n], scalar=cst[:, NG + ec:NG + ec + 1],
                                                                      in1=t2[ts][:, :n], op0=ALU.mult, op1=ALU.mult),
                              reads=[t1r[ts], t2r[ts], cstr], writes=[Zr[ec][b]])
                if getattr(self, "debug", None) == f"gla_fin{hd}":
                    dd = kb.dsem("dbg")
                    for e2 in range(2):
                        kb.dma("pool", dd, self.d_hout[e2], Z[:, e2, :], reads=Zr[0] + Zr[1])
                        kb.dma("sp", dd, self.d_hout[2 + e2], O[:, e2, :], reads=Or)
                    kb.eng["sp"].wait_ge(dd.sem, dd.cnt)
                    kb.eng["pool"].wait_ge(dd.sem, dd.cnt)
                    raise DebugStop()
                with ExitStack() as es2:
                    self.proj(es2, self.d_glawo[hd], range(NC_), Z, Zr, self.resid_evac(j), oblocks, nk=2, tag=f"glwo{hd}")
                    kb.barrier()
                    if getattr(self, "debug", None) == f"gla_wo{hd}":
                        self.store()
                        raise DebugStop()
            kb.barrier()
        self.layernorm(L, j, oblocks)


    def park_h(self):
        kb = self.kb
        if not hasattr(self, "d_hpark"):
            self.d_hpark = self.nc.dram_tensor("hpark", [NC_, 128, T], F32, kind="Internal").ap()
            self.ds_park = kb.dsem("park")
        for c in range(NC_):
            kb.dma("sp", self.ds_h[c % 4], self.d_hpark[c], self.H[:, c, :], reads=self.Hr[c])

    def unpark_h(self):
        kb = self.kb
        self.es_H = ExitStack()
        self.H = kb.sb(self.es_H, "H", [128, NC_, T], F32)
        for c in range(NC_):
            kb.dma("sp", self.ds_h[c % 4], self.H[:, c, :], self.d_hpark[c], writes=self.Hr[c])

    def rwkv_mixer(self, L, last):
        kb = self.kb
        assert last, "rwkv mixer implemented for the final layer (context output unused)"
        j = 1
        blocks = [0, 1, 2, 3, 4]
        lblocks = [0, 1, 2, 3]
        self.modulate(j, blocks)
        NCH = T // 64
        G = 4
        EM05 = float(np.exp(-0.5))
        U = self.U
        nbank = lambda: self._nb()
        W0, A0, KK_, KA, RK, GNG, GNB, MU, OMKA, OMU, HMU = 0, 16, 32, 40, 48, 56, 64, 72, 120, 128, 176
        self.park_h()
        kb.barrier()
        self.es_H.close()
        d_zpark = self.nc.dram_tensor("zpark", [NC_, 128, SEQ], BF16, kind="Internal").ap()
        ds_z = kb.dsem("zpark")
        dbg = getattr(self, "debug", None)
        if dbg == "rw0":
            return 'stop'
        with ExitStack() as es:
            prm = kb.sb(es, "rwprm", [128, 224], F32)
            prmr = Reg()
            kb.dma("sp", self.ds_misc, prm[:, 0:120], self.d_rwprm, writes=[prmr])
            kb.op("dve", lambda e: e.tensor_scalar(out=prm[:, OMKA:OMKA + 8], in0=prm[:, KA:KA + 8], scalar1=-1.0, scalar2=1.0, op0=ALU.mult, op1=ALU.add),
                  reads=[prmr], writes=[prmr])
            kb.op("dve", lambda e: e.tensor_scalar(out=prm[:, OMU:OMU + 48], in0=prm[:, MU:MU + 48], scalar1=-1.0, scalar2=1.0, op0=ALU.mult, op1=ALU.add),
                  reads=[prmr], writes=[prmr])
            kb.op("dve", lambda e: e.tensor_scalar(out=prm[:, HMU:HMU + 48], in0=prm[:, MU:MU + 48], scalar1=0.5, scalar2=None, op0=ALU.mult),
                  reads=[prmr], writes=[prmr])
            cstf = kb.sb(es, "rwcstf", [128, 256], F32)
            cstr = Reg()
            kb.dma("sp", self.ds_misc, cstf[:], self.d_rwcstf, writes=[cstr])
            ident = cstf[:, 0:128]
            bdmask = cstf[:, 128:256]
            gmask = kb.sb(es, "rwgmask", [64, 2, 5, 2, 64], BF16)
            kb.dma("pool", self.ds_misc, gmask[:].rearrange("p a b c d -> p (a b c d)"), self.d_rwgmask, writes=[cstr])
            identb = kb.sb(es, "rwidb", [128, 128], BF16)
            id64 = kb.sb(es, "rwid64", [64, 2, 64], BF16)
            onesbd = kb.sb(es, "rwonesbd", [128, 128], BF16)
            ones64 = kb.sb(es, "rwones64", [128, 128], BF16)
            kb.op("dve", lambda e: e.tensor_copy(out=identb[:], in_=ident), reads=[cstr], writes=[cstr])
            for h in range(2):
                kb.op("dve", lambda e: e.tensor_copy(out=id64[:, h, :], in_=cstf[0:64, 0:64]), reads=[cstr], writes=[cstr])
            kb.op("dve", lambda e: e.tensor_copy(out=onesbd[:], in_=bdmask), reads=[cstr], writes=[cstr])
            kb.op("dve", lambda e: e.tensor_scalar(out=ones64[:], in0=bdmask, scalar1=1.0 / 64.0, scalar2=None, op0=ALU.mult), reads=[cstr], writes=[cstr])
            epsg = kb.sb(es, "rwepsg", [128, 2], F32)
            kb.op("dve", lambda e: e.memset(epsg[:, 0:1], 64e-5), writes=[cstr])
            kb.op("dve", lambda e: e.memset(epsg[:, 1:2], 1e-24), writes=[cstr])
            XX = kb.sb(es, "rwXX", [128, NC_, T], BF16)
            XXr = [[Reg() for _ in range(5)] for _ in range(NC_)]
            with ExitStack() as es0:
                tf = [kb.sb(es0, f"rwtf{s_}", [128, T], F32) for s_ in range(2)]
                tfr = [Reg(), Reg()]
                for c in range(NC_):
                    s_ = c % 2
                    for (s0, s1) in ((0, SEQ), (SEQ, T)):
                        kb.op("pool", lambda e: e.tensor_tensor(out=tf[s_][:, s0 + 1:s1 - 1], in0=U[:, c, s0:s1 - 2], in1=U[:, c, s0 + 2:s1], op=ALU.add),
                              reads=self.Ur[c], writes=[tfr[s_]])
                        kb.op("pool", lambda e: e.tensor_copy(out=tf[s_][:, s0:s0 + 1], in_=U[:, c, s0 + 1:s0 + 2]), reads=self.Ur[c], writes=[tfr[s_]])
                        kb.op("pool", lambda e: e.tensor_copy(out=tf[s_][:, s1 - 1:s1], in_=U[:, c, s1 - 2:s1 - 1]), reads=self.Ur[c], writes=[tfr[s_]])
                    kb.op("dve", lambda e: e.scalar_tensor_tensor(out=XX[:, c, :], in0=tf[s_][:], scalar=0.5, in1=U[:, c, :], op0=ALU.mult, op1=ALU.subtract),
                          reads=[tfr[s_]] + self.Ur[c], writes=XXr[c])
                kb.barrier()
            pw = [kb.sb(es, f"rwpw{s_}", [128, NC_, 128], BF16) for s_ in range(2)]
            pws = [kb.sb(es, f"rwpws{s_}", [128, NC_, 128], BF16) for s_ in range(2)]
            pwr = [Reg(), Reg()]
            pwsr = [Reg(), Reg()]
            pwd = [kb.dsem(f"rwpw{s_}") for s_ in range(2)]
            wc = {"n": 0}

            def xproj(Wd_oc, jkind, blks, evac):
                s_ = wc["n"] % 2
                wc["n"] += 1
                kb.dma("pool", pwd[s_], pw[s_][:].rearrange("p k m -> p (k m)"), Wd_oc, writes=[pwr[s_]])
                for kc in range(NC_):
                    kb.op("dve" if kc % 2 == 0 else "pool",
                          lambda e: e.tensor_scalar(out=pws[s_][:, kc, :], in0=pw[s_][:, kc, :], scalar1=prm[:, MU + jkind * 8 + kc:MU + jkind * 8 + kc + 1],
                                                    scalar2=None, op0=ALU.mult),
                          reads=[pwr[s_], prmr], writes=[pwsr[s_]])
                for b in blks:
                    t0, n, kind = BLOCKS[b]
                    pb = nbank()
                    for kc in range(NC_):
                        kb.op("pe", lambda e: e.matmul(self.ps[pb][:, :n], lhsT=pw[s_][:, kc, :], rhs=U[:, kc, t0:t0 + n], start=(kc == 0), stop=False),
                              reads=[pwr[s_], self.Ur[kc][b]], writes=[self.psr[pb]])
                    for kc in range(NC_):
                        kb.op("pe", lambda e: e.matmul(self.ps[pb][:, :n], lhsT=pws[s_][:, kc, :], rhs=XX[:, kc, t0:t0 + n], start=False, stop=(kc == NC_ - 1)),
                              reads=[pwsr[s_], XXr[kc][b]], writes=[self.psr[pb]])
                    evac(b, pb, t0, n)

            LR = kb.sb(es, "rwLR", [128, 3, T], BF16)
            LRr = [[Reg() for _ in range(5)] for _ in range(3)]
            for (li, jk, fn) in ((0, 5, AF.Sigmoid), (1, 1, AF.Tanh), (2, 4, AF.Copy)):
                def ev(b, pb, t0, n, li=li, fn=fn):
                    kb.op("act", lambda e: e.activation(out=LR[:, li, t0:t0 + n], in_=self.ps[pb][:, :n], func=fn),
                          reads=[self.psr[pb]], writes=[LRr[li][b]])
                xproj(self.d_rwlr[li], jk, lblocks if li == 0 else blocks, ev)
            if dbg == "rw1":
                kb.barrier()
                return 'stop'
            lrw = kb.sb(es, "rwlrw", [128, 3, 2, 128], BF16)
            lrwr = [Reg(), Reg()]
            lrwd = [kb.dsem(f"rwlrw{s_}") for s_ in range(2)]
            ZT1 = kb.sb(es, "rwZT1", [128, SEQ], BF16)
            ZT1r = [Reg() for _ in range(5)]
            Rr_ = kb.sb(es, "rwR", [128, SEQ], BF16); Rr = [Reg() for _ in range(5)]
            Kk = kb.sb(es, "rwK", [128, T], BF16); Kr = [Reg() for _ in range(5)]
            KKn = kb.sb(es, "rwKK", [128, T], BF16); KKr = [Reg() for _ in range(5)]
            VTf = kb.sb(es, "rwVT", [128, T], BF16); VTr = [Reg() for _ in range(5)]
            Gg = kb.sb(es, "rwG", [128, SEQ], BF16); Ggr = [Reg() for _ in range(5)]
            BON = kb.sb(es, "rwBON", [128, SEQ], BF16); BONr = [Reg() for _ in range(5)]
            YA = kb.sb(es, "rwYA", [128, SEQ], F32); YAr = [Reg() for _ in range(5)]
            LW = kb.sb(es, "rwLW", [128, T], F32); LWr = [Reg() for _ in range(5)]
            KD = kb.sb(es, "rwKD", [128, T], BF16); KDr = [Reg() for _ in range(5)]
            BD = kb.sb(es, "rwBD", [128, T], BF16); BDr = [Reg() for _ in range(5)]
            tA = [kb.sb(es, f"rwtA{s_}", [128, 512], F32) for s_ in range(2)]
            tAr = [Reg(), Reg()]
            tB = [kb.sb(es, f"rwtB{s_}", [128, 512], BF16) for s_ in range(2)]
            tBr = [Reg(), Reg()]
            SCs = [kb.sb(es, f"rwSC{s_}", [128, 2, 64], F32) for s_ in range(G)]; SCr = [Reg() for _ in range(G)]
            TOT = [kb.sb(es, f"rwTOT{s_}", [128, 2], F32) for s_ in range(G)]; TOTr = [Reg() for _ in range(G)]
            EE = [kb.sb(es, f"rwEE{s_}", [128, 4, 64], F32) for s_ in range(G)]; EEr = [Reg() for _ in range(G)]
            OPS = [kb.sb(es, f"rwOPS{s_}", [128, 6, 64], BF16) for s_ in range(G)]; OPSr = [Reg() for _ in range(G)]
            RT32 = [kb.sb(es, f"rwRT{s_}", [128, 64], F32) for s_ in range(G)]; RT32r = [Reg() for _ in range(G)]
            TOK = [kb.sb(es, f"rwTOK{s_}", [64, 4, 128], BF16) for s_ in range(G)]; TOKr = [Reg() for _ in range(G)]
            GM = [kb.sb(es, f"rwGM{s_}", [64, 5, 2, 64], BF16) for s_ in range(G)]; GMr = [Reg() for _ in range(G)]
            NF = [kb.sb(es, f"rwNF{s_}", [64, 2, 2, 64], F32) for s_ in range(G)]; NFr = [Reg() for _ in range(G)]
            PP = [kb.sb(es, f"rwPP{s_}", [64, 2, 2, 2, 64], F32) for s_ in range(G)]; PPr = [[Reg() for _ in range(2)] for _ in range(G)]
            TT32 = [kb.sb(es, f"rwTT{s_}", [64, 2, 2, 64], F32) for s_ in range(G)]; TTr = [[Reg() for _ in range(2)] for _ in range(G)]
            TTb = [kb.sb(es, f"rwTTb{s_}", [64, 2, 64], BF16) for s_ in range(G)]; TTbr = [Reg() for _ in range(G)]
            id64f = kb.sb(es, "rwid64f", [64, 2, 64], F32)
            for h in range(2):
                kb.op("dve", lambda e: e.tensor_copy(out=id64f[:, h, :], in_=cstf[0:64, 0:64]), reads=[cstr], writes=[cstr])
            GS = [kb.sb(es, f"rwGS{s_}", [64, 3, 2, 64], BF16) for s_ in range(G)]; GSr = [[Reg() for _ in range(3)] for _ in range(G)]
            QH = [kb.sb(es, f"rwQH{s_}", [128, 64], BF16) for s_ in range(G)]; QHr = [Reg() for _ in range(G)]
            PH = [kb.sb(es, f"rwPH{s_}", [128, 128], F32) for s_ in range(G)]; PHr = [Reg() for _ in range(G)]
            Abd = kb.sb(es, "rwA", [128, 128], F32); Ar = Reg()
            Abf = kb.sb(es, "rwAbf", [128, 128], BF16); Abfr = Reg()
            psbf = [self.ps[i].bitcast(BF16) for i in range(8)]

            for pr in range(NC_):
                sl = pr % 2
                kb.dma("pool", lrwd[sl], lrw[:, :, sl, :], self.d_rwlrw[pr].rearrange("a p m -> p a m"), writes=[lrwr[sl]])
                def ev_r(b, pb, t0, n):
                    kb.op("act", lambda e: e.activation(out=Rr_[:, t0:t0 + n], in_=self.ps[pb][:, :n], func=AF.Copy), reads=[self.psr[pb]], writes=[Rr[b]])
                xproj(self.d_rwwrkv[0, pr], 0, lblocks, ev_r)

                def ev_k(b, pb, t0, n):
                    s_ = b % 2
                    kb.op("act", lambda e: e.activation(out=Kk[:, t0:t0 + n], in_=self.ps[pb][:, :n], func=AF.Copy), reads=[self.psr[pb]], writes=[Kr[b]])
                    kb.op("act", lambda e: e.activation(out=tA[s_][:, :n], in_=self.ps[pb][:, :n], func=AF.Copy, scale=prm[:, KK_ + pr:KK_ + pr + 1]),
                          reads=[self.psr[pb], prmr], writes=[tAr[s_]])
                    kb.op("act", lambda e: e.activation(out=tB[s_][:, :n], in_=tA[s_][:, :n], func=AF.Square), reads=[tAr[s_]], writes=[tBr[s_]])
                    p2 = nbank()
                    kb.op("pe", lambda e: e.matmul(self.ps[p2][:, :n], lhsT=onesbd[:], rhs=tB[s_][:, :n], start=True, stop=True),
                          reads=[cstr, tBr[s_]], writes=[self.psr[p2]])
                    kb.op("act", lambda e: e.activation(out=tB[s_][:, :n], in_=self.ps[p2][:, :n], func=AF.Ln, bias=epsg[:, 1:2]), reads=[self.psr[p2], cstr], writes=[tBr[s_]])
                    kb.op("act", lambda e: e.activation(out=tB[s_][:, :n], in_=tB[s_][:, :n], func=AF.Exp, scale=-0.5), reads=[tBr[s_]], writes=[tBr[s_]])
                    kb.op("dve", lambda e: e.tensor_tensor(out=KKn[:, t0:t0 + n], in0=tA[s_][:, :n], in1=tB[s_][:, :n], op=ALU.mult),
                          reads=[tAr[s_], tBr[s_]], writes=[KKr[b]])
                xproj(self.d_rwwrkv[1, pr], 2, blocks, ev_k)

                def ev_v(b, pb, t0, n):
                    kb.op("act", lambda e: e.activation(out=VTf[:, t0:t0 + n], in_=self.ps[pb][:, :n], func=AF.Copy), reads=[self.psr[pb]], writes=[VTr[b]])
                xproj(self.d_rwwrkv[2, pr], 3, blocks, ev_v)
                for b in lblocks:
                    t0, n, kind = BLOCKS[b]
                    pb = nbank()
                    kb.op("pe", lambda e: e.matmul(self.ps[pb][:, :n], lhsT=lrw[:, 0, sl, :], rhs=LR[:, 0, t0:t0 + n], start=True, stop=True),
                          reads=[lrwr[sl], LRr[0][b]], writes=[self.psr[pb]])
                    kb.op("act", lambda e: e.activation(out=Gg[:, t0:t0 + n], in_=self.ps[pb][:, :n], func=AF.Copy), reads=[self.psr[pb]], writes=[Ggr[b]])
                for d in range(2):
                    for b in blocks:
                        t0, n, kind = BLOCKS[b]
                        s_ = b % 2
                        pb = nbank()
                        kb.op("pe", lambda e: e.matmul(self.ps[pb][:, :n], lhsT=lrw[d * 64:(d + 1) * 64, 1, sl, :], rhs=LR[d * 64:(d + 1) * 64, 1, t0:t0 + n], start=True, stop=True),
                              reads=[lrwr[sl], LRr[1][b]], writes=[self.psr[pb]])
                        kb.op("act", lambda e: e.activation(out=tA[s_][:, :n], in_=self.ps[pb][:, :n], func=AF.Sigmoid, bias=prm[:, W0 + d * 8 + pr:W0 + d * 8 + pr + 1]),
                              reads=[self.psr[pb], prmr], writes=[tAr[s_]])
                        kb.op("dve", lambda e: e.tensor_scalar(out=LW[:, t0:t0 + n], in0=tA[s_][:, :n], scalar1=-EM05, scalar2=None, op0=ALU.mult),
                              reads=[tAr[s_]], writes=[LWr[b]])
                        pb = nbank()
                        kb.op("pe", lambda e: e.matmul(self.ps[pb][:, :n], lhsT=lrw[d * 64:(d + 1) * 64, 2, sl, :], rhs=LR[d * 64:(d + 1) * 64, 2, t0:t0 + n], start=True, stop=True),
                              reads=[lrwr[sl], LRr[2][b]], writes=[self.psr[pb]])
                        kb.op("act", lambda e: e.activation(out=tA[s_][:, :n], in_=self.ps[pb][:, :n], func=AF.Sigmoid, bias=prm[:, A0 + d * 8 + pr:A0 + d * 8 + pr + 1]),
                              reads=[self.psr[pb], prmr], writes=[tAr[s_]])
                        kb.op("dve", lambda e: e.tensor_tensor(out=BD[:, t0:t0 + n], in0=KKn[:, t0:t0 + n], in1=tA[s_][:, :n], op=ALU.mult),
                              reads=[KKr[b], tAr[s_]], writes=[BDr[b]])
                        kb.op("act", lambda e: e.activation(out=tA[s_][:, :n], in_=tA[s_][:, :n], func=AF.Identity, scale=prm[:, KA + pr:KA + pr + 1], bias=prm[:, OMKA + pr:OMKA + pr + 1]),
                              reads=[tAr[s_], prmr], writes=[tAr[s_]])
                        kb.op("dve", lambda e: e.tensor_tensor(out=KD[:, t0:t0 + n], in0=Kk[:, t0:t0 + n], in1=tA[s_][:, :n], op=ALU.mult),
                              reads=[Kr[b], tAr[s_]], writes=[KDr[b]])
                        if kind == 0:
                            kb.op("dve", lambda e: e.scalar_tensor_tensor(out=tB[s_][:, :n], in0=KD[:, t0:t0 + n], scalar=prm[:, RK + pr:RK + pr + 1], in1=Rr_[:, t0:t0 + n], op0=ALU.mult, op1=ALU.mult),
                                  reads=[KDr[b], Rr[b], prmr], writes=[tBr[s_]])
                            pb = nbank()
                            kb.op("pe", lambda e: e.matmul(self.ps[pb][:, :n], lhsT=onesbd[:], rhs=tB[s_][:, :n], start=True, stop=True),
                                  reads=[cstr, tBr[s_]], writes=[self.psr[pb]])
                            if d == 0:
                                kb.op("act", lambda e: e.activation(out=BON[:, t0:t0 + n], in_=self.ps[pb][:, :n], func=AF.Copy), reads=[self.psr[pb]], writes=[BONr[b]])
                            else:
                                kb.op("dve", lambda e: e.tensor_tensor(out=BON[:, t0:t0 + n], in0=self.ps[pb][:, :n], in1=BON[:, t0:t0 + n], op=ALU.add),
                                      reads=[self.psr[pb], BONr[b]], writes=[BONr[b]])
                    if dbg == "rw2":
                        kb.barrier()
                        return 'stop'
                    kb.op("dve", lambda e: e.memset(Abd[:], 0.0), writes=[Ar])
                    kb.op("dve", lambda e: e.memset(Abf[:], 0.0), writes=[Abfr])
                    order = ([32, 33, 34, 35] + list(range(32))) if d == 0 else ([35, 34, 33, 32] + list(range(31, -1, -1)))
                    def chunk_gen(ci, n_):
                        c0 = 64 * n_
                        b = min(c0 // 512, 4)
                        lat = n_ < 32
                        q = ci % G
                        cbs = {'i': 0}

                        def nbank():
                            cbs['i'] += 1
                            return 2 * q + (cbs['i'] % 2)
                        ng = 5 if lat else 3
                        kb.op("dve", lambda e: e.tensor_tensor_scan(out=SCs[q][:, 0, :], data0=self.onesf[:, 0:64], data1=LW[:, c0:c0 + 64], initial=0.0, op0=ALU.mult, op1=ALU.add),
                              reads=[LWr[b], self.onesr], writes=[SCr[q]])
                        kb.op("dve", lambda e: e.tensor_tensor(out=SCs[q][:, 1, :], in0=SCs[q][:, 0, :], in1=LW[:, c0:c0 + 64], op=ALU.subtract),
                              reads=[SCr[q], LWr[b]], writes=[SCr[q]])
                        kb.op("dve", lambda e: e.tensor_copy(out=TOT[q][:, 0:1], in_=SCs[q][:, 0, 63:64]), reads=[SCr[q]], writes=[TOTr[q]])
                        kb.op("dve", lambda e: e.tensor_scalar(out=TOT[q][:, 1:2], in0=SCs[q][:, 0, 63:64], scalar1=-1.0, scalar2=None, op0=ALU.mult), reads=[SCr[q]], writes=[TOTr[q]])
                        cs, cxf = SCs[q][:, 0, :], SCs[q][:, 1, :]
                        tot, ntot = TOT[q][:, 0:1], TOT[q][:, 1:2]
                        if d == 0:
                            exs = [(cxf, 1.0, None), (cs, -1.0, None), (cs, 1.0, None), (cs, -1.0, tot)]
                        else:
                            exs = [(cs, -1.0, tot), (cxf, 1.0, ntot), (cxf, -1.0, tot), (cxf, 1.0, None)]
                        for i, (src, sc_, bi) in enumerate(exs):
                            if i == 2 and not lat:
                                continue
                            if bi is None:
                                kb.op("act", lambda e: e.activation(out=EE[q][:, i, :], in_=src, func=AF.Exp, scale=sc_), reads=[SCr[q]], writes=[EEr[q]])
                            else:
                                kb.op("act", lambda e: e.activation(out=EE[q][:, i, :], in_=src, func=AF.Exp, scale=sc_, bias=bi), reads=[SCr[q], TOTr[q]], writes=[EEr[q]])
                        yield
                        opl = [(0, KKn, KKr, 0), (1, BD, BDr, 1), (2, KD, KDr, 1), (4, BD, BDr, 3), (5, KD, KDr, 3)]
                        for oi, (o_, srcb, srcr, ei) in enumerate(opl):
                            kb.op("dve" if oi % 2 == 0 else "pool",
                                  lambda e: e.tensor_tensor(out=OPS[q][:, o_, :], in0=srcb[:, c0:c0 + 64], in1=EE[q][:, ei, :], op=ALU.mult),
                                  reads=[srcr[b], EEr[q]], writes=[OPSr[q]])
                        if lat:
                            kb.op("dve", lambda e: e.tensor_tensor(out=RT32[q][:], in0=Rr_[:, c0:c0 + 64], in1=EE[q][:, 2, :], op=ALU.mult),
                                  reads=[Rr[b], EEr[q]], writes=[RT32r[q]])
                            kb.op("pool", lambda e: e.tensor_copy(out=OPS[q][:, 3, :], in_=RT32[q][:]), reads=[RT32r[q]], writes=[OPSr[q]])
                        yield
                        pb = nbank()
                        for i, o_ in enumerate((0, 4, 5)):
                            kb.op("pe", lambda e: e.transpose(psbf[pb][0:64, i * 128:(i + 1) * 128], OPS[q][:, o_, :], identb[:]),
                                  reads=[OPSr[q], cstr], writes=[self.psr[pb]])
                        kb.op("pe", lambda e: e.transpose(psbf[pb][0:64, 384:512], VTf[:, c0:c0 + 64], identb[:]),
                              reads=[VTr[b], cstr], writes=[self.psr[pb]])
                        kb.op("act", lambda e: e.activation(out=TOK[q][:].rearrange("p a b -> p (a b)"), in_=psbf[pb][0:64, 0:512], func=AF.Copy),
                              reads=[self.psr[pb]], writes=[TOKr[q]])
                        yield
                        gpairs = [(0, 1), (1, 0), (2, 0), (1, 3), (2, 3)]
                        pbh = [nbank(), nbank()]
                        for wi in range(ng):
                            li_, ri_ = gpairs[wi]
                            for h in range(2):
                                kb.op("pe", lambda e: e.matmul(self.ps[pbh[h]][0:64, wi * 64:(wi + 1) * 64], lhsT=OPS[q][h * 64:(h + 1) * 64, li_, :],
                                                                rhs=OPS[q][h * 64:(h + 1) * 64, ri_, :], start=True, stop=True),
                                      reads=[OPSr[q]], writes=[self.psr[pbh[h]]])
                        for h in range(2):
                            kb.op("dve", lambda e: e.tensor_tensor(out=GM[q][:, 0:ng, h, :], in0=self.ps[pbh[h]][0:64, 0:ng * 64].rearrange("p (a c) -> p a c", a=ng),
                                                                   in1=gmask[:, d, 0:ng, h, :], op=ALU.mult),
                                  reads=[self.psr[pbh[h]], cstr], writes=[GMr[q]])
                            kb.op("dve", lambda e: e.tensor_tensor(out=NF[q][:, :, h, :], in0=self.ps[pbh[h]][0:64, 0:128].rearrange("p (a c) -> p a c", a=2),
                                                                   in1=gmask[:, d, 0:2, h, :], op=ALU.mult),
                                  reads=[self.psr[pbh[h]], cstr], writes=[NFr[q]])
                        yield
                        kb.op("pool", lambda e: e.tensor_tensor(out=TT32[q][:, 0, :, :], in0=NF[q][:, 1, :, :], in1=id64f[:], op=ALU.add),
                              reads=[NFr[q], cstr], writes=[TTr[q][0]])
                        Pm = lambda lv, tr: (NF[q][:, tr, :, :] if lv == 0 else PP[q][:, lv % 2, tr, :, :])
                        Pmr = lambda lv: (NFr[q] if lv == 0 else PPr[q][lv % 2])
                        for lv in range(1, 6):
                            pb = nbank()
                            ntr = 2 if lv < 5 else 1
                            for tr in range(ntr):
                                for h in range(2):
                                    lt, rt = (1, 0) if tr == 0 else (0, 1)
                                    kb.op("pe", lambda e: e.matmul(self.ps[pb][0:64, (tr * 2 + h) * 64:(tr * 2 + h + 1) * 64], lhsT=Pm(lv - 1, lt)[:, h, :], rhs=Pm(lv - 1, rt)[:, h, :],
                                                                    start=True, stop=True),
                                          reads=[Pmr(lv - 1)], writes=[self.psr[pb]])
                            kb.op("act", lambda e: e.activation(out=PP[q][:, lv % 2, 0:ntr, :, :].rearrange("p a b c -> p (a b c)"), in_=self.ps[pb][0:64, 0:ntr * 128], func=AF.Copy),
                                  reads=[self.psr[pb]], writes=[PPr[q][lv % 2]])
                            yield
                            pb = nbank()
                            for h in range(2):
                                kb.op("pe", lambda e: e.matmul(self.ps[pb][0:64, h * 64:(h + 1) * 64], lhsT=PP[q][:, lv % 2, 0, h, :], rhs=TT32[q][:, (lv - 1) % 2, h, :], start=True, stop=True),
                                      reads=[PPr[q][lv % 2], TTr[q][(lv - 1) % 2]], writes=[self.psr[pb]])
                            kb.op("dve", lambda e: e.tensor_tensor(out=TT32[q][:, lv % 2, :, :].rearrange("p b c -> p (b c)"), in0=self.ps[pb][0:64, 0:128],
                                                                   in1=TT32[q][:, (lv - 1) % 2, :, :].rearrange("p b c -> p (b c)"), op=ALU.add),
                                  reads=[self.psr[pb], TTr[q][(lv - 1) % 2]], writes=[TTr[q][lv % 2]])
                            yield
                        kb.op("act", lambda e: e.activation(out=TTb[q][:].rearrange("p b c -> p (b c)"), in_=TT32[q][:, 1, :, :].rearrange("p b c -> p (b c)"), func=AF.Copy),
                              reads=[TTr[q][1]], writes=[TTbr[q]])
                        TTf = TTb[q]
                        TTfr = TTbr[q]
                        yield
                        pb = nbank()
                        for h in range(2):
                            kb.op("pe", lambda e: e.matmul(self.ps[pb][0:64, h * 64:(h + 1) * 64], lhsT=TTf[:, h, :], rhs=TOK[q][:, 0, h * 64:(h + 1) * 64], start=True, stop=True),
                                  reads=[TTfr, TOKr[q]], writes=[self.psr[pb]])
                            kb.op("pe", lambda e: e.matmul(self.ps[pb][0:64, 128 + h * 64:128 + (h + 1) * 64], lhsT=GM[q][:, 2, h, :], rhs=TOK[q][:, 3, h * 64:(h + 1) * 64], start=True, stop=True),
                                  reads=[GMr[q], TOKr[q]], writes=[self.psr[pb]])
                        kb.op("act", lambda e: e.activation(out=GS[q][:, 0:2, :, :].rearrange("p a b c -> p (a b c)"), in_=self.ps[pb][0:64, 0:256], func=AF.Copy),
                              reads=[self.psr[pb]], writes=[GSr[q][0], GSr[q][1]])
                        yield
                        pb = nbank()
                        for h in range(2):
                            kb.op("pe", lambda e: e.matmul(self.ps[pb][0:64, h * 64:(h + 1) * 64], lhsT=TTf[:, h, :], rhs=GS[q][:, 1, h, :], start=True, stop=True),
                                  reads=[TTfr, GSr[q][1]], writes=[self.psr[pb]])
                        kb.op("act", lambda e: e.activation(out=GS[q][:, 2, :, :].rearrange("p b c -> p (b c)"), in_=self.ps[pb][0:64, 0:128], func=AF.Copy),
                              reads=[self.psr[pb]], writes=[GSr[q][2]])
                        Gfl = GS[q][:, 0, :, :].rearrange("p b c -> p (b c)")
                        U0fl = GS[q][:, 2, :, :].rearrange("p b c -> p (b c)")
                        yield
                        gcol = EE[q][:, 2, 63:64] if d == 0 else EE[q][:, 2, 0:1]
                        if not lat:
                            kb.op("act", lambda e: e.activation(out=EE[q][:, 2, 0:1], in_=TOT[q][:, 0:1], func=AF.Exp), reads=[TOTr[q], EEr[q]], writes=[EEr[q]])
                            gcol = EE[q][:, 2, 0:1]
                        pb = nbank()
                        kb.op("pe", lambda e: e.matmul(self.ps[pb][:, 0:128], lhsT=Gfl, rhs=TOK[q][:, 1, :], start=True, stop=True),
                              reads=[GSr[q][0], TOKr[q]], writes=[self.psr[pb]])
                        kb.op("dve", lambda e: e.scalar_tensor_tensor(out=PH[q][:], in0=ident, scalar=gcol, in1=self.ps[pb][:, 0:128], op0=ALU.mult, op1=ALU.subtract),
                              reads=[cstr, EEr[q], self.psr[pb]], writes=[PHr[q]])
                        kb.op("pool", lambda e: e.tensor_tensor(out=PH[q][:], in0=PH[q][:], in1=bdmask, op=ALU.mult), reads=[PHr[q], cstr], writes=[PHr[q]])
                        yield
                        if lat:
                            pb = nbank()
                            for h in range(2):
                                kb.op("pe", lambda e: e.matmul(self.ps[pb][:, h * 64:(h + 1) * 64], lhsT=Gfl, rhs=GM[q][:, 3, h, :], start=True, stop=True),
                                      reads=[GSr[q][0], GMr[q]], writes=[self.psr[pb]])
                            for h in range(2):
                                kb.op("dve", lambda e: e.tensor_tensor(out=QH[q][h * 64:(h + 1) * 64, :], in0=RT32[q][h * 64:(h + 1) * 64, :],
                                                                       in1=self.ps[pb][h * 64:(h + 1) * 64, h * 64:(h + 1) * 64], op=ALU.subtract),
                                      reads=[RT32r[q], self.psr[pb]], writes=[QHr[q]])
                            yield
                            pb = nbank()
                            for h in range(2):
                                kb.op("pe", lambda e: e.matmul(self.ps[pb][:, h * 64:(h + 1) * 64], lhsT=U0fl, rhs=GM[q][:, 3, h, :], start=True, stop=False),
                                      reads=[GSr[q][2], GMr[q]], writes=[self.psr[pb]])
                                kb.op("pe", lambda e: e.matmul(self.ps[pb][:, h * 64:(h + 1) * 64], lhsT=TOK[q][:, 3, :], rhs=GM[q][:, 4, h, :], start=False, stop=False),
                                      reads=[TOKr[q], GMr[q]], writes=[self.psr[pb]])
                                kb.op("pe", lambda e: e.matmul(self.ps[pb][:, h * 64:(h + 1) * 64], lhsT=Abf[:], rhs=QH[q][:], start=False, stop=True),
                                      reads=[Abfr, QHr[q]], writes=[self.psr[pb]])
                            for h in range(2):
                                if d == 0:
                                    kb.op("act", lambda e: e.activation(out=YA[h * 64:(h + 1) * 64, c0:c0 + 64], in_=self.ps[pb][h * 64:(h + 1) * 64, h * 64:(h + 1) * 64], func=AF.Copy),
                                          reads=[self.psr[pb]], writes=[YAr[b]])
                                else:
                                    kb.op("dve", lambda e: e.tensor_tensor(out=YA[h * 64:(h + 1) * 64, c0:c0 + 64], in0=self.ps[pb][h * 64:(h + 1) * 64, h * 64:(h + 1) * 64],
                                                                           in1=YA[h * 64:(h + 1) * 64, c0:c0 + 64], op=ALU.add),
                                          reads=[self.psr[pb], YAr[b]], writes=[YAr[b]])
                        yield
                        pb = nbank()
                        kb.op("pe", lambda e: e.matmul(self.ps[pb][:, 0:128], lhsT=TOK[q][:, 1, :], rhs=U0fl, start=True, stop=False),
                              reads=[TOKr[q], GSr[q][2]], writes=[self.psr[pb]])
                        kb.op("pe", lambda e: e.matmul(self.ps[pb][:, 0:128], lhsT=TOK[q][:, 2, :], rhs=TOK[q][:, 3, :], start=False, stop=False),
                              reads=[TOKr[q]], writes=[self.psr[pb]])
                        kb.op("pe", lambda e: e.matmul(self.ps[pb][:, 0:128], lhsT=PH[q][:], rhs=Abd[:], start=False, stop=True),
                              reads=[PHr[q], Ar], writes=[self.psr[pb]])
                        kb.op("dve", lambda e: e.tensor_tensor(out=Abd[:], in0=self.ps[pb][:, 0:128], in1=bdmask, op=ALU.mult),
                              reads=[self.psr[pb], cstr], writes=[Ar])
                        kb.op("pool", lambda e: e.tensor_copy(out=Abf[:], in_=Abd[:]), reads=[Ar], writes=[Abfr])
                    active = []
                    pending = list(enumerate(order))
                    while pending or active:
                        if pending and len(active) < G:
                            active.append(chunk_gen(*pending.pop(0)))
                        for g_ in list(active):
                            try:
                                next(g_)
                            except StopIteration:
                                active.remove(g_)
                    if dbg == f"rwdump{pr}_{d}":
                        dd = kb.dsem("dbg")
                        self.d_dbg = self.nc.dram_tensor("dbgout", [8, 128, T], F32, kind="ExternalOutput").ap()
                        kb.dma("sp", dd, self.d_dbg[0][:, 0:SEQ], YA[:], reads=YAr)
                        kb.dma("sp", dd, self.d_dbg[1], LW[:], reads=LWr)
                        kb.dma("pool", dd, self.d_dbg[2], KKn[:], reads=KKr)
                        kb.dma("pool", dd, self.d_dbg[3], KD[:], reads=KDr)
                        kb.dma("pool", dd, self.d_dbg[4], BD[:], reads=BDr)
                        kb.dma("pool", dd, self.d_dbg[5][:, 0:SEQ], Rr_[:], reads=Rr)
                        kb.dma("pool", dd, self.d_dbg[6], VTf[:], reads=VTr)
                        kb.dma("sp", dd, self.d_dbg[7][:, 0:128], Abd[:], reads=[Ar])
                        kb.eng["sp"].wait_ge(dd.sem, dd.cnt)
                        kb.eng["pool"].wait_ge(dd.sem, dd.cnt)
                        kb.barrier()
                        return 'stop'
                for b in lblocks:
                    t0, n, kind = BLOCKS[b]
                    s_ = b % 2
                    kb.op("act", lambda e: e.activation(out=tB[s_][:, :n], in_=YA[:, t0:t0 + n], func=AF.Copy), reads=[YAr[b]], writes=[tBr[s_]])
                    pm = nbank()
                    kb.op("pe", lambda e: e.matmul(self.ps[pm][:, :n], lhsT=ones64[:], rhs=tB[s_][:, :n], start=True, stop=True), reads=[cstr, tBr[s_]], writes=[self.psr[pm]])
                    kb.op("act", lambda e: e.activation(out=tB[s_][:, :n], in_=YA[:, t0:t0 + n], func=AF.Square), reads=[YAr[b]], writes=[tBr[s_]])
                    pq = nbank()
                    kb.op("pe", lambda e: e.matmul(self.ps[pq][:, :n], lhsT=ones64[:], rhs=tB[s_][:, :n], start=True, stop=True), reads=[cstr, tBr[s_]], writes=[self.psr[pq]])
                    kb.op("act", lambda e: e.activation(out=tA[s_][:, :n], in_=self.ps[pm][:, :n], func=AF.Square), reads=[self.psr[pm]], writes=[tAr[s_]])
                    kb.op("dve", lambda e: e.tensor_tensor(out=tA[s_][:, :n], in0=self.ps[pq][:, :n], in1=tA[s_][:, :n], op=ALU.subtract), reads=[self.psr[pq], tAr[s_]], writes=[tAr[s_]])
                    kb.op("act", lambda e: e.activation(out=tA[s_][:, :n], in_=tA[s_][:, :n], func=AF.Ln, bias=epsg[:, 0:1]), reads=[tAr[s_], cstr], writes=[tAr[s_]])
                    kb.op("act", lambda e: e.activation(out=tA[s_][:, :n], in_=tA[s_][:, :n], func=AF.Exp, scale=-0.5), reads=[tAr[s_]], writes=[tAr[s_]])
                    kb.op("dve", lambda e: e.tensor_tensor(out=YA[:, t0:t0 + n], in0=YA[:, t0:t0 + n], in1=self.ps[pm][:, :n], op=ALU.subtract), reads=[YAr[b], self.psr[pm]], writes=[YAr[b]])
                    kb.op("pool", lambda e: e.tensor_tensor(out=YA[:, t0:t0 + n], in0=YA[:, t0:t0 + n], in1=tA[s_][:, :n], op=ALU.mult), reads=[YAr[b], tAr[s_]], writes=[YAr[b]])
                    kb.op("act", lambda e: e.activation(out=YA[:, t0:t0 + n], in_=YA[:, t0:t0 + n], func=AF.Identity, scale=prm[:, GNG + pr:GNG + pr + 1], bias=prm[:, GNB + pr:GNB + pr + 1]),
                          reads=[YAr[b], prmr], writes=[YAr[b]])
                    kb.op("dve", lambda e: e.tensor_tensor(out=BON[:, t0:t0 + n], in0=BON[:, t0:t0 + n], in1=VTf[:, t0:t0 + n], op=ALU.mult), reads=[BONr[b], VTr[b]], writes=[BONr[b]])
                    kb.op("pool", lambda e: e.tensor_tensor(out=YA[:, t0:t0 + n], in0=YA[:, t0:t0 + n], in1=BON[:, t0:t0 + n], op=ALU.add), reads=[YAr[b], BONr[b]], writes=[YAr[b]])
                    kb.op("dve", lambda e: e.tensor_tensor(out=ZT1[:, t0:t0 + n], in0=YA[:, t0:t0 + n], in1=Gg[:, t0:t0 + n], op=ALU.mult), reads=[YAr[b], Ggr[b]], writes=[ZT1r[b]])
                kb.dma("sp", ds_z, d_zpark[pr], ZT1[:], reads=ZT1r)
                kb.barrier()
        self.unpark_h()
        with ExitStack() as es:
            ZT = kb.sb(es, "rwZT", [128, NC_, SEQ], BF16)
            ZTr = [[Reg() for _ in range(5)] for _ in range(NC_)]
            for c in range(NC_):
                kb.dma("sp", ds_z, ZT[:, c, :], d_zpark[c], writes=ZTr[c])
            self.proj(es, self.d_rwwo, range(NC_), ZT, ZTr, self.resid_evac(j), lblocks, tag="rwwo")
            kb.barrier()
        self.layernorm(L, j, lblocks)

    def _nb(self):
        self.psrot += 1
        return self.psrot % 8

def _prep_common(inp):
    f = np.float32
    out = {}
    aw = np.asarray(inp["ada_w"], f).reshape(DEPTH, NC_, 128, 72, 128)
    out["adaw"] = np.ascontiguousarray(aw.transpose(0, 3, 2, 1, 4)).reshape(DEPTH, 72, 128, NC_ * 128)
    out["adab"] = np.ascontiguousarray(np.asarray(inp["ada_b"], f).reshape(DEPTH, 72, 128).transpose(0, 2, 1))
    out["lng"] = np.ascontiguousarray(np.asarray(inp["ln_g"], f).reshape(DEPTH * 3 * NC_, 128).T)
    out["lnb"] = np.ascontiguousarray(np.asarray(inp["ln_b"], f).reshape(DEPTH * 3 * NC_, 128).T)
    w13 = np.asarray(inp["ffn_w13"], f).reshape(DEPTH, 2, NC_, 128, 2, NF, 128)
    out["w13"] = np.ascontiguousarray(w13.transpose(0, 1, 5, 3, 4, 2, 6)).reshape(DEPTH, 2, NF, 128, 2 * NC_ * 128)
    w2 = np.asarray(inp["ffn_w2"], f).reshape(DEPTH, 2, NF, 128, NC_, 128)
    out["w2"] = np.ascontiguousarray(w2.transpose(0, 1, 4, 3, 2, 5)).reshape(DEPTH, 2, NC_, 128, NF * 128)
    out["ident"] = np.eye(128, dtype=f)
    fm = lambda v: np.asarray(v, f).reshape(-1, 128).T
    wl = lambda W: np.ascontiguousarray(np.asarray(W, f).reshape(W.shape[0] // 128, 128, W.shape[1] // 128, 128)
                                        .transpose(2, 1, 0, 3)).reshape(W.shape[1] // 128, 128, W.shape[0])
    w1 = np.asarray(inp["cv_w1"][0], f).reshape(NC_, 128, 2, NC_, 128)
    out["cvw1"] = np.ascontiguousarray(w1.transpose(3, 1, 2, 0, 4)).reshape(NC_, 128, 2 * NC_ * 128)
    out["cvw2"] = wl(inp["cv_w2"][0])
    wdw = np.asarray(inp["cv_wdw"][0], f).reshape(31, NC_, 128).transpose(2, 0, 1).reshape(128, 31 * NC_)
    out["cvprm"] = np.ascontiguousarray(np.concatenate(
        [fm(inp["cv_b1"][0]), fm(inp["cv_bdw"][0]), fm(inp["cv_ln_g"][0]), fm(inp["cv_ln_b"][0]), fm(inp["cv_b2"][0]), wdw], axis=1))
    out["nawqkv"] = wl(inp["na_wqkv"][0])
    out["nawo"] = wl(inp["na_wo"][0])
    out["nastrip"] = _na_strips(np.asarray(inp["na_rpb"][0], f))
    win = np.asarray(inp["gla_win"][0], f)
    wlq = wl(win)
    out["glawqk"] = np.ascontiguousarray(wlq[0:8])
    m = np.arange(128)
    partner = (m // 64) * 64 + ((m % 64) + 32) % 64
    qk = win[:, 0:1024].reshape(D, 8, 128)[:, :, partner].reshape(D, 1024)
    out["glawqkp"] = wl(qk)
    wv = win[:, 1024:2048].reshape(NC_, 128, 4, 256)
    out["glawv"] = np.ascontiguousarray(wv.transpose(2, 1, 0, 3)).reshape(4, 128, NC_ * 256)
    out["glawg"] = np.ascontiguousarray(wlq[16:24])
    wo = np.asarray(inp["gla_wo"][0], f).reshape(4, 2, 128, NC_, 128)
    out["glawo"] = np.ascontiguousarray(wo.transpose(0, 3, 2, 1, 4)).reshape(4, NC_, 128, 256)
    wa1 = np.asarray(inp["gla_wa1"][0], f)
    wa1c = np.concatenate([wa1[0], wa1[1]], axis=1).reshape(NC_, 128, 32)
    out["glawa1"] = np.ascontiguousarray(wa1c.transpose(1, 0, 2)).reshape(128, NC_ * 32)
    wa2 = np.asarray(inp["gla_wa2"][0], f)
    wa2p = np.zeros((32, 2, 512), f)
    wa2p[0:16, 0] = wa2[0]
    wa2p[16:32, 1] = wa2[1]
    out["glawa2"] = wa2p.reshape(32, 1024)
    i = (m % 64) % 32
    freq = (10000.0 ** (-(i.astype(np.float64)) / 32.0))[:, None]
    t = np.arange(SEQ)[None, :]
    pos = np.where((m < 64)[:, None], t // 64, t % 64)
    ang = (pos.astype(np.float32) * freq.astype(np.float32)).astype(np.float32)
    sgn = np.where((m % 64) < 32, -1.0, 1.0)[:, None]
    out["glarope"] = np.stack([np.cos(ang), sgn * np.sin(ang)]).astype(f)
    sl = np.arange(64)[:, None]
    cc = np.arange(64)[None, :]
    masks = np.zeros((128, 4, 64), f)
    for par in range(2):
        masks[par * 64:(par + 1) * 64, par * 2 + 0] = (cc >= sl)
        masks[par * 64:(par + 1) * 64, par * 2 + 1] = (cc <= sl)
    ba = np.asarray(inp["gla_ba"][0], f).reshape(2, 4, 128).transpose(2, 0, 1).reshape(128, 8)
    out["glacst"] = np.ascontiguousarray(np.concatenate(
        [masks.reshape(128, 256), np.eye(128, dtype=f), fm(inp["gla_norm_g"][0]), ba], axis=1))
    out["rwwrkv"] = np.stack([wl(inp["rw_wrkv"][0][i]) for i in range(3)])
    out["rwwo"] = wl(inp["rw_wo"][0])
    cat2 = lambda a: np.concatenate([np.asarray(a[0], f), np.asarray(a[1], f)], axis=1)
    out["rwlr"] = np.stack([wl(np.asarray(inp["rw_g1"][0], f))[0], wl(cat2(inp["rw_w1"][0]))[0], wl(cat2(inp["rw_a1"][0]))[0]])
    g2 = np.asarray(inp["rw_g2"][0], f).reshape(128, NC_, 128)
    w2 = np.concatenate([np.asarray(inp["rw_w2"][0][0], f), np.asarray(inp["rw_w2"][0][1], f)], axis=0).reshape(128, NC_, 128)
    a2 = np.concatenate([np.asarray(inp["rw_a2"][0][0], f), np.asarray(inp["rw_a2"][0][1], f)], axis=0).reshape(128, NC_, 128)
    out["rwlrw"] = np.ascontiguousarray(np.stack([g2, w2, a2]).transpose(2, 0, 1, 3))
    out["rwprm"] = np.ascontiguousarray(np.concatenate(
        [fm(inp["rw_w0"][0]), fm(inp["rw_a0"][0]), fm(inp["rw_kk"][0]), fm(inp["rw_ka"][0]), fm(inp["rw_rk"][0]),
         fm(inp["rw_gn_g"][0]), fm(inp["rw_gn_b"][0]), fm(inp["rw_mu"][0])], axis=1))
    bd = np.zeros((128, 128), f)
    bd[:64, :64] = 1.0
    bd[64:, 64:] = 1.0
    out["rwcstf"] = np.ascontiguousarray(np.concatenate([np.eye(128, dtype=f), bd], axis=1))
    pi = np.arange(64)[:, None]
    fi = np.arange(64)[None, :]
    gm = np.zeros((64, 2, 5, 2, 64), f)
    for d_ in range(2):
        lt = (fi < pi) if d_ == 0 else (fi > pi)
        st = (pi < fi) if d_ == 0 else (pi > fi)
        se = (pi <= fi) if d_ == 0 else (pi >= fi)
        for h_ in range(2):
            gm[:, d_, 0, h_] = -1.0 * lt
            gm[:, d_, 1, h_] = -1.0 * st
            gm[:, d_, 2, h_] = -1.0 * st
            gm[:, d_, 3, h_] = 1.0 * se
            gm[:, d_, 4, h_] = 1.0 * se
    out["rwgmask"] = np.ascontiguousarray(gm.reshape(64, -1))
    return out


def _na_strips(rpb):
    MASK = np.float32(-30000.0)
    kc = np.arange(64)[:, None]
    qc = np.arange(64)[None, :]
    cstart = np.clip(qc - 8, 0, 48)
    col_ok = (kc >= cstart) & (kc < cstart + 16)
    dc_idx = np.clip(kc - qc + 15, 0, 30)
    R = np.where(col_ok[None, None], rpb[:, :, dc_idx], MASK)
    maskblk = np.full((16, 64, 64), MASK, np.float32)
    strip = np.empty((16, 2, 64, 37, 64), np.float32)
    for half in range(2):
        for jj in range(23):
            d = 11 - jj + half
            strip[:, half, :, jj, :] = R[:, d + 7] if -4 <= d <= 3 else maskblk
        for jj in range(14):
            d = 6 - jj + half
            strip[:, half, :, 23 + jj, :] = R[:, d + 7] if -7 <= d <= 7 else maskblk
    return np.ascontiguousarray(strip.reshape(16, 128, 37 * 64))


def _prep_core(inp, b):
    f = np.float32
    xb = np.concatenate([np.asarray(inp["x"][b], f), np.asarray(inp["ctx"][b], f)], axis=0)
    hin = np.ascontiguousarray(xb.T).reshape(NC_, 128, T)
    cond = np.stack([np.asarray(inp["c"][b], f), np.asarray(inp["c_ctx"], f)], axis=-1)
    cond = np.ascontiguousarray(cond.reshape(NC_, 128, 2).transpose(1, 0, 2))
    return {"hin": hin, "cond": cond}


def build_full():
    p = Prog(list(range(DEPTH)))
    p.setup()
    mixers = [p.na_mixer, p.conv_mixer, p.gla_mixer, p.rwkv_mixer]
    for L in range(DEPTH):
        last = L == DEPTH - 1
        p.ada(L)
        p.ffn(L, 0, 0)
        mixers[L % 4](L, last)
        p.ffn(L, 2, 1, blocks=([0, 1, 2, 3] if last else range(5)))
    p.store()
    p.es_H.close()
    p.kb.es.close()
    return p


def kernel(**inputs):
    common = _prep_common(inputs)
    p = build_full()
    in_maps = []
    for b in range(8):
        m = dict(common)
        m.update(_prep_core(inputs, b))
        in_maps.append(m)
    res = run_bass_kernel_spmd(p.nc, in_maps, core_ids=list(range(8)))
    out = np.stack([np.asarray(r["hout"]).reshape(D, T)[:, :SEQ].T for r in res.results], axis=0)
    return np.ascontiguousarray(out.astype(np.float32))
```

```python
import numpy as np
from contextlib import ExitStack
import concourse.bass as bass
import concourse.mybir as mybir
from concourse.bass_utils import run_bass_kernel_spmd

F32 = mybir.dt.float32
BF16 = mybir.dt.bfloat16
AF = mybir.ActivationFunctionType
ALU = mybir.AluOpType

D = 1024
NC_ = 8
SEQ = 2048
CTX = 256
T = SEQ + CTX
DEPTH = 4
DFF = 2816
NF = DFF // 128
ALPHA = (2.0 * DEPTH) ** 0.25
LN_EPS = 1e-5
EPS_P = LN_EPS / (ALPHA * ALPHA)

BLOCKS = [(0, 512, 0), (512, 512, 0), (1024, 512, 0), (1536, 512, 0), (2048, 256, 1)]
HALVES = [[0, 1], [2, 3, 4]]


class DebugStop(Exception):
    pass


class Reg:
    __slots__ = ("w", "r", "name")

    def __init__(self, name=""):
        self.w = None
        self.r = {}
        self.name = name


class DSem:
    def __init__(self, sem):
        self.sem = sem
        self.cnt = 0


class KB:
    def __init__(self):
        self.nc = bass.Bass("TRN2", target_bir_lowering=False)
        nc = self.nc
        self.es = ExitStack()
        self.eng = {"pe": nc.tensor, "dve": nc.vector, "act": nc.scalar, "pool": nc.gpsimd, "sp": nc.sync}
        self.sem = {e: self.es.enter_context(nc.semaphore("prog_" + e)) for e in self.eng}
        self.cnt = {e: 0 for e in self.eng}
        self.seen = {e: {} for e in self.eng}
        self.dsems = []
        self.n_ins = 0

    def dsem(self, name):
        self.n_ds = getattr(self, "n_ds", 0) + 1
        d = DSem(self.es.enter_context(self.nc.semaphore(f"d_{name}_{self.n_ds}")))
        self.dsems.append(d)
        return d

    def sb(self, es, name, shape, dt):
        self.n_sb = getattr(self, "n_sb", 0) + 1
        return es.enter_context(self.nc.sbuf_tensor(f"{name}_s{self.n_sb}", shape, dt))

    def _wait(self, e, ev):
        if ev is None:
            return
        sem, val = ev[0], ev[1]
        if len(ev) > 2:
            val = max(val, ev[2].cnt)
        if e == "pe" and sem is self.sem["pe"]:
            return
        k = id(sem)
        if self.seen[e].get(k, 0) >= val:
            return
        self.eng[e].wait_ge(sem, val)
        self.seen[e][k] = val

    def _deps(self, e, reads, writes):
        for r in reads:
            self._wait(e, r.w)
        for r in writes:
            self._wait(e, r.w)
            for ev in r.r.values():
                self._wait(e, ev)

    def op(self, e, fn, reads=(), writes=()):
        self._deps(e, reads, writes)
        ins = fn(self.eng[e])
        self.cnt[e] += 1
        self.n_ins += 1
        ins.then_inc(self.sem[e], 1)
        ev = (self.sem[e], self.cnt[e])
        for r in reads:
            r.r[id(ev[0])] = ev
        for r in writes:
            r.w = ev
            r.r = {}
        return ev

    def dma(self, q, ds, out, in_, reads=(), writes=()):
        self._deps(q, reads, writes)
        if ds.cnt > 0:
            self._wait(q, (ds.sem, ds.cnt, ds))
        ins = self.eng[q].dma_start(out=out, in_=in_)
        ds.cnt += 16
        self.n_ins += 1
        ins.then_inc(ds.sem, 16)
        ev = (ds.sem, ds.cnt, ds)
        for r in reads:
            r.r[id(ev[0])] = ev
        for r in writes:
            r.w = ev
            r.r = {}
        return ev

    def barrier(self):
        for e in self.eng:
            for e2 in self.eng:
                if e2 != e and self.cnt[e2] > 0:
                    self._wait(e, (self.sem[e2], self.cnt[e2]))
            for d in self.dsems:
                if d.cnt > 0:
                    self._wait(e, (d.sem, d.cnt))


def midx(j, k, c):
    return j * 24 + k * 8 + c


class Prog:
    def __init__(self, layers, final_layer_is_last=True, load_h=True):
        self.kb = KB()
        kb = self.kb
        nc = kb.nc
        self.nc = nc
        es = kb.es
        self.layers = layers
        dr = lambda name, shape, dt=F32, kind="ExternalInput": nc.dram_tensor(name, shape, dt, kind=kind).ap()
        self.d_hin = dr("hin", [NC_, 128, T])
        self.d_hout = dr("hout", [NC_, 128, T], kind="ExternalOutput")
        self.d_cond = dr("cond", [128, NC_, 2])
        self.d_adaw = dr("adaw", [DEPTH, 72, 128, NC_ * 128])
        self.d_adab = dr("adab", [DEPTH, 128, 72])
        self.d_lng = dr("lng", [128, DEPTH * 3 * NC_])
        self.d_lnb = dr("lnb", [128, DEPTH * 3 * NC_])
        self.d_w13 = dr("w13", [DEPTH, 2, NF, 128, 2 * NC_ * 128])
        self.d_w2 = dr("w2", [DEPTH, 2, NC_, 128, NF * 128])
        self.d_ident = dr("ident", [128, 128])
        self.d_rwprm = dr("rwprm", [128, 120])
        self.d_rwcstf = dr("rwcstf", [128, 256])
        self.d_rwgmask = dr("rwgmask", [64, 2 * 5 * 2 * 64])
        self.d_rwlr = dr("rwlr", [3, 128, NC_ * 128])
        self.d_rwlrw = dr("rwlrw", [NC_, 3, 128, 128])
        self.d_rwwrkv = dr("rwwrkv", [3, NC_, 128, NC_ * 128])
        self.d_rwwo = dr("rwwo", [NC_, 128, NC_ * 128])
        self.d_glawqk = dr("glawqk", [8, 128, NC_ * 128])
        self.d_glawqkp = dr("glawqkp", [8, 128, NC_ * 128])
        self.d_glawv = dr("glawv", [4, 128, NC_ * 256])
        self.d_glawg = dr("glawg", [8, 128, NC_ * 128])
        self.d_glawo = dr("glawo", [4, NC_, 128, 2 * 128])
        self.d_glawa1 = dr("glawa1", [128, NC_ * 32])
        self.d_glawa2 = dr("glawa2", [32, 2 * 512])
        self.d_glarope = dr("glarope", [2, 128, SEQ])
        self.d_glacst = dr("glacst", [128, 4 * 64 + 128 + 2 + 8])
        self.d_nawqkv = dr("nawqkv", [3 * NC_, 128, NC_ * 128])
        self.d_nawo = dr("nawo", [NC_, 128, NC_ * 128])
        self.d_nastrip = dr("nastrip", [16, 128, 37 * 64])
        self.d_cvw1 = dr("cvw1", [NC_, 128, 2 * NC_ * 128])
        self.d_cvw2 = dr("cvw2", [NC_, 128, NC_ * 128])
        self.d_cvprm = dr("cvprm", [128, 6 * NC_ + 31 * NC_])
        self.U = kb.sb(es, "U", [128, NC_, T], BF16)
        self.Hr = [[Reg(f"H{c}_{b}") for b in range(5)] for c in range(NC_)]
        self.Ur = [[Reg(f"U{c}_{b}") for b in range(5)] for c in range(NC_)]
        self.P = kb.sb(es, "P", [128, 72, 2], F32)
        self.Pr = Reg("P")
        self.condT = kb.sb(es, "condT", [128, NC_, 2], F32)
        self.condr = Reg("cond")
        self.lng = kb.sb(es, "lng", [128, DEPTH * 3 * NC_], F32)
        self.lnb = kb.sb(es, "lnb", [128, DEPTH * 3 * NC_], F32)
        self.lnr = Reg("ln")
        self.ones_bf = kb.sb(es, "ones_bf", [128, 128], BF16)
        self.onesr = Reg("ones")
        self.epsP = kb.sb(es, "epsP", [128, 1], F32)
        self.onesf = kb.sb(es, "onesf", [128, 64], F32)
        self.ps = [es.enter_context(nc.psum_tensor(f"ps{i}", [128, 512], F32)) for i in range(8)]
        self.psr = [Reg(f"ps{i}") for i in range(8)]
        self.ds_misc = kb.dsem("misc")
        self.ds_out = kb.dsem("out")
        self.ds_h = [kb.dsem(f"h{i}") for i in range(4)]
        self.psrot = 0
        self.es_H = ExitStack()
        self.H = kb.sb(self.es_H, "H", [128, NC_, T], F32)

    def hregs(self, c, t0, n):
        return [self.Hr[c][b] for b, (bt, bn, _) in enumerate(BLOCKS) if bt < t0 + n and t0 < bt + bn]

    def uregs(self, c, t0, n):
        return [self.Ur[c][b] for b, (bt, bn, _) in enumerate(BLOCKS) if bt < t0 + n and t0 < bt + bn]

    def setup(self):
        kb = self.kb
        kb.dma("sp", self.ds_misc, self.condT[:], self.d_cond, writes=[self.condr])
        kb.dma("sp", self.ds_misc, self.lng[:], self.d_lng, writes=[self.lnr])
        kb.dma("sp", self.ds_misc, self.lnb[:], self.d_lnb, writes=[self.lnr])
        for c in range(NC_):
            kb.dma("sp", self.ds_h[c % 4], self.H[:, c, :], self.d_hin[c], writes=self.Hr[c])
        kb.op("dve", lambda e: e.memset(self.ones_bf[:], 1.0 / 1024.0), writes=[self.onesr])
        kb.op("dve", lambda e: e.memset(self.epsP[:], EPS_P), writes=[self.onesr])
        kb.op("dve", lambda e: e.memset(self.onesf[:], 1.0), writes=[self.onesr])
        kb.op("act", lambda e: e.activation(out=self.condT[:], in_=self.condT[:], func=AF.Silu),
              reads=[self.condr], writes=[self.condr])

    def store(self):
        kb = self.kb
        for c in range(NC_):
            kb.dma("sp", self.ds_h[c % 4], self.d_hout[c], self.H[:, c, :], reads=self.Hr[c])
        for d_ in self.ds_h:
            kb.eng["sp"].wait_ge(d_.sem, d_.cnt)

    def ada(self, L):
        kb = self.kb
        kb.barrier()
        with ExitStack() as es:
            NG = 4
            wst = [kb.sb(es, f"adaw{s}", [128, NG, NC_ * 128], BF16) for s in range(2)]
            condb = kb.sb(es, "condb", [128, NC_, 2], BF16)
            kb.op("dve", lambda e: e.tensor_copy(out=condb[:], in_=self.condT[:]), reads=[self.condr], writes=[self.condr])
            wr = [Reg(), Reg()]
            wds = [kb.dsem(f"adaw{L}_{s}") for s in range(2)]
            bia = kb.sb(es, "adab", [128, 72], F32)
            br = Reg()
            kb.dma("sp", self.ds_misc, bia[:], self.d_adab[L], writes=[br])
            pst = self.ps[0]
            psreg = self.psr[0]
            for g in range(72 // NG):
                s = g % 2
                kb.dma("pool", wds[s], wst[s][:], self.d_adaw[L, g * NG:(g + 1) * NG].rearrange("o p f -> p o f"),
                       writes=[wr[s]])
                for o in range(NG):
                    oc = g * NG + o
                    for kc in range(NC_):
                        kb.op("pe", lambda e: e.matmul(pst[:, oc * 2:oc * 2 + 2], lhsT=wst[s][:, o, kc * 128:(kc + 1) * 128],
                                                        rhs=condb[:, kc, :], start=(kc == 0), stop=(kc == NC_ - 1)),
                              reads=[wr[s], self.condr], writes=[psreg])
            for kind in range(2):
                kb.op("dve", lambda e: e.tensor_tensor(out=self.P[:, :, kind], in0=pst[:, 0:144].rearrange("p (o k) -> p o k", k=2)[:, :, kind],
                                                       in1=bia[:], op=ALU.add),
                      reads=[psreg, br], writes=[self.Pr])
            for j in range(3):
                wj = (1.0 if j == 1 else 0.5) / ALPHA
                kb.op("dve", lambda e: e.tensor_scalar(out=self.P[:, midx(j, 1, 0):midx(j, 1, 0) + 8, :], in0=self.P[:, midx(j, 1, 0):midx(j, 1, 0) + 8, :],
                                                       scalar1=1.0, scalar2=None, op0=ALU.add),
                      reads=[self.Pr], writes=[self.Pr])
                kb.op("dve", lambda e: e.tensor_scalar(out=self.P[:, midx(j, 2, 0):midx(j, 2, 0) + 8, :], in0=self.P[:, midx(j, 2, 0):midx(j, 2, 0) + 8, :],
                                                       scalar1=wj, scalar2=None, op0=ALU.mult),
                      reads=[self.Pr], writes=[self.Pr])
            kb.barrier()

    def modulate(self, j, blocks=range(5)):
        kb = self.kb
        for c in range(NC_):
            for b in blocks:
                t0, n, kind = BLOCKS[b]
                kb.op("act", lambda e: e.activation(out=self.U[:, c, t0:t0 + n], in_=self.H[:, c, t0:t0 + n], func=AF.Identity,
                                                    scale=self.P[:, midx(j, 1, c), kind:kind + 1],
                                                    bias=self.P[:, midx(j, 0, c), kind:kind + 1]),
                      reads=[self.Hr[c][b], self.Pr], writes=[self.Ur[c][b]])

    def ffn(self, L, j, kidx, blocks=range(5)):
        kb = self.kb
        blocks = list(blocks)
        self.modulate(j, blocks)
        with ExitStack() as es:
            G = kb.sb(es, "G", [128, NF, 1280], BF16)
            wA = [kb.sb(es, f"wA{s}", [128, 2, NC_, 128], BF16) for s in range(2)]
            wAr = [Reg(), Reg()]
            wAd = [kb.dsem(f"wA{L}{j}{s}") for s in range(2)]
            wB = [kb.sb(es, f"wB{s}", [128, NF, 128], BF16) for s in range(2)]
            wBr = [Reg(), Reg()]
            wBd = [kb.dsem(f"wB{L}{j}{s}") for s in range(2)]
            sa = [kb.sb(es, f"sa{s}", [128, 512], BF16) for s in range(2)]
            sar = [Reg(), Reg()]
            issuedA, issuedB = set(), set()

            def loadA(hi_, jf_):
                if (hi_, jf_) in issuedA:
                    return
                issuedA.add((hi_, jf_))
                s_ = jf_ % 2
                kb.dma("pool", wAd[s_], wA[s_][:].rearrange("p s k m -> p (s k m)"), self.d_w13[L, kidx, jf_], writes=[wAr[s_]])

            def loadB(hi_, dc_):
                if (hi_, dc_) in issuedB:
                    return
                issuedB.add((hi_, dc_))
                s_ = dc_ % 2
                kb.dma("pool", wBd[s_], wB[s_][:].rearrange("p f m -> p (f m)"), self.d_w2[L, kidx, dc_], writes=[wBr[s_]])

            live = [hi_ for hi_, half_ in enumerate(HALVES) if any(b in blocks for b in half_)]
            for hi, half in enumerate(HALVES):
                blks = [b for b in half if b in blocks]
                if not blks:
                    continue
                nxt = [h_ for h_ in live if h_ > hi]
                base = BLOCKS[blks[0]][0]
                Gr = [[Reg() for _ in blks] for _ in range(NF)]
                cntA = 0
                for jf in range(NF):
                    s = jf % 2
                    loadA(hi, jf)
                    if jf == 1:
                        loadB(hi, 0)
                        loadB(hi, 1)
                    for bi, b in enumerate(blks):
                        t0, n, kind = BLOCKS[b]
                        pa = (cntA % 4) * 2
                        pu = pa + 1
                        ss = cntA % 2
                        cntA += 1
                        for (pb, si) in ((pa, 0), (pu, 1)):
                            for kc in range(NC_):
                                kb.op("pe", lambda e: e.matmul(self.ps[pb][:, :n], lhsT=wA[s][:, si, kc, :], rhs=self.U[:, kc, t0:t0 + n],
                                                                start=(kc == 0), stop=(kc == NC_ - 1)),
                                      reads=[wAr[s], self.Ur[kc][b]], writes=[self.psr[pb]])
                        kb.op("act", lambda e: e.activation(out=sa[ss][:, :n], in_=self.ps[pa][:, :n], func=AF.Silu),
                              reads=[self.psr[pa]], writes=[sar[ss]])
                        kb.op("dve", lambda e: e.tensor_tensor(out=G[:, jf, t0 - base:t0 - base + n], in0=sa[ss][:, :n], in1=self.ps[pu][:, :n], op=ALU.mult),
                              reads=[sar[ss], self.psr[pu]], writes=[Gr[jf][bi]])
                cntB = 0
                for dc in range(NC_):
                    s = dc % 2
                    loadB(hi, dc)
                    if dc == 1 and nxt:
                        loadA(nxt[0], 0)
                        loadA(nxt[0], 1)
                    for bi, b in enumerate(blks):
                        t0, n, kind = BLOCKS[b]
                        pb = cntB % 8
                        cntB += 1
                        for fc in range(NF):
                            kb.op("pe", lambda e: e.matmul(self.ps[pb][:, :n], lhsT=wB[s][:, fc, :], rhs=G[:, fc, t0 - base:t0 - base + n],
                                                            start=(fc == 0), stop=(fc == NF - 1)),
                                  reads=[wBr[s], Gr[fc][bi]], writes=[self.psr[pb]])
                        kb.op("dve", lambda e: e.scalar_tensor_tensor(out=self.H[:, dc, t0:t0 + n], in0=self.ps[pb][:, :n],
                                                                      scalar=self.P[:, midx(j, 2, dc), kind:kind + 1],
                                                                      in1=self.H[:, dc, t0:t0 + n], op0=ALU.mult, op1=ALU.add),
                              reads=[self.psr[pb], self.Pr, self.Hr[dc][b]], writes=[self.Hr[dc][b]])
                kb.barrier()
        self.layernorm(L, j, blocks)

    def layernorm(self, L, j, blocks=range(5)):
        kb = self.kb
        goff = (L * 3 + j) * NC_
        with ExitStack() as es:
            zb = [kb.sb(es, f"zb{s}", [128, NC_, 512], BF16) for s in range(2)]
            z2 = [kb.sb(es, f"z2{s}", [128, NC_, 512], BF16) for s in range(2)]
            zr = [[Reg() for _ in range(NC_)] for _ in range(2)]
            z2r = [[Reg() for _ in range(NC_)] for _ in range(2)]
            msq = [kb.sb(es, f"msq{s}", [128, 512], F32) for s in range(2)]
            rstd = [kb.sb(es, f"rstd{s}", [128, 512], F32) for s in range(2)]
            tmp = [kb.sb(es, f"lntmp{s}", [128, 512], F32) for s in range(4)]
            msqr = [Reg(), Reg()]
            rstdr = [Reg(), Reg()]
            tmpr = [Reg() for _ in range(4)]
            tcnt = 0
            for bi, b in enumerate(blocks):
                t0, n, kind = BLOCKS[b]
                s = bi % 2
                pm = (bi % 4) * 2
                pq = pm + 1
                for c in range(NC_):
                    kb.op("dve", lambda e: e.tensor_copy(out=zb[s][:, c, :n], in_=self.H[:, c, t0:t0 + n]),
                          reads=[self.Hr[c][b]], writes=[zr[s][c]])
                    kb.op("act", lambda e: e.activation(out=z2[s][:, c, :n], in_=self.H[:, c, t0:t0 + n], func=AF.Square),
                          reads=[self.Hr[c][b]], writes=[z2r[s][c]])
                for c in range(NC_):
                    kb.op("pe", lambda e: e.matmul(self.ps[pm][:, :n], lhsT=self.ones_bf[:], rhs=zb[s][:, c, :n],
                                                    start=(c == 0), stop=(c == NC_ - 1)),
                          reads=[self.onesr, zr[s][c]], writes=[self.psr[pm]])
                for c in range(NC_):
                    kb.op("pe", lambda e: e.matmul(self.ps[pq][:, :n], lhsT=self.ones_bf[:], rhs=z2[s][:, c, :n],
                                                    start=(c == 0), stop=(c == NC_ - 1)),
                          reads=[self.onesr, z2r[s][c]], writes=[self.psr[pq]])
                kb.op("act", lambda e: e.activation(out=msq[s][:, :n], in_=self.ps[pm][:, :n], func=AF.Square),
                      reads=[self.psr[pm]], writes=[msqr[s]])
                kb.op("dve", lambda e: e.tensor_tensor(out=msq[s][:, :n], in0=self.ps[pq][:, :n], in1=msq[s][:, :n], op=ALU.subtract),
                      reads=[self.psr[pq], msqr[s]], writes=[msqr[s]])
                kb.op("act", lambda e: e.activation(out=msq[s][:, :n], in_=msq[s][:, :n], func=AF.Ln, bias=self.epsP[:, 0:1]),
                      reads=[msqr[s], self.onesr], writes=[msqr[s]])
                kb.op("act", lambda e: e.activation(out=rstd[s][:, :n], in_=msq[s][:, :n], func=AF.Exp, scale=-0.5),
                      reads=[msqr[s]], writes=[rstdr[s]])
                for c in range(NC_):
                    ts = tcnt % 4
                    tcnt += 1
                    kb.op("dve", lambda e: e.tensor_tensor(out=tmp[ts][:, :n], in0=self.H[:, c, t0:t0 + n], in1=self.ps[pm][:, :n], op=ALU.subtract),
                          reads=[self.Hr[c][b], self.psr[pm]], writes=[tmpr[ts]])
                    kb.op("pool", lambda e: e.tensor_tensor(out=tmp[ts][:, :n], in0=tmp[ts][:, :n], in1=rstd[s][:, :n], op=ALU.mult),
                          reads=[tmpr[ts], rstdr[s]], writes=[tmpr[ts]])
                    kb.op("act", lambda e: e.activation(out=self.H[:, c, t0:t0 + n], in_=tmp[ts][:, :n], func=AF.Identity,
                                                        scale=self.lng[:, goff + c:goff + c + 1], bias=self.lnb[:, goff + c:goff + c + 1]),
                          reads=[tmpr[ts], self.lnr], writes=[self.Hr[c][b]])
            kb.barrier()


    def proj(self, es, Wd, ocs, src, src_regs, evac, blocks, nk=NC_, tag="pj", src_off=0):
        kb = self.kb
        w = [kb.sb(es, f"{tag}w{s}", [128, nk, 128], BF16) for s in range(2)]
        wr = [Reg(), Reg()]
        wd = [kb.dsem(f"{tag}{s}") for s in range(2)]
        cnt = 0
        for i, oc in enumerate(ocs):
            s = i % 2
            kb.dma("pool", wd[s], w[s][:].rearrange("p k m -> p (k m)"), Wd[oc], writes=[wr[s]])
            for b in blocks:
                t0, n, kind = BLOCKS[b]
                pb = self.psrot % 8
                self.psrot += 1
                for kc in range(nk):
                    kb.op("pe", lambda e: e.matmul(self.ps[pb][:, :n], lhsT=w[s][:, kc, :], rhs=src[:, kc, src_off + t0:src_off + t0 + n],
                                                    start=(kc == 0), stop=(kc == nk - 1)),
                          reads=[wr[s], src_regs[kc][b]], writes=[self.psr[pb]])
                evac(oc, b, pb, t0, n, kind)

    def resid_evac(self, j, bias=None, bias_reg=None, es=None):
        kb = self.kb
        if bias is not None:
            tmpy = [kb.sb(es, f"tmpy{s}", [128, 512], F32) for s in range(2)]
            tmpr = [Reg(), Reg()]
        state = {"c": 0}

        def evac(dc, b, pb, t0, n, kind):
            if bias is not None:
                s = state["c"] % 2
                state["c"] += 1
                kb.op("act", lambda e: e.activation(out=tmpy[s][:, :n], in_=self.ps[pb][:, :n], func=AF.Identity, bias=bias[:, dc:dc + 1]),
                      reads=[self.psr[pb], bias_reg], writes=[tmpr[s]])
                src, sreg = tmpy[s], tmpr[s]
            else:
                src, sreg = self.ps[pb], self.psr[pb]
            kb.op("dve", lambda e: e.scalar_tensor_tensor(out=self.H[:, dc, t0:t0 + n], in0=src[:, :n],
                                                          scalar=self.P[:, midx(j, 2, dc), kind:kind + 1],
                                                          in1=self.H[:, dc, t0:t0 + n], op0=ALU.mult, op1=ALU.add),
                  reads=[sreg, self.Pr, self.Hr[dc][b]], writes=[self.Hr[dc][b]])
        return evac

    def feat_stats(self, es_unused, src_fn, b, n, s, zsq, zsqr, msq, msqr, rstd, rstdr, eps_ap, nchunks=NC_, ones=None):
        kb = self.kb
        ones = self.ones_bf if ones is None else ones
        pm = self.psrot % 8
        pq = (self.psrot + 1) % 8
        self.psrot += 2
        for c in range(nchunks):
            ap, rg = src_fn(c)
            kb.op("act", lambda e: e.activation(out=zsq[s][:, c, :n], in_=ap, func=AF.Square),
                  reads=[rg], writes=[zsqr[s][c]])
        for c in range(nchunks):
            ap, rg = src_fn(c)
            kb.op("pe", lambda e: e.matmul(self.ps[pm][:, :n], lhsT=ones[:], rhs=ap, start=(c == 0), stop=(c == nchunks - 1)),
                  reads=[self.onesr, rg], writes=[self.psr[pm]])
        for c in range(nchunks):
            kb.op("pe", lambda e: e.matmul(self.ps[pq][:, :n], lhsT=ones[:], rhs=zsq[s][:, c, :n], start=(c == 0), stop=(c == nchunks - 1)),
                  reads=[self.onesr, zsqr[s][c]], writes=[self.psr[pq]])
        kb.op("act", lambda e: e.activation(out=msq[s][:, :n], in_=self.ps[pm][:, :n], func=AF.Square),
              reads=[self.psr[pm]], writes=[msqr[s]])
        kb.op("dve", lambda e: e.tensor_tensor(out=msq[s][:, :n], in0=self.ps[pq][:, :n], in1=msq[s][:, :n], op=ALU.subtract),
              reads=[self.psr[pq], msqr[s]], writes=[msqr[s]])
        kb.op("act", lambda e: e.activation(out=msq[s][:, :n], in_=msq[s][:, :n], func=AF.Ln, bias=eps_ap),
              reads=[msqr[s], self.onesr], writes=[msqr[s]])
        kb.op("act", lambda e: e.activation(out=rstd[s][:, :n], in_=msq[s][:, :n], func=AF.Exp, scale=-0.5),
              reads=[msqr[s]], writes=[rstdr[s]])
        return pm

    def conv_mixer(self, L, last):
        kb = self.kb
        j = 1
        blocks = [0, 1, 2, 3] if last else [0, 1, 2, 3, 4]
        self.modulate(j, blocks)
        PADL = 15
        OFFX = PADL
        OFFC = PADL + SEQ + 2 * PADL
        VW = OFFC + CTX + PADL
        voff = lambda b: (OFFX if BLOCKS[b][2] == 0 else OFFC - SEQ)
        with ExitStack() as es:
            V = kb.sb(es, "cvV", [128, NC_, VW], BF16)
            Vr = [[Reg() for _ in range(5)] for _ in range(NC_)]
            prm = kb.sb(es, "cvprm", [128, 6 * NC_ + 31 * NC_], F32)
            prmr = Reg()
            ident = kb.sb(es, "ident", [128, 128], F32)
            idr = Reg()
            kb.dma("sp", self.ds_misc, prm[:], self.d_cvprm, writes=[prmr])
            kb.dma("sp", self.ds_misc, ident[:], self.d_ident, writes=[idr])
            for c in range(NC_):
                kb.op("pool", lambda e: e.memset(V[:, c, :], 0.0), writes=Vr[c])
            B1A, B1G, BDW, LNG, LNB, B2, WDW = 0, 8, 16, 24, 32, 40, 48
            sg = [kb.sb(es, f"cvsg{s}", [128, 512], F32) for s in range(2)]
            sgr = [Reg(), Reg()]
            with ExitStack() as es1:
                wA = [kb.sb(es1, f"cvw1{s}", [128, 2, NC_, 128], BF16) for s in range(2)]
                wAr = [Reg(), Reg()]
                wAd = [kb.dsem(f"cvw1{s}") for s in range(2)]
                cnt = 0
                for c in range(NC_):
                    s = c % 2
                    kb.dma("pool", wAd[s], wA[s][:].rearrange("p s k m -> p (s k m)"), self.d_cvw1[c], writes=[wAr[s]])
                    for b in blocks:
                        t0, n, kind = BLOCKS[b]
                        pa = (cnt % 4) * 2
                        pg = pa + 1
                        ss = cnt % 2
                        cnt += 1
                        for (pb, si) in ((pa, 0), (pg, 1)):
                            for kc in range(NC_):
                                kb.op("pe", lambda e: e.matmul(self.ps[pb][:, :n], lhsT=wA[s][:, si, kc, :], rhs=self.U[:, kc, t0:t0 + n],
                                                                start=(kc == 0), stop=(kc == NC_ - 1)),
                                      reads=[wAr[s], self.Ur[kc][b]], writes=[self.psr[pb]])
                        kb.op("act", lambda e: e.activation(out=sg[ss][:, :n], in_=self.ps[pg][:, :n], func=AF.Sigmoid, bias=prm[:, B1G + c:B1G + c + 1]),
                              reads=[self.psr[pg], prmr], writes=[sgr[ss]])
                        kb.op("dve", lambda e: e.scalar_tensor_tensor(out=V[:, c, voff(b) + t0:voff(b) + t0 + n], in0=self.ps[pa][:, :n],
                                                                      scalar=prm[:, B1A + c:B1A + c + 1], in1=sg[ss][:, :n],
                                                                      op0=ALU.add, op1=ALU.mult),
                              reads=[self.psr[pa], prmr, sgr[ss]], writes=[Vr[c][b]])
                kb.barrier()
            with ExitStack() as es2:
                Dg = [kb.sb(es2, f"cvDg{s}", [128, 31, 128], BF16) for s in range(2)]
                Dgr = [Reg(), Reg()]
                for c in range(NC_):
                    s = c % 2
                    for k in range(31):
                        kb.op("dve" if k % 2 == 0 else "pool",
                              lambda e: e.tensor_scalar(out=Dg[s][:, k, :], in0=ident[:], scalar1=prm[:, WDW + k * NC_ + c:WDW + k * NC_ + c + 1],
                                                        scalar2=None, op0=ALU.mult),
                              reads=[idr, prmr], writes=[Dgr[s]])
                    for b in blocks:
                        t0, n, kind = BLOCKS[b]
                        pb = self.psrot % 8
                        self.psrot += 1
                        vregs = [Vr[c][bb] for bb in blocks if BLOCKS[bb][2] == kind]
                        for k in range(31):
                            col = voff(b) + t0 + k - 15
                            kb.op("pe", lambda e: e.matmul(self.ps[pb][:, :n], lhsT=Dg[s][:, k, :], rhs=V[:, c, col:col + n],
                                                            start=(k == 0), stop=(k == 30)),
                                  reads=[Dgr[s]] + vregs, writes=[self.psr[pb]])
                        kb.op("act", lambda e: e.activation(out=self.U[:, c, t0:t0 + n], in_=self.ps[pb][:, :n], func=AF.Identity,
                                                            bias=prm[:, BDW + c:BDW + c + 1]),
                              reads=[self.psr[pb], prmr], writes=[self.Ur[c][b]])
                kb.barrier()
            zsq = [kb.sb(es, f"cvz2{s}", [128, NC_, 512], BF16) for s in range(2)]
            zsqr = [[Reg() for _ in range(NC_)] for _ in range(2)]
            msq = [kb.sb(es, f"cvmsq{s}", [128, 512], F32) for s in range(2)]
            rstd = [kb.sb(es, f"cvrstd{s}", [128, 512], F32) for s in range(2)]
            tmp = [kb.sb(es, f"cvtmp{s}", [128, 512], F32) for s in range(4)]
            msqr = [Reg(), Reg()]
            rstdr = [Reg(), Reg()]
            tmpr = [Reg() for _ in range(4)]
            epsl = kb.sb(es, "cveps", [128, 1], F32)
            kb.op("dve", lambda e: e.memset(epsl[:], LN_EPS), writes=[self.onesr])
            tc = 0
            for bi, b in enumerate(blocks):
                t0, n, kind = BLOCKS[b]
                s = bi % 2
                pm = self.feat_stats(None, lambda c: (self.U[:, c, t0:t0 + n], self.Ur[c][b]), b, n, s, zsq, zsqr, msq, msqr, rstd, rstdr, epsl[:, 0:1])
                for c in range(NC_):
                    ts = tc % 4
                    tc += 1
                    kb.op("dve", lambda e: e.tensor_tensor(out=tmp[ts][:, :n], in0=self.U[:, c, t0:t0 + n], in1=self.ps[pm][:, :n], op=ALU.subtract),
                          reads=[self.Ur[c][b], self.psr[pm]], writes=[tmpr[ts]])
                    kb.op("pool", lambda e: e.tensor_tensor(out=tmp[ts][:, :n], in0=tmp[ts][:, :n], in1=rstd[s][:, :n], op=ALU.mult),
                          reads=[tmpr[ts], rstdr[s]], writes=[tmpr[ts]])
                    kb.op("act", lambda e: e.activation(out=self.U[:, c, t0:t0 + n], in_=tmp[ts][:, :n], func=AF.Silu,
                                                        scale=prm[:, LNG + c:LNG + c + 1], bias=prm[:, LNB + c:LNB + c + 1]),
                          reads=[tmpr[ts], prmr], writes=[self.Ur[c][b]])
            b2t = prm[:, B2:B2 + NC_]
            self.proj(es, self.d_cvw2, range(NC_), self.U, self.Ur, self.resid_evac(j, bias=b2t, bias_reg=prmr, es=es), blocks, tag="cvw2")
            kb.barrier()
        self.layernorm(L, j, blocks)


    def na_mixer(self, L, last):
        kb = self.kb
        j = 1
        blocks = [0, 1, 2, 3, 4]
        self.modulate(j, blocks)
        NB_I, NB_F = 23, 14
        FB = NB_I
        SW = (NB_I + NB_F) * 64
        qtiles = []
        qtiles.append((0, 256, [(128 * i, (FB + 6 - 2 * i) * 64) for i in range(4)]))
        for q0r in (4, 12, 20):
            qtiles.append((64 * q0r, 512, [(64 * (q0r - 4) + 128 * i, (15 - 2 * i) * 64) for i in range(8)]))
        qtiles.append((1792, 256, [(1536 + 128 * i, (FB + 10 - 2 * i) * 64) for i in range(4)]))
        qtiles.append((2048, 256, []))
        ctx_keys = [2048, 2176]
        bregs = lambda regs, t0, n: [regs[b] for b, (bt, bn, _) in enumerate(BLOCKS) if bt < t0 + n and t0 < bt + bn]
        with ExitStack() as es:
            ZT = kb.sb(es, "naZT", [128, NC_, T], BF16)
            ZTr = [[Reg() for _ in range(5)] for _ in range(NC_)]
            ones1 = kb.sb(es, "naones", [128, 128], BF16)
            o1r = Reg()
            kb.op("dve", lambda e: e.memset(ones1[:], 1.0), writes=[o1r])
            with ExitStack() as es1:
                QT = [kb.sb(es1, f"naQ{s}", [128, T], BF16) for s in range(2)]
                KT = [kb.sb(es1, f"naK{s}", [128, T], BF16) for s in range(2)]
                VT = [kb.sb(es1, f"naV{s}", [128, 18, 128], BF16) for s in range(2)]
                QTr = [[Reg() for _ in range(5)] for _ in range(2)]
                KTr = [[Reg() for _ in range(5)] for _ in range(2)]
                VTr = [Reg(), Reg()]
                strip = [kb.sb(es1, f"nastrip{s}", [128, SW], BF16) for s in range(2)]
                stripr = [Reg(), Reg()]
                stripd = [kb.dsem(f"nastrip{s}") for s in range(2)]
                tmp = [kb.sb(es1, f"natmp{s}", [128, 512], F32) for s in range(3)]
                tmpr = [Reg() for _ in range(3)]
                PT = [kb.sb(es1, f"naPT{s}", [128, 512], BF16) for s in range(3)]
                PTr = [Reg() for _ in range(3)]
                rc = [kb.sb(es1, f"narc{s}", [128, 512], F32) for s in range(2)]
                rcr = [Reg(), Reg()]
                wv = [kb.sb(es1, f"nawv{s}", [128, NC_, 128], BF16) for s in range(2)]
                wvr = [Reg(), Reg()]
                wvd = [kb.dsem(f"nawv{s}") for s in range(2)]
                pw = [kb.sb(es1, f"napw{s}", [128, NC_, 128], BF16) for s in range(2)]
                pwr = [Reg(), Reg()]
                pwd = [kb.dsem(f"napw{s}") for s in range(2)]
                cnt = {"s": 0, "t": 0, "p": 0, "o": 0, "w": 0}
                for ch in range(NC_):
                    sl = ch % 2
                    for (dst, dstr, oc) in ((QT[sl], QTr[sl], ch), (KT[sl], KTr[sl], NC_ + ch)):
                        ws = cnt["w"] % 2
                        cnt["w"] += 1
                        kb.dma("pool", pwd[ws], pw[ws][:].rearrange("p k m -> p (k m)"), self.d_nawqkv[oc], writes=[pwr[ws]])
                        for b in blocks:
                            t0, n, kind = BLOCKS[b]
                            pb = 4 + (self.psrot % 4)
                            self.psrot += 1
                            for kc in range(NC_):
                                kb.op("pe", lambda e: e.matmul(self.ps[pb][:, :n], lhsT=pw[ws][:, kc, :], rhs=self.U[:, kc, t0:t0 + n],
                                                                start=(kc == 0), stop=(kc == NC_ - 1)),
                                      reads=[pwr[ws], self.Ur[kc][b]], writes=[self.psr[pb]])
                            kb.op("act", lambda e: e.activation(out=dst[:, t0:t0 + n], in_=self.ps[pb][:, :n], func=AF.Copy),
                                  reads=[self.psr[pb]], writes=[dstr[b]])
                    kb.dma("pool", wvd[sl], wv[sl][:].rearrange("p k m -> p (k m)"), self.d_nawqkv[2 * NC_ + ch], writes=[wvr[sl]])
                    for g in range(5):
                        tiles = list(range(4 * g, min(18, 4 * g + 4)))
                        pb = 4 + (self.psrot % 4)
                        self.psrot += 1
                        for ti, tt in enumerate(tiles):
                            b = min(tt // 4, 4)
                            for kc in range(NC_):
                                kb.op("pe", lambda e: e.matmul(self.ps[pb][:, ti * 128:(ti + 1) * 128], lhsT=self.U[:, kc, tt * 128:(tt + 1) * 128],
                                                                rhs=wv[sl][:, kc, :], start=(kc == 0), stop=(kc == NC_ - 1)),
                                      reads=[wvr[sl], self.Ur[kc][b]], writes=[self.psr[pb]])
                        nt = len(tiles)
                        kb.op("dve", lambda e: e.tensor_copy(out=VT[sl][:, tiles[0]:tiles[0] + nt, :].rearrange("p t m -> p (t m)"),
                                                             in_=self.ps[pb][:, :nt * 128]),
                              reads=[self.psr[pb]], writes=[VTr[sl]])
                    items = []
                    for hh in range(2):
                        for qi, (q0, nq, loc) in enumerate(qtiles):
                            keys = [(k0, c0) for (k0, c0) in loc] + [(k0, None) for k0 in ctx_keys]
                            for ki, (k0, c0) in enumerate(keys):
                                items.append((hh, qi, ki, len(keys), k0, c0))
                    obase = cnt["o"]
                    cnt["o"] += 2 * len(qtiles)
                    ibase = cnt["s"]
                    cnt["s"] += len(items)

                    def stage_a(it, idx):
                        hh, qi, ki, nk, k0, c0 = it
                        q0, nq, _ = qtiles[qi]
                        h = 2 * ch + hh
                        r0 = hh * 64
                        ss = h % 2
                        if qi == 0 and ki == 0:
                            kb.dma("pool", stripd[ss], strip[ss][:], self.d_nastrip[h], writes=[stripr[ss]])
                        g = ibase + idx
                        pss = g % 4
                        pt = g % 3
                        kb.op("pe", lambda e: e.matmul(self.ps[pss][:, :nq], lhsT=KT[sl][r0:r0 + 64, k0:k0 + 128],
                                                        rhs=QT[sl][r0:r0 + 64, q0:q0 + nq], start=True, stop=True),
                              reads=bregs(KTr[sl], k0, 128) + bregs(QTr[sl], q0, nq), writes=[self.psr[pss]])
                        if c0 is not None:
                            tm = g % 3
                            kb.op("dve", lambda e: e.scalar_tensor_tensor(out=tmp[tm][:, :nq], in0=self.ps[pss][:, :nq], scalar=0.125,
                                                                          in1=strip[ss][:, c0:c0 + nq], op0=ALU.mult, op1=ALU.add),
                                  reads=[self.psr[pss], stripr[ss]], writes=[tmpr[tm]])
                            kb.op("act", lambda e: e.activation(out=PT[pt][:, :nq], in_=tmp[tm][:, :nq], func=AF.Exp),
                                  reads=[tmpr[tm]], writes=[PTr[pt]])
                        else:
                            kb.op("act", lambda e: e.activation(out=PT[pt][:, :nq], in_=self.ps[pss][:, :nq], func=AF.Exp, scale=0.125),
                                  reads=[self.psr[pss]], writes=[PTr[pt]])

                    def stage_b(it, idx):
                        hh, qi, ki, nk, k0, c0 = it
                        q0, nq, _ = qtiles[qi]
                        r0 = hh * 64
                        o_ = obase + hh * len(qtiles) + qi
                        ppv = 4 + (o_ % 2)
                        psm = 6 + (o_ % 2)
                        pt = (ibase + idx) % 3
                        kt = k0 // 128
                        kb.op("pe", lambda e: e.matmul(self.ps[ppv][:, :nq], lhsT=VT[sl][:, kt, :], rhs=PT[pt][:, :nq],
                                                        start=(ki == 0), stop=(ki == nk - 1)),
                              reads=[VTr[sl], PTr[pt]], writes=[self.psr[ppv]])
                        kb.op("pe", lambda e: e.matmul(self.ps[psm][:, :nq], lhsT=ones1[:], rhs=PT[pt][:, :nq],
                                                        start=(ki == 0), stop=(ki == nk - 1)),
                              reads=[o1r, PTr[pt]], writes=[self.psr[psm]])
                        if ki == nk - 1:
                            rs = o_ % 2
                            kb.op("act", lambda e: e.activation(out=rc[rs][r0:r0 + 64, :nq], in_=self.ps[psm][r0:r0 + 64, :nq], func=AF.Ln),
                                  reads=[self.psr[psm]], writes=[rcr[rs]])
                            kb.op("act", lambda e: e.activation(out=rc[rs][r0:r0 + 64, :nq], in_=rc[rs][r0:r0 + 64, :nq], func=AF.Exp, scale=-1.0),
                                  reads=[rcr[rs]], writes=[rcr[rs]])
                            kb.op("dve", lambda e: e.tensor_tensor(out=ZT[r0:r0 + 64, ch, q0:q0 + nq], in0=self.ps[ppv][r0:r0 + 64, :nq],
                                                                   in1=rc[rs][r0:r0 + 64, :nq], op=ALU.mult),
                                  reads=[self.psr[ppv], rcr[rs]], writes=bregs(ZTr[ch], q0, nq))

                    LA = 2
                    for idx in range(len(items) + LA):
                        if idx < len(items):
                            stage_a(items[idx], idx)
                        if idx >= LA:
                            stage_b(items[idx - LA], idx - LA)
                kb.barrier()
            self.proj(es, self.d_nawo, range(NC_), ZT, ZTr, self.resid_evac(j), blocks, tag="nawo")
            kb.barrier()
        self.layernorm(L, j, blocks)


    def gla_mixer(self, L, last):
        kb = self.kb
        j = 1
        blocks = [0, 1, 2, 3, 4]
        oblocks = [0, 1, 2, 3] if last else blocks
        self.modulate(j, blocks)
        SC = 128.0 ** -0.5
        NCH = T // 64
        with ExitStack() as es:
            cst = kb.sb(es, "glcst", [128, 4 * 64 + 128 + 2 + 8 + 8], F32)
            cstr = Reg()
            kb.dma("sp", self.ds_misc, cst[:, 0:4 * 64 + 128 + 2 + 8], self.d_glacst, writes=[cstr])
            MK, ID, NG, BA, NBA = 0, 256, 384, 386, 394
            kb.op("dve", lambda e: e.tensor_scalar(out=cst[:, NBA:NBA + 8], in0=cst[:, BA:BA + 8], scalar1=-1.0, scalar2=None, op0=ALU.mult),
                  reads=[cstr], writes=[cstr])
            maskb = kb.sb(es, "glmask", [128, 4, 64], BF16)
            identb = kb.sb(es, "glidb", [128, 128], BF16)
            ones256 = kb.sb(es, "glones", [128, 128], BF16)
            onecol = kb.sb(es, "glone", [128, 1], F32)
            epsc = kb.sb(es, "gleps", [128, 1], F32)
            kb.op("dve", lambda e: e.tensor_copy(out=maskb[:].rearrange("p a b -> p (a b)"), in_=cst[:, MK:MK + 256]), reads=[cstr], writes=[cstr])
            kb.op("dve", lambda e: e.tensor_copy(out=identb[:], in_=cst[:, ID:ID + 128]), reads=[cstr], writes=[cstr])
            kb.op("dve", lambda e: e.memset(ones256[:], 1.0 / 256.0), writes=[cstr])
            kb.op("dve", lambda e: e.memset(onecol[:], 1.0), writes=[cstr])
            kb.op("dve", lambda e: e.memset(epsc[:], LN_EPS), writes=[cstr])
            wa1 = kb.sb(es, "glwa1", [128, NC_, 32], BF16)
            wa1r = Reg()
            kb.dma("pool", self.ds_misc, wa1[:].rearrange("p k m -> p (k m)"), self.d_glawa1, writes=[wa1r])
            wa2 = kb.sb(es, "glwa2", [32, 2, 512], F32)
            wa2r = Reg()
            kb.dma("sp", self.ds_misc, wa2[:].rearrange("p d m -> p (d m)"), self.d_glawa2, writes=[wa2r])
            rT = kb.sb(es, "glrT", [32, T], F32)
            rTr = [Reg() for _ in range(5)]
            for b in blocks:
                t0, n, kind = BLOCKS[b]
                pb = self.psrot % 8
                self.psrot += 1
                for kc in range(NC_):
                    kb.op("pe", lambda e: e.matmul(self.ps[pb][0:32, :n], lhsT=wa1[:, kc, :], rhs=self.U[:, kc, t0:t0 + n],
                                                    start=(kc == 0), stop=(kc == NC_ - 1)),
                          reads=[wa1r, self.Ur[kc][b]], writes=[self.psr[pb]])
                kb.op("act", lambda e: e.activation(out=rT[:, t0:t0 + n], in_=self.ps[pb][0:32, :n], func=AF.Copy),
                      reads=[self.psr[pb]], writes=[rTr[b]])
            QK = kb.sb(es, "glQK", [128, 2, T], BF16)
            QKr = [[Reg() for _ in range(5)] for _ in range(2)]
            VT = kb.sb(es, "glVT", [128, 18, 256], BF16)
            VTr = Reg()
            O = kb.sb(es, "glO", [128, 2, T], F32)
            Or = [Reg() for _ in range(5)]
            Z, Zr = QK, QKr
            SA = kb.sb(es, "glSA", [128, 2, T], F32)
            SAr = Reg()
            sad = kb.dsem("glrope")
            pw = [kb.sb(es, f"glpw{s}", [128, NC_, 128], BF16) for s in range(2)]
            pwr = [Reg(), Reg()]
            pwd = [kb.dsem(f"glpw{s}") for s in range(2)]
            wv = kb.sb(es, "glwv", [128, NC_, 256], BF16)
            wvr = Reg()
            wvd = kb.dsem("glwv")
            t1 = [kb.sb(es, f"glt1{s}", [128, 512], F32) for s in range(2)]
            t1r = [Reg(), Reg()]
            t2 = [kb.sb(es, f"glt2{s}", [128, 512], F32) for s in range(2)]
            t2r = [Reg(), Reg()]
            S = kb.sb(es, "glS", [128, 256], F32)
            Sb = kb.sb(es, "glSb", [128, 256], BF16)
            Sr, Sbr = Reg(), Reg()
            kd = [kb.sb(es, f"glkd{s}", [128, 128], BF16) for s in range(3)]
            kt = [kb.sb(es, f"glkt{s}", [128, 128], BF16) for s in range(3)]
            kdr = [Reg() for _ in range(3)]
            ktr = [Reg() for _ in range(3)]
            for s_ in range(3):
                kb.op("dve", lambda e: e.memset(kd[s_][:], 0.0), writes=[kdr[s_]])
                kb.op("dve", lambda e: e.memset(kt[s_][:], 0.0), writes=[ktr[s_]])
            ktT = [kb.sb(es, f"glktT{s}", [128, 128], BF16) for s in range(5)]
            ktTr = [Reg() for _ in range(5)]
            qd = [kb.sb(es, f"glqd{s}", [128, 64], BF16) for s in range(5)]
            qdr = [Reg() for _ in range(5)]
            ex = [kb.sb(es, f"glex{s}", [128, 3, 64], F32) for s in range(5)]
            exr = [Reg() for _ in range(5)]
            attm = [kb.sb(es, f"glatt{s}", [128, 64], BF16) for s in range(5)]
            attr = [Reg() for _ in range(5)]
            nbp = [kb.sb(es, f"glnb{s}", [128, 2], F32) for s in range(5)]
            nbr = [Reg() for _ in range(5)]
            psb = [self.ps[i].bitcast(BF16) for i in range(8)]
            wcnt = 0
            for hd in range(4):
                kb.dma("sp", sad, SA[:, :, 0:SEQ], self.d_glarope.rearrange("a p t -> p a t"), writes=[SAr])
                for qk in range(2):
                    oc = qk * 4 + hd
                    ws0 = wcnt % 2
                    ws1 = (wcnt + 1) % 2
                    wcnt += 2
                    kb.dma("pool", pwd[ws0], pw[ws0][:].rearrange("p k m -> p (k m)"), self.d_glawqk[oc], writes=[pwr[ws0]])
                    kb.dma("pool", pwd[ws1], pw[ws1][:].rearrange("p k m -> p (k m)"), self.d_glawqkp[oc], writes=[pwr[ws1]])
                    for b in blocks:
                        t0, n, kind = BLOCKS[b]
                        pa = (self.psrot % 4) * 2
                        pp = pa + 1
                        ts = self.psrot % 2
                        self.psrot += 1
                        for kc in range(NC_):
                            kb.op("pe", lambda e: e.matmul(self.ps[pa][:, :n], lhsT=pw[ws0][:, kc, :], rhs=self.U[:, kc, t0:t0 + n],
                                                            start=(kc == 0), stop=(kc == NC_ - 1)),
                                  reads=[pwr[ws0], self.Ur[kc][b]], writes=[self.psr[pa]])
                        if kind == 1:
                            kb.op("act", lambda e: e.activation(out=QK[:, qk, t0:t0 + n], in_=self.ps[pa][:, :n], func=AF.Copy),
                                  reads=[self.psr[pa]], writes=[QKr[qk][b]])
                            continue
                        for kc in range(NC_):
                            kb.op("pe", lambda e: e.matmul(self.ps[pp][:, :n], lhsT=pw[ws1][:, kc, :], rhs=self.U[:, kc, t0:t0 + n],
                                                            start=(kc == 0), stop=(kc == NC_ - 1)),
                                  reads=[pwr[ws1], self.Ur[kc][b]], writes=[self.psr[pp]])
                        kb.op("dve", lambda e: e.tensor_tensor(out=t1[ts][:, :n], in0=self.ps[pa][:, :n], in1=SA[:, 0, t0:t0 + n], op=ALU.mult),
                              reads=[self.psr[pa], SAr], writes=[t1r[ts]])
                        kb.op("dve", lambda e: e.tensor_tensor(out=t2[ts][:, :n], in0=self.ps[pp][:, :n], in1=SA[:, 1, t0:t0 + n], op=ALU.mult),
                              reads=[self.psr[pp], SAr], writes=[t2r[ts]])
                        kb.op("pool", lambda e: e.tensor_tensor(out=QK[:, qk, t0:t0 + n], in0=t1[ts][:, :n], in1=t2[ts][:, :n], op=ALU.add),
                              reads=[t1r[ts], t2r[ts]], writes=[QKr[qk][b]])
                kb.dma("pool", wvd, wv[:].rearrange("p k m -> p (k m)"), self.d_glawv[hd], writes=[wvr])
                for g in range(9):
                    pb = self.psrot % 8
                    self.psrot += 1
                    for ti in range(2):
                        tt = 2 * g + ti
                        b = min(tt // 4, 4)
                        for kc in range(NC_):
                            kb.op("pe", lambda e: e.matmul(self.ps[pb][:, ti * 256:(ti + 1) * 256], lhsT=self.U[:, kc, tt * 128:(tt + 1) * 128],
                                                            rhs=wv[:, kc, :], start=(kc == 0), stop=(kc == NC_ - 1)),
                                  reads=[wvr, self.Ur[kc][b]], writes=[self.psr[pb]])
                    kb.op("act", lambda e: e.activation(out=VT[:, 2 * g:2 * g + 2, :].rearrange("p t m -> p (t m)"), in_=self.ps[pb][:, :512], func=AF.Copy),
                          reads=[self.psr[pb]], writes=[VTr])
                for d in range(2):
                    for b in blocks:
                        t0, n, kind = BLOCKS[b]
                        pb = self.psrot % 8
                        ts = self.psrot % 2
                        self.psrot += 1
                        kb.op("pe", lambda e: e.matmul(self.ps[pb][:, :n], lhsT=wa2[:, d, hd * 128:(hd + 1) * 128], rhs=rT[:, t0:t0 + n],
                                                        start=True, stop=True),
                              reads=[wa2r, rTr[b]], writes=[self.psr[pb]])
                        kb.op("act", lambda e: e.activation(out=t1[ts][:, :n], in_=self.ps[pb][:, :n], func=AF.Exp, scale=-1.0,
                                                            bias=cst[:, NBA + d * 4 + hd:NBA + d * 4 + hd + 1]),
                              reads=[self.psr[pb], cstr], writes=[t1r[ts]])
                        kb.op("act", lambda e: e.activation(out=SA[:, 0, t0:t0 + n], in_=t1[ts][:, :n], func=AF.Ln, bias=onecol[:, 0:1]),
                              reads=[t1r[ts], cstr], writes=[SAr])
                    for n_ in range(NCH):
                        c0 = 64 * n_
                        kb.op("dve", lambda e: e.tensor_tensor_scan(out=SA[:, 1, c0:c0 + 64], data0=self.onesf[:, 0:64], data1=SA[:, 0, c0:c0 + 64],
                                                                    initial=0.0, op0=ALU.mult, op1=ALU.add),
                              reads=[SAr, self.onesr], writes=[SAr])
                    if d == 1:
                        kb.op("pool", lambda e: e.tensor_tensor(out=SA[:, 0, :], in0=SA[:, 0, :], in1=SA[:, 1, :], op=ALU.subtract),
                              reads=[SAr], writes=[SAr])
                    kb.op("dve", lambda e: e.memset(S[:], 0.0), writes=[Sr])
                    kb.op("dve", lambda e: e.memset(Sb[:], 0.0), writes=[Sbr])
                    order = ([32, 33, 34, 35] + list(range(32))) if d == 0 else ([35, 34, 33, 32] + list(range(31, -1, -1)))
                    def chunk_gen(ci, n_):
                        c0 = 64 * n_
                        tt, par = n_ // 2, n_ % 2
                        tb = tt % 3
                        b = min(c0 // 512, 4)
                        xs = ci % 5
                        last_col = c0 + 63
                        kb.op("dve", lambda e: e.tensor_scalar(out=nbp[xs][:, 0:1], in0=SA[:, 1, last_col:last_col + 1], scalar1=-1.0 / 16.0, scalar2=None, op0=ALU.mult),
                              reads=[SAr], writes=[nbr[xs]])
                        kb.op("dve", lambda e: e.tensor_scalar(out=nbp[xs][:, 1:2], in0=SA[:, 1, last_col:last_col + 1], scalar1=1.0 / 16.0, scalar2=None, op0=ALU.mult),
                              reads=[SAr], writes=[nbr[xs]])
                        if d == 0:
                            src = SA[:, 1, c0:c0 + 64]
                            kb.op("act", lambda e: e.activation(out=ex[xs][:, 0, :], in_=src, func=AF.Exp, scale=-1.0 / 16.0), reads=[SAr], writes=[exr[xs]])
                            kb.op("act", lambda e: e.activation(out=ex[xs][:, 1, :], in_=src, func=AF.Exp, scale=1.0 / 16.0), reads=[SAr], writes=[exr[xs]])
                            kb.op("act", lambda e: e.activation(out=ex[xs][:, 2, :], in_=src, func=AF.Exp, scale=1.0 / 16.0, bias=nbp[xs][:, 0:1]),
                                  reads=[SAr, nbr[xs]], writes=[exr[xs]])
                            dec = ex[xs][:, 0, 63:64]
                        else:
                            src = SA[:, 0, c0:c0 + 64]
                            kb.op("act", lambda e: e.activation(out=ex[xs][:, 0, :], in_=src, func=AF.Exp, scale=-1.0 / 16.0, bias=nbp[xs][:, 0:1]),
                                  reads=[SAr, nbr[xs]], writes=[exr[xs]])
                            kb.op("act", lambda e: e.activation(out=ex[xs][:, 1, :], in_=src, func=AF.Exp, scale=1.0 / 16.0, bias=nbp[xs][:, 1:2]),
                                  reads=[SAr, nbr[xs]], writes=[exr[xs]])
                            kb.op("act", lambda e: e.activation(out=ex[xs][:, 2, :], in_=src, func=AF.Exp, scale=1.0 / 16.0), reads=[SAr], writes=[exr[xs]])
                            dec = ex[xs][:, 0, 0:1]
                        yield
                        kb.op("dve", lambda e: e.tensor_tensor(out=qd[xs][:], in0=QK[:, 0, c0:c0 + 64], in1=ex[xs][:, 0, :], op=ALU.mult),
                              reads=[QKr[0][b], exr[xs]], writes=[qdr[xs]])
                        kb.op("dve", lambda e: e.tensor_tensor(out=kd[tb][:, par * 64:par * 64 + 64], in0=QK[:, 1, c0:c0 + 64], in1=ex[xs][:, 1, :], op=ALU.mult),
                              reads=[QKr[1][b], exr[xs]], writes=[kdr[tb]])
                        kb.op("pool", lambda e: e.tensor_tensor(out=kt[tb][:, par * 64:par * 64 + 64], in0=QK[:, 1, c0:c0 + 64], in1=ex[xs][:, 2, :], op=ALU.mult),
                              reads=[QKr[1][b], exr[xs]], writes=[ktr[tb]])
                        yield
                        bk = ci % 5
                        kb.op("pe", lambda e: e.matmul(self.ps[bk][:, 0:64], lhsT=kd[tb][:], rhs=qd[xs][:], start=True, stop=True),
                              reads=[kdr[tb], qdr[xs]], writes=[self.psr[bk]])
                        kb.op("dve", lambda e: e.tensor_tensor(out=attm[xs][:], in0=self.ps[bk][:, 0:64], in1=maskb[:, par * 2 + d, :], op=ALU.mult),
                              reads=[self.psr[bk], cstr], writes=[attr[xs]])
                        yield
                        kb.op("pe", lambda e: e.transpose(psb[bk][:, 896:1024], kt[tb][:], identb[:]),
                              reads=[ktr[tb], cstr], writes=[self.psr[bk]])
                        kb.op("act", lambda e: e.activation(out=ktT[xs][par * 64:par * 64 + 64, :], in_=psb[bk][par * 64:par * 64 + 64, 896:1024], func=AF.Copy),
                              reads=[self.psr[bk]], writes=[ktTr[xs]])
                        yield
                        for ec in range(2):
                            kb.op("pe", lambda e: e.matmul(self.ps[bk][:, 64 + ec * 64:64 + ec * 64 + 64], lhsT=VT[:, tt, ec * 128:(ec + 1) * 128], rhs=attm[xs][:],
                                                            start=True, stop=False),
                                  reads=[VTr, attr[xs]], writes=[self.psr[bk]])
                            kb.op("pe", lambda e: e.matmul(self.ps[bk][:, 64 + ec * 64:64 + ec * 64 + 64], lhsT=Sb[:, ec * 128:(ec + 1) * 128], rhs=qd[xs][:],
                                                            start=False, stop=True),
                                  reads=[Sbr, qdr[xs]], writes=[self.psr[bk]])
                        yield
                        kb.op("pe", lambda e: e.matmul(self.ps[bk][:, 192:448], lhsT=ktT[xs][par * 64:par * 64 + 64, :], rhs=VT[par * 64:par * 64 + 64, tt, :],
                                                        start=True, stop=True),
                              reads=[ktTr[xs], VTr], writes=[self.psr[bk]])
                        kb.op("dve", lambda e: e.scalar_tensor_tensor(out=S[:], in0=S[:], scalar=dec, in1=self.ps[bk][:, 192:448], op0=ALU.mult, op1=ALU.add),
                              reads=[Sr, exr[xs], self.psr[bk]], writes=[Sr])
                        kb.op("act", lambda e: e.activation(out=Sb[:], in_=S[:], func=AF.Copy), reads=[Sr], writes=[Sbr])
                        yield
                        pov = self.ps[bk][:, 64:192].rearrange("p (e c) -> p e c", e=2)
                        if d == 0:
                            kb.op("act", lambda e: e.activation(out=O[:, :, c0:c0 + 64], in_=pov, func=AF.Identity, scale=SC),
                                  reads=[self.psr[bk]], writes=[Or[b]])
                        else:
                            kb.op("dve", lambda e: e.scalar_tensor_tensor(out=O[:, :, c0:c0 + 64], in0=pov, scalar=SC, in1=O[:, :, c0:c0 + 64],
                                                                          op0=ALU.mult, op1=ALU.add),
                                  reads=[self.psr[bk], Or[b]], writes=[Or[b]])
                    active = []
                    pending = list(enumerate(order))
                    while pending or active:
                        if pending and len(active) < 5:
                            active.append(chunk_gen(*pending.pop(0)))
                        for g_ in list(active):
                            try:
                                next(g_)
                            except StopIteration:
                                active.remove(g_)
                if getattr(self, "debug", None) == f"gla_scan{hd}":
                    dd = kb.dsem("dbg")
                    for e2 in range(2):
                        kb.dma("sp", dd, self.d_hout[e2], O[:, e2, :], reads=Or)
                        kb.dma("pool", dd, self.d_hout[2 + e2], QK[:, e2, :], reads=QKr[0] + QKr[1])
                        kb.dma("sp", dd, self.d_hout[4 + e2], SA[:, e2, :], reads=[SAr])
                    kb.dma("sp", dd, self.d_hout[6][:, 0:256], S[:], reads=[Sr])
                    kb.dma("sp", dd, self.d_hout[7][0:32, :], rT[:], reads=rTr)
                    kb.eng["sp"].wait_ge(dd.sem, dd.cnt)
                    kb.eng["pool"].wait_ge(dd.sem, dd.cnt)
                    raise DebugStop()
                for ec in range(2):
                    ws = wcnt % 2
                    wcnt += 1
                    kb.dma("pool", pwd[ws], pw[ws][:].rearrange("p k m -> p (k m)"), self.d_glawg[hd * 2 + ec], writes=[pwr[ws]])
                    for b in oblocks:
                        t0, n, kind = BLOCKS[b]
                        pg = self.psrot % 8
                        pq = (self.psrot + 1) % 8
                        ts = (self.psrot // 2) % 2
                        self.psrot += 2
                        for kc in range(NC_):
                            kb.op("pe", lambda e: e.matmul(self.ps[pg][:, :n], lhsT=pw[ws][:, kc, :], rhs=self.U[:, kc, t0:t0 + n],
                                                            start=(kc == 0), stop=(kc == NC_ - 1)),
                                  reads=[pwr[ws], self.Ur[kc][b]], writes=[self.psr[pg]])
                        for e2 in range(2):
                            kb.op("act", lambda e: e.activation(out=t1[ts][:, :n], in_=O[:, e2, t0:t0 + n], func=AF.Square), reads=[Or[b]], writes=[t1r[ts]])
                            kb.op("act", lambda e: e.activation(out=Z[:, ec, t0:t0 + n], in_=t1[ts][:, :n], func=AF.Copy), reads=[t1r[ts]], writes=[Zr[ec][b]])
                            kb.op("pe", lambda e: e.matmul(self.ps[pq][:, :n], lhsT=ones256[:], rhs=Z[:, ec, t0:t0 + n], start=(e2 == 0), stop=(e2 == 1)),
                                  reads=[cstr, Zr[ec][b]], writes=[self.psr[pq]])
                        kb.op("act", lambda e: e.activation(out=t1[ts][:, :n], in_=self.ps[pq][:, :n], func=AF.Ln, bias=epsc[:, 0:1]),
                              reads=[self.psr[pq], cstr], writes=[t1r[ts]])
                        kb.op("act", lambda e: e.activation(out=t1[ts][:, :n], in_=t1[ts][:, :n], func=AF.Exp, scale=-0.5), reads=[t1r[ts]], writes=[t1r[ts]])
                        kb.op("dve", lambda e: e.tensor_tensor(out=t1[ts][:, :n], in0=O[:, ec, t0:t0 + n], in1=t1[ts][:, :n], op=ALU.mult),
                              reads=[Or[b], t1r[ts]], writes=[t1r[ts]])
                        kb.op("act", lambda e: e.activation(out=t2[ts][:, :n], in_=self.ps[pg][:, :n], func=AF.Silu), reads=[self.psr[pg]], writes=[t2r[ts]])
                        kb.op("dve", lambda e: e.scalar_tensor_tensor(out=Z[:, ec, t0:t0 + n], in0=t1[ts][:, :n], scalar=cst[:, NG + ec:NG + ec + 1],
                                                                      in1=t2[ts][:, :n], op0=ALU.mult, op1=ALU.mult),
                              reads=[t1r[ts], t2r[ts], cstr], writes=[Zr[ec][b]])
                if getattr(self, "debug", None) == f"gla_fin{hd}":
                    dd = kb.dsem("dbg")
                    for e2 in range(2):
                        kb.dma("pool", dd, self.d_hout[e2], Z[:, e2, :], reads=Zr[0] + Zr[1])
                        kb.dma("sp", dd, self.d_hout[2 + e2], O[:, e2, :], reads=Or)
                    kb.eng["sp"].wait_ge(dd.sem, dd.cnt)
                    kb.eng["pool"].wait_ge(dd.sem, dd.cnt)
                    raise DebugStop()
                with ExitStack() as es2:
                    self.proj(es2, self.d_glawo[hd], range(NC_), Z, Zr, self.resid_evac(j), oblocks, nk=2, tag=f"glwo{hd}")
                    kb.barrier()
                    if getattr(self, "debug", None) == f"gla_wo{hd}":
                        self.store()
                        raise DebugStop()
            kb.barrier()
        self.layernorm(L, j, oblocks)


    def park_h(self):
        kb = self.kb
        if not hasattr(self, "d_hpark"):
            self.d_hpark = self.nc.dram_tensor("hpark", [NC_, 128, T], F32, kind="Internal").ap()
            self.ds_park = kb.dsem("park")
        for c in range(NC_):
            kb.dma("sp", self.ds_h[c % 4], self.d_hpark[c], self.H[:, c, :], reads=self.Hr[c])

    def unpark_h(self):
        kb = self.kb
        self.es_H = ExitStack()
        self.H = kb.sb(self.es_H, "H", [128, NC_, T], F32)
        for c in range(NC_):
            kb.dma("sp", self.ds_h[c % 4], self.H[:, c, :], self.d_hpark[c], writes=self.Hr[c])

    def rwkv_mixer(self, L, last):
        kb = self.kb
        assert last, "rwkv mixer implemented for the final layer (context output unused)"
        j = 1
        blocks = [0, 1, 2, 3, 4]
        lblocks = [0, 1, 2, 3]
        self.modulate(j, blocks)
        NCH = T // 64
        G = 4
        EM05 = float(np.exp(-0.5))
        U = self.U
        nbank = lambda: self._nb()
        W0, A0, KK_, KA, RK, GNG, GNB, MU, OMKA, OMU, HMU = 0, 16, 32, 40, 48, 56, 64, 72, 120, 128, 176
        self.park_h()
        kb.barrier()
        self.es_H.close()
        d_zpark = self.nc.dram_tensor("zpark", [NC_, 128, SEQ], BF16, kind="Internal").ap()
        ds_z = kb.dsem("zpark")
        dbg = getattr(self, "debug", None)
        if dbg == "rw0":
            return 'stop'
        with ExitStack() as es:
            prm = kb.sb(es, "rwprm", [128, 224], F32)
            prmr = Reg()
            kb.dma("sp", self.ds_misc, prm[:, 0:120], self.d_rwprm, writes=[prmr])
            kb.op("dve", lambda e: e.tensor_scalar(out=prm[:, OMKA:OMKA + 8], in0=prm[:, KA:KA + 8], scalar1=-1.0, scalar2=1.0, op0=ALU.mult, op1=ALU.add),
                  reads=[prmr], writes=[prmr])
            kb.op("dve", lambda e: e.tensor_scalar(out=prm[:, OMU:OMU + 48], in0=prm[:, MU:MU + 48], scalar1=-1.0, scalar2=1.0, op0=ALU.mult, op1=ALU.add),
                  reads=[prmr], writes=[prmr])
            kb.op("dve", lambda e: e.tensor_scalar(out=prm[:, HMU:HMU + 48], in0=prm[:, MU:MU + 48], scalar1=0.5, scalar2=None, op0=ALU.mult),
                  reads=[prmr], writes=[prmr])
            cstf = kb.sb(es, "rwcstf", [128, 256], F32)
            cstr = Reg()
            kb.dma("sp", self.ds_misc, cstf[:], self.d_rwcstf, writes=[cstr])
            ident = cstf[:, 0:128]
            bdmask = cstf[:, 128:256]
            gmask = kb.sb(es, "rwgmask", [64, 2, 5, 2, 64], BF16)
            kb.dma("pool", self.ds_misc, gmask[:].rearrange("p a b c d -> p (a b c d)"), self.d_rwgmask, writes=[cstr])
            identb = kb.sb(es, "rwidb", [128, 128], BF16)
            id64 = kb.sb(es, "rwid64", [64, 2, 64], BF16)
            onesbd = kb.sb(es, "rwonesbd", [128, 128], BF16)
            ones64 = kb.sb(es, "rwones64", [128, 128], BF16)
            kb.op("dve", lambda e: e.tensor_copy(out=identb[:], in_=ident), reads=[cstr], writes=[cstr])
            for h in range(2):
                kb.op("dve", lambda e: e.tensor_copy(out=id64[:, h, :], in_=cstf[0:64, 0:64]), reads=[cstr], writes=[cstr])
            kb.op("dve", lambda e: e.tensor_copy(out=onesbd[:], in_=bdmask), reads=[cstr], writes=[cstr])
            kb.op("dve", lambda e: e.tensor_scalar(out=ones64[:], in0=bdmask, scalar1=1.0 / 64.0, scalar2=None, op0=ALU.mult), reads=[cstr], writes=[cstr])
            epsg = kb.sb(es, "rwepsg", [128, 2], F32)
            kb.op("dve", lambda e: e.memset(epsg[:, 0:1], 64e-5), writes=[cstr])
            kb.op("dve", lambda e: e.memset(epsg[:, 1:2], 1e-24), writes=[cstr])
            XX = kb.sb(es, "rwXX", [128, NC_, T], BF16)
            XXr = [[Reg() for _ in range(5)] for _ in range(NC_)]
            with ExitStack() as es0:
                tf = [kb.sb(es0, f"rwtf{s_}", [128, T], F32) for s_ in range(2)]
                tfr = [Reg(), Reg()]
                for c in range(NC_):
                    s_ = c % 2
                    for (s0, s1) in ((0, SEQ), (SEQ, T)):
                        kb.op("pool", lambda e: e.tensor_tensor(out=tf[s_][:, s0 + 1:s1 - 1], in0=U[:, c, s0:s1 - 2], in1=U[:, c, s0 + 2:s1], op=ALU.add),
                              reads=self.Ur[c], writes=[tfr[s_]])
                        kb.op("pool", lambda e: e.tensor_copy(out=tf[s_][:, s0:s0 + 1], in_=U[:, c, s0 + 1:s0 + 2]), reads=self.Ur[c], writes=[tfr[s_]])
                        kb.op("pool", lambda e: e.tensor_copy(out=tf[s_][:, s1 - 1:s1], in_=U[:, c, s1 - 2:s1 - 1]), reads=self.Ur[c], writes=[tfr[s_]])
                    kb.op("dve", lambda e: e.scalar_tensor_tensor(out=XX[:, c, :], in0=tf[s_][:], scalar=0.5, in1=U[:, c, :], op0=ALU.mult, op1=ALU.subtract),
                          reads=[tfr[s_]] + self.Ur[c], writes=XXr[c])
                kb.barrier()
            pw = [kb.sb(es, f"rwpw{s_}", [128, NC_, 128], BF16) for s_ in range(2)]
            pws = [kb.sb(es, f"rwpws{s_}", [128, NC_, 128], BF16) for s_ in range(2)]
            pwr = [Reg(), Reg()]
            pwsr = [Reg(), Reg()]
            pwd = [kb.dsem(f"rwpw{s_}") for s_ in range(2)]
            wc = {"n": 0}

            def xproj(Wd_oc, jkind, blks, evac):
                s_ = wc["n"] % 2
                wc["n"] += 1
                kb.dma("pool", pwd[s_], pw[s_][:].rearrange("p k m -> p (k m)"), Wd_oc, writes=[pwr[s_]])
                for kc in range(NC_):
                    kb.op("dve" if kc % 2 == 0 else "pool",
                          lambda e: e.tensor_scalar(out=pws[s_][:, kc, :], in0=pw[s_][:, kc, :], scalar1=prm[:, MU + jkind * 8 + kc:MU + jkind * 8 + kc + 1],
                                                    scalar2=None, op0=ALU.mult),
                          reads=[pwr[s_], prmr], writes=[pwsr[s_]])
                for b in blks:
                    t0, n, kind = BLOCKS[b]
                    pb = nbank()
                    for kc in range(NC_):
                        kb.op("pe", lambda e: e.matmul(self.ps[pb][:, :n], lhsT=pw[s_][:, kc, :], rhs=U[:, kc, t0:t0 + n], start=(kc == 0), stop=False),
                              reads=[pwr[s_], self.Ur[kc][b]], writes=[self.psr[pb]])
                    for kc in range(NC_):
                        kb.op("pe", lambda e: e.matmul(self.ps[pb][:, :n], lhsT=pws[s_][:, kc, :], rhs=XX[:, kc, t0:t0 + n], start=False, stop=(kc == NC_ - 1)),
                              reads=[pwsr[s_], XXr[kc][b]], writes=[self.psr[pb]])
                    evac(b, pb, t0, n)

            LR = kb.sb(es, "rwLR", [128, 3, T], BF16)
            LRr = [[Reg() for _ in range(5)] for _ in range(3)]
            for (li, jk, fn) in ((0, 5, AF.Sigmoid), (1, 1, AF.Tanh), (2, 4, AF.Copy)):
                def ev(b, pb, t0, n, li=li, fn=fn):
                    kb.op("act", lambda e: e.activation(out=LR[:, li, t0:t0 + n], in_=self.ps[pb][:, :n], func=fn),
                          reads=[self.psr[pb]], writes=[LRr[li][b]])
                xproj(self.d_rwlr[li], jk, lblocks if li == 0 else blocks, ev)
            if dbg == "rw1":
                kb.barrier()
                return 'stop'
            lrw = kb.sb(es, "rwlrw", [128, 3, 2, 128], BF16)
            lrwr = [Reg(), Reg()]
            lrwd = [kb.dsem(f"rwlrw{s_}") for s_ in range(2)]
            ZT1 = kb.sb(es, "rwZT1", [128, SEQ], BF16)
            ZT1r = [Reg() for _ in range(5)]
            Rr_ = kb.sb(es, "rwR", [128, SEQ], BF16); Rr = [Reg() for _ in range(5)]
            Kk = kb.sb(es, "rwK", [128, T], BF16); Kr = [Reg() for _ in range(5)]
            KKn = kb.sb(es, "rwKK", [128, T], BF16); KKr = [Reg() for _ in range(5)]
            VTf = kb.sb(es, "rwVT", [128, T], BF16); VTr = [Reg() for _ in range(5)]
            Gg = kb.sb(es, "rwG", [128, SEQ], BF16); Ggr = [Reg() for _ in range(5)]
            BON = kb.sb(es, "rwBON", [128, SEQ], BF16); BONr = [Reg() for _ in range(5)]
            YA = kb.sb(es, "rwYA", [128, SEQ], F32); YAr = [Reg() for _ in range(5)]
            LW = kb.sb(es, "rwLW", [128, T], F32); LWr = [Reg() for _ in range(5)]
            KD = kb.sb(es, "rwKD", [128, T], BF16); KDr = [Reg() for _ in range(5)]
            BD = kb.sb(es, "rwBD", [128, T], BF16); BDr = [Reg() for _ in range(5)]
            tA = [kb.sb(es, f"rwtA{s_}", [128, 512], F32) for s_ in range(2)]
            tAr = [Reg(), Reg()]
            tB = [kb.sb(es, f"rwtB{s_}", [128, 512], BF16) for s_ in range(2)]
            tBr = [Reg(), Reg()]
            SCs = [kb.sb(es, f"rwSC{s_}", [128, 2, 64], F32) for s_ in range(G)]; SCr = [Reg() for _ in range(G)]
            TOT = [kb.sb(es, f"rwTOT{s_}", [128, 2], F32) for s_ in range(G)]; TOTr = [Reg() for _ in range(G)]
            EE = [kb.sb(es, f"rwEE{s_}", [128, 4, 64], F32) for s_ in range(G)]; EEr = [Reg() for _ in range(G)]
            OPS = [kb.sb(es, f"rwOPS{s_}", [128, 6, 64], BF16) for s_ in range(G)]; OPSr = [Reg() for _ in range(G)]
            RT32 = [kb.sb(es, f"rwRT{s_}", [128, 64], F32) for s_ in range(G)]; RT32r = [Reg() for _ in range(G)]
            TOK = [kb.sb(es, f"rwTOK{s_}", [64, 4, 128], BF16) for s_ in range(G)]; TOKr = [Reg() for _ in range(G)]
            GM = [kb.sb(es, f"rwGM{s_}", [64, 5, 2, 64], BF16) for s_ in range(G)]; GMr = [Reg() for _ in range(G)]
            NF = [kb.sb(es, f"rwNF{s_}", [64, 2, 2, 64], F32) for s_ in range(G)]; NFr = [Reg() for _ in range(G)]
            PP = [kb.sb(es, f"rwPP{s_}", [64, 2, 2, 2, 64], F32) for s_ in range(G)]; PPr = [[Reg() for _ in range(2)] for _ in range(G)]
            TT32 = [kb.sb(es, f"rwTT{s_}", [64, 2, 2, 64], F32) for s_ in range(G)]; TTr = [[Reg() for _ in range(2)] for _ in range(G)]
            TTb = [kb.sb(es, f"rwTTb{s_}", [64, 2, 64], BF16) for s_ in range(G)]; TTbr = [Reg() for _ in range(G)]
            id64f = kb.sb(es, "rwid64f", [64, 2, 64], F32)
            for h in range(2):
                kb.op("dve", lambda e: e.tensor_copy(out=id64f[:, h, :], in_=cstf[0:64, 0:64]), reads=[cstr], writes=[cstr])
            GS = [kb.sb(es, f"rwGS{s_}", [64, 3, 2, 64], BF16) for s_ in range(G)]; GSr = [[Reg() for _ in range(3)] for _ in range(G)]
            QH = [kb.sb(es, f"rwQH{s_}", [128, 64], BF16) for s_ in range(G)]; QHr = [Reg() for _ in range(G)]
            PH = [kb.sb(es, f"rwPH{s_}", [128, 128], F32) for s_ in range(G)]; PHr = [Reg() for _ in range(G)]
            Abd = kb.sb(es, "rwA", [128, 128], F32); Ar = Reg()
            Abf = kb.sb(es, "rwAbf", [128, 128], BF16); Abfr = Reg()
            psbf = [self.ps[i].bitcast(BF16) for i in range(8)]

            for pr in range(NC_):
                sl = pr % 2
                kb.dma("pool", lrwd[sl], lrw[:, :, sl, :], self.d_rwlrw[pr].rearrange("a p m -> p a m"), writes=[lrwr[sl]])
                def ev_r(b, pb, t0, n):
                    kb.op("act", lambda e: e.activation(out=Rr_[:, t0:t0 + n], in_=self.ps[pb][:, :n], func=AF.Copy), reads=[self.psr[pb]], writes=[Rr[b]])
                xproj(self.d_rwwrkv[0, pr], 0, lblocks, ev_r)

                def ev_k(b, pb, t0, n):
                    s_ = b % 2
                    kb.op("act", lambda e: e.activation(out=Kk[:, t0:t0 + n], in_=self.ps[pb][:, :n], func=AF.Copy), reads=[self.psr[pb]], writes=[Kr[b]])
                    kb.op("act", lambda e: e.activation(out=tA[s_][:, :n], in_=self.ps[pb][:, :n], func=AF.Copy, scale=prm[:, KK_ + pr:KK_ + pr + 1]),
                          reads=[self.psr[pb], prmr], writes=[tAr[s_]])
                    kb.op("act", lambda e: e.activation(out=tB[s_][:, :n], in_=tA[s_][:, :n], func=AF.Square), reads=[tAr[s_]], writes=[tBr[s_]])
                    p2 = nbank()
                    kb.op("pe", lambda e: e.matmul(self.ps[p2][:, :n], lhsT=onesbd[:], rhs=tB[s_][:, :n], start=True, stop=True),
                          reads=[cstr, tBr[s_]], writes=[self.psr[p2]])
                    kb.op("act", lambda e: e.activation(out=tB[s_][:, :n], in_=self.ps[p2][:, :n], func=AF.Ln, bias=epsg[:, 1:2]), reads=[self.psr[p2], cstr], writes=[tBr[s_]])
                    kb.op("act", lambda e: e.activation(out=tB[s_][:, :n], in_=tB[s_][:, :n], func=AF.Exp, scale=-0.5), reads=[tBr[s_]], writes=[tBr[s_]])
                    kb.op("dve", lambda e: e.tensor_tensor(out=KKn[:, t0:t0 + n], in0=tA[s_][:, :n], in1=tB[s_][:, :n], op=ALU.mult),
                          reads=[tAr[s_], tBr[s_]], writes=[KKr[b]])
                xproj(self.d_rwwrkv[1, pr], 2, blocks, ev_k)

                def ev_v(b, pb, t0, n):
                    kb.op("act", lambda e: e.activation(out=VTf[:, t0:t0 + n], in_=self.ps[pb][:, :n], func=AF.Copy), reads=[self.psr[pb]], writes=[VTr[b]])
                xproj(self.d_rwwrkv[2, pr], 3, blocks, ev_v)
                for b in lblocks:
                    t0, n, kind = BLOCKS[b]
                    pb = nbank()
                    kb.op("pe", lambda e: e.matmul(self.ps[pb][:, :n], lhsT=lrw[:, 0, sl, :], rhs=LR[:, 0, t0:t0 + n], start=True, stop=True),
                          reads=[lrwr[sl], LRr[0][b]], writes=[self.psr[pb]])
                    kb.op("act", lambda e: e.activation(out=Gg[:, t0:t0 + n], in_=self.ps[pb][:, :n], func=AF.Copy), reads=[self.psr[pb]], writes=[Ggr[b]])
                for d in range(2):
                    for b in blocks:
                        t0, n, kind = BLOCKS[b]
                        s_ = b % 2
                        pb = nbank()
                        kb.op("pe", lambda e: e.matmul(self.ps[pb][:, :n], lhsT=lrw[d * 64:(d + 1) * 64, 1, sl, :], rhs=LR[d * 64:(d + 1) * 64, 1, t0:t0 + n], start=True, stop=True),
                              reads=[lrwr[sl], LRr[1][b]], writes=[self.psr[pb]])
                        kb.op("act", lambda e: e.activation(out=tA[s_][:, :n], in_=self.ps[pb][:, :n], func=AF.Sigmoid, bias=prm[:, W0 + d * 8 + pr:W0 + d * 8 + pr + 1]),
                              reads=[self.psr[pb], prmr], writes=[tAr[s_]])
                        kb.op("dve", lambda e: e.tensor_scalar(out=LW[:, t0:t0 + n], in0=tA[s_][:, :n], scalar1=-EM05, scalar2=None, op0=ALU.mult),
                              reads=[tAr[s_]], writes=[LWr[b]])
                        pb = nbank()
                        kb.op("pe", lambda e: e.matmul(self.ps[pb][:, :n], lhsT=lrw[d * 64:(d + 1) * 64, 2, sl, :], rhs=LR[d * 64:(d + 1) * 64, 2, t0:t0 + n], start=True, stop=True),
                              reads=[lrwr[sl], LRr[2][b]], writes=[self.psr[pb]])
                        kb.op("act", lambda e: e.activation(out=tA[s_][:, :n], in_=self.ps[pb][:, :n], func=AF.Sigmoid, bias=prm[:, A0 + d * 8 + pr:A0 + d * 8 + pr + 1]),
                              reads=[self.psr[pb], prmr], writes=[tAr[s_]])
                        kb.op("dve", lambda e: e.tensor_tensor(out=BD[:, t0:t0 + n], in0=KKn[:, t0:t0 + n], in1=tA[s_][:, :n], op=ALU.mult),
                              reads=[KKr[b], tAr[s_]], writes=[BDr[b]])
                        kb.op("act", lambda e: e.activation(out=tA[s_][:, :n], in_=tA[s_][:, :n], func=AF.Identity, scale=prm[:, KA + pr:KA + pr + 1], bias=prm[:, OMKA + pr:OMKA + pr + 1]),
                              reads=[tAr[s_], prmr], writes=[tAr[s_]])
                        kb.op("dve", lambda e: e.tensor_tensor(out=KD[:, t0:t0 + n], in0=Kk[:, t0:t0 + n], in1=tA[s_][:, :n], op=ALU.mult),
                              reads=[Kr[b], tAr[s_]], writes=[KDr[b]])
                        if kind == 0:
                            kb.op("dve", lambda e: e.scalar_tensor_tensor(out=tB[s_][:, :n], in0=KD[:, t0:t0 + n], scalar=prm[:, RK + pr:RK + pr + 1], in1=Rr_[:, t0:t0 + n], op0=ALU.mult, op1=ALU.mult),
                                  reads=[KDr[b], Rr[b], prmr], writes=[tBr[s_]])
                            pb = nbank()
                            kb.op("pe", lambda e: e.matmul(self.ps[pb][:, :n], lhsT=onesbd[:], rhs=tB[s_][:, :n], start=True, stop=True),
                                  reads=[cstr, tBr[s_]], writes=[self.psr[pb]])
                            if d == 0:
                                kb.op("act", lambda e: e.activation(out=BON[:, t0:t0 + n], in_=self.ps[pb][:, :n], func=AF.Copy), reads=[self.psr[pb]], writes=[BONr[b]])
                            else:
                                kb.op("dve", lambda e: e.tensor_tensor(out=BON[:, t0:t0 + n], in0=self.ps[pb][:, :n], in1=BON[:, t0:t0 + n], op=ALU.add),
                                      reads=[self.psr[pb], BONr[b]], writes=[BONr[b]])
                    if dbg == "rw2":
                        kb.barrier()
                        return 'stop'
                    kb.op("dve", lambda e: e.memset(Abd[:], 0.0), writes=[Ar])
                    kb.op("dve", lambda e: e.memset(Abf[:], 0.0), writes=[Abfr])
                    order = ([32, 33, 34, 35] + list(range(32))) if d == 0 else ([35, 34, 33, 32] + list(range(31, -1, -1)))
                    def chunk_gen(ci, n_):
                        c0 = 64 * n_
                        b = min(c0 // 512, 4)
                        lat = n_ < 32
                        q = ci % G
                        cbs = {'i': 0}

                        def nbank():
                            cbs['i'] += 1
                            return 2 * q + (cbs['i'] % 2)
                        ng = 5 if lat else 3
                        kb.op("dve", lambda e: e.tensor_tensor_scan(out=SCs[q][:, 0, :], data0=self.onesf[:, 0:64], data1=LW[:, c0:c0 + 64], initial=0.0, op0=ALU.mult, op1=ALU.add),
                              reads=[LWr[b], self.onesr], writes=[SCr[q]])
                        kb.op("dve", lambda e: e.tensor_tensor(out=SCs[q][:, 1, :], in0=SCs[q][:, 0, :], in1=LW[:, c0:c0 + 64], op=ALU.subtract),
                              reads=[SCr[q], LWr[b]], writes=[SCr[q]])
                        kb.op("dve", lambda e: e.tensor_copy(out=TOT[q][:, 0:1], in_=SCs[q][:, 0, 63:64]), reads=[SCr[q]], writes=[TOTr[q]])
                        kb.op("dve", lambda e: e.tensor_scalar(out=TOT[q][:, 1:2], in0=SCs[q][:, 0, 63:64], scalar1=-1.0, scalar2=None, op0=ALU.mult), reads=[SCr[q]], writes=[TOTr[q]])
                        cs, cxf = SCs[q][:, 0, :], SCs[q][:, 1, :]
                        tot, ntot = TOT[q][:, 0:1], TOT[q][:, 1:2]
                        if d == 0:
                            exs = [(cxf, 1.0, None), (cs, -1.0, None), (cs, 1.0, None), (cs, -1.0, tot)]
                        else:
                            exs = [(cs, -1.0, tot), (cxf, 1.0, ntot), (cxf, -1.0, tot), (cxf, 1.0, None)]
                        for i, (src, sc_, bi) in enumerate(exs):
                            if i == 2 and not lat:
                                continue
                            if bi is None:
                                kb.op("act", lambda e: e.activation(out=EE[q][:, i, :], in_=src, func=AF.Exp, scale=sc_), reads=[SCr[q]], writes=[EEr[q]])
                            else:
                                kb.op("act", lambda e: e.activation(out=EE[q][:, i, :], in_=src, func=AF.Exp, scale=sc_, bias=bi), reads=[SCr[q], TOTr[q]], writes=[EEr[q]])
                        yield
                        opl = [(0, KKn, KKr, 0), (1, BD, BDr, 1), (2, KD, KDr, 1), (4, BD, BDr, 3), (5, KD, KDr, 3)]
                        for oi, (o_, srcb, srcr, ei) in enumerate(opl):
                            kb.op("dve" if oi % 2 == 0 else "pool",
                                  lambda e: e.tensor_tensor(out=OPS[q][:, o_, :], in0=srcb[:, c0:c0 + 64], in1=EE[q][:, ei, :], op=ALU.mult),
                                  reads=[srcr[b], EEr[q]], writes=[OPSr[q]])
                        if lat:
                            kb.op("dve", lambda e: e.tensor_tensor(out=RT32[q][:], in0=Rr_[:, c0:c0 + 64], in1=EE[q][:, 2, :], op=ALU.mult),
                                  reads=[Rr[b], EEr[q]], writes=[RT32r[q]])
                            kb.op("pool", lambda e: e.tensor_copy(out=OPS[q][:, 3, :], in_=RT32[q][:]), reads=[RT32r[q]], writes=[OPSr[q]])
                        yield
                        pb = nbank()
                        for i, o_ in enumerate((0, 4, 5)):
                            kb.op("pe", lambda e: e.transpose(psbf[pb][0:64, i * 128:(i + 1) * 128], OPS[q][:, o_, :], identb[:]),
                                  reads=[OPSr[q], cstr], writes=[self.psr[pb]])
                        kb.op("pe", lambda e: e.transpose(psbf[pb][0:64, 384:512], VTf[:, c0:c0 + 64], identb[:]),
                              reads=[VTr[b], cstr], writes=[self.psr[pb]])
                        kb.op("act", lambda e: e.activation(out=TOK[q][:].rearrange("p a b -> p (a b)"), in_=psbf[pb][0:64, 0:512], func=AF.Copy),
                              reads=[self.psr[pb]], writes=[TOKr[q]])
                        yield
                        gpairs = [(0, 1), (1, 0), (2, 0), (1, 3), (2, 3)]
                        pbh = [nbank(), nbank()]
                        for wi in range(ng):
                            li_, ri_ = gpairs[wi]
                            for h in range(2):
                                kb.op("pe", lambda e: e.matmul(self.ps[pbh[h]][0:64, wi * 64:(wi + 1) * 64], lhsT=OPS[q][h * 64:(h + 1) * 64, li_, :],
                                                                rhs=OPS[q][h * 64:(h + 1) * 64, ri_, :], start=True, stop=True),
                                      reads=[OPSr[q]], writes=[self.psr[pbh[h]]])
                        for h in range(2):
                            kb.op("dve", lambda e: e.tensor_tensor(out=GM[q][:, 0:ng, h, :], in0=self.ps[pbh[h]][0:64, 0:ng * 64].rearrange("p (a c) -> p a c", a=ng),
                                                                   in1=gmask[:, d, 0:ng, h, :], op=ALU.mult),
                                  reads=[self.psr[pbh[h]], cstr], writes=[GMr[q]])
                            kb.op("dve", lambda e: e.tensor_tensor(out=NF[q][:, :, h, :], in0=self.ps[pbh[h]][0:64, 0:128].rearrange("p (a c) -> p a c", a=2),
                                                                   in1=gmask[:, d, 0:2, h, :], op=ALU.mult),
                                  reads=[self.psr[pbh[h]], cstr], writes=[NFr[q]])
                        yield
                        kb.op("pool", lambda e: e.tensor_tensor(out=TT32[q][:, 0, :, :], in0=NF[q][:, 1, :, :], in1=id64f[:], op=ALU.add),
                              reads=[NFr[q], cstr], writes=[TTr[q][0]])
                        Pm = lambda lv, tr: (NF[q][:, tr, :, :] if lv == 0 else PP[q][:, lv % 2, tr, :, :])
                        Pmr = lambda lv: (NFr[q] if lv == 0 else PPr[q][lv % 2])
                        for lv in range(1, 6):
                            pb = nbank()
                            ntr = 2 if lv < 5 else 1
                            for tr in range(ntr):
                                for h in range(2):
                                    lt, rt = (1, 0) if tr == 0 else (0, 1)
                                    kb.op("pe", lambda e: e.matmul(self.ps[pb][0:64, (tr * 2 + h) * 64:(tr * 2 + h + 1) * 64], lhsT=Pm(lv - 1, lt)[:, h, :], rhs=Pm(lv - 1, rt)[:, h, :],
                                                                    start=True, stop=True),
                                          reads=[Pmr(lv - 1)], writes=[self.psr[pb]])
                            kb.op("act", lambda e: e.activation(out=PP[q][:, lv % 2, 0:ntr, :, :].rearrange("p a b c -> p (a b c)"), in_=self.ps[pb][0:64, 0:ntr * 128], func=AF.Copy),
                                  reads=[self.psr[pb]], writes=[PPr[q][lv % 2]])
                            yield
                            pb = nbank()
                            for h in range(2):
                                kb.op("pe", lambda e: e.matmul(self.ps[pb][0:64, h * 64:(h + 1) * 64], lhsT=PP[q][:, lv % 2, 0, h, :], rhs=TT32[q][:, (lv - 1) % 2, h, :], start=True, stop=True),
                                      reads=[PPr[q][lv % 2], TTr[q][(lv - 1) % 2]], writes=[self.psr[pb]])
                            kb.op("dve", lambda e: e.tensor_tensor(out=TT32[q][:, lv % 2, :, :].rearrange("p b c -> p (b c)"), in0=self.ps[pb][0:64, 0:128],
                                                                   in1=TT32[q][:, (lv - 1) % 2, :, :].rearrange("p b c -> p (b c)"), op=ALU.add),
                                  reads=[self.psr[pb], TTr[q][(lv - 1) % 2]], writes=[TTr[q][lv % 2]])
                            yield
                        kb.op("act", lambda e: e.activation(out=TTb[q][:].rearrange("p b c -> p (b c)"), in_=TT32[q][:, 1, :, :].rearrange("p b c -> p (b c)"), func=AF.Copy),
                              reads=[TTr[q][1]], writes=[TTbr[q]])
                        TTf = TTb[q]
                        TTfr = TTbr[q]
                        yield
                        pb = nbank()
                        for h in range(2):
                            kb.op("pe", lambda e: e.matmul(self.ps[pb][0:64, h * 64:(h + 1) * 64], lhsT=TTf[:, h, :], rhs=TOK[q][:, 0, h * 64:(h + 1) * 64], start=True, stop=True),
                                  reads=[TTfr, TOKr[q]], writes=[self.psr[pb]])
                            kb.op("pe", lambda e: e.matmul(self.ps[pb][0:64, 128 + h * 64:128 + (h + 1) * 64], lhsT=GM[q][:, 2, h, :], rhs=TOK[q][:, 3, h * 64:(h + 1) * 64], start=True, stop=True),
                                  reads=[GMr[q], TOKr[q]], writes=[self.psr[pb]])
                        kb.op("act", lambda e: e.activation(out=GS[q][:, 0:2, :, :].rearrange("p a b c -> p (a b c)"), in_=self.ps[pb][0:64, 0:256], func=AF.Copy),
                              reads=[self.psr[pb]], writes=[GSr[q][0], GSr[q][1]])
                        yield
                        pb = nbank()
                        for h in range(2):
                            kb.op("pe", lambda e: e.matmul(self.ps[pb][0:64, h * 64:(h + 1) * 64], lhsT=TTf[:, h, :], rhs=GS[q][:, 1, h, :], start=True, stop=True),
                                  reads=[TTfr, GSr[q][1]], writes=[self.psr[pb]])
                        kb.op("act", lambda e: e.activation(out=GS[q][:, 2, :, :].rearrange("p b c -> p (b c)"), in_=self.ps[pb][0:64, 0:128], func=AF.Copy),
                              reads=[self.psr[pb]], writes=[GSr[q][2]])
                        Gfl = GS[q][:, 0, :, :].rearrange("p b c -> p (b c)")
                        U0fl = GS[q][:, 2, :, :].rearrange("p b c -> p (b c)")
                        yield
                        gcol = EE[q][:, 2, 63:64] if d == 0 else EE[q][:, 2, 0:1]
                        if not lat:
                            kb.op("act", lambda e: e.activation(out=EE[q][:, 2, 0:1], in_=TOT[q][:, 0:1], func=AF.Exp), reads=[TOTr[q], EEr[q]], writes=[EEr[q]])
                            gcol = EE[q][:, 2, 0:1]
                        pb = nbank()
                        kb.op("pe", lambda e: e.matmul(self.ps[pb][:, 0:128], lhsT=Gfl, rhs=TOK[q][:, 1, :], start=True, stop=True),
                              reads=[GSr[q][0], TOKr[q]], writes=[self.psr[pb]])
                        kb.op("dve", lambda e: e.scalar_tensor_tensor(out=PH[q][:], in0=ident, scalar=gcol, in1=self.ps[pb][:, 0:128], op0=ALU.mult, op1=ALU.subtract),
                              reads=[cstr, EEr[q], self.psr[pb]], writes=[PHr[q]])
                        kb.op("pool", lambda e: e.tensor_tensor(out=PH[q][:], in0=PH[q][:], in1=bdmask, op=ALU.mult), reads=[PHr[q], cstr], writes=[PHr[q]])
                        yield
                        if lat:
                            pb = nbank()
                            for h in range(2):
                                kb.op("pe", lambda e: e.matmul(self.ps[pb][:, h * 64:(h + 1) * 64], lhsT=Gfl, rhs=GM[q][:, 3, h, :], start=True, stop=True),
                                      reads=[GSr[q][0], GMr[q]], writes=[self.psr[pb]])
                            for h in range(2):
                                kb.op("dve", lambda e: e.tensor_tensor(out=QH[q][h * 64:(h + 1) * 64, :], in0=RT32[q][h * 64:(h + 1) * 64, :],
                                                                       in1=self.ps[pb][h * 64:(h + 1) * 64, h * 64:(h + 1) * 64], op=ALU.subtract),
                                      reads=[RT32r[q], self.psr[pb]], writes=[QHr[q]])
                            yield
                            pb = nbank()
                            for h in range(2):
                                kb.op("pe", lambda e: e.matmul(self.ps[pb][:, h * 64:(h + 1) * 64], lhsT=U0fl, rhs=GM[q][:, 3, h, :], start=True, stop=False),
                                      reads=[GSr[q][2], GMr[q]], writes=[self.psr[pb]])
                                kb.op("pe", lambda e: e.matmul(self.ps[pb][:, h * 64:(h + 1) * 64], lhsT=TOK[q][:, 3, :], rhs=GM[q][:, 4, h, :], start=False, stop=False),
                                      reads=[TOKr[q], GMr[q]], writes=[self.psr[pb]])
                                kb.op("pe", lambda e: e.matmul(self.ps[pb][:, h * 64:(h + 1) * 64], lhsT=Abf[:], rhs=QH[q][:], start=False, stop=True),
                                      reads=[Abfr, QHr[q]], writes=[self.psr[pb]])
                            for h in range(2):
                                if d == 0:
                                    kb.op("act", lambda e: e.activation(out=YA[h * 64:(h + 1) * 64, c0:c0 + 64], in_=self.ps[pb][h * 64:(h + 1) * 64, h * 64:(h + 1) * 64], func=AF.Copy),
                                          reads=[self.psr[pb]], writes=[YAr[b]])
                                else:
                                    kb.op("dve", lambda e: e.tensor_tensor(out=YA[h * 64:(h + 1) * 64, c0:c0 + 64], in0=self.ps[pb][h * 64:(h + 1) * 64, h * 64:(h + 1) * 64],
                                                                           in1=YA[h * 64:(h + 1) * 64, c0:c0 + 64], op=ALU.add),
                                          reads=[self.psr[pb], YAr[b]], writes=[YAr[b]])
                        yield
                        pb = nbank()
                        kb.op("pe", lambda e: e.matmul(self.ps[pb][:, 0:128], lhsT=TOK[q][:, 1, :], rhs=U0fl, start=True, stop=False),
                              reads=[TOKr[q], GSr[q][2]], writes=[self.psr[pb]])
                        kb.op("pe", lambda e: e.matmul(self.ps[pb][:, 0:128], lhsT=TOK[q][:, 2, :], rhs=TOK[q][:, 3, :], start=False, stop=False),
                              reads=[TOKr[q]], writes=[self.psr[pb]])
                        kb.op("pe", lambda e: e.matmul(self.ps[pb][:, 0:128], lhsT=PH[q][:], rhs=Abd[:], start=False, stop=True),
                              reads=[PHr[q], Ar], writes=[self.psr[pb]])
                        kb.op("dve", lambda e: e.tensor_tensor(out=Abd[:], in0=self.ps[pb][:, 0:128], in1=bdmask, op=ALU.mult),
                              reads=[self.psr[pb], cstr], writes=[Ar])
                        kb.op("pool", lambda e: e.tensor_copy(out=Abf[:], in_=Abd[:]), reads=[Ar], writes=[Abfr])
                    active = []
                    pending = list(enumerate(order))
                    while pending or active:
                        if pending and len(active) < G:
                            active.append(chunk_gen(*pending.pop(0)))
                        for g_ in list(active):
                            try:
                                next(g_)
                            except StopIteration:
                                active.remove(g_)
                    if dbg == f"rwdump{pr}_{d}":
                        dd = kb.dsem("dbg")
                        self.d_dbg = self.nc.dram_tensor("dbgout", [8, 128, T], F32, kind="ExternalOutput").ap()
                        kb.dma("sp", dd, self.d_dbg[0][:, 0:SEQ], YA[:], reads=YAr)
                        kb.dma("sp", dd, self.d_dbg[1], LW[:], reads=LWr)
                        kb.dma("pool", dd, self.d_dbg[2], KKn[:], reads=KKr)
                        kb.dma("pool", dd, self.d_dbg[3], KD[:], reads=KDr)
                        kb.dma("pool", dd, self.d_dbg[4], BD[:], reads=BDr)
                        kb.dma("pool", dd, self.d_dbg[5][:, 0:SEQ], Rr_[:], reads=Rr)
                        kb.dma("pool", dd, self.d_dbg[6], VTf[:], reads=VTr)
                        kb.dma("sp", dd, self.d_dbg[7][:, 0:128], Abd[:], reads=[Ar])
                        kb.eng["sp"].wait_ge(dd.sem, dd.cnt)
                        kb.eng["pool"].wait_ge(dd.sem, dd.cnt)
                        kb.barrier()
                        return 'stop'
                for b in lblocks:
                    t0, n, kind = BLOCKS[b]
                    s_ = b % 2
                    kb.op("act", lambda e: e.activation(out=tB[s_][:, :n], in_=YA[:, t0:t0 + n], func=AF.Copy), reads=[YAr[b]], writes=[tBr[s_]])
                    pm = nbank()
                    kb.op("pe", lambda e: e.matmul(self.ps[pm][:, :n], lhsT=ones64[:], rhs=tB[s_][:, :n], start=True, stop=True), reads=[cstr, tBr[s_]], writes=[self.psr[pm]])
                    kb.op("act", lambda e: e.activation(out=tB[s_][:, :n], in_=YA[:, t0:t0 + n], func=AF.Square), reads=[YAr[b]], writes=[tBr[s_]])
                    pq = nbank()
                    kb.op("pe", lambda e: e.matmul(self.ps[pq][:, :n], lhsT=ones64[:], rhs=tB[s_][:, :n], start=True, stop=True), reads=[cstr, tBr[s_]], writes=[self.psr[pq]])
                    kb.op("act", lambda e: e.activation(out=tA[s_][:, :n], in_=self.ps[pm][:, :n], func=AF.Square), reads=[self.psr[pm]], writes=[tAr[s_]])
                    kb.op("dve", lambda e: e.tensor_tensor(out=tA[s_][:, :n], in0=self.ps[pq][:, :n], in1=tA[s_][:, :n], op=ALU.subtract), reads=[self.psr[pq], tAr[s_]], writes=[tAr[s_]])
                    kb.op("act", lambda e: e.activation(out=tA[s_][:, :n], in_=tA[s_][:, :n], func=AF.Ln, bias=epsg[:, 0:1]), reads=[tAr[s_], cstr], writes=[tAr[s_]])
                    kb.op("act", lambda e: e.activation(out=tA[s_][:, :n], in_=tA[s_][:, :n], func=AF.Exp, scale=-0.5), reads=[tAr[s_]], writes=[tAr[s_]])
                    kb.op("dve", lambda e: e.tensor_tensor(out=YA[:, t0:t0 + n], in0=YA[:, t0:t0 + n], in1=self.ps[pm][:, :n], op=ALU.subtract), reads=[YAr[b], self.psr[pm]], writes=[YAr[b]])
                    kb.op("pool", lambda e: e.tensor_tensor(out=YA[:, t0:t0 + n], in0=YA[:, t0:t0 + n], in1=tA[s_][:, :n], op=ALU.mult), reads=[YAr[b], tAr[s_]], writes=[YAr[b]])
                    kb.op("act", lambda e: e.activation(out=YA[:, t0:t0 + n], in_=YA[:, t0:t0 + n], func=AF.Identity, scale=prm[:, GNG + pr:GNG + pr + 1], bias=prm[:, GNB + pr:GNB + pr + 1]),
                          reads=[YAr[b], prmr], writes=[YAr[b]])
                    kb.op("dve", lambda e: e.tensor_tensor(out=BON[:, t0:t0 + n], in0=BON[:, t0:t0 + n], in1=VTf[:, t0:t0 + n], op=ALU.mult), reads=[BONr[b], VTr[b]], writes=[BONr[b]])
                    kb.op("pool", lambda e: e.tensor_tensor(out=YA[:, t0:t0 + n], in0=YA[:, t0:t0 + n], in1=BON[:, t0:t0 + n], op=ALU.add), reads=[YAr[b], BONr[b]], writes=[YAr[b]])
                    kb.op("dve", lambda e: e.tensor_tensor(out=ZT1[:, t0:t0 + n], in0=YA[:, t0:t0 + n], in1=Gg[:, t0:t0 + n], op=ALU.mult), reads=[YAr[b], Ggr[b]], writes=[ZT1r[b]])
                kb.dma("sp", ds_z, d_zpark[pr], ZT1[:], reads=ZT1r)
                kb.barrier()
        self.unpark_h()
        with ExitStack() as es:
            ZT = kb.sb(es, "rwZT", [128, NC_, SEQ], BF16)
            ZTr = [[Reg() for _ in range(5)] for _ in range(NC_)]
            for c in range(NC_):
                kb.dma("sp", ds_z, ZT[:, c, :], d_zpark[c], writes=ZTr[c])
            self.proj(es, self.d_rwwo, range(NC_), ZT, ZTr, self.resid_evac(j), lblocks, tag="rwwo")
            kb.barrier()
        self.layernorm(L, j, lblocks)

    def _nb(self):
        self.psrot += 1
        return self.psrot % 8

def _prep_common(inp):
    f = np.float32
    out = {}
    aw = np.asarray(inp["ada_w"], f).reshape(DEPTH, NC_, 128, 72, 128)
    out["adaw"] = np.ascontiguousarray(aw.transpose(0, 3, 2, 1, 4)).reshape(DEPTH, 72, 128, NC_ * 128)
    out["adab"] = np.ascontiguousarray(np.asarray(inp["ada_b"], f).reshape(DEPTH, 72, 128).transpose(0, 2, 1))
    out["lng"] = np.ascontiguousarray(np.asarray(inp["ln_g"], f).reshape(DEPTH * 3 * NC_, 128).T)
    out["lnb"] = np.ascontiguousarray(np.asarray(inp["ln_b"], f).reshape(DEPTH * 3 * NC_, 128).T)
    w13 = np.asarray(inp["ffn_w13"], f).reshape(DEPTH, 2, NC_, 128, 2, NF, 128)
    out["w13"] = np.ascontiguousarray(w13.transpose(0, 1, 5, 3, 4, 2, 6)).reshape(DEPTH, 2, NF, 128, 2 * NC_ * 128)
    w2 = np.asarray(inp["ffn_w2"], f).reshape(DEPTH, 2, NF, 128, NC_, 128)
    out["w2"] = np.ascontiguousarray(w2.transpose(0, 1, 4, 3, 2, 5)).reshape(DEPTH, 2, NC_, 128, NF * 128)
    out["ident"] = np.eye(128, dtype=f)
    fm = lambda v: np.asarray(v, f).reshape(-1, 128).T
    wl = lambda W: np.ascontiguousarray(np.asarray(W, f).reshape(W.shape[0] // 128, 128, W.shape[1] // 128, 128)
                                        .transpose(2, 1, 0, 3)).reshape(W.shape[1] // 128, 128, W.shape[0])
    w1 = np.asarray(inp["cv_w1"][0], f).reshape(NC_, 128, 2, NC_, 128)
    out["cvw1"] = np.ascontiguousarray(w1.transpose(3, 1, 2, 0, 4)).reshape(NC_, 128, 2 * NC_ * 128)
    out["cvw2"] = wl(inp["cv_w2"][0])
    wdw = np.asarray(inp["cv_wdw"][0], f).reshape(31, NC_, 128).transpose(2, 0, 1).reshape(128, 31 * NC_)
    out["cvprm"] = np.ascontiguousarray(np.concatenate(
        [fm(inp["cv_b1"][0]), fm(inp["cv_bdw"][0]), fm(inp["cv_ln_g"][0]), fm(inp["cv_ln_b"][0]), fm(inp["cv_b2"][0]), wdw], axis=1))
    out["nawqkv"] = wl(inp["na_wqkv"][0])
    out["nawo"] = wl(inp["na_wo"][0])
    out["nastrip"] = _na_strips(np.asarray(inp["na_rpb"][0], f))
    win = np.asarray(inp["gla_win"][0], f)
    wlq = wl(win)
    out["glawqk"] = np.ascontiguousarray(wlq[0:8])
    m = np.arange(128)
    partner = (m // 64) * 64 + ((m % 64) + 32) % 64
    qk = win[:, 0:1024].reshape(D, 8, 128)[:, :, partner].reshape(D, 1024)
    out["glawqkp"] = wl(qk)
    wv = win[:, 1024:2048].reshape(NC_, 128, 4, 256)
    out["glawv"] = np.ascontiguousarray(wv.transpose(2, 1, 0, 3)).reshape(4, 128, NC_ * 256)
    out["glawg"] = np.ascontiguousarray(wlq[16:24])
    wo = np.asarray(inp["gla_wo"][0], f).reshape(4, 2, 128, NC_, 128)
    out["glawo"] = np.ascontiguousarray(wo.transpose(0, 3, 2, 1, 4)).reshape(4, NC_, 128, 256)
    wa1 = np.asarray(inp["gla_wa1"][0], f)
    wa1c = np.concatenate([wa1[0], wa1[1]], axis=1).reshape(NC_, 128, 32)
    out["glawa1"] = np.ascontiguousarray(wa1c.transpose(1, 0, 2)).reshape(128, NC_ * 32)
    wa2 = np.asarray(inp["gla_wa2"][0], f)
    wa2p = np.zeros((32, 2, 512), f)
    wa2p[0:16, 0] = wa2[0]
    wa2p[16:32, 1] = wa2[1]
    out["glawa2"] = wa2p.reshape(32, 1024)
    i = (m % 64) % 32
    freq = (10000.0 ** (-(i.astype(np.float64)) / 32.0))[:, None]
    t = np.arange(SEQ)[None, :]
    pos = np.where((m < 64)[:, None], t // 64, t % 64)
    ang = (pos.astype(np.float32) * freq.astype(np.float32)).astype(np.float32)
    sgn = np.where((m % 64) < 32, -1.0, 1.0)[:, None]
    out["glarope"] = np.stack([np.cos(ang), sgn * np.sin(ang)]).astype(f)
    sl = np.arange(64)[:, None]
    cc = np.arange(64)[None, :]
    masks = np.zeros((128, 4, 64), f)
    for par in range(2):
        masks[par * 64:(par + 1) * 64, par * 2 + 0] = (cc >= sl)
        masks[par * 64:(par + 1) * 64, par * 2 + 1] = (cc <= sl)
    ba = np.asarray(inp["gla_ba"][0], f).reshape(2, 4, 128).transpose(2, 0, 1).reshape(128, 8)
    out["glacst"] = np.ascontiguousarray(np.concatenate(
        [masks.reshape(128, 256), np.eye(128, dtype=f), fm(inp["gla_norm_g"][0]), ba], axis=1))
    out["rwwrkv"] = np.stack([wl(inp["rw_wrkv"][0][i]) for i in range(3)])
    out["rwwo"] = wl(inp["rw_wo"][0])
    cat2 = lambda a: np.concatenate([np.asarray(a[0], f), np.asarray(a[1], f)], axis=1)
    out["rwlr"] = np.stack([wl(np.asarray(inp["rw_g1"][0], f))[0], wl(cat2(inp["rw_w1"][0]))[0], wl(cat2(inp["rw_a1"][0]))[0]])
    g2 = np.asarray(inp["rw_g2"][0], f).reshape(128, NC_, 128)
    w2 = np.concatenate([np.asarray(inp["rw_w2"][0][0], f), np.asarray(inp["rw_w2"][0][1], f)], axis=0).reshape(128, NC_, 128)
    a2 = np.concatenate([np.asarray(inp["rw_a2"][0][0], f), np.asarray(inp["rw_a2"][0][1], f)], axis=0).reshape(128, NC_, 128)
    out["rwlrw"] = np.ascontiguousarray(np.stack([g2, w2, a2]).transpose(2, 0, 1, 3))
    out["rwprm"] = np.ascontiguousarray(np.concatenate(
        [fm(inp["rw_w0"][0]), fm(inp["rw_a0"][0]), fm(inp["rw_kk"][0]), fm(inp["rw_ka"][0]), fm(inp["rw_rk"][0]),
         fm(inp["rw_gn_g"][0]), fm(inp["rw_gn_b"][0]), fm(inp["rw_mu"][0])], axis=1))
    bd = np.zeros((128, 128), f)
    bd[:64, :64] = 1.0
    bd[64:, 64:] = 1.0
    out["rwcstf"] = np.ascontiguousarray(np.concatenate([np.eye(128, dtype=f), bd], axis=1))
    pi = np.arange(64)[:, None]
    fi = np.arange(64)[None, :]
    gm = np.zeros((64, 2, 5, 2, 64), f)
    for d_ in range(2):
        lt = (fi < pi) if d_ == 0 else (fi > pi)
        st = (pi < fi) if d_ == 0 else (pi > fi)
        se = (pi <= fi) if d_ == 0 else (pi >= fi)
        for h_ in range(2):
            gm[:, d_, 0, h_] = -1.0 * lt
            gm[:, d_, 1, h_] = -1.0 * st
            gm[:, d_, 2, h_] = -1.0 * st
            gm[:, d_, 3, h_] = 1.0 * se
            gm[:, d_, 4, h_] = 1.0 * se
    out["rwgmask"] = np.ascontiguousarray(gm.reshape(64, -1))
    return out


def _na_strips(rpb):
    MASK = np.float32(-30000.0)
    kc = np.arange(64)[:, None]
    qc = np.arange(64)[None, :]
    cstart = np.clip(qc - 8, 0, 48)
    col_ok = (kc >= cstart) & (kc < cstart + 16)
    dc_idx = np.clip(kc - qc + 15, 0, 30)
    R = np.where(col_ok[None, None], rpb[:, :, dc_idx], MASK)
    maskblk = np.full((16, 64, 64), MASK, np.float32)
    strip = np.empty((16, 2, 64, 37, 64), np.float32)
    for half in range(2):
        for jj in range(23):
            d = 11 - jj + half
            strip[:, half, :, jj, :] = R[:, d + 7] if -4 <= d <= 3 else maskblk
        for jj in range(14):
            d = 6 - jj + half
            strip[:, half, :, 23 + jj, :] = R[:, d + 7] if -7 <= d <= 7 else maskblk
    return np.ascontiguousarray(strip.reshape(16, 128, 37 * 64))


def _prep_core(inp, b):
    f = np.float32
    xb = np.concatenate([np.asarray(inp["x"][b], f), np.asarray(inp["ctx"][b], f)], axis=0)
    hin = np.ascontiguousarray(xb.T).reshape(NC_, 128, T)
    cond = np.stack([np.asarray(inp["c"][b], f), np.asarray(inp["c_ctx"], f)], axis=-1)
    cond = np.ascontiguousarray(cond.reshape(NC_, 128, 2).transpose(1, 0, 2))
    return {"hin": hin, "cond": cond}


def build_full():
    p = Prog(list(range(DEPTH)))
    p.setup()
    mixers = [p.na_mixer, p.conv_mixer, p.gla_mixer, p.rwkv_mixer]
    for L in range(DEPTH):
        last = L == DEPTH - 1
        p.ada(L)
        p.ffn(L, 0, 0)
        mixers[L % 4](L, last)
        p.ffn(L, 2, 1, blocks=([0, 1, 2, 3] if last else range(5)))
    p.store()
    p.es_H.close()
    p.kb.es.close()
    return p


def kernel(**inputs):
    common = _prep_common(inputs)
    p = build_full()
    in_maps = []
    for b in range(8):
        m = dict(common)
        m.update(_prep_core(inputs, b))
        in_maps.append(m)
    res = run_bass_kernel_spmd(p.nc, in_maps, core_ids=list(range(8)))
    out = np.stack([np.asarray(r["hout"]).reshape(D, T)[:, :SEQ].T for r in res.results], axis=0)
    return np.ascontiguousarray(out.astype(np.float32))
```

```python
import numpy as np
from contextlib import ExitStack
import concourse.bass as bass
import concourse.mybir as mybir
from concourse.bass_utils import run_bass_kernel_spmd

F32 = mybir.dt.float32
BF16 = mybir.dt.bfloat16
AF = mybir.ActivationFunctionType
ALU = mybir.AluOpType

D = 1024
NC_ = 8
SEQ = 2048
CTX = 256
T = SEQ + CTX
DEPTH = 4
DFF = 2816
NF = DFF // 128
ALPHA = (2.0 * DEPTH) ** 0.25
LN_EPS = 1e-5
EPS_P = LN_EPS / (ALPHA * ALPHA)

BLOCKS = [(0, 512, 0), (512, 512, 0), (1024, 512, 0), (1536, 512, 0), (2048, 256, 1)]
HALVES = [[0, 1], [2, 3, 4]]


class DebugStop(Exception):
    pass


class Reg:
    __slots__ = ("w", "r", "name")

    def __init__(self, name=""):
        self.w = None
        self.r = {}
        self.name = name


class DSem:
    def __init__(self, sem):
        self.sem = sem
        self.cnt = 0


class KB:
    def __init__(self):
        self.nc = bass.Bass("TRN2", target_bir_lowering=False)
        nc = self.nc
        self.es = ExitStack()
        self.eng = {"pe": nc.tensor, "dve": nc.vector, "act": nc.scalar, "pool": nc.gpsimd, "sp": nc.sync}
        self.sem = {e: self.es.enter_context(nc.semaphore("prog_" + e)) for e in self.eng}
        self.cnt = {e: 0 for e in self.eng}
        self.seen = {e: {} for e in self.eng}
        self.dsems = []
        self.n_ins = 0

    def dsem(self, name):
        self.n_ds = getattr(self, "n_ds", 0) + 1
        d = DSem(self.es.enter_context(self.nc.semaphore(f"d_{name}_{self.n_ds}")))
        self.dsems.append(d)
        return d

    def sb(self, es, name, shape, dt):
        self.n_sb = getattr(self, "n_sb", 0) + 1
        return es.enter_context(self.nc.sbuf_tensor(f"{name}_s{self.n_sb}", shape, dt))

    def _wait(self, e, ev):
        if ev is None:
            return
        sem, val = ev[0], ev[1]
        if len(ev) > 2:
            val = max(val, ev[2].cnt)
        if e == "pe" and sem is self.sem["pe"]:
            return
        k = id(sem)
        if self.seen[e].get(k, 0) >= val:
            return
        self.eng[e].wait_ge(sem, val)
        self.seen[e][k] = val

    def _deps(self, e, reads, writes):
        for r in reads:
            self._wait(e, r.w)
        for r in writes:
            self._wait(e, r.w)
            for ev in r.r.values():
                self._wait(e, ev)

    def op(self, e, fn, reads=(), writes=()):
        self._deps(e, reads, writes)
        ins = fn(self.eng[e])
        self.cnt[e] += 1
        self.n_ins += 1
        ins.then_inc(self.sem[e], 1)
        ev = (self.sem[e], self.cnt[e])
        for r in reads:
            r.r[id(ev[0])] = ev
        for r in writes:
            r.w = ev
            r.r = {}
        return ev

    def dma(self, q, ds, out, in_, reads=(), writes=()):
        self._deps(q, reads, writes)
        if ds.cnt > 0:
            self._wait(q, (ds.sem, ds.cnt, ds))
        ins = self.eng[q].dma_start(out=out, in_=in_)
        ds.cnt += 16
        self.n_ins += 1
        ins.then_inc(ds.sem, 16)
        ev = (ds.sem, ds.cnt, ds)
        for r in reads:
            r.r[id(ev[0])] = ev
        for r in writes:
            r.w = ev
            r.r = {}
        return ev

    def barrier(self):
        for e in self.eng:
            for e2 in self.eng:
                if e2 != e and self.cnt[e2] > 0:
                    self._wait(e, (self.sem[e2], self.cnt[e2]))
            for d in self.dsems:
                if d.cnt > 0:
                    self._wait(e, (d.sem, d.cnt))


def midx(j, k, c):
    return j * 24 + k * 8 + c


class Prog:
    def __init__(self, layers, final_layer_is_last=True, load_h=True):
        self.kb = KB()
        kb = self.kb
        nc = kb.nc
        self.nc = nc
        es = kb.es
        self.layers = layers
        dr = lambda name, shape, dt=F32, kind="ExternalInput": nc.dram_tensor(name, shape, dt, kind=kind).ap()
        self.d_hin = dr("hin", [NC_, 128, T])
        self.d_hout = dr("hout", [NC_, 128, T], kind="ExternalOutput")
        self.d_cond = dr("cond", [128, NC_, 2])
        self.d_adaw = dr("adaw", [DEPTH, 72, 128, NC_ * 128])
        self.d_adab = dr("adab", [DEPTH, 128, 72])
        self.d_lng = dr("lng", [128, DEPTH * 3 * NC_])
        self.d_lnb = dr("lnb", [128, DEPTH * 3 * NC_])
        self.d_w13 = dr("w13", [DEPTH, 2, NF, 128, 2 * NC_ * 128])
        self.d_w2 = dr("w2", [DEPTH, 2, NC_, 128, NF * 128])
        self.d_ident = dr("ident", [128, 128])
        self.d_rwprm = dr("rwprm", [128, 120])
        self.d_rwcstf = dr("rwcstf", [128, 256])
        self.d_rwgmask = dr("rwgmask", [64, 2 * 5 * 2 * 64])
        self.d_rwlr = dr("rwlr", [3, 128, NC_ * 128])
        self.d_rwlrw = dr("rwlrw", [NC_, 3, 128, 128])
        self.d_rwwrkv = dr("rwwrkv", [3, NC_, 128, NC_ * 128])
        self.d_rwwo = dr("rwwo", [NC_, 128, NC_ * 128])
        self.d_glawqk = dr("glawqk", [8, 128, NC_ * 128])
        self.d_glawqkp = dr("glawqkp", [8, 128, NC_ * 128])
        self.d_glawv = dr("glawv", [4, 128, NC_ * 256])
        self.d_glawg = dr("glawg", [8, 128, NC_ * 128])
        self.d_glawo = dr("glawo", [4, NC_, 128, 2 * 128])
        self.d_glawa1 = dr("glawa1", [128, NC_ * 32])
        self.d_glawa2 = dr("glawa2", [32, 2 * 512])
        self.d_glarope = dr("glarope", [2, 128, SEQ])
        self.d_glacst = dr("glacst", [128, 4 * 64 + 128 + 2 + 8])
        self.d_nawqkv = dr("nawqkv", [3 * NC_, 128, NC_ * 128])
        self.d_nawo = dr("nawo", [NC_, 128, NC_ * 128])
        self.d_nastrip = dr("nastrip", [16, 128, 37 * 64])
        self.d_cvw1 = dr("cvw1", [NC_, 128, 2 * NC_ * 128])
        self.d_cvw2 = dr("cvw2", [NC_, 128, NC_ * 128])
        self.d_cvprm = dr("cvprm", [128, 6 * NC_ + 31 * NC_])
        self.U = kb.sb(es, "U", [128, NC_, T], BF16)
        self.Hr = [[Reg(f"H{c}_{b}") for b in range(5)] for c in range(NC_)]
        self.Ur = [[Reg(f"U{c}_{b}") for b in range(5)] for c in range(NC_)]
        self.P = kb.sb(es, "P", [128, 72, 2], F32)
        self.Pr = Reg("P")
        self.condT = kb.sb(es, "condT", [128, NC_, 2], F32)
        self.condr = Reg("cond")
        self.lng = kb.sb(es, "lng", [128, DEPTH * 3 * NC_], F32)
        self.lnb = kb.sb(es, "lnb", [128, DEPTH * 3 * NC_], F32)
        self.lnr = Reg("ln")
        self.ones_bf = kb.sb(es, "ones_bf", [128, 128], BF16)
        self.onesr = Reg("ones")
        self.epsP = kb.sb(es, "epsP", [128, 1], F32)
        self.onesf = kb.sb(es, "onesf", [128, 64], F32)
        self.ps = [es.enter_context(nc.psum_tensor(f"ps{i}", [128, 512], F32)) for i in range(8)]
        self.psr = [Reg(f"ps{i}") for i in range(8)]
        self.ds_misc = kb.dsem("misc")
        self.ds_miscp = kb.dsem("miscp")
        self.ds_out = kb.dsem("out")
        self.ds_h = [kb.dsem(f"h{i}") for i in range(4)]
        self.psrot = 0
        self.es_H = ExitStack()
        self.H = kb.sb(self.es_H, "H", [128, NC_, T], F32)

    def hregs(self, c, t0, n):
        return [self.Hr[c][b] for b, (bt, bn, _) in enumerate(BLOCKS) if bt < t0 + n and t0 < bt + bn]

    def uregs(self, c, t0, n):
        return [self.Ur[c][b] for b, (bt, bn, _) in enumerate(BLOCKS) if bt < t0 + n and t0 < bt + bn]

    def setup(self):
        kb = self.kb
        kb.dma("sp", self.ds_misc, self.condT[:], self.d_cond, writes=[self.condr])
        kb.dma("sp", self.ds_misc, self.lng[:], self.d_lng, writes=[self.lnr])
        kb.dma("sp", self.ds_misc, self.lnb[:], self.d_lnb, writes=[self.lnr])
        for c in range(NC_):
            kb.dma("sp", self.ds_h[c % 4], self.H[:, c, :], self.d_hin[c], writes=self.Hr[c])
        kb.op("dve", lambda e: e.memset(self.ones_bf[:], 1.0 / 1024.0), writes=[self.onesr])
        kb.op("dve", lambda e: e.memset(self.epsP[:], EPS_P), writes=[self.onesr])
        kb.op("dve", lambda e: e.memset(self.onesf[:], 1.0), writes=[self.onesr])
        kb.op("act", lambda e: e.activation(out=self.condT[:], in_=self.condT[:], func=AF.Silu),
              reads=[self.condr], writes=[self.condr])

    def store(self):
        kb = self.kb
        for c in range(NC_):
            kb.dma("sp", self.ds_h[c % 4], self.d_hout[c], self.H[:, c, :], reads=self.Hr[c])
        for d_ in self.ds_h:
            kb.eng["sp"].wait_ge(d_.sem, d_.cnt)

    def ada(self, L):
        kb = self.kb
        kb.barrier()
        with ExitStack() as es:
            NG = 4
            wst = [kb.sb(es, f"adaw{s}", [128, NG, NC_ * 128], BF16) for s in range(2)]
            condb = kb.sb(es, "condb", [128, NC_, 2], BF16)
            kb.op("dve", lambda e: e.tensor_copy(out=condb[:], in_=self.condT[:]), reads=[self.condr], writes=[self.condr])
            wr = [Reg(), Reg()]
            wds = [kb.dsem(f"adaw{L}_{s}") for s in range(2)]
            bia = kb.sb(es, "adab", [128, 72], F32)
            br = Reg()
            kb.dma("sp", self.ds_misc, bia[:], self.d_adab[L], writes=[br])
            pst = self.ps[0]
            psreg = self.psr[0]
            for g in range(72 // NG):
                s = g % 2
                kb.dma("pool", wds[s], wst[s][:], self.d_adaw[L, g * NG:(g + 1) * NG].rearrange("o p f -> p o f"),
                       writes=[wr[s]])
                for o in range(NG):
                    oc = g * NG + o
                    for kc in range(NC_):
                        kb.op("pe", lambda e: e.matmul(pst[:, oc * 2:oc * 2 + 2], lhsT=wst[s][:, o, kc * 128:(kc + 1) * 128],
                                                        rhs=condb[:, kc, :], start=(kc == 0), stop=(kc == NC_ - 1)),
                              reads=[wr[s], self.condr], writes=[psreg])
            for kind in range(2):
                kb.op("dve", lambda e: e.tensor_tensor(out=self.P[:, :, kind], in0=pst[:, 0:144].rearrange("p (o k) -> p o k", k=2)[:, :, kind],
                                                       in1=bia[:], op=ALU.add),
                      reads=[psreg, br], writes=[self.Pr])
            for j in range(3):
                wj = (1.0 if j == 1 else 0.5) / ALPHA
                kb.op("dve", lambda e: e.tensor_scalar(out=self.P[:, midx(j, 1, 0):midx(j, 1, 0) + 8, :], in0=self.P[:, midx(j, 1, 0):midx(j, 1, 0) + 8, :],
                                                       scalar1=1.0, scalar2=None, op0=ALU.add),
                      reads=[self.Pr], writes=[self.Pr])
                kb.op("dve", lambda e: e.tensor_scalar(out=self.P[:, midx(j, 2, 0):midx(j, 2, 0) + 8, :], in0=self.P[:, midx(j, 2, 0):midx(j, 2, 0) + 8, :],
                                                       scalar1=wj, scalar2=None, op0=ALU.mult),
                      reads=[self.Pr], writes=[self.Pr])
            kb.barrier()

    def modulate(self, j, blocks=range(5)):
        kb = self.kb
        for c in range(NC_):
            for b in blocks:
                t0, n, kind = BLOCKS[b]
                kb.op("act", lambda e: e.activation(out=self.U[:, c, t0:t0 + n], in_=self.H[:, c, t0:t0 + n], func=AF.Identity,
                                                    scale=self.P[:, midx(j, 1, c), kind:kind + 1],
                                                    bias=self.P[:, midx(j, 0, c), kind:kind + 1]),
                      reads=[self.Hr[c][b], self.Pr], writes=[self.Ur[c][b]])

    def ffn(self, L, j, kidx, blocks=range(5)):
        kb = self.kb
        blocks = list(blocks)
        self.modulate(j, blocks)
        with ExitStack() as es:
            G = kb.sb(es, "G", [128, NF, 1280], BF16)
            wA = [kb.sb(es, f"wA{s}", [128, 2, NC_, 128], BF16) for s in range(2)]
            wAr = [Reg(), Reg()]
            wAd = [kb.dsem(f"wA{L}{j}{s}") for s in range(2)]
            wB = [kb.sb(es, f"wB{s}", [128, NF, 128], BF16) for s in range(2)]
            wBr = [Reg(), Reg()]
            wBd = [kb.dsem(f"wB{L}{j}{s}") for s in range(2)]
            sa = [kb.sb(es, f"sa{s}", [128, 512], BF16) for s in range(2)]
            sar = [Reg(), Reg()]
            issuedA, issuedB = set(), set()

            def loadA(hi_, jf_):
                if (hi_, jf_) in issuedA:
                    return
                issuedA.add((hi_, jf_))
                s_ = jf_ % 2
                kb.dma("pool", wAd[s_], wA[s_][:].rearrange("p s k m -> p (s k m)"), self.d_w13[L, kidx, jf_], writes=[wAr[s_]])

            def loadB(hi_, dc_):
                if (hi_, dc_) in issuedB:
                    return
                issuedB.add((hi_, dc_))
                s_ = dc_ % 2
                kb.dma("pool", wBd[s_], wB[s_][:].rearrange("p f m -> p (f m)"), self.d_w2[L, kidx, dc_], writes=[wBr[s_]])

            live = [hi_ for hi_, half_ in enumerate(HALVES) if any(b in blocks for b in half_)]
            for hi, half in enumerate(HALVES):
                blks = [b for b in half if b in blocks]
                if not blks:
                    continue
                nxt = [h_ for h_ in live if h_ > hi]
                base = BLOCKS[blks[0]][0]
                Gr = [[Reg() for _ in blks] for _ in range(NF)]
                cntA = 0
                for jf in range(NF):
                    s = jf % 2
                    loadA(hi, jf)
                    if jf == 1:
                        loadB(hi, 0)
                        loadB(hi, 1)
                    for bi, b in enumerate(blks):
                        t0, n, kind = BLOCKS[b]
                        pa = (cntA % 4) * 2
                        pu = pa + 1
                        ss = cntA % 2
                        cntA += 1
                        for (pb, si) in ((pa, 0), (pu, 1)):
                            for kc in range(NC_):
                                kb.op("pe", lambda e: e.matmul(self.ps[pb][:, :n], lhsT=wA[s][:, si, kc, :], rhs=self.U[:, kc, t0:t0 + n],
                                                                start=(kc == 0), stop=(kc == NC_ - 1)),
                                      reads=[wAr[s], self.Ur[kc][b]], writes=[self.psr[pb]])
                        kb.op("act", lambda e: e.activation(out=sa[ss][:, :n], in_=self.ps[pa][:, :n], func=AF.Silu),
                              reads=[self.psr[pa]], writes=[sar[ss]])
                        kb.op("dve", lambda e: e.tensor_tensor(out=G[:, jf, t0 - base:t0 - base + n], in0=sa[ss][:, :n], in1=self.ps[pu][:, :n], op=ALU.mult),
                              reads=[sar[ss], self.psr[pu]], writes=[Gr[jf][bi]])
                cntB = 0
                for dc in range(NC_):
                    s = dc % 2
                    loadB(hi, dc)
                    if dc == 1 and nxt:
                        loadA(nxt[0], 0)
                        loadA(nxt[0], 1)
                    for bi, b in enumerate(blks):
                        t0, n, kind = BLOCKS[b]
                        pb = cntB % 8
                        cntB += 1
                        for fc in range(NF):
                            kb.op("pe", lambda e: e.matmul(self.ps[pb][:, :n], lhsT=wB[s][:, fc, :], rhs=G[:, fc, t0 - base:t0 - base + n],
                                                            start=(fc == 0), stop=(fc == NF - 1)),
                                  reads=[wBr[s], Gr[fc][bi]], writes=[self.psr[pb]])
                        kb.op("dve", lambda e: e.scalar_tensor_tensor(out=self.H[:, dc, t0:t0 + n], in0=self.ps[pb][:, :n],
                                                                      scalar=self.P[:, midx(j, 2, dc), kind:kind + 1],
                                                                      in1=self.H[:, dc, t0:t0 + n], op0=ALU.mult, op1=ALU.add),
                              reads=[self.psr[pb], self.Pr, self.Hr[dc][b]], writes=[self.Hr[dc][b]])
                kb.barrier()
        self.layernorm(L, j, blocks)

    def layernorm(self, L, j, blocks=range(5)):
        kb = self.kb
        goff = (L * 3 + j) * NC_
        with ExitStack() as es:
            zb = [kb.sb(es, f"zb{s}", [128, NC_, 512], BF16) for s in range(2)]
            z2 = [kb.sb(es, f"z2{s}", [128, NC_, 512], BF16) for s in range(2)]
            zr = [[Reg() for _ in range(NC_)] for _ in range(2)]
            z2r = [[Reg() for _ in range(NC_)] for _ in range(2)]
            msq = [kb.sb(es, f"msq{s}", [128, 512], F32) for s in range(2)]
            rstd = [kb.sb(es, f"rstd{s}", [128, 512], F32) for s in range(2)]
            tmp = [kb.sb(es, f"lntmp{s}", [128, 512], F32) for s in range(4)]
            msqr = [Reg(), Reg()]
            rstdr = [Reg(), Reg()]
            tmpr = [Reg() for _ in range(4)]
            tcnt = 0
            for bi, b in enumerate(blocks):
                t0, n, kind = BLOCKS[b]
                s = bi % 2
                pm = (bi % 4) * 2
                pq = pm + 1
                for c in range(NC_):
                    kb.op("dve", lambda e: e.tensor_copy(out=zb[s][:, c, :n], in_=self.H[:, c, t0:t0 + n]),
                          reads=[self.Hr[c][b]], writes=[zr[s][c]])
                    kb.op("act", lambda e: e.activation(out=z2[s][:, c, :n], in_=self.H[:, c, t0:t0 + n], func=AF.Square),
                          reads=[self.Hr[c][b]], writes=[z2r[s][c]])
                for c in range(NC_):
                    kb.op("pe", lambda e: e.matmul(self.ps[pm][:, :n], lhsT=self.ones_bf[:], rhs=zb[s][:, c, :n],
                                                    start=(c == 0), stop=(c == NC_ - 1)),
                          reads=[self.onesr, zr[s][c]], writes=[self.psr[pm]])
                for c in range(NC_):
                    kb.op("pe", lambda e: e.matmul(self.ps[pq][:, :n], lhsT=self.ones_bf[:], rhs=z2[s][:, c, :n],
                                                    start=(c == 0), stop=(c == NC_ - 1)),
                          reads=[self.onesr, z2r[s][c]], writes=[self.psr[pq]])
                kb.op("act", lambda e: e.activation(out=msq[s][:, :n], in_=self.ps[pm][:, :n], func=AF.Square),
                      reads=[self.psr[pm]], writes=[msqr[s]])
                kb.op("dve", lambda e: e.tensor_tensor(out=msq[s][:, :n], in0=self.ps[pq][:, :n], in1=msq[s][:, :n], op=ALU.subtract),
                      reads=[self.psr[pq], msqr[s]], writes=[msqr[s]])
                kb.op("act", lambda e: e.activation(out=msq[s][:, :n], in_=msq[s][:, :n], func=AF.Ln, bias=self.epsP[:, 0:1]),
                      reads=[msqr[s], self.onesr], writes=[msqr[s]])
                kb.op("act", lambda e: e.activation(out=rstd[s][:, :n], in_=msq[s][:, :n], func=AF.Exp, scale=-0.5),
                      reads=[msqr[s]], writes=[rstdr[s]])
                for c in range(NC_):
                    ts = tcnt % 4
                    tcnt += 1
                    kb.op("dve", lambda e: e.tensor_tensor(out=tmp[ts][:, :n], in0=self.H[:, c, t0:t0 + n], in1=self.ps[pm][:, :n], op=ALU.subtract),
                          reads=[self.Hr[c][b], self.psr[pm]], writes=[tmpr[ts]])
                    kb.op("pool", lambda e: e.tensor_tensor(out=tmp[ts][:, :n], in0=tmp[ts][:, :n], in1=rstd[s][:, :n], op=ALU.mult),
                          reads=[tmpr[ts], rstdr[s]], writes=[tmpr[ts]])
                    kb.op("act", lambda e: e.activation(out=self.H[:, c, t0:t0 + n], in_=tmp[ts][:, :n], func=AF.Identity,
                                                        scale=self.lng[:, goff + c:goff + c + 1], bias=self.lnb[:, goff + c:goff + c + 1]),
                          reads=[tmpr[ts], self.lnr], writes=[self.Hr[c][b]])
            kb.barrier()


    def proj(self, es, Wd, ocs, src, src_regs, evac, blocks, nk=NC_, tag="pj", src_off=0):
        kb = self.kb
        w = [kb.sb(es, f"{tag}w{s}", [128, nk, 128], BF16) for s in range(2)]
        wr = [Reg(), Reg()]
        wd = [kb.dsem(f"{tag}{s}") for s in range(2)]
        cnt = 0
        for i, oc in enumerate(ocs):
            s = i % 2
            kb.dma("pool", wd[s], w[s][:].rearrange("p k m -> p (k m)"), Wd[oc], writes=[wr[s]])
            for b in blocks:
                t0, n, kind = BLOCKS[b]
                pb = self.psrot % 8
                self.psrot += 1
                for kc in range(nk):
                    kb.op("pe", lambda e: e.matmul(self.ps[pb][:, :n], lhsT=w[s][:, kc, :], rhs=src[:, kc, src_off + t0:src_off + t0 + n],
                                                    start=(kc == 0), stop=(kc == nk - 1)),
                          reads=[wr[s], src_regs[kc][b]], writes=[self.psr[pb]])
                evac(oc, b, pb, t0, n, kind)

    def resid_evac(self, j, bias=None, bias_reg=None, es=None):
        kb = self.kb
        if bias is not None:
            tmpy = [kb.sb(es, f"tmpy{s}", [128, 512], F32) for s in range(2)]
            tmpr = [Reg(), Reg()]
        state = {"c": 0}

        def evac(dc, b, pb, t0, n, kind):
            if bias is not None:
                s = state["c"] % 2
                state["c"] += 1
                kb.op("act", lambda e: e.activation(out=tmpy[s][:, :n], in_=self.ps[pb][:, :n], func=AF.Identity, bias=bias[:, dc:dc + 1]),
                      reads=[self.psr[pb], bias_reg], writes=[tmpr[s]])
                src, sreg = tmpy[s], tmpr[s]
            else:
                src, sreg = self.ps[pb], self.psr[pb]
            kb.op("dve", lambda e: e.scalar_tensor_tensor(out=self.H[:, dc, t0:t0 + n], in0=src[:, :n],
                                                          scalar=self.P[:, midx(j, 2, dc), kind:kind + 1],
                                                          in1=self.H[:, dc, t0:t0 + n], op0=ALU.mult, op1=ALU.add),
                  reads=[sreg, self.Pr, self.Hr[dc][b]], writes=[self.Hr[dc][b]])
        return evac

    def feat_stats(self, es_unused, src_fn, b, n, s, zsq, zsqr, msq, msqr, rstd, rstdr, eps_ap, nchunks=NC_, ones=None):
        kb = self.kb
        ones = self.ones_bf if ones is None else ones
        pm = self.psrot % 8
        pq = (self.psrot + 1) % 8
        self.psrot += 2
        for c in range(nchunks):
            ap, rg = src_fn(c)
            kb.op("act", lambda e: e.activation(out=zsq[s][:, c, :n], in_=ap, func=AF.Square),
                  reads=[rg], writes=[zsqr[s][c]])
        for c in range(nchunks):
            ap, rg = src_fn(c)
            kb.op("pe", lambda e: e.matmul(self.ps[pm][:, :n], lhsT=ones[:], rhs=ap, start=(c == 0), stop=(c == nchunks - 1)),
                  reads=[self.onesr, rg], writes=[self.psr[pm]])
        for c in range(nchunks):
            kb.op("pe", lambda e: e.matmul(self.ps[pq][:, :n], lhsT=ones[:], rhs=zsq[s][:, c, :n], start=(c == 0), stop=(c == nchunks - 1)),
                  reads=[self.onesr, zsqr[s][c]], writes=[self.psr[pq]])
        kb.op("act", lambda e: e.activation(out=msq[s][:, :n], in_=self.ps[pm][:, :n], func=AF.Square),
              reads=[self.psr[pm]], writes=[msqr[s]])
        kb.op("dve", lambda e: e.tensor_tensor(out=msq[s][:, :n], in0=self.ps[pq][:, :n], in1=msq[s][:, :n], op=ALU.subtract),
              reads=[self.psr[pq], msqr[s]], writes=[msqr[s]])
        kb.op("act", lambda e: e.activation(out=msq[s][:, :n], in_=msq[s][:, :n], func=AF.Ln, bias=eps_ap),
              reads=[msqr[s], self.onesr], writes=[msqr[s]])
        kb.op("act", lambda e: e.activation(out=rstd[s][:, :n], in_=msq[s][:, :n], func=AF.Exp, scale=-0.5),
              reads=[msqr[s]], writes=[rstdr[s]])
        return pm

    def conv_mixer(self, L, last):
        kb = self.kb
        j = 1
        blocks = [0, 1, 2, 3] if last else [0, 1, 2, 3, 4]
        self.modulate(j, blocks)
        PADL = 15
        OFFX = PADL
        OFFC = PADL + SEQ + 2 * PADL
        VW = OFFC + CTX + PADL
        voff = lambda b: (OFFX if BLOCKS[b][2] == 0 else OFFC - SEQ)
        with ExitStack() as es:
            V = kb.sb(es, "cvV", [128, NC_, VW], BF16)
            Vr = [[Reg() for _ in range(5)] for _ in range(NC_)]
            prm = kb.sb(es, "cvprm", [128, 6 * NC_ + 31 * NC_], F32)
            prmr = Reg()
            ident = kb.sb(es, "ident", [128, 128], F32)
            idr = Reg()
            kb.dma("sp", self.ds_misc, prm[:], self.d_cvprm, writes=[prmr])
            kb.dma("sp", self.ds_misc, ident[:], self.d_ident, writes=[idr])
            for c in range(NC_):
                kb.op("pool", lambda e: e.memset(V[:, c, :], 0.0), writes=Vr[c])
            B1A, B1G, BDW, LNG, LNB, B2, WDW = 0, 8, 16, 24, 32, 40, 48
            sg = [kb.sb(es, f"cvsg{s}", [128, 512], F32) for s in range(2)]
            sgr = [Reg(), Reg()]
            with ExitStack() as es1:
                wA = [kb.sb(es1, f"cvw1{s}", [128, 2, NC_, 128], BF16) for s in range(2)]
                wAr = [Reg(), Reg()]
                wAd = [kb.dsem(f"cvw1{s}") for s in range(2)]
                cnt = 0
                for c in range(NC_):
                    s = c % 2
                    kb.dma("pool", wAd[s], wA[s][:].rearrange("p s k m -> p (s k m)"), self.d_cvw1[c], writes=[wAr[s]])
                    for b in blocks:
                        t0, n, kind = BLOCKS[b]
                        pa = (cnt % 4) * 2
                        pg = pa + 1
                        ss = cnt % 2
                        cnt += 1
                        for (pb, si) in ((pa, 0), (pg, 1)):
                            for kc in range(NC_):
                                kb.op("pe", lambda e: e.matmul(self.ps[pb][:, :n], lhsT=wA[s][:, si, kc, :], rhs=self.U[:, kc, t0:t0 + n],
                                                                start=(kc == 0), stop=(kc == NC_ - 1)),
                                      reads=[wAr[s], self.Ur[kc][b]], writes=[self.psr[pb]])
                        kb.op("act", lambda e: e.activation(out=sg[ss][:, :n], in_=self.ps[pg][:, :n], func=AF.Sigmoid, bias=prm[:, B1G + c:B1G + c + 1]),
                              reads=[self.psr[pg], prmr], writes=[sgr[ss]])
                        kb.op("dve", lambda e: e.scalar_tensor_tensor(out=V[:, c, voff(b) + t0:voff(b) + t0 + n], in0=self.ps[pa][:, :n],
                                                                      scalar=prm[:, B1A + c:B1A + c + 1], in1=sg[ss][:, :n],
                                                                      op0=ALU.add, op1=ALU.mult),
                              reads=[self.psr[pa], prmr, sgr[ss]], writes=[Vr[c][b]])
                kb.barrier()
            with ExitStack() as es2:
                Dg = [kb.sb(es2, f"cvDg{s}", [128, 31, 128], BF16) for s in range(2)]
                Dgr = [Reg(), Reg()]
                for c in range(NC_):
                    s = c % 2
                    for k in range(31):
                        kb.op("dve" if k % 2 == 0 else "pool",
                              lambda e: e.tensor_scalar(out=Dg[s][:, k, :], in0=ident[:], scalar1=prm[:, WDW + k * NC_ + c:WDW + k * NC_ + c + 1],
                                                        scalar2=None, op0=ALU.mult),
                              reads=[idr, prmr], writes=[Dgr[s]])
                    for b in blocks:
                        t0, n, kind = BLOCKS[b]
                        pb = self.psrot % 8
                        self.psrot += 1
                        vregs = [Vr[c][bb] for bb in blocks if BLOCKS[bb][2] == kind]
                        for k in range(31):
                            col = voff(b) + t0 + k - 15
                            kb.op("pe", lambda e: e.matmul(self.ps[pb][:, :n], lhsT=Dg[s][:, k, :], rhs=V[:, c, col:col + n],
                                                            start=(k == 0), stop=(k == 30)),
                                  reads=[Dgr[s]] + vregs, writes=[self.psr[pb]])
                        kb.op("act", lambda e: e.activation(out=self.U[:, c, t0:t0 + n], in_=self.ps[pb][:, :n], func=AF.Identity,
                                                            bias=prm[:, BDW + c:BDW + c + 1]),
                              reads=[self.psr[pb], prmr], writes=[self.Ur[c][b]])
                kb.barrier()
            zsq = [kb.sb(es, f"cvz2{s}", [128, NC_, 512], BF16) for s in range(2)]
            zsqr = [[Reg() for _ in range(NC_)] for _ in range(2)]
            msq = [kb.sb(es, f"cvmsq{s}", [128, 512], F32) for s in range(2)]
            rstd = [kb.sb(es, f"cvrstd{s}", [128, 512], F32) for s in range(2)]
            tmp = [kb.sb(es, f"cvtmp{s}", [128, 512], F32) for s in range(4)]
            msqr = [Reg(), Reg()]
            rstdr = [Reg(), Reg()]
            tmpr = [Reg() for _ in range(4)]
            epsl = kb.sb(es, "cveps", [128, 1], F32)
            kb.op("dve", lambda e: e.memset(epsl[:], LN_EPS), writes=[self.onesr])
            tc = 0
            for bi, b in enumerate(blocks):
                t0, n, kind = BLOCKS[b]
                s = bi % 2
                pm = self.feat_stats(None, lambda c: (self.U[:, c, t0:t0 + n], self.Ur[c][b]), b, n, s, zsq, zsqr, msq, msqr, rstd, rstdr, epsl[:, 0:1])
                for c in range(NC_):
                    ts = tc % 4
                    tc += 1
                    kb.op("dve", lambda e: e.tensor_tensor(out=tmp[ts][:, :n], in0=self.U[:, c, t0:t0 + n], in1=self.ps[pm][:, :n], op=ALU.subtract),
                          reads=[self.Ur[c][b], self.psr[pm]], writes=[tmpr[ts]])
                    kb.op("pool", lambda e: e.tensor_tensor(out=tmp[ts][:, :n], in0=tmp[ts][:, :n], in1=rstd[s][:, :n], op=ALU.mult),
                          reads=[tmpr[ts], rstdr[s]], writes=[tmpr[ts]])
                    kb.op("act", lambda e: e.activation(out=self.U[:, c, t0:t0 + n], in_=tmp[ts][:, :n], func=AF.Silu,
                                                        scale=prm[:, LNG + c:LNG + c + 1], bias=prm[:, LNB + c:LNB + c + 1]),
                          reads=[tmpr[ts], prmr], writes=[self.Ur[c][b]])
            b2t = prm[:, B2:B2 + NC_]
            self.proj(es, self.d_cvw2, range(NC_), self.U, self.Ur, self.resid_evac(j, bias=b2t, bias_reg=prmr, es=es), blocks, tag="cvw2")
            kb.barrier()
        self.layernorm(L, j, blocks)


    def na_mixer(self, L, last):
        kb = self.kb
        j = 1
        blocks = [0, 1, 2, 3, 4]
        self.modulate(j, blocks)
        NB_I, NB_F = 23, 14
        FB = NB_I
        SW = (NB_I + NB_F) * 64
        qtiles = []
        qtiles.append((0, 256, [(128 * i, (FB + 6 - 2 * i) * 64) for i in range(4)]))
        for q0r in (4, 12, 20):
            qtiles.append((64 * q0r, 512, [(64 * (q0r - 4) + 128 * i, (15 - 2 * i) * 64) for i in range(8)]))
        qtiles.append((1792, 256, [(1536 + 128 * i, (FB + 10 - 2 * i) * 64) for i in range(4)]))
        qtiles.append((2048, 256, []))
        ctx_keys = [2048, 2176]
        bregs = lambda regs, t0, n: [regs[b] for b, (bt, bn, _) in enumerate(BLOCKS) if bt < t0 + n and t0 < bt + bn]
        with ExitStack() as es:
            ZT = kb.sb(es, "naZT", [128, NC_, T], BF16)
            ZTr = [[Reg() for _ in range(5)] for _ in range(NC_)]
            ones1 = kb.sb(es, "naones", [128, 128], BF16)
            o1r = Reg()
            kb.op("dve", lambda e: e.memset(ones1[:], 1.0), writes=[o1r])
            with ExitStack() as es1:
                QT = [kb.sb(es1, f"naQ{s}", [128, T], BF16) for s in range(2)]
                KT = [kb.sb(es1, f"naK{s}", [128, T], BF16) for s in range(2)]
                VT = [kb.sb(es1, f"naV{s}", [128, 18, 128], BF16) for s in range(2)]
                QTr = [[Reg() for _ in range(5)] for _ in range(2)]
                KTr = [[Reg() for _ in range(5)] for _ in range(2)]
                VTr = [Reg(), Reg()]
                strip = [kb.sb(es1, f"nastrip{s}", [128, SW], BF16) for s in range(2)]
                stripr = [Reg(), Reg()]
                stripd = [kb.dsem(f"nastrip{s}") for s in range(2)]
                tmp = [kb.sb(es1, f"natmp{s}", [128, 512], F32) for s in range(3)]
                tmpr = [Reg() for _ in range(3)]
                PT = [kb.sb(es1, f"naPT{s}", [128, 512], BF16) for s in range(3)]
                PTr = [Reg() for _ in range(3)]
                rc = [kb.sb(es1, f"narc{s}", [128, 512], F32) for s in range(2)]
                rcr = [Reg(), Reg()]
                wv = [kb.sb(es1, f"nawv{s}", [128, NC_, 128], BF16) for s in range(2)]
                wvr = [Reg(), Reg()]
                wvd = [kb.dsem(f"nawv{s}") for s in range(2)]
                pw = [kb.sb(es1, f"napw{s}", [128, NC_, 128], BF16) for s in range(2)]
                pwr = [Reg(), Reg()]
                pwd = [kb.dsem(f"napw{s}") for s in range(2)]
                cnt = {"s": 0, "t": 0, "p": 0, "o": 0, "w": 0}
                for ch in range(NC_):
                    sl = ch % 2
                    for (dst, dstr, oc) in ((QT[sl], QTr[sl], ch), (KT[sl], KTr[sl], NC_ + ch)):
                        ws = cnt["w"] % 2
                        cnt["w"] += 1
                        kb.dma("pool", pwd[ws], pw[ws][:].rearrange("p k m -> p (k m)"), self.d_nawqkv[oc], writes=[pwr[ws]])
                        for b in blocks:
                            t0, n, kind = BLOCKS[b]
                            pb = 4 + (self.psrot % 4)
                            self.psrot += 1
                            for kc in range(NC_):
                                kb.op("pe", lambda e: e.matmul(self.ps[pb][:, :n], lhsT=pw[ws][:, kc, :], rhs=self.U[:, kc, t0:t0 + n],
                                                                start=(kc == 0), stop=(kc == NC_ - 1)),
                                      reads=[pwr[ws], self.Ur[kc][b]], writes=[self.psr[pb]])
                            kb.op("act", lambda e: e.activation(out=dst[:, t0:t0 + n], in_=self.ps[pb][:, :n], func=AF.Copy),
                                  reads=[self.psr[pb]], writes=[dstr[b]])
                    kb.dma("pool", wvd[sl], wv[sl][:].rearrange("p k m -> p (k m)"), self.d_nawqkv[2 * NC_ + ch], writes=[wvr[sl]])
                    for g in range(5):
                        tiles = list(range(4 * g, min(18, 4 * g + 4)))
                        pb = 4 + (self.psrot % 4)
                        self.psrot += 1
                        for ti, tt in enumerate(tiles):
                            b = min(tt // 4, 4)
                            for kc in range(NC_):
                                kb.op("pe", lambda e: e.matmul(self.ps[pb][:, ti * 128:(ti + 1) * 128], lhsT=self.U[:, kc, tt * 128:(tt + 1) * 128],
                                                                rhs=wv[sl][:, kc, :], start=(kc == 0), stop=(kc == NC_ - 1)),
                                      reads=[wvr[sl], self.Ur[kc][b]], writes=[self.psr[pb]])
                        nt = len(tiles)
                        kb.op("dve", lambda e: e.tensor_copy(out=VT[sl][:, tiles[0]:tiles[0] + nt, :].rearrange("p t m -> p (t m)"),
                                                             in_=self.ps[pb][:, :nt * 128]),
                              reads=[self.psr[pb]], writes=[VTr[sl]])
                    items = []
                    for hh in range(2):
                        for qi, (q0, nq, loc) in enumerate(qtiles):
                            keys = [(k0, c0) for (k0, c0) in loc] + [(k0, None) for k0 in ctx_keys]
                            for ki, (k0, c0) in enumerate(keys):
                                items.append((hh, qi, ki, len(keys), k0, c0))
                    obase = cnt["o"]
                    cnt["o"] += 2 * len(qtiles)
                    ibase = cnt["s"]
                    cnt["s"] += len(items)

                    def stage_a(it, idx):
                        hh, qi, ki, nk, k0, c0 = it
                        q0, nq, _ = qtiles[qi]
                        h = 2 * ch + hh
                        r0 = hh * 64
                        ss = h % 2
                        if qi == 0 and ki == 0:
                            kb.dma("pool", stripd[ss], strip[ss][:], self.d_nastrip[h], writes=[stripr[ss]])
                        g = ibase + idx
                        pss = g % 4
                        pt = g % 3
                        kb.op("pe", lambda e: e.matmul(self.ps[pss][:, :nq], lhsT=KT[sl][r0:r0 + 64, k0:k0 + 128],
                                                        rhs=QT[sl][r0:r0 + 64, q0:q0 + nq], start=True, stop=True),
                              reads=bregs(KTr[sl], k0, 128) + bregs(QTr[sl], q0, nq), writes=[self.psr[pss]])
                        if c0 is not None:
                            tm = g % 3
                            kb.op("dve", lambda e: e.scalar_tensor_tensor(out=tmp[tm][:, :nq], in0=self.ps[pss][:, :nq], scalar=0.125,
                                                                          in1=strip[ss][:, c0:c0 + nq], op0=ALU.mult, op1=ALU.add),
                                  reads=[self.psr[pss], stripr[ss]], writes=[tmpr[tm]])
                            kb.op("act", lambda e: e.activation(out=PT[pt][:, :nq], in_=tmp[tm][:, :nq], func=AF.Exp),
                                  reads=[tmpr[tm]], writes=[PTr[pt]])
                        else:
                            kb.op("act", lambda e: e.activation(out=PT[pt][:, :nq], in_=self.ps[pss][:, :nq], func=AF.Exp, scale=0.125),
                                  reads=[self.psr[pss]], writes=[PTr[pt]])

                    def stage_b(it, idx):
                        hh, qi, ki, nk, k0, c0 = it
                        q0, nq, _ = qtiles[qi]
                        r0 = hh * 64
                        o_ = obase + hh * len(qtiles) + qi
                        ppv = 4 + (o_ % 2)
                        psm = 6 + (o_ % 2)
                        pt = (ibase + idx) % 3
                        kt = k0 // 128
                        kb.op("pe", lambda e: e.matmul(self.ps[ppv][:, :nq], lhsT=VT[sl][:, kt, :], rhs=PT[pt][:, :nq],
                                                        start=(ki == 0), stop=(ki == nk - 1)),
                              reads=[VTr[sl], PTr[pt]], writes=[self.psr[ppv]])
                        kb.op("pe", lambda e: e.matmul(self.ps[psm][:, :nq], lhsT=ones1[:], rhs=PT[pt][:, :nq],
                                                        start=(ki == 0), stop=(ki == nk - 1)),
                              reads=[o1r, PTr[pt]], writes=[self.psr[psm]])
                        if ki == nk - 1:
                            rs = o_ % 2
                            kb.op("act", lambda e: e.activation(out=rc[rs][r0:r0 + 64, :nq], in_=self.ps[psm][r0:r0 + 64, :nq], func=AF.Ln),
                                  reads=[self.psr[psm]], writes=[rcr[rs]])
                            kb.op("act", lambda e: e.activation(out=rc[rs][r0:r0 + 64, :nq], in_=rc[rs][r0:r0 + 64, :nq], func=AF.Exp, scale=-1.0),
                                  reads=[rcr[rs]], writes=[rcr[rs]])
                            kb.op("dve", lambda e: e.tensor_tensor(out=ZT[r0:r0 + 64, ch, q0:q0 + nq], in0=self.ps[ppv][r0:r0 + 64, :nq],
                                                                   in1=rc[rs][r0:r0 + 64, :nq], op=ALU.mult),
                                  reads=[self.psr[ppv], rcr[rs]], writes=bregs(ZTr[ch], q0, nq))

                    LA = 2
                    for idx in range(len(items) + LA):
                        if idx < len(items):
                            stage_a(items[idx], idx)
                        if idx >= LA:
                            stage_b(items[idx - LA], idx - LA)
                kb.barrier()
            self.proj(es, self.d_nawo, range(NC_), ZT, ZTr, self.resid_evac(j), blocks, tag="nawo")
            kb.barrier()
        self.layernorm(L, j, blocks)


    def gla_mixer(self, L, last):
        kb = self.kb
        j = 1
        blocks = [0, 1, 2, 3, 4]
        oblocks = [0, 1, 2, 3] if last else blocks
        self.modulate(j, blocks)
        SC = 128.0 ** -0.5
        NCH = T // 64
        with ExitStack() as es:
            cst = kb.sb(es, "glcst", [128, 4 * 64 + 128 + 2 + 8 + 8], F32)
            cstr = Reg()
            kb.dma("sp", self.ds_misc, cst[:, 0:4 * 64 + 128 + 2 + 8], self.d_glacst, writes=[cstr])
            MK, ID, NG, BA, NBA = 0, 256, 384, 386, 394
            kb.op("dve", lambda e: e.tensor_scalar(out=cst[:, NBA:NBA + 8], in0=cst[:, BA:BA + 8], scalar1=-1.0, scalar2=None, op0=ALU.mult),
                  reads=[cstr], writes=[cstr])
            maskb = kb.sb(es, "glmask", [128, 4, 64], BF16)
            identb = kb.sb(es, "glidb", [128, 128], BF16)
            ones256 = kb.sb(es, "glones", [128, 128], BF16)
            onecol = kb.sb(es, "glone", [128, 1], F32)
            epsc = kb.sb(es, "gleps", [128, 1], F32)
            kb.op("dve", lambda e: e.tensor_copy(out=maskb[:].rearrange("p a b -> p (a b)"), in_=cst[:, MK:MK + 256]), reads=[cstr], writes=[cstr])
            kb.op("dve", lambda e: e.tensor_copy(out=identb[:], in_=cst[:, ID:ID + 128]), reads=[cstr], writes=[cstr])
            kb.op("dve", lambda e: e.memset(ones256[:], 1.0 / 256.0), writes=[cstr])
            kb.op("dve", lambda e: e.memset(onecol[:], 1.0), writes=[cstr])
            kb.op("dve", lambda e: e.memset(epsc[:], LN_EPS), writes=[cstr])
            wa1 = kb.sb(es, "glwa1", [128, NC_, 32], BF16)
            wa1r = Reg()
            kb.dma("pool", self.ds_miscp, wa1[:].rearrange("p k m -> p (k m)"), self.d_glawa1, writes=[wa1r])
            wa2 = kb.sb(es, "glwa2", [32, 2, 512], F32)
            wa2r = Reg()
            kb.dma("sp", self.ds_misc, wa2[:].rearrange("p d m -> p (d m)"), self.d_glawa2, writes=[wa2r])
            rT = kb.sb(es, "glrT", [32, T], F32)
            rTr = [Reg() for _ in range(5)]
            for b in blocks:
                t0, n, kind = BLOCKS[b]
                pb = self.psrot % 8
                self.psrot += 1
                for kc in range(NC_):
                    kb.op("pe", lambda e: e.matmul(self.ps[pb][0:32, :n], lhsT=wa1[:, kc, :], rhs=self.U[:, kc, t0:t0 + n],
                                                    start=(kc == 0), stop=(kc == NC_ - 1)),
                          reads=[wa1r, self.Ur[kc][b]], writes=[self.psr[pb]])
                kb.op("act", lambda e: e.activation(out=rT[:, t0:t0 + n], in_=self.ps[pb][0:32, :n], func=AF.Copy),
                      reads=[self.psr[pb]], writes=[rTr[b]])
            QK = kb.sb(es, "glQK", [128, 2, T], BF16)
            QKr = [[Reg() for _ in range(5)] for _ in range(2)]
            VT = kb.sb(es, "glVT", [128, 18, 256], BF16)
            VTr = Reg()
            O = kb.sb(es, "glO", [128, 2, T], F32)
            Or = [Reg() for _ in range(5)]
            Z, Zr = QK, QKr
            SA = kb.sb(es, "glSA", [128, 2, T], F32)
            SAr = Reg()
            sad = kb.dsem("glrope")
            pw = [kb.sb(es, f"glpw{s}", [128, NC_, 128], BF16) for s in range(2)]
            pwr = [Reg(), Reg()]
            pwd = [kb.dsem(f"glpw{s}") for s in range(2)]
            wv = kb.sb(es, "glwv", [128, NC_, 256], BF16)
            wvr = Reg()
            wvd = kb.dsem("glwv")
            t1 = [kb.sb(es, f"glt1{s}", [128, 512], F32) for s in range(2)]
            t1r = [Reg(), Reg()]
            t2 = [kb.sb(es, f"glt2{s}", [128, 512], F32) for s in range(2)]
            t2r = [Reg(), Reg()]
            S = kb.sb(es, "glS", [128, 256], F32)
            Sb = kb.sb(es, "glSb", [128, 256], BF16)
            Sr, Sbr = Reg(), Reg()
            kd = [kb.sb(es, f"glkd{s}", [128, 128], BF16) for s in range(3)]
            kt = [kb.sb(es, f"glkt{s}", [128, 128], BF16) for s in range(3)]
            kdr = [Reg() for _ in range(3)]
            ktr = [Reg() for _ in range(3)]
            for s_ in range(3):
                kb.op("dve", lambda e: e.memset(kd[s_][:], 0.0), writes=[kdr[s_]])
                kb.op("dve", lambda e: e.memset(kt[s_][:], 0.0), writes=[ktr[s_]])
            ktT = [kb.sb(es, f"glktT{s}", [128, 128], BF16) for s in range(5)]
            ktTr = [Reg() for _ in range(5)]
            qd = [kb.sb(es, f"glqd{s}", [128, 64], BF16) for s in range(5)]
            qdr = [Reg() for _ in range(5)]
            ex = [kb.sb(es, f"glex{s}", [128, 3, 64], F32) for s in range(5)]
            exr = [Reg() for _ in range(5)]
            attm = [kb.sb(es, f"glatt{s}", [128, 64], BF16) for s in range(5)]
            attr = [Reg() for _ in range(5)]
            nbp = [kb.sb(es, f"glnb{s}", [128, 2], F32) for s in range(5)]
            nbr = [Reg() for _ in range(5)]
            psb = [self.ps[i].bitcast(BF16) for i in range(8)]
            wcnt = 0
            for hd in range(4):
                kb.dma("sp", sad, SA[:, :, 0:SEQ], self.d_glarope.rearrange("a p t -> p a t"), writes=[SAr])
                for qk in range(2):
                    oc = qk * 4 + hd
                    ws0 = wcnt % 2
                    ws1 = (wcnt + 1) % 2
                    wcnt += 2
                    kb.dma("pool", pwd[ws0], pw[ws0][:].rearrange("p k m -> p (k m)"), self.d_glawqk[oc], writes=[pwr[ws0]])
                    kb.dma("pool", pwd[ws1], pw[ws1][:].rearrange("p k m -> p (k m)"), self.d_glawqkp[oc], writes=[pwr[ws1]])
                    for b in blocks:
                        t0, n, kind = BLOCKS[b]
                        pa = (self.psrot % 4) * 2
                        pp = pa + 1
                        ts = self.psrot % 2
                        self.psrot += 1
                        for kc in range(NC_):
                            kb.op("pe", lambda e: e.matmul(self.ps[pa][:, :n], lhsT=pw[ws0][:, kc, :], rhs=self.U[:, kc, t0:t0 + n],
                                                            start=(kc == 0), stop=(kc == NC_ - 1)),
                                  reads=[pwr[ws0], self.Ur[kc][b]], writes=[self.psr[pa]])
                        if kind == 1:
                            kb.op("act", lambda e: e.activation(out=QK[:, qk, t0:t0 + n], in_=self.ps[pa][:, :n], func=AF.Copy),
                                  reads=[self.psr[pa]], writes=[QKr[qk][b]])
                            continue
                        for kc in range(NC_):
                            kb.op("pe", lambda e: e.matmul(self.ps[pp][:, :n], lhsT=pw[ws1][:, kc, :], rhs=self.U[:, kc, t0:t0 + n],
                                                            start=(kc == 0), stop=(kc == NC_ - 1)),
                                  reads=[pwr[ws1], self.Ur[kc][b]], writes=[self.psr[pp]])
                        kb.op("dve", lambda e: e.tensor_tensor(out=t1[ts][:, :n], in0=self.ps[pa][:, :n], in1=SA[:, 0, t0:t0 + n], op=ALU.mult),
                              reads=[self.psr[pa], SAr], writes=[t1r[ts]])
                        kb.op("dve", lambda e: e.tensor_tensor(out=t2[ts][:, :n], in0=self.ps[pp][:, :n], in1=SA[:, 1, t0:t0 + n], op=ALU.mult),
                              reads=[self.psr[pp], SAr], writes=[t2r[ts]])
                        kb.op("pool", lambda e: e.tensor_tensor(out=QK[:, qk, t0:t0 + n], in0=t1[ts][:, :n], in1=t2[ts][:, :n], op=ALU.add),
                              reads=[t1r[ts], t2r[ts]], writes=[QKr[qk][b]])
                kb.dma("pool", wvd, wv[:].rearrange("p k m -> p (k m)"), self.d_glawv[hd], writes=[wvr])
                for g in range(9):
                    pb = self.psrot % 8
                    self.psrot += 1
                    for ti in range(2):
                        tt = 2 * g + ti
                        b = min(tt // 4, 4)
                        for kc in range(NC_):
                            kb.op("pe", lambda e: e.matmul(self.ps[pb][:, ti * 256:(ti + 1) * 256], lhsT=self.U[:, kc, tt * 128:(tt + 1) * 128],
                                                            rhs=wv[:, kc, :], start=(kc == 0), stop=(kc == NC_ - 1)),
                                  reads=[wvr, self.Ur[kc][b]], writes=[self.psr[pb]])
                    kb.op("act", lambda e: e.activation(out=VT[:, 2 * g:2 * g + 2, :].rearrange("p t m -> p (t m)"), in_=self.ps[pb][:, :512], func=AF.Copy),
                          reads=[self.psr[pb]], writes=[VTr])
                for d in range(2):
                    for b in blocks:
                        t0, n, kind = BLOCKS[b]
                        pb = self.psrot % 8
                        ts = self.psrot % 2
                        self.psrot += 1
                        kb.op("pe", lambda e: e.matmul(self.ps[pb][:, :n], lhsT=wa2[:, d, hd * 128:(hd + 1) * 128], rhs=rT[:, t0:t0 + n],
                                                        start=True, stop=True),
                              reads=[wa2r, rTr[b]], writes=[self.psr[pb]])
                        kb.op("act", lambda e: e.activation(out=t1[ts][:, :n], in_=self.ps[pb][:, :n], func=AF.Exp, scale=-1.0,
                                                            bias=cst[:, NBA + d * 4 + hd:NBA + d * 4 + hd + 1]),
                              reads=[self.psr[pb], cstr], writes=[t1r[ts]])
                        kb.op("act", lambda e: e.activation(out=SA[:, 0, t0:t0 + n], in_=t1[ts][:, :n], func=AF.Ln, bias=onecol[:, 0:1]),
                              reads=[t1r[ts], cstr], writes=[SAr])
                    for n_ in range(NCH):
                        c0 = 64 * n_
                        kb.op("dve", lambda e: e.tensor_tensor_scan(out=SA[:, 1, c0:c0 + 64], data0=self.onesf[:, 0:64], data1=SA[:, 0, c0:c0 + 64],
                                                                    initial=0.0, op0=ALU.mult, op1=ALU.add),
                              reads=[SAr, self.onesr], writes=[SAr])
                    if d == 1:
                        kb.op("pool", lambda e: e.tensor_tensor(out=SA[:, 0, :], in0=SA[:, 0, :], in1=SA[:, 1, :], op=ALU.subtract),
                              reads=[SAr], writes=[SAr])
                    kb.op("dve", lambda e: e.memset(S[:], 0.0), writes=[Sr])
                    kb.op("dve", lambda e: e.memset(Sb[:], 0.0), writes=[Sbr])
                    order = ([32, 33, 34, 35] + list(range(32))) if d == 0 else ([35, 34, 33, 32] + list(range(31, -1, -1)))
                    def chunk_gen(ci, n_):
                        c0 = 64 * n_
                        tt, par = n_ // 2, n_ % 2
                        tb = tt % 3
                        b = min(c0 // 512, 4)
                        xs = ci % 5
                        last_col = c0 + 63
                        kb.op("dve", lambda e: e.tensor_scalar(out=nbp[xs][:, 0:1], in0=SA[:, 1, last_col:last_col + 1], scalar1=-1.0 / 16.0, scalar2=None, op0=ALU.mult),
                              reads=[SAr], writes=[nbr[xs]])
                        kb.op("dve", lambda e: e.tensor_scalar(out=nbp[xs][:, 1:2], in0=SA[:, 1, last_col:last_col + 1], scalar1=1.0 / 16.0, scalar2=None, op0=ALU.mult),
                              reads=[SAr], writes=[nbr[xs]])
                        if d == 0:
                            src = SA[:, 1, c0:c0 + 64]
                            kb.op("act", lambda e: e.activation(out=ex[xs][:, 0, :], in_=src, func=AF.Exp, scale=-1.0 / 16.0), reads=[SAr], writes=[exr[xs]])
                            kb.op("act", lambda e: e.activation(out=ex[xs][:, 1, :], in_=src, func=AF.Exp, scale=1.0 / 16.0), reads=[SAr], writes=[exr[xs]])
                            kb.op("act", lambda e: e.activation(out=ex[xs][:, 2, :], in_=src, func=AF.Exp, scale=1.0 / 16.0, bias=nbp[xs][:, 0:1]),
                                  reads=[SAr, nbr[xs]], writes=[exr[xs]])
                            dec = ex[xs][:, 0, 63:64]
                        else:
                            src = SA[:, 0, c0:c0 + 64]
                            kb.op("act", lambda e: e.activation(out=ex[xs][:, 0, :], in_=src, func=AF.Exp, scale=-1.0 / 16.0, bias=nbp[xs][:, 0:1]),
                                  reads=[SAr, nbr[xs]], writes=[exr[xs]])
                            kb.op("act", lambda e: e.activation(out=ex[xs][:, 1, :], in_=src, func=AF.Exp, scale=1.0 / 16.0, bias=nbp[xs][:, 1:2]),
                                  reads=[SAr, nbr[xs]], writes=[exr[xs]])
                            kb.op("act", lambda e: e.activation(out=ex[xs][:, 2, :], in_=src, func=AF.Exp, scale=1.0 / 16.0), reads=[SAr], writes=[exr[xs]])
                            dec = ex[xs][:, 0, 0:1]
                        yield
                        kb.op("dve", lambda e: e.tensor_tensor(out=qd[xs][:], in0=QK[:, 0, c0:c0 + 64], in1=ex[xs][:, 0, :], op=ALU.mult),
                              reads=[QKr[0][b], exr[xs]], writes=[qdr[xs]])
                        kb.op("dve", lambda e: e.tensor_tensor(out=kd[tb][:, par * 64:par * 64 + 64], in0=QK[:, 1, c0:c0 + 64], in1=ex[xs][:, 1, :], op=ALU.mult),
                              reads=[QKr[1][b], exr[xs]], writes=[kdr[tb]])
                        kb.op("pool", lambda e: e.tensor_tensor(out=kt[tb][:, par * 64:par * 64 + 64], in0=QK[:, 1, c0:c0 + 64], in1=ex[xs][:, 2, :], op=ALU.mult),
                              reads=[QKr[1][b], exr[xs]], writes=[ktr[tb]])
                        yield
                        bk = ci % 5
                        kb.op("pe", lambda e: e.matmul(self.ps[bk][:, 0:64], lhsT=kd[tb][:], rhs=qd[xs][:], start=True, stop=True),
                              reads=[kdr[tb], qdr[xs]], writes=[self.psr[bk]])
                        kb.op("dve", lambda e: e.tensor_tensor(out=attm[xs][:], in0=self.ps[bk][:, 0:64], in1=maskb[:, par * 2 + d, :], op=ALU.mult),
                              reads=[self.psr[bk], cstr], writes=[attr[xs]])
                        yield
                        kb.op("pe", lambda e: e.transpose(psb[bk][:, 896:1024], kt[tb][:], identb[:]),
                              reads=[ktr[tb], cstr], writes=[self.psr[bk]])
                        kb.op("act", lambda e: e.activation(out=ktT[xs][par * 64:par * 64 + 64, :], in_=psb[bk][par * 64:par * 64 + 64, 896:1024], func=AF.Copy),
                              reads=[self.psr[bk]], writes=[ktTr[xs]])
                        yield
                        for ec in range(2):
                            kb.op("pe", lambda e: e.matmul(self.ps[bk][:, 64 + ec * 64:64 + ec * 64 + 64], lhsT=VT[:, tt, ec * 128:(ec + 1) * 128], rhs=attm[xs][:],
                                                            start=True, stop=False),
                                  reads=[VTr, attr[xs]], writes=[self.psr[bk]])
                            kb.op("pe", lambda e: e.matmul(self.ps[bk][:, 64 + ec * 64:64 + ec * 64 + 64], lhsT=Sb[:, ec * 128:(ec + 1) * 128], rhs=qd[xs][:],
                                                            start=False, stop=True),
                                  reads=[Sbr, qdr[xs]], writes=[self.psr[bk]])
                        yield
                        kb.op("pe", lambda e: e.matmul(self.ps[bk][:, 192:448], lhsT=ktT[xs][par * 64:par * 64 + 64, :], rhs=VT[par * 64:par * 64 + 64, tt, :],
                                                        start=True, stop=True),
                              reads=[ktTr[xs], VTr], writes=[self.psr[bk]])
                        kb.op("dve", lambda e: e.scalar_tensor_tensor(out=S[:], in0=S[:], scalar=dec, in1=self.ps[bk][:, 192:448], op0=ALU.mult, op1=ALU.add),
                              reads=[Sr, exr[xs], self.psr[bk]], writes=[Sr])
                        kb.op("act", lambda e: e.activation(out=Sb[:], in_=S[:], func=AF.Copy), reads=[Sr], writes=[Sbr])
                        yield
                        pov = self.ps[bk][:, 64:192].rearrange("p (e c) -> p e c", e=2)
                        if d == 0:
                            kb.op("act", lambda e: e.activation(out=O[:, :, c0:c0 + 64], in_=pov, func=AF.Identity, scale=SC),
                                  reads=[self.psr[bk]], writes=[Or[b]])
                        else:
                            kb.op("dve", lambda e: e.scalar_tensor_tensor(out=O[:, :, c0:c0 + 64], in0=pov, scalar=SC, in1=O[:, :, c0:c0 + 64],
                                                                          op0=ALU.mult, op1=ALU.add),
                                  reads=[self.psr[bk], Or[b]], writes=[Or[b]])
                    active = []
                    pending = list(enumerate(order))
                    while pending or active:
                        if pending and len(active) < 5:
                            active.append(chunk_gen(*pending.pop(0)))
                        for g_ in list(active):
                            try:
                                next(g_)
                            except StopIteration:
                                active.remove(g_)
                if getattr(self, "debug", None) == f"gla_scan{hd}":
                    dd = kb.dsem("dbg")
                    for e2 in range(2):
                        kb.dma("sp", dd, self.d_hout[e2], O[:, e2, :], reads=Or)
                        kb.dma("pool", dd, self.d_hout[2 + e2], QK[:, e2, :], reads=QKr[0] + QKr[1])
                        kb.dma("sp", dd, self.d_hout[4 + e2], SA[:, e2, :], reads=[SAr])
                    kb.dma("sp", dd, self.d_hout[6][:, 0:256], S[:], reads=[Sr])
                    kb.dma("sp", dd, self.d_hout[7][0:32, :], rT[:], reads=rTr)
                    kb.eng["sp"].wait_ge(dd.sem, dd.cnt)
                    kb.eng["pool"].wait_ge(dd.sem, dd.cnt)
                    raise DebugStop()
                for ec in range(2):
                    ws = wcnt % 2
                    wcnt += 1
                    kb.dma("pool", pwd[ws], pw[ws][:].rearrange("p k m -> p (k m)"), self.d_glawg[hd * 2 + ec], writes=[pwr[ws]])
                    for b in oblocks:
                        t0, n, kind = BLOCKS[b]
                        pg = self.psrot % 8
                        pq = (self.psrot + 1) % 8
                        ts = (self.psrot // 2) % 2
                        self.psrot += 2
                        for kc in range(NC_):
                            kb.op("pe", lambda e: e.matmul(self.ps[pg][:, :n], lhsT=pw[ws][:, kc, :], rhs=self.U[:, kc, t0:t0 + n],
                                                            start=(kc == 0), stop=(kc == NC_ - 1)),
                                  reads=[pwr[ws], self.Ur[kc][b]], writes=[self.psr[pg]])
                        for e2 in range(2):
                            kb.op("act", lambda e: e.activation(out=t1[ts][:, :n], in_=O[:, e2, t0:t0 + n], func=AF.Square), reads=[Or[b]], writes=[t1r[ts]])
                            kb.op("act", lambda e: e.activation(out=Z[:, ec, t0:t0 + n], in_=t1[ts][:, :n], func=AF.Copy), reads=[t1r[ts]], writes=[Zr[ec][b]])
                            kb.op("pe", lambda e: e.matmul(self.ps[pq][:, :n], lhsT=ones256[:], rhs=Z[:, ec, t0:t0 + n], start=(e2 == 0), stop=(e2 == 1)),
                                  reads=[cstr, Zr[ec][b]], writes=[self.psr[pq]])
                        kb.op("act", lambda e: e.activation(out=t1[ts][:, :n], in_=self.ps[pq][:, :n], func=AF.Ln, bias=epsc[:, 0:1]),
                              reads=[self.psr[pq], cstr], writes=[t1r[ts]])
                        kb.op("act", lambda e: e.activation(out=t1[ts][:, :n], in_=t1[ts][:, :n], func=AF.Exp, scale=-0.5), reads=[t1r[ts]], writes=[t1r[ts]])
                        kb.op("dve", lambda e: e.tensor_tensor(out=t1[ts][:, :n], in0=O[:, ec, t0:t0 + n], in1=t1[ts][:, :n], op=ALU.mult),
                              reads=[Or[b], t1r[ts]], writes=[t1r[ts]])
                        kb.op("act", lambda e: e.activation(out=t2[ts][:, :n], in_=self.ps[pg][:, :n], func=AF.Silu), reads=[self.psr[pg]], writes=[t2r[ts]])
                        kb.op("dve", lambda e: e.scalar_tensor_tensor(out=Z[:, ec, t0:t0 + n], in0=t1[ts][:, :n], scalar=cst[:, NG + ec:NG + ec + 1],
                                                                      in1=t2[ts][:, :n], op0=ALU.mult, op1=ALU.mult),
                              reads=[t1r[ts], t2r[ts], cstr], writes=[Zr[ec][b]])
                if getattr(self, "debug", None) == f"gla_fin{hd}":
                    dd = kb.dsem("dbg")
                    for e2 in range(2):
                        kb.dma("pool", dd, self.d_hout[e2], Z[:, e2, :], reads=Zr[0] + Zr[1])
                        kb.dma("sp", dd, self.d_hout[2 + e2], O[:, e2, :], reads=Or)
                    kb.eng["sp"].wait_ge(dd.sem, dd.cnt)
                    kb.eng["pool"].wait_ge(dd.sem, dd.cnt)
                    raise DebugStop()
                with ExitStack() as es2:
                    self.proj(es2, self.d_glawo[hd], range(NC_), Z, Zr, self.resid_evac(j), oblocks, nk=2, tag=f"glwo{hd}")
                    kb.barrier()
                    if getattr(self, "debug", None) == f"gla_wo{hd}":
                        self.store()
                        raise DebugStop()
            kb.barrier()
        self.layernorm(L, j, oblocks)


    def park_h(self):
        kb = self.kb
        if not hasattr(self, "d_hpark"):
            self.d_hpark = self.nc.dram_tensor("hpark", [NC_, 128, T], F32, kind="Internal").ap()
            self.ds_park = kb.dsem("park")
        for c in range(NC_):
            kb.dma("sp", self.ds_h[c % 4], self.d_hpark[c], self.H[:, c, :], reads=self.Hr[c])

    def unpark_h(self):
        kb = self.kb
        self.es_H = ExitStack()
        self.H = kb.sb(self.es_H, "H", [128, NC_, T], F32)
        for c in range(NC_):
            kb.dma("sp", self.ds_h[c % 4], self.H[:, c, :], self.d_hpark[c], writes=self.Hr[c])

    def rwkv_mixer(self, L, last):
        kb = self.kb
        assert last, "rwkv mixer implemented for the final layer (context output unused)"
        j = 1
        blocks = [0, 1, 2, 3, 4]
        lblocks = [0, 1, 2, 3]
        self.modulate(j, blocks)
        NCH = T // 64
        G = 4
        EM05 = float(np.exp(-0.5))
        U = self.U
        nbank = lambda: self._nb()
        W0, A0, KK_, KA, RK, GNG, GNB, MU, OMKA, OMU, HMU = 0, 16, 32, 40, 48, 56, 64, 72, 120, 128, 176
        self.park_h()
        kb.barrier()
        self.es_H.close()
        d_zpark = self.nc.dram_tensor("zpark", [NC_, 128, SEQ], BF16, kind="Internal").ap()
        ds_z = kb.dsem("zpark")
        dbg = getattr(self, "debug", None)
        if dbg == "rw0":
            return 'stop'
        with ExitStack() as es:
            prm = kb.sb(es, "rwprm", [128, 224], F32)
            prmr = Reg()
            kb.dma("sp", self.ds_misc, prm[:, 0:120], self.d_rwprm, writes=[prmr])
            kb.op("dve", lambda e: e.tensor_scalar(out=prm[:, OMKA:OMKA + 8], in0=prm[:, KA:KA + 8], scalar1=-1.0, scalar2=1.0, op0=ALU.mult, op1=ALU.add),
                  reads=[prmr], writes=[prmr])
            kb.op("dve", lambda e: e.tensor_scalar(out=prm[:, OMU:OMU + 48], in0=prm[:, MU:MU + 48], scalar1=-1.0, scalar2=1.0, op0=ALU.mult, op1=ALU.add),
                  reads=[prmr], writes=[prmr])
            kb.op("dve", lambda e: e.tensor_scalar(out=prm[:, HMU:HMU + 48], in0=prm[:, MU:MU + 48], scalar1=0.5, scalar2=None, op0=ALU.mult),
                  reads=[prmr], writes=[prmr])
            cstf = kb.sb(es, "rwcstf", [128, 256], F32)
            cstr = Reg()
            kb.dma("sp", self.ds_misc, cstf[:], self.d_rwcstf, writes=[cstr])
            ident = cstf[:, 0:128]
            bdmask = cstf[:, 128:256]
            gmask = kb.sb(es, "rwgmask", [64, 2, 5, 2, 64], BF16)
            kb.dma("pool", self.ds_miscp, gmask[:].rearrange("p a b c d -> p (a b c d)"), self.d_rwgmask, writes=[cstr])
            identb = kb.sb(es, "rwidb", [128, 128], BF16)
            id64 = kb.sb(es, "rwid64", [64, 2, 64], BF16)
            onesbd = kb.sb(es, "rwonesbd", [128, 128], BF16)
            ones64 = kb.sb(es, "rwones64", [128, 128], BF16)
            kb.op("dve", lambda e: e.tensor_copy(out=identb[:], in_=ident), reads=[cstr], writes=[cstr])
            for h in range(2):
                kb.op("dve", lambda e: e.tensor_copy(out=id64[:, h, :], in_=cstf[0:64, 0:64]), reads=[cstr], writes=[cstr])
            kb.op("dve", lambda e: e.tensor_copy(out=onesbd[:], in_=bdmask), reads=[cstr], writes=[cstr])
            kb.op("dve", lambda e: e.tensor_scalar(out=ones64[:], in0=bdmask, scalar1=1.0 / 64.0, scalar2=None, op0=ALU.mult), reads=[cstr], writes=[cstr])
            epsg = kb.sb(es, "rwepsg", [128, 2], F32)
            kb.op("dve", lambda e: e.memset(epsg[:, 0:1], 64e-5), writes=[cstr])
            kb.op("dve", lambda e: e.memset(epsg[:, 1:2], 1e-24), writes=[cstr])
            XX = kb.sb(es, "rwXX", [128, NC_, T], BF16)
            XXr = [[Reg() for _ in range(5)] for _ in range(NC_)]
            with ExitStack() as es0:
                tf = [kb.sb(es0, f"rwtf{s_}", [128, T], F32) for s_ in range(2)]
                tfr = [Reg(), Reg()]
                for c in range(NC_):
                    s_ = c % 2
                    for (s0, s1) in ((0, SEQ), (SEQ, T)):
                        kb.op("pool", lambda e: e.tensor_tensor(out=tf[s_][:, s0 + 1:s1 - 1], in0=U[:, c, s0:s1 - 2], in1=U[:, c, s0 + 2:s1], op=ALU.add),
                              reads=self.Ur[c], writes=[tfr[s_]])
                        kb.op("pool", lambda e: e.tensor_copy(out=tf[s_][:, s0:s0 + 1], in_=U[:, c, s0 + 1:s0 + 2]), reads=self.Ur[c], writes=[tfr[s_]])
                        kb.op("pool", lambda e: e.tensor_copy(out=tf[s_][:, s1 - 1:s1], in_=U[:, c, s1 - 2:s1 - 1]), reads=self.Ur[c], writes=[tfr[s_]])
                    kb.op("dve", lambda e: e.scalar_tensor_tensor(out=XX[:, c, :], in0=tf[s_][:], scalar=0.5, in1=U[:, c, :], op0=ALU.mult, op1=ALU.subtract),
                          reads=[tfr[s_]] + self.Ur[c], writes=XXr[c])
                kb.barrier()
            pw = [kb.sb(es, f"rwpw{s_}", [128, NC_, 128], BF16) for s_ in range(2)]
            pws = [kb.sb(es, f"rwpws{s_}", [128, NC_, 128], BF16) for s_ in range(2)]
            pwr = [Reg(), Reg()]
            pwsr = [Reg(), Reg()]
            pwd = [kb.dsem(f"rwpw{s_}") for s_ in range(2)]
            wc = {"n": 0}

            def xproj(Wd_oc, jkind, blks, evac):
                s_ = wc["n"] % 2
                wc["n"] += 1
                kb.dma("pool", pwd[s_], pw[s_][:].rearrange("p k m -> p (k m)"), Wd_oc, writes=[pwr[s_]])
                for kc in range(NC_):
                    kb.op("dve" if kc % 2 == 0 else "pool",
                          lambda e: e.tensor_scalar(out=pws[s_][:, kc, :], in0=pw[s_][:, kc, :], scalar1=prm[:, MU + jkind * 8 + kc:MU + jkind * 8 + kc + 1],
                                                    scalar2=None, op0=ALU.mult),
                          reads=[pwr[s_], prmr], writes=[pwsr[s_]])
                for b in blks:
                    t0, n, kind = BLOCKS[b]
                    pb = nbank()
                    for kc in range(NC_):
                        kb.op("pe", lambda e: e.matmul(self.ps[pb][:, :n], lhsT=pw[s_][:, kc, :], rhs=U[:, kc, t0:t0 + n], start=(kc == 0), stop=False),
                              reads=[pwr[s_], self.Ur[kc][b]], writes=[self.psr[pb]])
                    for kc in range(NC_):
                        kb.op("pe", lambda e: e.matmul(self.ps[pb][:, :n], lhsT=pws[s_][:, kc, :], rhs=XX[:, kc, t0:t0 + n], start=False, stop=(kc == NC_ - 1)),
                              reads=[pwsr[s_], XXr[kc][b]], writes=[self.psr[pb]])
                    evac(b, pb, t0, n)

            LR = kb.sb(es, "rwLR", [128, 3, T], BF16)
            LRr = [[Reg() for _ in range(5)] for _ in range(3)]
            for (li, jk, fn) in ((0, 5, AF.Sigmoid), (1, 1, AF.Tanh), (2, 4, AF.Copy)):
                def ev(b, pb, t0, n, li=li, fn=fn):
                    kb.op("act", lambda e: e.activation(out=LR[:, li, t0:t0 + n], in_=self.ps[pb][:, :n], func=fn),
                          reads=[self.psr[pb]], writes=[LRr[li][b]])
                xproj(self.d_rwlr[li], jk, lblocks if li == 0 else blocks, ev)
            if dbg == "rw1":
                kb.barrier()
                return 'stop'
            lrw = kb.sb(es, "rwlrw", [128, 3, 2, 128], BF16)
            lrwr = [Reg(), Reg()]
            lrwd = [kb.dsem(f"rwlrw{s_}") for s_ in range(2)]
            ZT1 = kb.sb(es, "rwZT1", [128, SEQ], BF16)
            ZT1r = [Reg() for _ in range(5)]
            Rr_ = kb.sb(es, "rwR", [128, SEQ], BF16); Rr = [Reg() for _ in range(5)]
            Kk = kb.sb(es, "rwK", [128, T], BF16); Kr = [Reg() for _ in range(5)]
            KKn = kb.sb(es, "rwKK", [128, T], BF16); KKr = [Reg() for _ in range(5)]
            VTf = kb.sb(es, "rwVT", [128, T], BF16); VTr = [Reg() for _ in range(5)]
            Gg = kb.sb(es, "rwG", [128, SEQ], BF16); Ggr = [Reg() for _ in range(5)]
            BON = kb.sb(es, "rwBON", [128, SEQ], BF16); BONr = [Reg() for _ in range(5)]
            YA = kb.sb(es, "rwYA", [128, SEQ], F32); YAr = [Reg() for _ in range(5)]
            LW = kb.sb(es, "rwLW", [128, T], F32); LWr = [Reg() for _ in range(5)]
            KD = kb.sb(es, "rwKD", [128, T], BF16); KDr = [Reg() for _ in range(5)]
            BD = kb.sb(es, "rwBD", [128, T], BF16); BDr = [Reg() for _ in range(5)]
            tA = [kb.sb(es, f"rwtA{s_}", [128, 512], F32) for s_ in range(2)]
            tAr = [Reg(), Reg()]
            tB = [kb.sb(es, f"rwtB{s_}", [128, 512], BF16) for s_ in range(2)]
            tBr = [Reg(), Reg()]
            SCs = [kb.sb(es, f"rwSC{s_}", [128, 2, 64], F32) for s_ in range(G)]; SCr = [Reg() for _ in range(G)]
            TOT = [kb.sb(es, f"rwTOT{s_}", [128, 2], F32) for s_ in range(G)]; TOTr = [Reg() for _ in range(G)]
            EE = [kb.sb(es, f"rwEE{s_}", [128, 4, 64], F32) for s_ in range(G)]; EEr = [Reg() for _ in range(G)]
            OPS = [kb.sb(es, f"rwOPS{s_}", [128, 6, 64], BF16) for s_ in range(G)]; OPSr = [Reg() for _ in range(G)]
            RT32 = [kb.sb(es, f"rwRT{s_}", [128, 64], F32) for s_ in range(G)]; RT32r = [Reg() for _ in range(G)]
            TOK = [kb.sb(es, f"rwTOK{s_}", [64, 4, 128], BF16) for s_ in range(G)]; TOKr = [Reg() for _ in range(G)]
            GM = [kb.sb(es, f"rwGM{s_}", [64, 5, 2, 64], BF16) for s_ in range(G)]; GMr = [Reg() for _ in range(G)]
            NF = [kb.sb(es, f"rwNF{s_}", [64, 2, 2, 64], F32) for s_ in range(G)]; NFr = [Reg() for _ in range(G)]
            PP = [kb.sb(es, f"rwPP{s_}", [64, 2, 2, 2, 64], F32) for s_ in range(G)]; PPr = [[Reg() for _ in range(2)] for _ in range(G)]
            TT32 = [kb.sb(es, f"rwTT{s_}", [64, 2, 2, 64], F32) for s_ in range(G)]; TTr = [[Reg() for _ in range(2)] for _ in range(G)]
            TTb = [kb.sb(es, f"rwTTb{s_}", [64, 2, 64], BF16) for s_ in range(G)]; TTbr = [Reg() for _ in range(G)]
            id64f = kb.sb(es, "rwid64f", [64, 2, 64], F32)
            for h in range(2):
                kb.op("dve", lambda e: e.tensor_copy(out=id64f[:, h, :], in_=cstf[0:64, 0:64]), reads=[cstr], writes=[cstr])
            GS = [kb.sb(es, f"rwGS{s_}", [64, 3, 2, 64], BF16) for s_ in range(G)]; GSr = [[Reg() for _ in range(3)] for _ in range(G)]
            QH = [kb.sb(es, f"rwQH{s_}", [128, 64], BF16) for s_ in range(G)]; QHr = [Reg() for _ in range(G)]
            PH = [kb.sb(es, f"rwPH{s_}", [128, 128], F32) for s_ in range(G)]; PHr = [Reg() for _ in range(G)]
            Abd = kb.sb(es, "rwA", [128, 128], F32); Ar = Reg()
            Abf = kb.sb(es, "rwAbf", [128, 128], BF16); Abfr = Reg()
            psbf = [self.ps[i].bitcast(BF16) for i in range(8)]

            for pr in range(NC_):
                sl = pr % 2
                kb.dma("pool", lrwd[sl], lrw[:, :, sl, :], self.d_rwlrw[pr].rearrange("a p m -> p a m"), writes=[lrwr[sl]])
                def ev_r(b, pb, t0, n):
                    kb.op("act", lambda e: e.activation(out=Rr_[:, t0:t0 + n], in_=self.ps[pb][:, :n], func=AF.Copy), reads=[self.psr[pb]], writes=[Rr[b]])
                xproj(self.d_rwwrkv[0, pr], 0, lblocks, ev_r)

                def ev_k(b, pb, t0, n):
                    s_ = b % 2
                    kb.op("act", lambda e: e.activation(out=Kk[:, t0:t0 + n], in_=self.ps[pb][:, :n], func=AF.Copy), reads=[self.psr[pb]], writes=[Kr[b]])
                    kb.op("act", lambda e: e.activation(out=tA[s_][:, :n], in_=self.ps[pb][:, :n], func=AF.Copy, scale=prm[:, KK_ + pr:KK_ + pr + 1]),
                          reads=[self.psr[pb], prmr], writes=[tAr[s_]])
                    kb.op("act", lambda e: e.activation(out=tB[s_][:, :n], in_=tA[s_][:, :n], func=AF.Square), reads=[tAr[s_]], writes=[tBr[s_]])
                    p2 = nbank()
                    kb.op("pe", lambda e: e.matmul(self.ps[p2][:, :n], lhsT=onesbd[:], rhs=tB[s_][:, :n], start=True, stop=True),
                          reads=[cstr, tBr[s_]], writes=[self.psr[p2]])
                    kb.op("act", lambda e: e.activation(out=tB[s_][:, :n], in_=self.ps[p2][:, :n], func=AF.Ln, bias=epsg[:, 1:2]), reads=[self.psr[p2], cstr], writes=[tBr[s_]])
                    kb.op("act", lambda e: e.activation(out=tB[s_][:, :n], in_=tB[s_][:, :n], func=AF.Exp, scale=-0.5), reads=[tBr[s_]], writes=[tBr[s_]])
                    kb.op("dve", lambda e: e.tensor_tensor(out=KKn[:, t0:t0 + n], in0=tA[s_][:, :n], in1=tB[s_][:, :n], op=ALU.mult),
                          reads=[tAr[s_], tBr[s_]], writes=[KKr[b]])
                xproj(self.d_rwwrkv[1, pr], 2, blocks, ev_k)

                def ev_v(b, pb, t0, n):
                    kb.op("act", lambda e: e.activation(out=VTf[:, t0:t0 + n], in_=self.ps[pb][:, :n], func=AF.Copy), reads=[self.psr[pb]], writes=[VTr[b]])
                xproj(self.d_rwwrkv[2, pr], 3, blocks, ev_v)
                for b in lblocks:
                    t0, n, kind = BLOCKS[b]
                    pb = nbank()
                    kb.op("pe", lambda e: e.matmul(self.ps[pb][:, :n], lhsT=lrw[:, 0, sl, :], rhs=LR[:, 0, t0:t0 + n], start=True, stop=True),
                          reads=[lrwr[sl], LRr[0][b]], writes=[self.psr[pb]])
                    kb.op("act", lambda e: e.activation(out=Gg[:, t0:t0 + n], in_=self.ps[pb][:, :n], func=AF.Copy), reads=[self.psr[pb]], writes=[Ggr[b]])
                for d in range(2):
                    for b in blocks:
                        t0, n, kind = BLOCKS[b]
                        s_ = b % 2
                        pb = nbank()
                        kb.op("pe", lambda e: e.matmul(self.ps[pb][:, :n], lhsT=lrw[d * 64:(d + 1) * 64, 1, sl, :], rhs=LR[d * 64:(d + 1) * 64, 1, t0:t0 + n], start=True, stop=True),
                              reads=[lrwr[sl], LRr[1][b]], writes=[self.psr[pb]])
                        kb.op("act", lambda e: e.activation(out=tA[s_][:, :n], in_=self.ps[pb][:, :n], func=AF.Sigmoid, bias=prm[:, W0 + d * 8 + pr:W0 + d * 8 + pr + 1]),
                              reads=[self.psr[pb], prmr], writes=[tAr[s_]])
                        kb.op("dve", lambda e: e.tensor_scalar(out=LW[:, t0:t0 + n], in0=tA[s_][:, :n], scalar1=-EM05, scalar2=None, op0=ALU.mult),
                              reads=[tAr[s_]], writes=[LWr[b]])
                        pb = nbank()
                        kb.op("pe", lambda e: e.matmul(self.ps[pb][:, :n], lhsT=lrw[d * 64:(d + 1) * 64, 2, sl, :], rhs=LR[d * 64:(d + 1) * 64, 2, t0:t0 + n], start=True, stop=True),
                              reads=[lrwr[sl], LRr[2][b]], writes=[self.psr[pb]])
                        kb.op("act", lambda e: e.activation(out=tA[s_][:, :n], in_=self.ps[pb][:, :n], func=AF.Sigmoid, bias=prm[:, A0 + d * 8 + pr:A0 + d * 8 + pr + 1]),
                              reads=[self.psr[pb], prmr], writes=[tAr[s_]])
                        kb.op("dve", lambda e: e.tensor_tensor(out=BD[:, t0:t0 + n], in0=KKn[:, t0:t0 + n], in1=tA[s_][:, :n], op=ALU.mult),
                              reads=[KKr[b], tAr[s_]], writes=[BDr[b]])
                        kb.op("act", lambda e: e.activation(out=tA[s_][:, :n], in_=tA[s_][:, :n], func=AF.Identity, scale=prm[:, KA + pr:KA + pr + 1], bias=prm[:, OMKA + pr:OMKA + pr + 1]),
                              reads=[tAr[s_], prmr], writes=[tAr[s_]])
                        kb.op("dve", lambda e: e.tensor_tensor(out=KD[:, t0:t0 + n], in0=Kk[:, t0:t0 + n], in1=tA[s_][:, :n], op=ALU.mult),
                              reads=[Kr[b], tAr[s_]], writes=[KDr[b]])
                        if kind == 0:
                            kb.op("dve", lambda e: e.scalar_tensor_tensor(out=tB[s_][:, :n], in0=KD[:, t0:t0 + n], scalar=prm[:, RK + pr:RK + pr + 1], in1=Rr_[:, t0:t0 + n], op0=ALU.mult, op1=ALU.mult),
                                  reads=[KDr[b], Rr[b], prmr], writes=[tBr[s_]])
                            pb = nbank()
                            kb.op("pe", lambda e: e.matmul(self.ps[pb][:, :n], lhsT=onesbd[:], rhs=tB[s_][:, :n], start=True, stop=True),
                                  reads=[cstr, tBr[s_]], writes=[self.psr[pb]])
                            if d == 0:
                                kb.op("act", lambda e: e.activation(out=BON[:, t0:t0 + n], in_=self.ps[pb][:, :n], func=AF.Copy), reads=[self.psr[pb]], writes=[BONr[b]])
                            else:
                                kb.op("dve", lambda e: e.tensor_tensor(out=BON[:, t0:t0 + n], in0=self.ps[pb][:, :n], in1=BON[:, t0:t0 + n], op=ALU.add),
                                      reads=[self.psr[pb], BONr[b]], writes=[BONr[b]])
                    if dbg == "rw2":
                        kb.barrier()
                        return 'stop'
                    kb.op("dve", lambda e: e.memset(Abd[:], 0.0), writes=[Ar])
                    kb.op("dve", lambda e: e.memset(Abf[:], 0.0), writes=[Abfr])
                    order = ([32, 33, 34, 35] + list(range(32))) if d == 0 else ([35, 34, 33, 32] + list(range(31, -1, -1)))
                    def chunk_gen(ci, n_):
                        c0 = 64 * n_
                        b = min(c0 // 512, 4)
                        lat = n_ < 32
                        q = ci % G
                        cbs = {'i': 0}

                        def nbank():
                            cbs['i'] += 1
                            return 2 * q + (cbs['i'] % 2)
                        ng = 5 if lat else 3
                        kb.op("dve", lambda e: e.tensor_tensor_scan(out=SCs[q][:, 0, :], data0=self.onesf[:, 0:64], data1=LW[:, c0:c0 + 64], initial=0.0, op0=ALU.mult, op1=ALU.add),
                              reads=[LWr[b], self.onesr], writes=[SCr[q]])
                        kb.op("dve", lambda e: e.tensor_tensor(out=SCs[q][:, 1, :], in0=SCs[q][:, 0, :], in1=LW[:, c0:c0 + 64], op=ALU.subtract),
                              reads=[SCr[q], LWr[b]], writes=[SCr[q]])
                        kb.op("dve", lambda e: e.tensor_copy(out=TOT[q][:, 0:1], in_=SCs[q][:, 0, 63:64]), reads=[SCr[q]], writes=[TOTr[q]])
                        kb.op("dve", lambda e: e.tensor_scalar(out=TOT[q][:, 1:2], in0=SCs[q][:, 0, 63:64], scalar1=-1.0, scalar2=None, op0=ALU.mult), reads=[SCr[q]], writes=[TOTr[q]])
                        cs, cxf = SCs[q][:, 0, :], SCs[q][:, 1, :]
                        tot, ntot = TOT[q][:, 0:1], TOT[q][:, 1:2]
                        if d == 0:
                            exs = [(cxf, 1.0, None), (cs, -1.0, None), (cs, 1.0, None), (cs, -1.0, tot)]
                        else:
                            exs = [(cs, -1.0, tot), (cxf, 1.0, ntot), (cxf, -1.0, tot), (cxf, 1.0, None)]
                        for i, (src, sc_, bi) in enumerate(exs):
                            if i == 2 and not lat:
                                continue
                            if bi is None:
                                kb.op("act", lambda e: e.activation(out=EE[q][:, i, :], in_=src, func=AF.Exp, scale=sc_), reads=[SCr[q]], writes=[EEr[q]])
                            else:
                                kb.op("act", lambda e: e.activation(out=EE[q][:, i, :], in_=src, func=AF.Exp, scale=sc_, bias=bi), reads=[SCr[q], TOTr[q]], writes=[EEr[q]])
                        yield
                        opl = [(0, KKn, KKr, 0), (1, BD, BDr, 1), (2, KD, KDr, 1), (4, BD, BDr, 3), (5, KD, KDr, 3)]
                        for oi, (o_, srcb, srcr, ei) in enumerate(opl):
                            kb.op("dve" if oi % 2 == 0 else "pool",
                                  lambda e: e.tensor_tensor(out=OPS[q][:, o_, :], in0=srcb[:, c0:c0 + 64], in1=EE[q][:, ei, :], op=ALU.mult),
                                  reads=[srcr[b], EEr[q]], writes=[OPSr[q]])
                        if lat:
                            kb.op("dve", lambda e: e.tensor_tensor(out=RT32[q][:], in0=Rr_[:, c0:c0 + 64], in1=EE[q][:, 2, :], op=ALU.mult),
                                  reads=[Rr[b], EEr[q]], writes=[RT32r[q]])
                            kb.op("pool", lambda e: e.tensor_copy(out=OPS[q][:, 3, :], in_=RT32[q][:]), reads=[RT32r[q]], writes=[OPSr[q]])
                        yield
                        pb = nbank()
                        for i, o_ in enumerate((0, 4, 5)):
                            kb.op("pe", lambda e: e.transpose(psbf[pb][0:64, i * 128:(i + 1) * 128], OPS[q][:, o_, :], identb[:]),
                                  reads=[OPSr[q], cstr], writes=[self.psr[pb]])
                        kb.op("pe", lambda e: e.transpose(psbf[pb][0:64, 384:512], VTf[:, c0:c0 + 64], identb[:]),
                              reads=[VTr[b], cstr], writes=[self.psr[pb]])
                        kb.op("act", lambda e: e.activation(out=TOK[q][:].rearrange("p a b -> p (a b)"), in_=psbf[pb][0:64, 0:512], func=AF.Copy),
                              reads=[self.psr[pb]], writes=[TOKr[q]])
                        yield
                        gpairs = [(0, 1), (1, 0), (2, 0), (1, 3), (2, 3)]
                        pbh = [nbank(), nbank()]
                        for wi in range(ng):
                            li_, ri_ = gpairs[wi]
                            for h in range(2):
                                kb.op("pe", lambda e: e.matmul(self.ps[pbh[h]][0:64, wi * 64:(wi + 1) * 64], lhsT=OPS[q][h * 64:(h + 1) * 64, li_, :],
                                                                rhs=OPS[q][h * 64:(h + 1) * 64, ri_, :], start=True, stop=True),
                                      reads=[OPSr[q]], writes=[self.psr[pbh[h]]])
                        for h in range(2):
                            kb.op("dve", lambda e: e.tensor_tensor(out=GM[q][:, 0:ng, h, :], in0=self.ps[pbh[h]][0:64, 0:ng * 64].rearrange("p (a c) -> p a c", a=ng),
                                                                   in1=gmask[:, d, 0:ng, h, :], op=ALU.mult),
                                  reads=[self.psr[pbh[h]], cstr], writes=[GMr[q]])
                            kb.op("dve", lambda e: e.tensor_tensor(out=NF[q][:, :, h, :], in0=self.ps[pbh[h]][0:64, 0:128].rearrange("p (a c) -> p a c", a=2),
                                                                   in1=gmask[:, d, 0:2, h, :], op=ALU.mult),
                                  reads=[self.psr[pbh[h]], cstr], writes=[NFr[q]])
                        yield
                        kb.op("pool", lambda e: e.tensor_tensor(out=TT32[q][:, 0, :, :], in0=NF[q][:, 1, :, :], in1=id64f[:], op=ALU.add),
                              reads=[NFr[q], cstr], writes=[TTr[q][0]])
                        Pm = lambda lv, tr: (NF[q][:, tr, :, :] if lv == 0 else PP[q][:, lv % 2, tr, :, :])
                        Pmr = lambda lv: (NFr[q] if lv == 0 else PPr[q][lv % 2])
                        for lv in range(1, 6):
                            pb = nbank()
                            ntr = 2 if lv < 5 else 1
                            for tr in range(ntr):
                                for h in range(2):
                                    lt, rt = (1, 0) if tr == 0 else (0, 1)
                                    kb.op("pe", lambda e: e.matmul(self.ps[pb][0:64, (tr * 2 + h) * 64:(tr * 2 + h + 1) * 64], lhsT=Pm(lv - 1, lt)[:, h, :], rhs=Pm(lv - 1, rt)[:, h, :],
                                                                    start=True, stop=True),
                                          reads=[Pmr(lv - 1)], writes=[self.psr[pb]])
                            kb.op("act", lambda e: e.activation(out=PP[q][:, lv % 2, 0:ntr, :, :].rearrange("p a b c -> p (a b c)"), in_=self.ps[pb][0:64, 0:ntr * 128], func=AF.Copy),
                                  reads=[self.psr[pb]], writes=[PPr[q][lv % 2]])
                            yield
                            pb = nbank()
                            for h in range(2):
                                kb.op("pe", lambda e: e.matmul(self.ps[pb][0:64, h * 64:(h + 1) * 64], lhsT=PP[q][:, lv % 2, 0, h, :], rhs=TT32[q][:, (lv - 1) % 2, h, :], start=True, stop=True),
                                      reads=[PPr[q][lv % 2], TTr[q][(lv - 1) % 2]], writes=[self.psr[pb]])
                            kb.op("dve", lambda e: e.tensor_tensor(out=TT32[q][:, lv % 2, :, :].rearrange("p b c -> p (b c)"), in0=self.ps[pb][0:64, 0:128],
                                                                   in1=TT32[q][:, (lv - 1) % 2, :, :].rearrange("p b c -> p (b c)"), op=ALU.add),
                                  reads=[self.psr[pb], TTr[q][(lv - 1) % 2]], writes=[TTr[q][lv % 2]])
                            yield
                        kb.op("act", lambda e: e.activation(out=TTb[q][:].rearrange("p b c -> p (b c)"), in_=TT32[q][:, 1, :, :].rearrange("p b c -> p (b c)"), func=AF.Copy),
                              reads=[TTr[q][1]], writes=[TTbr[q]])
                        TTf = TTb[q]
                        TTfr = TTbr[q]
                        yield
                        pb = nbank()
                        for h in range(2):
                            kb.op("pe", lambda e: e.matmul(self.ps[pb][0:64, h * 64:(h + 1) * 64], lhsT=TTf[:, h, :], rhs=TOK[q][:, 0, h * 64:(h + 1) * 64], start=True, stop=True),
                                  reads=[TTfr, TOKr[q]], writes=[self.psr[pb]])
                            kb.op("pe", lambda e: e.matmul(self.ps[pb][0:64, 128 + h * 64:128 + (h + 1) * 64], lhsT=GM[q][:, 2, h, :], rhs=TOK[q][:, 3, h * 64:(h + 1) * 64], start=True, stop=True),
                                  reads=[GMr[q], TOKr[q]], writes=[self.psr[pb]])
                        kb.op("act", lambda e: e.activation(out=GS[q][:, 0:2, :, :].rearrange("p a b c -> p (a b c)"), in_=self.ps[pb][0:64, 0:256], func=AF.Copy),
                              reads=[self.psr[pb]], writes=[GSr[q][0], GSr[q][1]])
                        yield
                        pb = nbank()
                        for h in range(2):
                            kb.op("pe", lambda e: e.matmul(self.ps[pb][0:64, h * 64:(h + 1) * 64], lhsT=TTf[:, h, :], rhs=GS[q][:, 1, h, :], start=True, stop=True),
                                  reads=[TTfr, GSr[q][1]], writes=[self.psr[pb]])
                        kb.op("act", lambda e: e.activation(out=GS[q][:, 2, :, :].rearrange("p b c -> p (b c)"), in_=self.ps[pb][0:64, 0:128], func=AF.Copy),
                              reads=[self.psr[pb]], writes=[GSr[q][2]])
                        Gfl = GS[q][:, 0, :, :].rearrange("p b c -> p (b c)")
                        U0fl = GS[q][:, 2, :, :].rearrange("p b c -> p (b c)")
                        yield
                        gcol = EE[q][:, 2, 63:64] if d == 0 else EE[q][:, 2, 0:1]
                        if not lat:
                            kb.op("act", lambda e: e.activation(out=EE[q][:, 2, 0:1], in_=TOT[q][:, 0:1], func=AF.Exp), reads=[TOTr[q], EEr[q]], writes=[EEr[q]])
                            gcol = EE[q][:, 2, 0:1]
                        pb = nbank()
                        kb.op("pe", lambda e: e.matmul(self.ps[pb][:, 0:128], lhsT=Gfl, rhs=TOK[q][:, 1, :], start=True, stop=True),
                              reads=[GSr[q][0], TOKr[q]], writes=[self.psr[pb]])
                        kb.op("dve", lambda e: e.scalar_tensor_tensor(out=PH[q][:], in0=ident, scalar=gcol, in1=self.ps[pb][:, 0:128], op0=ALU.mult, op1=ALU.subtract),
                              reads=[cstr, EEr[q], self.psr[pb]], writes=[PHr[q]])
                        kb.op("pool", lambda e: e.tensor_tensor(out=PH[q][:], in0=PH[q][:], in1=bdmask, op=ALU.mult), reads=[PHr[q], cstr], writes=[PHr[q]])
                        yield
                        if lat:
                            pb = nbank()
                            for h in range(2):
                                kb.op("pe", lambda e: e.matmul(self.ps[pb][:, h * 64:(h + 1) * 64], lhsT=Gfl, rhs=GM[q][:, 3, h, :], start=True, stop=True),
                                      reads=[GSr[q][0], GMr[q]], writes=[self.psr[pb]])
                            for h in range(2):
                                kb.op("dve", lambda e: e.tensor_tensor(out=QH[q][h * 64:(h + 1) * 64, :], in0=RT32[q][h * 64:(h + 1) * 64, :],
                                                                       in1=self.ps[pb][h * 64:(h + 1) * 64, h * 64:(h + 1) * 64], op=ALU.subtract),
                                      reads=[RT32r[q], self.psr[pb]], writes=[QHr[q]])
                            yield
                            pb = nbank()
                            for h in range(2):
                                kb.op("pe", lambda e: e.matmul(self.ps[pb][:, h * 64:(h + 1) * 64], lhsT=U0fl, rhs=GM[q][:, 3, h, :], start=True, stop=False),
                                      reads=[GSr[q][2], GMr[q]], writes=[self.psr[pb]])
                                kb.op("pe", lambda e: e.matmul(self.ps[pb][:, h * 64:(h + 1) * 64], lhsT=TOK[q][:, 3, :], rhs=GM[q][:, 4, h, :], start=False, stop=False),
                                      reads=[TOKr[q], GMr[q]], writes=[self.psr[pb]])
                                kb.op("pe", lambda e: e.matmul(self.ps[pb][:, h * 64:(h + 1) * 64], lhsT=Abf[:], rhs=QH[q][:], start=False, stop=True),
                                      reads=[Abfr, QHr[q]], writes=[self.psr[pb]])
                            for h in range(2):
                                if d == 0:
                                    kb.op("act", lambda e: e.activation(out=YA[h * 64:(h + 1) * 64, c0:c0 + 64], in_=self.ps[pb][h * 64:(h + 1) * 64, h * 64:(h + 1) * 64], func=AF.Copy),
                                          reads=[self.psr[pb]], writes=[YAr[b]])
                                else:
                                    kb.op("dve", lambda e: e.tensor_tensor(out=YA[h * 64:(h + 1) * 64, c0:c0 + 64], in0=self.ps[pb][h * 64:(h + 1) * 64, h * 64:(h + 1) * 64],
                                                                           in1=YA[h * 64:(h + 1) * 64, c0:c0 + 64], op=ALU.add),
                                          reads=[self.psr[pb], YAr[b]], writes=[YAr[b]])
                        yield
                        pb = nbank()
                        kb.op("pe", lambda e: e.matmul(self.ps[pb][:, 0:128], lhsT=TOK[q][:, 1, :], rhs=U0fl, start=True, stop=False),
                              reads=[TOKr[q], GSr[q][2]], writes=[self.psr[pb]])
                        kb.op("pe", lambda e: e.matmul(self.ps[pb][:, 0:128], lhsT=TOK[q][:, 2, :], rhs=TOK[q][:, 3, :], start=False, stop=False),
                              reads=[TOKr[q]], writes=[self.psr[pb]])
                        kb.op("pe", lambda e: e.matmul(self.ps[pb][:, 0:128], lhsT=PH[q][:], rhs=Abd[:], start=False, stop=True),
                              reads=[PHr[q], Ar], writes=[self.psr[pb]])
                        kb.op("dve", lambda e: e.tensor_tensor(out=Abd[:], in0=self.ps[pb][:, 0:128], in1=bdmask, op=ALU.mult),
                              reads=[self.psr[pb], cstr], writes=[Ar])
                        kb.op("pool", lambda e: e.tensor_copy(out=Abf[:], in_=Abd[:]), reads=[Ar], writes=[Abfr])
                    active = []
                    pending = list(enumerate(order))
                    while pending or active:
                        if pending and len(active) < G:
                            active.append(chunk_gen(*pending.pop(0)))
                        for g_ in list(active):
                            try:
                                next(g_)
                            except StopIteration:
                                active.remove(g_)
                    if dbg == f"rwdump{pr}_{d}":
                        dd = kb.dsem("dbg")
                        self.d_dbg = self.nc.dram_tensor("dbgout", [8, 128, T], F32, kind="ExternalOutput").ap()
                        kb.dma("sp", dd, self.d_dbg[0][:, 0:SEQ], YA[:], reads=YAr)
                        kb.dma("sp", dd, self.d_dbg[1], LW[:], reads=LWr)
                        kb.dma("pool", dd, self.d_dbg[2], KKn[:], reads=KKr)
                        kb.dma("pool", dd, self.d_dbg[3], KD[:], reads=KDr)
                        kb.dma("pool", dd, self.d_dbg[4], BD[:], reads=BDr)
                        kb.dma("pool", dd, self.d_dbg[5][:, 0:SEQ], Rr_[:], reads=Rr)
                        kb.dma("pool", dd, self.d_dbg[6], VTf[:], reads=VTr)
                        kb.dma("sp", dd, self.d_dbg[7][:, 0:128], Abd[:], reads=[Ar])
                        kb.eng["sp"].wait_ge(dd.sem, dd.cnt)
                        kb.eng["pool"].wait_ge(dd.sem, dd.cnt)
                        kb.barrier()
                        return 'stop'
                for b in lblocks:
                    t0, n, kind = BLOCKS[b]
                    s_ = b % 2
                    kb.op("act", lambda e: e.activation(out=tB[s_][:, :n], in_=YA[:, t0:t0 + n], func=AF.Copy), reads=[YAr[b]], writes=[tBr[s_]])
                    pm = nbank()
                    kb.op("pe", lambda e: e.matmul(self.ps[pm][:, :n], lhsT=ones64[:], rhs=tB[s_][:, :n], start=True, stop=True), reads=[cstr, tBr[s_]], writes=[self.psr[pm]])
                    kb.op("act", lambda e: e.activation(out=tB[s_][:, :n], in_=YA[:, t0:t0 + n], func=AF.Square), reads=[YAr[b]], writes=[tBr[s_]])
                    pq = nbank()
                    kb.op("pe", lambda e: e.matmul(self.ps[pq][:, :n], lhsT=ones64[:], rhs=tB[s_][:, :n], start=True, stop=True), reads=[cstr, tBr[s_]], writes=[self.psr[pq]])
                    kb.op("act", lambda e: e.activation(out=tA[s_][:, :n], in_=self.ps[pm][:, :n], func=AF.Square), reads=[self.psr[pm]], writes=[tAr[s_]])
                    kb.op("dve", lambda e: e.tensor_tensor(out=tA[s_][:, :n], in0=self.ps[pq][:, :n], in1=tA[s_][:, :n], op=ALU.subtract), reads=[self.psr[pq], tAr[s_]], writes=[tAr[s_]])
                    kb.op("act", lambda e: e.activation(out=tA[s_][:, :n], in_=tA[s_][:, :n], func=AF.Ln, bias=epsg[:, 0:1]), reads=[tAr[s_], cstr], writes=[tAr[s_]])
                    kb.op("act", lambda e: e.activation(out=tA[s_][:, :n], in_=tA[s_][:, :n], func=AF.Exp, scale=-0.5), reads=[tAr[s_]], writes=[tAr[s_]])
                    kb.op("dve", lambda e: e.tensor_tensor(out=YA[:, t0:t0 + n], in0=YA[:, t0:t0 + n], in1=self.ps[pm][:, :n], op=ALU.subtract), reads=[YAr[b], self.psr[pm]], writes=[YAr[b]])
                    kb.op("pool", lambda e: e.tensor_tensor(out=YA[:, t0:t0 + n], in0=YA[:, t0:t0 + n], in1=tA[s_][:, :n], op=ALU.mult), reads=[YAr[b], tAr[s_]], writes=[YAr[b]])
                    kb.op("act", lambda e: e.activation(out=YA[:, t0:t0 + n], in_=YA[:, t0:t0 + n], func=AF.Identity, scale=prm[:, GNG + pr:GNG + pr + 1], bias=prm[:, GNB + pr:GNB + pr + 1]),
                          reads=[YAr[b], prmr], writes=[YAr[b]])
                    kb.op("dve", lambda e: e.tensor_tensor(out=BON[:, t0:t0 + n], in0=BON[:, t0:t0 + n], in1=VTf[:, t0:t0 + n], op=ALU.mult), reads=[BONr[b], VTr[b]], writes=[BONr[b]])
                    kb.op("pool", lambda e: e.tensor_tensor(out=YA[:, t0:t0 + n], in0=YA[:, t0:t0 + n], in1=BON[:, t0:t0 + n], op=ALU.add), reads=[YAr[b], BONr[b]], writes=[YAr[b]])
                    kb.op("dve", lambda e: e.tensor_tensor(out=ZT1[:, t0:t0 + n], in0=YA[:, t0:t0 + n], in1=Gg[:, t0:t0 + n], op=ALU.mult), reads=[YAr[b], Ggr[b]], writes=[ZT1r[b]])
                kb.dma("sp", ds_z, d_zpark[pr], ZT1[:], reads=ZT1r)
                kb.barrier()
        self.unpark_h()
        with ExitStack() as es:
            ZT = kb.sb(es, "rwZT", [128, NC_, SEQ], BF16)
            ZTr = [[Reg() for _ in range(5)] for _ in range(NC_)]
            for c in range(NC_):
                kb.dma("sp", ds_z, ZT[:, c, :], d_zpark[c], writes=ZTr[c])
            self.proj(es, self.d_rwwo, range(NC_), ZT, ZTr, self.resid_evac(j), lblocks, tag="rwwo")
            kb.barrier()
        self.layernorm(L, j, lblocks)

    def _nb(self):
        self.psrot += 1
        return self.psrot % 8

def _prep_common(inp):
    f = np.float32
    out = {}
    aw = np.asarray(inp["ada_w"], f).reshape(DEPTH, NC_, 128, 72, 128)
    out["adaw"] = np.ascontiguousarray(aw.transpose(0, 3, 2, 1, 4)).reshape(DEPTH, 72, 128, NC_ * 128)
    out["adab"] = np.ascontiguousarray(np.asarray(inp["ada_b"], f).reshape(DEPTH, 72, 128).transpose(0, 2, 1))
    out["lng"] = np.ascontiguousarray(np.asarray(inp["ln_g"], f).reshape(DEPTH * 3 * NC_, 128).T)
    out["lnb"] = np.ascontiguousarray(np.asarray(inp["ln_b"], f).reshape(DEPTH * 3 * NC_, 128).T)
    w13 = np.asarray(inp["ffn_w13"], f).reshape(DEPTH, 2, NC_, 128, 2, NF, 128)
    out["w13"] = np.ascontiguousarray(w13.transpose(0, 1, 5, 3, 4, 2, 6)).reshape(DEPTH, 2, NF, 128, 2 * NC_ * 128)
    w2 = np.asarray(inp["ffn_w2"], f).reshape(DEPTH, 2, NF, 128, NC_, 128)
    out["w2"] = np.ascontiguousarray(w2.transpose(0, 1, 4, 3, 2, 5)).reshape(DEPTH, 2, NC_, 128, NF * 128)
    out["ident"] = np.eye(128, dtype=f)
    fm = lambda v: np.asarray(v, f).reshape(-1, 128).T
    wl = lambda W: np.ascontiguousarray(np.asarray(W, f).reshape(W.shape[0] // 128, 128, W.shape[1] // 128, 128)
                                        .transpose(2, 1, 0, 3)).reshape(W.shape[1] // 128, 128, W.shape[0])
    w1 = np.asarray(inp["cv_w1"][0], f).reshape(NC_, 128, 2, NC_, 128)
    out["cvw1"] = np.ascontiguousarray(w1.transpose(3, 1, 2, 0, 4)).reshape(NC_, 128, 2 * NC_ * 128)
    out["cvw2"] = wl(inp["cv_w2"][0])
    wdw = np.asarray(inp["cv_wdw"][0], f).reshape(31, NC_, 128).transpose(2, 0, 1).reshape(128, 31 * NC_)
    out["cvprm"] = np.ascontiguousarray(np.concatenate(
        [fm(inp["cv_b1"][0]), fm(inp["cv_bdw"][0]), fm(inp["cv_ln_g"][0]), fm(inp["cv_ln_b"][0]), fm(inp["cv_b2"][0]), wdw], axis=1))
    out["nawqkv"] = wl(inp["na_wqkv"][0])
    out["nawo"] = wl(inp["na_wo"][0])
    out["nastrip"] = _na_strips(np.asarray(inp["na_rpb"][0], f))
    win = np.asarray(inp["gla_win"][0], f)
    wlq = wl(win)
    out["glawqk"] = np.ascontiguousarray(wlq[0:8])
    m = np.arange(128)
    partner = (m // 64) * 64 + ((m % 64) + 32) % 64
    qk = win[:, 0:1024].reshape(D, 8, 128)[:, :, partner].reshape(D, 1024)
    out["glawqkp"] = wl(qk)
    wv = win[:, 1024:2048].reshape(NC_, 128, 4, 256)
    out["glawv"] = np.ascontiguousarray(wv.transpose(2, 1, 0, 3)).reshape(4, 128, NC_ * 256)
    out["glawg"] = np.ascontiguousarray(wlq[16:24])
    wo = np.asarray(inp["gla_wo"][0], f).reshape(4, 2, 128, NC_, 128)
    out["glawo"] = np.ascontiguousarray(wo.transpose(0, 3, 2, 1, 4)).reshape(4, NC_, 128, 256)
    wa1 = np.asarray(inp["gla_wa1"][0], f)
    wa1c = np.concatenate([wa1[0], wa1[1]], axis=1).reshape(NC_, 128, 32)
    out["glawa1"] = np.ascontiguousarray(wa1c.transpose(1, 0, 2)).reshape(128, NC_ * 32)
    wa2 = np.asarray(inp["gla_wa2"][0], f)
    wa2p = np.zeros((32, 2, 512), f)
    wa2p[0:16, 0] = wa2[0]
    wa2p[16:32, 1] = wa2[1]
    out["glawa2"] = wa2p.reshape(32, 1024)
    i = (m % 64) % 32
    freq = (10000.0 ** (-(i.astype(np.float64)) / 32.0))[:, None]
    t = np.arange(SEQ)[None, :]
    pos = np.where((m < 64)[:, None], t // 64, t % 64)
    ang = (pos.astype(np.float32) * freq.astype(np.float32)).astype(np.float32)
    sgn = np.where((m % 64) < 32, -1.0, 1.0)[:, None]
    out["glarope"] = np.stack([np.cos(ang), sgn * np.sin(ang)]).astype(f)
    sl = np.arange(64)[:, None]
    cc = np.arange(64)[None, :]
    masks = np.zeros((128, 4, 64), f)
    for par in range(2):
        masks[par * 64:(par + 1) * 64, par * 2 + 0] = (cc >= sl)
        masks[par * 64:(par + 1) * 64, par * 2 + 1] = (cc <= sl)
    ba = np.asarray(inp["gla_ba"][0], f).reshape(2, 4, 128).transpose(2, 0, 1).reshape(128, 8)
    out["glacst"] = np.ascontiguousarray(np.concatenate(
        [masks.reshape(128, 256), np.eye(128, dtype=f), fm(inp["gla_norm_g"][0]), ba], axis=1))
    out["rwwrkv"] = np.stack([wl(inp["rw_wrkv"][0][i]) for i in range(3)])
    out["rwwo"] = wl(inp["rw_wo"][0])
    cat2 = lambda a: np.concatenate([np.asarray(a[0], f), np.asarray(a[1], f)], axis=1)
    out["rwlr"] = np.stack([wl(np.asarray(inp["rw_g1"][0], f))[0], wl(cat2(inp["rw_w1"][0]))[0], wl(cat2(inp["rw_a1"][0]))[0]])
    g2 = np.asarray(inp["rw_g2"][0], f).reshape(128, NC_, 128)
    w2 = np.concatenate([np.asarray(inp["rw_w2"][0][0], f), np.asarray(inp["rw_w2"][0][1], f)], axis=0).reshape(128, NC_, 128)
    a2 = np.concatenate([np.asarray(inp["rw_a2"][0][0], f), np.asarray(inp["rw_a2"][0][1], f)], axis=0).reshape(128, NC_, 128)
    out["rwlrw"] = np.ascontiguousarray(np.stack([g2, w2, a2]).transpose(2, 0, 1, 3))
    out["rwprm"] = np.ascontiguousarray(np.concatenate(
        [fm(inp["rw_w0"][0]), fm(inp["rw_a0"][0]), fm(inp["rw_kk"][0]), fm(inp["rw_ka"][0]), fm(inp["rw_rk"][0]),
         fm(inp["rw_gn_g"][0]), fm(inp["rw_gn_b"][0]), fm(inp["rw_mu"][0])], axis=1))
    bd = np.zeros((128, 128), f)
    bd[:64, :64] = 1.0
    bd[64:, 64:] = 1.0
    out["rwcstf"] = np.ascontiguousarray(np.concatenate([np.eye(128, dtype=f), bd], axis=1))
    pi = np.arange(64)[:, None]
    fi = np.arange(64)[None, :]
    gm = np.zeros((64, 2, 5, 2, 64), f)
    for d_ in range(2):
        lt = (fi < pi) if d_ == 0 else (fi > pi)
        st = (pi < fi) if d_ == 0 else (pi > fi)
        se = (pi <= fi) if d_ == 0 else (pi >= fi)
        for h_ in range(2):
            gm[:, d_, 0, h_] = -1.0 * lt
            gm[:, d_, 1, h_] = -1.0 * st
            gm[:, d_, 2, h_] = -1.0 * st
            gm[:, d_, 3, h_] = 1.0 * se
            gm[:, d_, 4, h_] = 1.0 * se
    out["rwgmask"] = np.ascontiguousarray(gm.reshape(64, -1))
    return out


def _na_strips(rpb):
    MASK = np.float32(-30000.0)
    kc = np.arange(64)[:, None]
    qc = np.arange(64)[None, :]
    cstart = np.clip(qc - 8, 0, 48)
    col_ok = (kc >= cstart) & (kc < cstart + 16)
    dc_idx = np.clip(kc - qc + 15, 0, 30)
    R = np.where(col_ok[None, None], rpb[:, :, dc_idx], MASK)
    maskblk = np.full((16, 64, 64), MASK, np.float32)
    strip = np.empty((16, 2, 64, 37, 64), np.float32)
    for half in range(2):
        for jj in range(23):
            d = 11 - jj + half
            strip[:, half, :, jj, :] = R[:, d + 7] if -4 <= d <= 3 else maskblk
        for jj in range(14):
            d = 6 - jj + half
            strip[:, half, :, 23 + jj, :] = R[:, d + 7] if -7 <= d <= 7 else maskblk
    return np.ascontiguousarray(strip.reshape(16, 128, 37 * 64))


def _prep_core(inp, b):
    f = np.float32
    xb = np.concatenate([np.asarray(inp["x"][b], f), np.asarray(inp["ctx"][b], f)], axis=0)
    hin = np.ascontiguousarray(xb.T).reshape(NC_, 128, T)
    cond = np.stack([np.asarray(inp["c"][b], f), np.asarray(inp["c_ctx"], f)], axis=-1)
    cond = np.ascontiguousarray(cond.reshape(NC_, 128, 2).transpose(1, 0, 2))
    return {"hin": hin, "cond": cond}


def build_full():
    p = Prog(list(range(DEPTH)))
    p.setup()
    mixers = [p.na_mixer, p.conv_mixer, p.gla_mixer, p.rwkv_mixer]
    for L in range(DEPTH):
        last = L == DEPTH - 1
        p.ada(L)
        p.ffn(L, 0, 0)
        mixers[L % 4](L, last)
        p.ffn(L, 2, 1, blocks=([0, 1, 2, 3] if last else range(5)))
    p.store()
    p.es_H.close()
    p.kb.es.close()
    return p


def kernel(**inputs):
    common = _prep_common(inputs)
    p = build_full()
    in_maps = []
    for b in range(8):
        m = dict(common)
        m.update(_prep_core(inputs, b))
        in_maps.append(m)
    res = run_bass_kernel_spmd(p.nc, in_maps, core_ids=list(range(8)))
    out = np.stack([np.asarray(r["hout"]).reshape(D, T)[:, :SEQ].T for r in res.results], axis=0)
    return np.ascontiguousarray(out.astype(np.float32))
```
